# Optimizing a Trainium2 kernel written in Bass

```python
import math
import jax, jax.numpy as jnp
from jax import lax
import numpy as np

D_MODEL = 1024
BATCH = 4
SEQ = 8192
DEPTH = 1

HGRN_HEADS = 8
HGRN_DK = 128
HGRN_DV = D_MODEL // HGRN_HEADS
HGRN_KEY_WIDTH = HGRN_HEADS * HGRN_DK
HGRN_WIDTH = HGRN_HEADS * HGRN_DV
HGRN_CHUNK = 64

HYENA_WIDTH = D_MODEL
HYENA_SHORT = 3
HYENA_BANDS = 16
HYENA_EMB = 1 + 2 * HYENA_BANDS
HYENA_FILTER_HIDDEN = 64
HYENA_MIN_DECAY = -math.log(1e-2) / 1.5
HYENA_MAX_DECAY = -math.log(1e-2) / 0.3

PEER_HEADS = 8
PEER_NKEYS = 128
PEER_EXPERTS = PEER_NKEYS * PEER_NKEYS
PEER_QDIM = 256
PEER_TOPK = 16
PEER_TOKEN_BLOCK = 128

N_BRANCHES = 2
IN_SPLITS = (HGRN_KEY_WIDTH, 2 * HGRN_KEY_WIDTH, 3 * HGRN_KEY_WIDTH,
             3 * HGRN_KEY_WIDTH + HGRN_WIDTH, 3 * HGRN_KEY_WIDTH + 2 * HGRN_WIDTH,
             3 * HGRN_KEY_WIDTH + 2 * HGRN_WIDTH + 3 * HYENA_WIDTH)
IN_COLS = 3 * HGRN_KEY_WIDTH + 2 * HGRN_WIDTH + 3 * HYENA_WIDTH + N_BRANCHES * D_MODEL
RMS_EPS = 1e-6

kernel_name = "hgrn2_hyena_peer_hybrid_block"


def rmsnorm(x, g):
    xf = x.astype(jnp.float32)
    y = xf * lax.rsqrt(jnp.mean(xf * xf, axis=-1, keepdims=True) + RMS_EPS)
    return (y * g.astype(jnp.float32)).astype(x.dtype)


def hgrn2_chunk_scan(q, k, logf, v):
    n, s, h, dk = q.shape
    dv = v.shape[-1]
    nc = s // HGRN_CHUNK

    def to_chunks(a):
        return a.reshape(n, nc, HGRN_CHUNK, h, a.shape[-1]).transpose(1, 0, 3, 2, 4)

    qc, kc, fc, vc = to_chunks(q), to_chunks(k), to_chunks(logf), to_chunks(v)
    incl = jnp.tril(jnp.ones((HGRN_CHUNK, HGRN_CHUNK), dtype=bool))[:, :, None]

    def step(state, inp):
        qb, kb, fb, vb = inp
        b = jnp.cumsum(fb, axis=2)
        b_last = b[:, :, -1:, :]
        inter = jnp.einsum('nhtk,nhkv->nhtv', qb * jnp.exp(b), state)
        diff = b[:, :, :, None, :] - b[:, :, None, :, :]
        decay = jnp.exp(jnp.where(incl, diff, -jnp.inf))
        scores = jnp.einsum('nhtk,nhsk,nhtsk->nhts', qb, kb, decay)
        intra = jnp.einsum('nhts,nhsv->nhtv', scores, vb)
        new_state = (jnp.exp(b_last[:, :, 0, :, None]) * state
                     + jnp.einsum('nhsk,nhsv->nhkv', kb * jnp.exp(b_last - b), vb))
        return new_state, inter + intra

    s0 = jnp.zeros((n, h, dk, dv), jnp.float32)
    _, out = lax.scan(step, s0, (qc, kc, fc, vc))
    return out.transpose(1, 0, 3, 2, 4).reshape(n, s, h, dv)


def centred_short_conv(x, w, b):
    L = x.shape[1]
    pad = HYENA_SHORT // 2
    xp = jnp.pad(x, ((0, 0), (pad, pad), (0, 0)))
    y = b
    for j in range(HYENA_SHORT):
        y = y + xp[:, j:j + L] * w[j]
    return y


def hyena_filters(seq_len, w1, b1, freq1, w2, b2, freq2, w3, decay_rate):
    f32 = jnp.float32
    pos = jnp.arange(seq_len, dtype=f32)
    t = pos / max(seq_len - 1, 1)
    bands = jnp.linspace(1e-4, HYENA_BANDS - 1, HYENA_BANDS, dtype=f32)
    ang = (2.0 * math.pi / seq_len) * pos[:, None] * bands[None, :]
    z = jnp.concatenate([t[:, None], jnp.cos(ang), -jnp.sin(ang)], axis=-1)
    hid = jnp.sin(freq1.astype(f32) * (z @ w1.astype(f32) + b1.astype(f32)))
    hid = jnp.sin(freq2.astype(f32) * (hid @ w2.astype(f32) + b2.astype(f32)))
    filt = (hid @ w3.astype(f32)) * jnp.exp(-t[:, None] * jnp.abs(decay_rate.astype(f32))[None, :])
    return filt[:, :HYENA_WIDTH], filt[:, HYENA_WIDTH:]


def bidirectional_fftconv(u, h_fwd, h_bwd, bias):
    L = u.shape[1]
    kern = jnp.concatenate([h_fwd, jnp.zeros_like(h_fwd[:1]), h_bwd[1:][::-1]], axis=0)
    uf32 = u.astype(jnp.float32)
    uf = jnp.fft.rfft(uf32, n=2 * L, axis=1)
    kf = jnp.fft.rfft(kern, n=2 * L, axis=0)
    y = jnp.fft.irfft(uf * kf[None], n=2 * L, axis=1)[:, :L]
    return (y + uf32 * bias.astype(jnp.float32)).astype(u.dtype)


def hybrid_mixer(xn, w_in, lb, hgrn_norm_g, conv_w, conv_b, filt_w1, filt_b1, filt_freq1,
                 filt_w2, filt_b2, filt_freq2, filt_w3, filt_decay, hyena_bias,
                 w_branch_a, w_branch_b, w_out):
    B, L, _ = xn.shape
    f32 = jnp.float32
    proj = jnp.einsum('bld,dc->blc', xn, w_in)
    q, zf_fwd, zf_bwd, inp, og, hy, gates = jnp.split(proj, IN_SPLITS, axis=-1)

    def heads(a, d):
        return a.astype(f32).reshape(B, L, HGRN_HEADS, d)

    lbh = lb.reshape(HGRN_HEADS, HGRN_DK)

    def forget(z):
        zh = heads(z, HGRN_DK)
        logf = jnp.log(lbh + (1.0 - lbh) * jax.nn.sigmoid(zh))
        k = (1.0 - lbh) * jax.nn.sigmoid(-zh)
        return logf, k

    logf_f, k_f = forget(zf_fwd)
    logf_b, k_b = forget(zf_bwd)
    qh = jax.nn.silu(heads(q, HGRN_DK))
    vh = heads(inp, HGRN_DV)
    rev = lambda a: a[:, ::-1]
    o2 = hgrn2_chunk_scan(jnp.concatenate([qh, rev(qh)], axis=0),
                          jnp.concatenate([k_f, rev(k_b)], axis=0),
                          jnp.concatenate([logf_f, rev(logf_b)], axis=0),
                          jnp.concatenate([vh, rev(vh)], axis=0))
    o = o2[:B] + rev(o2[B:])
    o = rmsnorm(o, hgrn_norm_g).reshape(B, L, HGRN_WIDTH)
    y_a = jnp.einsum('blc,cd->bld', (o * jax.nn.silu(og.astype(f32))).astype(xn.dtype), w_branch_a)

    hy = centred_short_conv(hy, conv_w, conv_b)
    x0, x1, v = jnp.split(hy, 3, axis=-1)
    h_f, h_b = hyena_filters(L, filt_w1, filt_b1, filt_freq1, filt_w2, filt_b2, filt_freq2,
                             filt_w3, filt_decay)
    y_b = x0 * bidirectional_fftconv(v * x1, h_f, h_b, hyena_bias)
    y_b = jnp.einsum('blc,cd->bld', y_b, w_branch_b)

    g_a, g_b = jnp.split(gates, N_BRANCHES, axis=-1)
    merged = jax.nn.sigmoid(g_a) * y_a + jax.nn.sigmoid(g_b) * y_b
    return jnp.einsum('bld,de->ble', merged, w_out)


def peer_ffn(xn, w_q, subkeys, u_tab, v_tab):
    B, L, D = xn.shape
    blocks = xn.reshape(-1, PEER_TOKEN_BLOCK, D)
    sk = subkeys.astype(jnp.float32)

    def block(xb):
        T = xb.shape[0]
        qh = (xb @ w_q).astype(jnp.float32).reshape(T, PEER_HEADS, 2, PEER_QDIM // 2)
        s = jnp.einsum('thpc,hpnc->thpn', qh, sk)
        s1, i1 = lax.top_k(s[:, :, 0], PEER_TOPK)
        s2, i2 = lax.top_k(s[:, :, 1], PEER_TOPK)
        cand_s = (s1[..., :, None] + s2[..., None, :]).reshape(T, PEER_HEADS, PEER_TOPK * PEER_TOPK)
        cand_i = (i1[..., :, None] * PEER_NKEYS + i2[..., None, :]).reshape(T, PEER_HEADS, PEER_TOPK * PEER_TOPK)
        top_s, pos = lax.top_k(cand_s, PEER_TOPK)
        experts = jnp.take_along_axis(cand_i, pos, axis=-1)
        g = jax.nn.softmax(top_s, axis=-1).astype(xb.dtype)
        u = jnp.take(u_tab, experts, axis=0)
        v = jnp.take(v_tab, experts, axis=0)
        act = jax.nn.gelu(jnp.einsum('td,thkd->thk', xb, u)) * g
        return jnp.einsum('thk,thkd->td', act, v)

    return lax.map(block, blocks).reshape(B, L, D)


def setup_inputs(seed: int = 0) -> dict:
    key = jax.random.key(seed)
    ks = jax.random.split(key, 26)

    def nrm(k, shape, scale):
        return jax.random.normal(k, shape, jnp.float32) * scale

    decay_base = jnp.linspace(HYENA_MIN_DECAY, HYENA_MAX_DECAY, 2 * HYENA_WIDTH, dtype=jnp.float32)
    return {
        "x": nrm(ks[0], (BATCH, SEQ, D_MODEL), 1.0),
        "norm_mix_g": 1.0 + nrm(ks[1], (DEPTH, D_MODEL), 0.02),
        "w_in": nrm(ks[2], (DEPTH, D_MODEL, IN_COLS), D_MODEL ** -0.5),
        "hgrn_lb_logits": nrm(ks[3], (DEPTH + 1, HGRN_KEY_WIDTH), 0.1),
        "hgrn_norm_g": 1.0 + nrm(ks[4], (DEPTH, HGRN_DV), 0.02),
        "hyena_conv_w": nrm(ks[5], (DEPTH, HYENA_SHORT, 3 * HYENA_WIDTH), HYENA_SHORT ** -0.5),
        "hyena_conv_b": nrm(ks[6], (DEPTH, 3 * HYENA_WIDTH), 0.02),
        "filt_w1": nrm(ks[7], (DEPTH, HYENA_EMB, HYENA_FILTER_HIDDEN), HYENA_EMB ** -0.5),
        "filt_b1": nrm(ks[8], (DEPTH, HYENA_FILTER_HIDDEN), 0.1),
        "filt_freq1": 1.0 + nrm(ks[9], (DEPTH, HYENA_FILTER_HIDDEN), 0.02),
        "filt_w2": nrm(ks[10], (DEPTH, HYENA_FILTER_HIDDEN, HYENA_FILTER_HIDDEN), HYENA_FILTER_HIDDEN ** -0.5),
        "filt_b2": nrm(ks[11], (DEPTH, HYENA_FILTER_HIDDEN), 0.1),
        "filt_freq2": 1.0 + nrm(ks[12], (DEPTH, HYENA_FILTER_HIDDEN), 0.02),
        "filt_w3": nrm(ks[13], (DEPTH, HYENA_FILTER_HIDDEN, 2 * HYENA_WIDTH), 0.02 * HYENA_FILTER_HIDDEN ** -0.5),
        "filt_decay": decay_base[None, :] + nrm(ks[14], (DEPTH, 2 * HYENA_WIDTH), 0.1),
        "hyena_bias": nrm(ks[15], (DEPTH, HYENA_WIDTH), 0.1),
        "w_branch_a": nrm(ks[16], (DEPTH, HGRN_WIDTH, D_MODEL), HGRN_WIDTH ** -0.5),
        "w_branch_b": nrm(ks[17], (DEPTH, HYENA_WIDTH, D_MODEL), HYENA_WIDTH ** -0.5),
        "w_out": nrm(ks[18], (DEPTH, D_MODEL, D_MODEL), D_MODEL ** -0.5),
        "norm_ffn_g": 1.0 + nrm(ks[19], (DEPTH, D_MODEL), 0.02),
        "peer_w_q": nrm(ks[20], (DEPTH, D_MODEL, PEER_HEADS * PEER_QDIM), D_MODEL ** -0.5),
        "peer_subkeys": nrm(ks[21], (DEPTH, PEER_HEADS, 2, PEER_NKEYS, PEER_QDIM // 2), (PEER_QDIM // 2) ** -0.5),
        "peer_u": nrm(ks[22], (DEPTH, PEER_EXPERTS, D_MODEL), D_MODEL ** -0.5),
        "peer_v": nrm(ks[23], (DEPTH, PEER_EXPERTS, D_MODEL), 0.2),
        "norm_final_g": 1.0 + nrm(ks[24], (D_MODEL,), 0.02),
    }


def reference(x, norm_mix_g, w_in, hgrn_lb_logits, hgrn_norm_g, hyena_conv_w, hyena_conv_b,
              filt_w1, filt_b1, filt_freq1, filt_w2, filt_b2, filt_freq2, filt_w3, filt_decay,
              hyena_bias, w_branch_a, w_branch_b, w_out, norm_ffn_g, peer_w_q, peer_subkeys,
              peer_u, peer_v, norm_final_g):
    lb_table = jnp.cumsum(jax.nn.softmax(hgrn_lb_logits.astype(jnp.float32), axis=0), axis=0)
    h = x
    for layer in range(DEPTH):
        h = h + hybrid_mixer(rmsnorm(h, norm_mix_g[layer]), w_in[layer], lb_table[layer],
                             hgrn_norm_g[layer], hyena_conv_w[layer], hyena_conv_b[layer],
                             filt_w1[layer], filt_b1[layer], filt_freq1[layer],
                             filt_w2[layer], filt_b2[layer], filt_freq2[layer],
                             filt_w3[layer], filt_decay[layer], hyena_bias[layer],
                             w_branch_a[layer], w_branch_b[layer], w_out[layer])
        h = h + peer_ffn(rmsnorm(h, norm_ffn_g[layer]), peer_w_q[layer], peer_subkeys[layer],
                         peer_u[layer], peer_v[layer])
    return rmsnorm(h, norm_final_g)
```

```python
import math
import numpy as np
from contextlib import ExitStack
import concourse.bass as bass
import concourse.mybir as mybir
from concourse.bass_utils import run_bass_kernel_spmd

F32 = mybir.dt.float32
BF16 = mybir.dt.bfloat16
U32 = mybir.dt.uint32
ALU = mybir.AluOpType
AF = mybir.ActivationFunctionType
AX = mybir.AxisListType

L = 8192
D = 1024
NCOL = 10240
TT = 512
NTILE = L // TT
EPS = 1e-6

ENGS = ("pe", "act", "dve", "pool", "sp")
ENGMAP = {"pe": "tensor", "act": "scalar", "dve": "vector", "pool": "gpsimd", "sp": "sync"}
SEM_LIMIT = 30000
NDMA = 32


class Em:
    def __init__(self, nc, es):
        self.nc = nc
        self.es = es
        self.q = {e: [] for e in ENGS}
        self.sems = {e: [es.enter_context(nc.semaphore(f"s_{e}_0"))] for e in ENGS}
        self.cnt = {e: 0 for e in ENGS}
        self.dsem = [es.enter_context(nc.semaphore(f"d_{i}")) for i in range(NDMA)]
        self.dcnt = [0] * NDMA
        self.dnext = 0
        self.waited = {e: {} for e in ENGS}
        self.lastw = {}
        self.readers = {}
        self.n_inst = 0
        self.n_wait = 0

    def _tok_new(self, eng):
        if self.cnt[eng] >= SEM_LIMIT:
            self.sems[eng].append(
                self.es.enter_context(self.nc.semaphore(f"s_{eng}_{len(self.sems[eng])}")))
            self.cnt[eng] = 0
        self.cnt[eng] += 1
        return (self.sems[eng][-1], self.cnt[eng])

    NOKEYS = frozenset(["WIN", "QD0", "QD1", "KD0", "KD1", "KDTM0", "KDTM1", "VT", "OGT", "HYT", "GT", "HF", "UT",
                        "X0T", "YCT", "OT", "AT", "DEC0", "DEC1", "UTS", "VBF", "UBF"])

    def _deps(self, reads, writes):
        reads = [k for k in reads if k not in self.NOKEYS]
        writes = [k for k in writes if k not in self.NOKEYS]
        deps = []
        for k in reads:
            lw = self.lastw.get(k)
            if lw is not None:
                deps.append(lw)
        for k in writes:
            lw = self.lastw.get(k)
            if lw is not None:
                deps.append(lw)
            deps.extend(self.readers.get(k, ()))
        return deps

    def _emit_waits(self, eng, deps, skip_sems=()):
        w = self.waited[eng]
        need = {}
        for (sem, val) in deps:
            sid = id(sem)
            if sid in skip_sems:
                continue
            if w.get(sid, 0) >= val:
                continue
            if sid not in need or need[sid][1] < val:
                need[sid] = (sem, val)
        for sid, (sem, val) in need.items():
            w[sid] = val
            self.q[eng].append(("wait", sem, val))
            self.n_wait += 1

    def _record(self, tok, reads, writes):
        reads = [k for k in reads if k not in self.NOKEYS]
        writes = [k for k in writes if k not in self.NOKEYS]
        for k in reads:
            self.readers.setdefault(k, []).append(tok)
        for k in writes:
            self.lastw[k] = tok
            self.readers[k] = []

    def op(self, eng, fn, reads=(), writes=()):
        deps = self._deps(reads, writes)
        skip = tuple(id(s) for s in self.sems["pe"]) if eng == "pe" else ()
        self._emit_waits(eng, deps, skip)
        tok = self._tok_new(eng)
        self.q[eng].append(("op", fn, tok, 1))
        self._record(tok, reads, writes)
        self.n_inst += 1
        return tok

    def dma(self, eng, fn, reads=(), writes=()):
        deps = self._deps(reads, writes)
        slot = self.dnext
        self.dnext = (self.dnext + 1) % NDMA
        if self.dcnt[slot] > 0:
            deps.append((self.dsem[slot], self.dcnt[slot]))
        self._emit_waits(eng, deps)
        self.dcnt[slot] += 16
        tok = (self.dsem[slot], self.dcnt[slot])
        self.q[eng].append(("op", fn, tok, 16))
        self._record(tok, reads, writes)
        self.n_inst += 1
        return tok

    def flush(self):
        nc = self.nc
        final = []
        for i in range(NDMA):
            if self.dcnt[i]:
                final.append((self.dsem[i], self.dcnt[i]))
        for e in ENGS:
            if self.cnt[e]:
                final.append((self.sems[e][-1], self.cnt[e]))
        self._emit_waits("sp", final)
        with nc.Block() as block:
            for e in ENGS:
                items = self.q[e]

                def body(engine, items=items):
                    for it in items:
                        if it[0] == "wait":
                            engine.wait_ge(it[1], it[2])
                        else:
                            it[1](engine).then_inc(it[2][0], it[3])
                getattr(block, ENGMAP[e])(body)
        self.q = {e: [] for e in ENGS}
        self.lastw = {}
        self.readers = {}


class Rot:
    def __init__(self, sbf, name, n, shape, dt):
        self.t = [sbf(f"{name}{i}", shape, dt) for i in range(n)]
        self.k = [f"{name}{i}" for i in range(n)]
        self.i = -1

    def nxt(self):
        self.i = (self.i + 1) % len(self.t)
        return self.t[self.i], self.k[self.i]


def host_consts():
    c = {}
    c["ident"] = np.eye(128, dtype=np.float32)
    s = np.arange(64)
    mf = (s[:, None] <= s[None, :]).astype(np.float32)
    mb = (s[:, None] >= s[None, :]).astype(np.float32)
    c["maskf"] = np.ascontiguousarray(np.broadcast_to(mf[:, None, :], (64, 8, 64))).reshape(64, 512)
    c["maskb"] = np.ascontiguousarray(np.broadcast_to(mb[:, None, :], (64, 8, 64))).reshape(64, 512)
    t = np.arange(TT)
    rf = np.ones((128, TT), np.float32); rf[:, t % 64 == 0] = 0
    rb = np.ones((128, TT), np.float32); rb[:, t % 64 == 63] = 0
    c["rmf"] = rf
    c["rmb"] = rb
    n = np.arange(128, dtype=np.float64)
    ang = 2 * np.pi * np.outer(n, n) / 128.0
    Fr = np.cos(ang); Fi = -np.sin(ang)
    c["FRI"] = np.concatenate([Fr, Fi], 1).astype(np.float32)
    c["FRnI"] = np.concatenate([Fr, -Fi], 1).astype(np.float32)
    c["FIR"] = np.concatenate([Fi, Fr], 1).astype(np.float32)
    c["nFI"] = (-Fi).astype(np.float32)
    angt = 2 * np.pi * np.outer(n, n) / 16384.0
    Tr = np.cos(angt); Ti = -np.sin(angt)
    c["TW"] = np.concatenate([Tr, Ti], 1).astype(np.float32)
    c["TWc"] = np.concatenate([Tr, -Ti], 1).astype(np.float32)
    pos = np.arange(L, dtype=np.float32)
    tpos = pos / np.float32(L - 1)
    bands = np.linspace(1e-4, 15, 16, dtype=np.float32)
    angz = (np.float32(2.0 * math.pi / L) * pos[:, None]) * bands[None, :]
    z = np.concatenate([tpos[:, None], np.cos(angz), -np.sin(angz)], -1).astype(np.float32)
    c["zT"] = np.ascontiguousarray(z.T)
    c["tpos"] = np.ascontiguousarray(tpos.reshape(1, L))
    c["iota"] = np.ascontiguousarray(np.broadcast_to(np.arange(128, dtype=np.float32)[None, :], (128, 128)))
    return c


def build(stop_after=99, dbg=(), skip=(), ext_in=()):
    nc = bass.Bass("TRN2", target_bir_lowering=False)
    _uqc = [0]

    def uq(n):
        _uqc[0] += 1
        return f"{n}_u{_uqc[0]}"
    EI = dict(kind="ExternalInput")
    def din(name, shape, dt=F32):
        return nc.dram_tensor(name, list(shape), dt, **EI).ap()
    def dscr(name, shape, dt):
        kind = "ExternalOutput" if name in dbg else ("ExternalInput" if name in ext_in else "Internal")
        return nc.dram_tensor(name, list(shape), dt, kind=kind).ap()

    x = din("x", [L, D])
    xh = din("xh", [L // 2, D])
    norm_mix_g = din("norm_mix_g", [1, D])
    w_in = din("w_in", [D, NCOL])
    lbl = din("hgrn_lb_logits", [2, D])
    ident = din("ident", [128, 128])
    maskf = din("maskf", [64, 512]); maskb = din("maskb", [64, 512])
    rmf = din("rmf", [128, TT]); rmb = din("rmb", [128, TT])
    out = nc.dram_tensor("out", [L // 2, D], F32, kind="ExternalOutput").ap()

    WIN = dscr("WIN", [D, NCOL], BF16)
    QD = [dscr(f"QD{d}", [8, 128, L], BF16) for d in range(2)]
    KD = [dscr(f"KD{d}", [8, 128, L], BF16) for d in range(2)]
    KDTM = [dscr(f"KDTM{d}", [L, D], BF16) for d in range(2)]
    DEC = [dscr(f"DEC{d}", [128, 8, 128], F32) for d in range(2)]
    VT = dscr("VT", [L, D], BF16)
    OGT = dscr("OGT", [D, L], BF16)
    HYT = dscr("HYT", [3 * D, L], F32)
    GT = dscr("GT", [2 * D, L], BF16)
    OF = dscr("OF", [8, 128, L], F32)
    OT = dscr("OT", [8, 128, L], F32)
    filt_w1 = din("filt_w1", [33, 64]); filt_w2 = din("filt_w2", [64, 64]); filt_w3 = din("filt_w3", [64, 2048])
    filt_vec = din("filt_vec", [4, 64])
    filt_decay = din("filt_decay", [1, 2048]); hyena_bias = din("hyena_bias", [1, 1024])
    conv_w = din("conv_w", [3, 3072]); conv_b = din("conv_b", [1, 3072])
    zT = din("zT", [33, L]); tpos = din("tpos", [1, L]); sel = din("sel", [128, 2])
    FRI = din("FRI", [128, 256]); FRnI = din("FRnI", [128, 256]); FIR = din("FIR", [128, 256]); nFI = din("nFI", [128, 128])
    TW = din("TW", [128, 256]); TWc = din("TWc", [128, 256])
    HF = dscr("HF", [2048, L], BF16)
    UT = dscr("UT", [D, L], BF16)
    X0T = dscr("X0T", [D, L], F32)
    KF = dscr("KF", [D, 128, 2, 128], F32)
    YCT = dscr("YCT", [D, L], F32)
    hgrn_norm_g = din("hgrn_norm_g", [1, 128])
    w_branch_a = din("w_branch_a", [D, D]); w_branch_b = din("w_branch_b", [D, D]); w_out = din("w_out", [D, D])
    norm_ffn_g = din("norm_ffn_g", [1, D]); norm_final_g = din("norm_final_g", [1, D])
    peer_w_q = din("peer_w_q", [D, 2048]); peer_sk = din("peer_sk", [16, 128, 128])
    peer_u = din("peer_u", [16384, D]); peer_v = din("peer_v", [16384, D])
    iota = din("iota", [128, 128])
    AT = dscr("AT", [D, L // 2], BF16) if "AT" in dbg else None
    MG = dscr("MG", [D, L // 2], BF16) if "MG" in dbg else None
    PEERO = dscr("PEERO", [L // 2, D], F32) if "PEERO" in dbg else None
    H1 = dscr("H1", [L // 2, D], F32)
    VBF = dscr("VBF", [16384, D], BF16)
    UTS = dscr("UTS", [128, 128, 8, 128], BF16)
    TOK0 = 0
    OTh = OT[:, :, 0:L // 2]; OGTh = OGT[:, 0:L // 2]; X0Th = X0T[:, 0:L // 2]; YCTh = YCT[:, 0:L // 2]; GTh = GT[:, 0:L // 2]

    es0 = ExitStack()
    with es0:
        em = Em(nc, es0)

        for r in range(8):
            em.dma("pool", lambda e, r=r: e.dma_start(out=WIN[r * 128:(r + 1) * 128, :], in_=w_in[r * 128:(r + 1) * 128, :]),
                   writes=["WIN"])
        em.flush()
        if stop_after <= 0:
            return nc

        if 1 not in skip:
          with ExitStack() as es:
              sbf = lambda n, s, d: es.enter_context(nc.sbuf_tensor(uq(n), list(s), d))
              psf = lambda n, s, d: es.enter_context(nc.psum_tensor(uq(n), list(s), d))
              gt = sbf("gt", [128, D], F32)
              idf = sbf("idf", [128, 128], F32)
              idb = sbf("idb", [128, 128], BF16)
              lb2 = sbf("lb2", [128, 2, 8], F32)
              lbt = sbf("lbt", [128, 8], F32)
              olt = sbf("olt", [128, 8], F32)
              nolt = sbf("nolt", [128, 8], F32)
              rmf_t = sbf("rmf_t", [128, TT], F32)
              rmb_t = sbf("rmb_t", [128, TT], F32)
              dec_t = [sbf(f"dec_t{d}", [128, 8, 128], F32) for d in range(2)]
              xt = sbf("xt", [128, 4, D], F32)
              sqj = sbf("sqj", [128, D], F32)
              ss = sbf("ss", [128, 4], F32)
              rstd = sbf("rstd", [128, 4], F32)
              xn = sbf("xn", [128, 4, D], BF16)
              xnT = Rot(sbf, "xnT", 2, [128, 8, TT], BF16)
              wg = Rot(sbf, "wg", 2, [128, 8, 1024], BF16)
              QS = sbf("QS", [128, 8, TT], F32)
              SG = Rot(sbf, "SG", 2, [128, TT], F32)
              KK = Rot(sbf, "KK", 2, [128, TT], F32)
              LF = Rot(sbf, "LF", 2, [128, TT], F32)
              BB = Rot(sbf, "BB", 2, [128, TT], F32)
              E1 = Rot(sbf, "E1", 2, [128, TT], F32)
              E2 = Rot(sbf, "E2", 2, [128, TT], F32)
              QDs = Rot(sbf, "QDs", 2, [128, TT], BF16)
              KDs = Rot(sbf, "KDs", 2, [128, TT], BF16)
              KTs = Rot(sbf, "KTs", 2, [128, 4, 128], BF16)
              OB = Rot(sbf, "OB", 3, [128, TT], BF16)
              OFt = Rot(sbf, "OFt", 3, [128, TT], F32)
              pT = psf("pT", [128, 8, 128], BF16)
              pM = [psf(f"pM{i}", [128, TT], F32) for i in range(4)]
              pK = [psf(f"pK{i}", [128, 4, 128], BF16) for i in range(2)]
              pmi = [0]

              em.dma("sp", lambda e: e.dma_start(out=gt[:], in_=norm_mix_g.partition_broadcast(128)), writes=["gt"])
              em.dma("sp", lambda e: e.dma_start(out=idf[:], in_=ident), writes=["idf"])
              em.dma("sp", lambda e: e.dma_start(out=rmf_t[:], in_=rmf), writes=["rmf"])
              em.dma("sp", lambda e: e.dma_start(out=rmb_t[:], in_=rmb), writes=["rmb"])
              em.dma("sp", lambda e: e.dma_start(out=lb2[:], in_=lbl.rearrange("t (h k) -> k t h", k=128), allow_slow_non_contiguous=True), writes=["lb2"])
              em.op("dve", lambda e: e.tensor_copy(out=idb[:], in_=idf[:]), reads=["idf"], writes=["idb"])
              em.op("dve", lambda e: e.tensor_tensor(out=lbt[:], in0=lb2[:, 1, :], in1=lb2[:, 0, :], op=ALU.subtract), reads=["lb2"], writes=["lbt"])
              em.op("act", lambda e: e.activation(out=lbt[:], in_=lbt[:], func=AF.Exp), reads=["lbt"], writes=["lbt"])
              em.op("dve", lambda e: e.tensor_scalar(out=lbt[:], in0=lbt[:], scalar1=1.0, scalar2=None, op0=ALU.add), reads=["lbt"], writes=["lbt"])
              em.op("dve", lambda e: e.reciprocal(out=lbt[:], in_=lbt[:]), reads=["lbt"], writes=["lbt"])
              em.op("dve", lambda e: e.tensor_scalar(out=olt[:], in0=lbt[:], scalar1=-1.0, scalar2=1.0, op0=ALU.mult, op1=ALU.add), reads=["lbt"], writes=["olt"])
              em.op("dve", lambda e: e.tensor_scalar(out=nolt[:], in0=olt[:], scalar1=-1.0, scalar2=None, op0=ALU.mult), reads=["olt"], writes=["nolt"])

              def next_pm():
                  pmi[0] = (pmi[0] + 1) % 4
                  return pM[pmi[0]], f"pM{pmi[0]}"

              ntile = NTILE if stop_after > 1 else 1
              for tt in range(ntile):
                  t0 = tt * TT
                  em.dma("sp", lambda e, t0=t0: e.dma_start(out=xt[:], in_=x[t0:t0 + TT, :].rearrange("(s p) d -> p s d", p=128)), writes=["xt"])
                  for s in range(4):
                      em.op("act", lambda e, s=s: e.activation(out=sqj[:], in_=xt[:, s, :], func=AF.Square, accum_out=ss[:, s:s + 1]),
                            reads=["xt"], writes=["sqj", "ss"])
                  em.op("act", lambda e: e.activation(out=rstd[:], in_=ss[:], func=AF.Ln, scale=1.0 / D, bias=EPS), reads=["ss"], writes=["rstd"])
                  em.op("act", lambda e: e.activation(out=rstd[:], in_=rstd[:], func=AF.Exp, scale=-0.5), reads=["rstd"], writes=["rstd"])
                  xT, xTk = xnT.nxt()
                  for s in range(4):
                      em.op("dve", lambda e, s=s: e.scalar_tensor_tensor(out=xn[:, s, :], in0=xt[:, s, :], scalar=rstd[:, s:s + 1], in1=gt[:], op0=ALU.mult, op1=ALU.mult),
                            reads=["xt", "rstd", "gt"], writes=[f"xn{s}"])
                      for k in range(8):
                          em.op("pe", lambda e, s=s, k=k: e.transpose(out=pT[:, k, :], in_=xn[:, s, k * 128:(k + 1) * 128], identity=idb[:]),
                                reads=[f"xn{s}", "idb"], writes=["pT"])
                      em.op("act" if s % 2 else "dve", lambda e, s=s, xT=xT: (e.activation(out=xT[:, :, s * 128:(s + 1) * 128], in_=pT[:], func=AF.Copy) if s % 2 else e.tensor_copy(out=xT[:, :, s * 128:(s + 1) * 128], in_=pT[:])),
                            reads=["pT"], writes=[xTk])
                  for g in range(10):
                      w, wk = wg.nxt()
                      em.dma("sp", lambda e, g=g, w=w: e.dma_start(out=w[:], in_=WIN[:, g * 1024:(g + 1) * 1024].rearrange("(k p) c -> p k c", p=128)),
                             reads=["WIN"], writes=[wk])
                      if g == 3:
                          for s in range(4):
                              for hf in range(2):
                                  pm, pmk = next_pm()
                                  for k in range(8):
                                      em.op("pe", lambda e, pm=pm, k=k, s=s, hf=hf, w=w, xT=xT: e.matmul(pm[:], lhsT=xT[:, k, s * 128:(s + 1) * 128], rhs=w[:, k, hf * 512:(hf + 1) * 512], start=(k == 0), stop=(k == 7)),
                                            reads=[xTk, wk], writes=[pmk])
                                  ob, obk = OB.nxt()
                                  em.op("dve", lambda e, pm=pm, ob=ob: e.tensor_copy(out=ob[:], in_=pm[:]), reads=[pmk], writes=[obk])
                                  em.dma("sp", lambda e, ob=ob, s=s, hf=hf, t0=t0: e.dma_start(out=VT[t0 + s * 128:t0 + (s + 1) * 128, hf * 512:(hf + 1) * 512], in_=ob[:]),
                                         reads=[obk], writes=["VT"])
                          continue
                      for cb in range(8):
                          pm, pmk = next_pm()
                          for k in range(8):
                              em.op("pe", lambda e, pm=pm, k=k, cb=cb, w=w, xT=xT: e.matmul(pm[:], lhsT=w[:, k, cb * 128:(cb + 1) * 128], rhs=xT[:, k, :], start=(k == 0), stop=(k == 7)),
                                    reads=[xTk, wk], writes=[pmk])
                          if g == 0:
                              em.op("act", lambda e, pm=pm, cb=cb: e.activation(out=QS[:, cb, :], in_=pm[:], func=AF.Silu), reads=[pmk], writes=[f"QS{cb}"])
                          elif g in (1, 2):
                              d = g - 1
                              h = cb
                              sg, sgk = SG.nxt(); kk, kkk = KK.nxt(); lf, lfk = LF.nxt(); bb, bbk = BB.nxt()
                              e1, e1k = E1.nxt(); e2, e2k = E2.nxt(); qd, qdk = QDs.nxt(); kd, kdk = KDs.nxt()
                              kt, ktk = KTs.nxt()
                              em.op("act", lambda e, pm=pm, sg=sg: e.activation(out=sg[:], in_=pm[:], func=AF.Sigmoid), reads=[pmk], writes=[sgk])
                              em.op("dve", lambda e, sg=sg, kk=kk, h=h: e.tensor_scalar(out=kk[:], in0=sg[:], scalar1=nolt[:, h:h + 1], scalar2=olt[:, h:h + 1], op0=ALU.mult, op1=ALU.add),
                                    reads=[sgk, "nolt", "olt"], writes=[kkk])
                              em.op("act", lambda e, sg=sg, lf=lf, h=h: e.activation(out=lf[:], in_=sg[:], func=AF.Ln, scale=olt[:, h:h + 1], bias=lbt[:, h:h + 1]),
                                    reads=[sgk, "olt", "lbt"], writes=[lfk])
                              if d == 0:
                                  em.op("dve", lambda e, bb=bb, lf=lf: e.tensor_tensor_scan(out=bb[:], data0=rmf_t[:], data1=lf[:], initial=0.0, op0=ALU.mult, op1=ALU.add),
                                        reads=[lfk, "rmf"], writes=[bbk])
                              else:
                                  em.op("dve", lambda e, bb=bb, lf=lf: e.tensor_tensor_scan(out=bb[:, ::-1], data0=rmb_t[:, ::-1], data1=lf[:, ::-1], initial=0.0, op0=ALU.mult, op1=ALU.add),
                                        reads=[lfk, "rmb"], writes=[bbk])
                              em.op("act", lambda e, bb=bb, e1=e1: e.activation(out=e1[:], in_=bb[:], func=AF.Exp), reads=[bbk], writes=[e1k])
                              em.op("act", lambda e, bb=bb, e2=e2: e.activation(out=e2[:], in_=bb[:], func=AF.Exp, scale=-1.0), reads=[bbk], writes=[e2k])
                              em.op("pool", lambda e, qd=qd, e1=e1, h=h: e.tensor_tensor(out=qd[:], in0=QS[:, h, :], in1=e1[:], op=ALU.mult), reads=[f"QS{h}", e1k], writes=[qdk])
                              em.op("pool", lambda e, kd=kd, e2=e2, kk=kk: e.tensor_tensor(out=kd[:], in0=kk[:], in1=e2[:], op=ALU.mult), reads=[kkk, e2k], writes=[kdk])
                              off = 63 if d == 0 else 0
                              em.op("dve", lambda e, e1=e1, d=d, h=h, tt=tt, off=off: e.tensor_copy(out=dec_t[d][:, h, tt * 8:(tt + 1) * 8], in_=e1[:, off::64]),
                                    reads=[e1k], writes=[f"dec{d}"])
                              em.dma("sp", lambda e, qd=qd, d=d, h=h, t0=t0: e.dma_start(out=QD[d][h, :, t0:t0 + TT], in_=qd[:]), reads=[qdk], writes=[f"QD{d}"])
                              em.dma("sp", lambda e, kd=kd, d=d, h=h, t0=t0: e.dma_start(out=KD[d][h, :, t0:t0 + TT], in_=kd[:]), reads=[kdk], writes=[f"KD{d}"])
                              pk = pK[h % 2]; pkk = f"pK{h % 2}"
                              for s in range(4):
                                  em.op("pe", lambda e, pk=pk, kd=kd, s=s: e.transpose(out=pk[:, s, :], in_=kd[:, s * 128:(s + 1) * 128], identity=idb[:]),
                                        reads=[kdk, "idb"], writes=[pkk])
                              em.op("dve", lambda e, pk=pk, kt=kt: e.tensor_copy(out=kt[:], in_=pk[:]), reads=[pkk], writes=[ktk])
                              em.dma("sp", lambda e, kt=kt, d=d, h=h, t0=t0: e.dma_start(out=KDTM[d][t0:t0 + TT, h * 128:(h + 1) * 128].rearrange("(s p) k -> p s k", p=128), in_=kt[:]),
                                     reads=[ktk], writes=[f"KDTM{d}"])
                          elif g == 4:
                              ob, obk = OB.nxt()
                              em.op("act", lambda e, pm=pm, ob=ob: e.activation(out=ob[:], in_=pm[:], func=AF.Silu), reads=[pmk], writes=[obk])
                              em.dma("sp", lambda e, ob=ob, cb=cb, t0=t0: e.dma_start(out=OGT[cb * 128:(cb + 1) * 128, t0:t0 + TT], in_=ob[:]), reads=[obk], writes=["OGT"])
                          elif g in (5, 6, 7):
                              of_, ofk = OFt.nxt()
                              r0 = (g - 5) * 1024 + cb * 128
                              em.op("dve", lambda e, pm=pm, of_=of_: e.tensor_copy(out=of_[:], in_=pm[:]), reads=[pmk], writes=[ofk])
                              em.dma("sp", lambda e, of_=of_, r0=r0, t0=t0: e.dma_start(out=HYT[r0:r0 + 128, t0:t0 + TT], in_=of_[:]), reads=[ofk], writes=["HYT"])
                          else:
                              ob, obk = OB.nxt()
                              r0 = (g - 8) * 1024 + cb * 128
                              em.op("act", lambda e, pm=pm, ob=ob: e.activation(out=ob[:], in_=pm[:], func=AF.Sigmoid), reads=[pmk], writes=[obk])
                              em.dma("sp", lambda e, ob=ob, r0=r0, t0=t0: e.dma_start(out=GT[r0:r0 + 128, t0:t0 + TT], in_=ob[:]), reads=[obk], writes=["GT"])
              for d in range(2):
                  em.dma("sp", lambda e, d=d: e.dma_start(out=DEC[d], in_=dec_t[d][:]), reads=[f"dec{d}"], writes=[f"DEC{d}"])
              em.flush()
        if stop_after <= 1:
            print("inst", em.n_inst, "waits", em.n_wait)
            return nc

        if 2 not in skip:
          with ExitStack() as es:
              sbf = lambda n, s, d: es.enter_context(nc.sbuf_tensor(uq(n), list(s), d))
              psf = lambda n, s, d: es.enter_context(nc.psum_tensor(uq(n), list(s), d))
              mk = [sbf("mkf", [64, 512], F32), sbf("mkb", [64, 512], F32)]
              dect = [sbf(f"dect{d}", [128, 8, 128], F32) for d in range(2)]
              S = sbf("S", [128, 8, 128], F32)
              Sb = sbf("Sb", [128, 8, 128], BF16)
              tmpS = sbf("tmpS", [128, 8, 128], F32)
              qdt = Rot(sbf, "qdt", 2, [128, 8, TT], BF16)
              kdt = Rot(sbf, "kdt", 2, [128, 8, TT], BF16)
              ktm = Rot(sbf, "ktm", 2, [64, 8, D], BF16)
              vtm = Rot(sbf, "vtm", 2, [64, 8, D], BF16)
              scb = Rot(sbf, "scb", 2, [64, 8, 64], BF16)
              Ot = Rot(sbf, "Ot", 2, [128, 8, TT], F32)
              Of = Rot(sbf, "Of", 2, [128, 8, TT], F32)
              pS = [psf(f"pS{i}", [64, 8, 64], F32) for i in range(2)]
              pO = [psf(f"pO{i}", [128, 8, 64], F32) for i in range(2)]
              pP = [psf(f"pP{i}", [128, 4, 128], F32) for i in range(2)]
              em.dma("sp", lambda e: e.dma_start(out=mk[0][:], in_=maskf), writes=["mk0"])
              em.dma("sp", lambda e: e.dma_start(out=mk[1][:], in_=maskb), writes=["mk1"])
              for d in range(2):
                  em.dma("sp", lambda e, d=d: e.dma_start(out=dect[d][:], in_=DEC[d]), writes=[f"dect{d}"])
              for d in range(2):
                  em.op("pool", lambda e: e.memset(S[:], 0.0), writes=["S"])
                  em.op("pool", lambda e: e.memset(Sb[:], 0.0), writes=["Sb"])
                  tiles = range(NTILE) if d == 0 else range(NTILE - 1, -1, -1)
                  for tt in tiles:
                      t0 = tt * TT
                      q_, qk = qdt.nxt(); k_, kk_ = kdt.nxt(); kt_, ktk = ktm.nxt(); v_, vk = vtm.nxt()
                      o_, ok = Ot.nxt()
                      em.dma("sp", lambda e, q_=q_, d=d, t0=t0: e.dma_start(out=q_[:], in_=QD[d][:, :, t0:t0 + TT].rearrange("h k t -> k h t")), writes=[qk])
                      em.dma("sp", lambda e, k_=k_, d=d, t0=t0: e.dma_start(out=k_[:], in_=KD[d][:, :, t0:t0 + TT].rearrange("h k t -> k h t")), writes=[kk_])
                      em.dma("sp", lambda e, kt_=kt_, d=d, t0=t0: e.dma_start(out=kt_[:], in_=KDTM[d][t0:t0 + TT, :].rearrange("(c s) k -> s c k", s=64)), writes=[ktk])
                      em.dma("sp", lambda e, v_=v_, t0=t0: e.dma_start(out=v_[:], in_=VT[t0:t0 + TT, :].rearrange("(c s) k -> s c k", s=64)), writes=[vk])
                      if d == 1:
                          f_, fk = Of.nxt()
                          em.dma("sp", lambda e, f_=f_, t0=t0: e.dma_start(out=f_[:], in_=OF[:, :, t0:t0 + TT].rearrange("h k t -> k h t")), reads=[f"OF{tt}"], writes=[fk])
                      chunks = range(8) if d == 0 else range(7, -1, -1)
                      for c in chunks:
                          gc = tt * 8 + c
                          cs = slice(c * 64, (c + 1) * 64)
                          ps_ = pS[gc % 2]; psk = f"pS{gc % 2}"
                          po_ = pO[gc % 2]; pok = f"pO{gc % 2}"
                          for h in range(8):
                              em.op("pe", lambda e, ps_=ps_, h=h, k_=k_, q_=q_, cs=cs: e.matmul(ps_[:, h, :], lhsT=k_[:, h, cs], rhs=q_[:, h, cs], start=True, stop=True),
                                    reads=[kk_, qk], writes=[psk])
                          sb_, sbk = scb.nxt()
                          em.op("dve", lambda e, sb_=sb_, ps_=ps_, d=d: e.tensor_tensor(out=sb_[:], in0=ps_[:], in1=mk[d][:].rearrange("p (h t) -> p h t", h=8), op=ALU.mult),
                                reads=[psk, f"mk{d}"], writes=[sbk])
                          for h in range(8):
                              hs = slice(h * 128, (h + 1) * 128)
                              em.op("pe", lambda e, po_=po_, h=h, v_=v_, sb_=sb_, c=c, hs=hs: e.matmul(po_[:, h, :], lhsT=v_[:, c, hs], rhs=sb_[:, h, :], start=True, stop=False),
                                    reads=[vk, sbk], writes=[pok])
                              em.op("pe", lambda e, po_=po_, h=h, q_=q_, cs=cs: e.matmul(po_[:, h, :], lhsT=Sb[:, h, :], rhs=q_[:, h, cs], start=False, stop=True),
                                    reads=["Sb", qk], writes=[pok])
                          if d == 0:
                              em.op("act", lambda e, o_=o_, po_=po_, cs=cs: e.activation(out=o_[:, :, cs], in_=po_[:], func=AF.Copy), reads=[pok], writes=[ok])
                          else:
                              em.op("dve", lambda e, o_=o_, po_=po_, f_=f_, cs=cs: e.tensor_tensor(out=o_[:, :, cs], in0=po_[:], in1=f_[:, :, cs], op=ALU.add), reads=[pok, fk], writes=[ok])
                          decb = dect[d][:, :, gc:gc + 1]
                          for hh in range(2):
                              pp = pP[hh]; ppk = f"pP{hh}"
                              for h4 in range(4):
                                  h = hh * 4 + h4
                                  hs = slice(h * 128, (h + 1) * 128)
                                  em.op("pe", lambda e, pp=pp, h4=h4, kt_=kt_, v_=v_, c=c, hs=hs: e.matmul(pp[:, h4, :], lhsT=kt_[:, c, hs], rhs=v_[:, c, hs], start=True, stop=True),
                                        reads=[ktk, vk], writes=[ppk])
                              h4s = slice(hh * 4, hh * 4 + 4)
                              em.op("dve", lambda e, pp=pp, h4s=h4s, decb=decb: e.tensor_tensor(out=tmpS[:, h4s, :], in0=pp[:], in1=decb[:, h4s, :].to_broadcast([128, 4, 128]), op=ALU.mult),
                                    reads=[ppk, f"dect{d}"], writes=[f"tmpS{hh}"])
                          em.op("pool", lambda e, decb=decb: e.tensor_tensor(out=S[:], in0=S[:], in1=decb.to_broadcast([128, 8, 128]), op=ALU.mult),
                                reads=["S", f"dect{d}"], writes=["S"])
                          em.op("pool", lambda e: e.tensor_tensor(out=S[:], in0=S[:], in1=tmpS[:], op=ALU.add), reads=["S", "tmpS0", "tmpS1"], writes=["S"])
                          em.op("act", lambda e: e.activation(out=Sb[:], in_=S[:], func=AF.Copy), reads=["S"], writes=["Sb"])
                      dst = OF if d == 0 else OT
                      em.dma("sp", lambda e, o_=o_, dst=dst, t0=t0: e.dma_start(out=dst[:, :, t0:t0 + TT].rearrange("h k t -> k h t"), in_=o_[:]), reads=[ok], writes=[f"OF{tt}" if d == 0 else "OT"])
              em.flush()
        print("inst", em.n_inst, "waits", em.n_wait)
        if stop_after <= 2:
            return nc
        if 3 not in skip:
          with ExitStack() as es:
            sbf = lambda n, s, d: es.enter_context(nc.sbuf_tensor(uq(n), list(s), d))
            psf = lambda n, s, d: es.enter_context(nc.psum_tensor(uq(n), list(s), d))
            w1t = sbf("w1t", [33, 64], F32); w2t = sbf("w2t", [64, 64], F32); w3t = sbf("w3t", [64, 2048], F32)
            fb = sbf("fb", [64, 4], F32)
            fs = sbf("fs", [64, 4], F32)
            dcy = sbf("dcy", [128, 16], F32)
            hbias = sbf("hbias", [128, 8], F32)
            tpb = sbf("tpb", [128, TT], F32)
            zt = Rot(sbf, "zt", 2, [33, TT], F32)
            ya = Rot(sbf, "ya", 2, [64, TT], F32)
            yb_ = Rot(sbf, "yb_", 2, [64, TT], F32)
            hd1 = Rot(sbf, "hd1", 2, [64, TT], F32)
            hd2 = Rot(sbf, "hd2", 2, [64, TT], F32)
            wn = Rot(sbf, "wn", 2, [128, TT], F32)
            fo = Rot(sbf, "fo", 2, [128, TT], F32)
            fob = Rot(sbf, "fob", 3, [128, TT], BF16)
            pF = [psf(f"pF{i}", [128, TT], F32) for i in range(3)]
            lag0 = sbf("lag0", [128, 16], F32); lagc = sbf("lagc", [128, 16], F32); lagb = sbf("lagb", [128, 16], BF16)
            selt = sbf("selt", [128, 2], F32)
            em.dma("sp", lambda e: e.dma_start(out=selt[:], in_=sel), writes=["selt"])
            em.dma("sp", lambda e: e.dma_start(out=w1t[:], in_=filt_w1), writes=["w1t"])
            em.dma("sp", lambda e: e.dma_start(out=w2t[:], in_=filt_w2), writes=["w2t"])
            em.dma("sp", lambda e: e.dma_start(out=w3t[:], in_=filt_w3), writes=["w3t"])
            em.dma("sp", lambda e: e.dma_start(out=fb[:], in_=filt_vec.rearrange("j k -> k j"), allow_slow_non_contiguous=True), writes=["fb"])
            em.dma("sp", lambda e: e.dma_start(out=dcy[:], in_=filt_decay.rearrange("o (b p) -> p (o b)", p=128), allow_slow_non_contiguous=True), writes=["dcy"])
            em.dma("sp", lambda e: e.dma_start(out=hbias[:], in_=hyena_bias.rearrange("o (b p) -> p (o b)", p=128), allow_slow_non_contiguous=True), writes=["hbias"])
            dcn = sbf("dcn", [128, 16], F32)
            em.op("dve", lambda e: e.tensor_scalar(out=dcn[:], in0=dcy[:], scalar1=-1.0, scalar2=None, op0=ALU.mult), reads=["dcy"], writes=["dcn"])
            em.op("dve", lambda e: e.tensor_tensor(out=dcy[:], in0=dcy[:], in1=dcn[:], op=ALU.min), reads=["dcy", "dcn"], writes=["dcy"])
            I2P = 1.0 / (2.0 * math.pi)
            for j in range(2):
                em.op("dve", lambda e, j=j: e.tensor_scalar(out=fs[:, 2 * j:2 * j + 1], in0=fb[:, 2 * j + 1:2 * j + 2], scalar1=I2P, scalar2=None, op0=ALU.mult), reads=["fb"], writes=["fs"])
                em.op("dve", lambda e, j=j: e.tensor_tensor(out=fs[:, 2 * j + 1:2 * j + 2], in0=fs[:, 2 * j:2 * j + 1], in1=fb[:, 2 * j:2 * j + 1], op=ALU.mult), reads=["fb", "fs"], writes=["fs"])
            MAGIC = 12582912.0

            def sin_layer(pm, pmk, j, hd, hdk):
                a, ak = ya.nxt(); b_, bk = yb_.nxt()
                em.op("dve", lambda e: e.tensor_scalar(out=a[:], in0=pm[0:64, :], scalar1=fs[:, 2 * j:2 * j + 1], scalar2=fs[:, 2 * j + 1:2 * j + 2], op0=ALU.mult, op1=ALU.add), reads=[pmk, "fs"], writes=[ak])
                em.op("dve", lambda e: e.tensor_scalar(out=b_[:], in0=a[:], scalar1=MAGIC, scalar2=None, op0=ALU.add), reads=[ak], writes=[bk])
                em.op("dve", lambda e: e.tensor_scalar(out=b_[:], in0=b_[:], scalar1=MAGIC, scalar2=None, op0=ALU.subtract), reads=[bk], writes=[bk])
                em.op("dve", lambda e: e.tensor_tensor(out=a[:], in0=a[:], in1=b_[:], op=ALU.subtract), reads=[ak, bk], writes=[ak])
                em.op("dve", lambda e: e.tensor_scalar(out=a[:], in0=a[:], scalar1=-0.499999, scalar2=0.499999, op0=ALU.max, op1=ALU.min), reads=[ak], writes=[ak])
                em.op("act", lambda e: e.activation(out=hd[:], in_=a[:], func=AF.Sin, scale=2.0 * math.pi), reads=[ak], writes=[hdk])

            pfi = 0
            for tt in range(NTILE if "3a" not in skip else 0):
                t0 = tt * TT
                z_, zk = zt.nxt()
                em.dma("sp", lambda e, z_=z_, t0=t0: e.dma_start(out=z_[:], in_=zT[:, t0:t0 + TT]), writes=[zk])
                em.dma("sp", lambda e, t0=t0: e.dma_start(out=tpb[:], in_=tpos[:, t0:t0 + TT].partition_broadcast(128)), writes=["tpb"])
                pm = pF[pfi % 3]; pmk = f"pF{pfi % 3}"; pfi += 1
                em.op("pe", lambda e, pm=pm, z_=z_: e.matmul(pm[0:64, :], lhsT=w1t[:], rhs=z_[:], start=True, stop=True), reads=["w1t", zk], writes=[pmk])
                h1_, h1k = hd1.nxt()
                sin_layer(pm, pmk, 0, h1_, h1k)
                pm = pF[pfi % 3]; pmk = f"pF{pfi % 3}"; pfi += 1
                em.op("pe", lambda e, pm=pm, h1_=h1_: e.matmul(pm[0:64, :], lhsT=w2t[:], rhs=h1_[:], start=True, stop=True), reads=["w2t", h1k], writes=[pmk])
                h2_, h2k = hd2.nxt()
                sin_layer(pm, pmk, 1, h2_, h2k)
                for cb in range(16):
                    pm = pF[pfi % 3]; pmk = f"pF{pfi % 3}"; pfi += 1
                    em.op("pe", lambda e, pm=pm, h2_=h2_, cb=cb: e.matmul(pm[:], lhsT=w3t[:, cb * 128:(cb + 1) * 128], rhs=h2_[:], start=True, stop=True), reads=["w3t", h2k], writes=[pmk])
                    w_, wk_ = wn.nxt(); f_, fk_ = fo.nxt(); fb_, fbk = fob.nxt()
                    em.op("act", lambda e, w_=w_, cb=cb: e.activation(out=w_[:], in_=tpb[:], func=AF.Exp, scale=dcy[:, cb:cb + 1]), reads=["tpb", "dcy"], writes=[wk_])
                    em.op("dve", lambda e, f_=f_, pm=pm, w_=w_: e.tensor_tensor(out=f_[:], in0=pm[:], in1=w_[:], op=ALU.mult), reads=[pmk, wk_], writes=[fk_])
                    if tt == 0:
                        em.op("dve", lambda e, f_=f_, cb=cb: e.tensor_copy(out=lag0[:, cb:cb + 1], in_=f_[:, 0:1]), reads=[fk_], writes=["lag0"])
                    em.op("pool", lambda e, fb_=fb_, f_=f_: e.tensor_copy(out=fb_[:], in_=f_[:]), reads=[fk_], writes=[fbk])
                    em.dma("sp", lambda e, fb_=fb_, cb=cb, t0=t0: e.dma_start(out=HF[cb * 128:(cb + 1) * 128, t0:t0 + TT], in_=fb_[:]), reads=[fbk], writes=[f"HF0_{cb}" if tt == 0 else "HF"])
                if tt == 0:
                    em.op("dve", lambda e: e.memset(lagc[:], 0.0), writes=["lagc"])
                    em.op("dve", lambda e: e.tensor_scalar(out=lagc[:, 0:8], in0=lag0[:, 0:8], scalar1=selt[:, 0:1], scalar2=None, op0=ALU.mult), reads=["lag0", "selt"], writes=["lagc"])
                    em.op("dve", lambda e: e.scalar_tensor_tensor(out=lagc[:, 0:8], in0=lag0[:, 8:16], scalar=selt[:, 1:2], in1=lagc[:, 0:8], op0=ALU.mult, op1=ALU.add), reads=["lag0", "selt", "lagc"], writes=["lagc"])
                    em.op("dve", lambda e: e.tensor_tensor(out=lagc[:, 0:8], in0=lagc[:, 0:8], in1=hbias[:], op=ALU.add), reads=["lagc", "hbias"], writes=["lagc"])
                    em.op("dve", lambda e: e.tensor_copy(out=lagb[:], in_=lagc[:]), reads=["lagc"], writes=["lagb"])
                    em.dma("sp", lambda e: e.dma_start(out=HF.rearrange("(b p) t -> p b t", p=128)[:, :, 0:1], in_=lagb[:].unsqueeze(2), allow_slow_non_contiguous=True),
                           reads=["lagb"], writes=[f"HF0_{c_}" for c_ in range(16)])
            em.flush()

          with ExitStack() as es:
            sbf = lambda n, s, d: es.enter_context(nc.sbuf_tensor(uq(n), list(s), d))
            PW = 2048
            cw = sbf("cw", [128, 3, 24], F32)
            cbias = sbf("cbias", [128, 24], F32)
            hyin = [Rot(sbf, f"hyin{j}", 2, [128, PW + 2], F32) for j in range(3)]
            cv = [Rot(sbf, f"cv{j}", 2, [128, PW], F32) for j in range(3)]
            ub = Rot(sbf, "ub", 2, [128, PW], BF16)
            em.dma("sp", lambda e: e.dma_start(out=cw[:], in_=conv_w.rearrange("j (b p) -> p j b", p=128), allow_slow_non_contiguous=True), writes=["cw"])
            em.dma("sp", lambda e: e.dma_start(out=cbias[:], in_=conv_b.rearrange("o (b p) -> p (o b)", p=128), allow_slow_non_contiguous=True), writes=["cbias"])
            for cb in range(8 if "3b" not in skip else 0):
                for pc in range(L // PW):
                    t0 = pc * PW
                    outs = []
                    for j in range(3):
                        hy_, hyk = hyin[j].nxt(); c_, ck = cv[j].nxt()
                        blk = j * 8 + cb
                        r0 = blk * 128
                        lo = max(t0 - 1, 0); hi = min(t0 + PW + 1, L)
                        if t0 == 0:
                            em.op("pool", lambda e, hy_=hy_: e.memset(hy_[:, 0:1], 0.0), writes=[hyk])
                        if t0 + PW == L:
                            em.op("pool", lambda e, hy_=hy_: e.memset(hy_[:, PW + 1:PW + 2], 0.0), writes=[hyk])
                        o0 = lo - (t0 - 1)
                        em.dma("sp", lambda e, hy_=hy_, r0=r0, lo=lo, hi=hi, o0=o0: e.dma_start(out=hy_[:, o0:o0 + hi - lo], in_=HYT[r0:r0 + 128, lo:hi]), writes=[hyk])
                        eng = "dve"
                        em.op(eng, lambda e, c_=c_, hy_=hy_, blk=blk: e.tensor_scalar(out=c_[:], in0=hy_[:, 1:PW + 1], scalar1=cw[:, 1, blk:blk + 1], scalar2=cbias[:, blk:blk + 1], op0=ALU.mult, op1=ALU.add), reads=[hyk, "cw", "cbias"], writes=[ck])
                        em.op(eng, lambda e, c_=c_, hy_=hy_, blk=blk: e.scalar_tensor_tensor(out=c_[:], in0=hy_[:, 0:PW], scalar=cw[:, 0, blk:blk + 1], in1=c_[:], op0=ALU.mult, op1=ALU.add), reads=[hyk, "cw", ck], writes=[ck])
                        em.op(eng, lambda e, c_=c_, hy_=hy_, blk=blk: e.scalar_tensor_tensor(out=c_[:], in0=hy_[:, 2:PW + 2], scalar=cw[:, 2, blk:blk + 1], in1=c_[:], op0=ALU.mult, op1=ALU.add), reads=[hyk, "cw", ck], writes=[ck])
                        outs.append((c_, ck))
                    u_, uk = ub.nxt()
                    em.op("pool", lambda e, u_=u_, a=outs[2][0], b=outs[1][0]: e.tensor_tensor(out=u_[:], in0=a[:], in1=b[:], op=ALU.mult), reads=[outs[2][1], outs[1][1]], writes=[uk])
                    em.dma("sp", lambda e, u_=u_, cb=cb, t0=t0: e.dma_start(out=UT[cb * 128:(cb + 1) * 128, t0:t0 + PW], in_=u_[:]), reads=[uk], writes=["UT"])
                    em.dma("sp", lambda e, a=outs[0][0], cb=cb, t0=t0: e.dma_start(out=X0T[cb * 128:(cb + 1) * 128, t0:t0 + PW], in_=a[:]), reads=[outs[0][1]], writes=["X0T"])
            em.flush()

          with ExitStack() as es:
            sbf = lambda n, s, d: es.enter_context(nc.sbuf_tensor(uq(n), list(s), d))
            psf = lambda n, s, d: es.enter_context(nc.psum_tensor(uq(n), list(s), d))
            cst = sbf("cst", [128, 256], F32)
            FRIb = sbf("FRIb", [128, 256], BF16); FRnIb = sbf("FRnIb", [128, 256], BF16)
            FIRb = sbf("FIRb", [128, 256], BF16); nFIb = sbf("nFIb", [128, 128], BF16)
            TWt = sbf("TWt", [128, 2, 128], F32); TWct = sbf("TWct", [128, 2, 128], F32)
            for nm, src, dst in (("FRI", FRI, FRIb), ("FRnI", FRnI, FRnIb), ("FIR", FIR, FIRb)):
                em.dma("sp", lambda e, src=src: e.dma_start(out=cst[:], in_=src), writes=["cst"])
                em.op("dve", lambda e, dst=dst: e.tensor_copy(out=dst[:], in_=cst[:]), reads=["cst"], writes=[nm])
            em.dma("sp", lambda e: e.dma_start(out=cst[:, 0:128], in_=nFI), writes=["cst"])
            em.op("dve", lambda e: e.tensor_copy(out=nFIb[:], in_=cst[:, 0:128]), reads=["cst"], writes=["nFI"])
            em.dma("sp", lambda e: e.dma_start(out=TWt[:].rearrange("p a b -> p (a b)"), in_=TW), writes=["TW"])
            em.dma("sp", lambda e: e.dma_start(out=TWct[:].rearrange("p a b -> p (a b)"), in_=TWc), writes=["TWc"])
            Min = Rot(sbf, "Min", 3, [64, 2, 128], BF16)
            P1 = Rot(sbf, "P1", 3, [128, 2, 2, 128], F32)
            P2 = Rot(sbf, "P2", 3, [128, 2, 2, 128], F32)
            B3 = Rot(sbf, "B3", 3, [128, 2, 3, 128], BF16)
            Y2 = Rot(sbf, "Y2", 2, [128, 2, 2, 128], BF16)
            D2 = Rot(sbf, "D2", 2, [128, 2, 2, 128], BF16)
            KFs = Rot(sbf, "KFs", 2, [128, 2, 2, 128], F32)
            YO = Rot(sbf, "YO", 2, [64, 2, 128], F32)
            pA = [psf(f"pA{i}", [128, 2, 2, 128], F32) for i in range(2)]
            pX = [psf(f"pX{i}", [128, 2, 2, 128], F32) for i in range(2)]
            pC = [psf(f"pC{i}", [128, 2, 2, 128], F32) for i in range(2)]
            pY = [psf(f"pY{i}", [64, 2, 128], F32) for i in range(1)]
            cnt = {"a": 0, "x": 0, "c": 0}

            def bc4(t3, ri):
                return t3[:, ri:ri + 1, :].unsqueeze(1).to_broadcast([128, 2, 2, 128])

            def cmul(src, srck, tw, twk, p1, p1k, p2, p2k):
                em.op("dve", lambda e: e.tensor_tensor(out=p1[:], in0=src[:], in1=bc4(tw, 0), op=ALU.mult), reads=[srck, twk], writes=[p1k])
                em.op("dve", lambda e: e.tensor_tensor(out=p2[:], in0=src[:, :, ::-1, :], in1=bc4(tw, 1), op=ALU.mult), reads=[srck, twk], writes=[p2k])

            def stage1_tw(src_dram, c0, rhs1, rhs1k, tw, twk):
                m_, mk_ = Min.nxt()
                em.dma("sp", lambda e: e.dma_start(out=m_[:], in_=src_dram[c0:c0 + 2, :].rearrange("c (a b) -> a c b", b=128)), writes=[mk_])
                pa = pA[cnt["a"] % 2]; pak = f"pA{cnt['a'] % 2}"; cnt["a"] += 1
                for ch in range(2):
                    em.op("pe", lambda e, ch=ch: e.matmul(pa[:, ch, :, :].rearrange("p a b -> p (a b)"), lhsT=m_[:, ch, :], rhs=rhs1[0:64, :], start=True, stop=True), reads=[mk_, rhs1k], writes=[pak])
                p1, p1k = P1.nxt(); p2, p2k = P2.nxt(); b3, b3k = B3.nxt()
                cmul(pa, pak, tw, twk, p1, p1k, p2, p2k)
                em.op("pool", lambda e: e.tensor_tensor(out=b3[:, :, 1, :], in0=p1[:, :, 0, :], in1=p2[:, :, 0, :], op=ALU.subtract), reads=[p1k, p2k], writes=[b3k])
                em.op("pool", lambda e: e.tensor_tensor(out=b3[:, :, 2, :], in0=p1[:, :, 1, :], in1=p2[:, :, 1, :], op=ALU.add), reads=[p1k, p2k], writes=[b3k])
                em.op("pool", lambda e: e.tensor_scalar(out=b3[:, :, 0, :], in0=b3[:, :, 2, :], scalar1=-1.0, scalar2=None, op0=ALU.mult), reads=[b3k], writes=[b3k])
                return b3, b3k

            def stage2(px, pxk, b3, b3k, conj, first, last):
                for ch in range(2):
                    em.op("pe", lambda e, ch=ch: e.matmul(px[:, ch, :, :].rearrange("p a b -> p (a b)"), lhsT=FRIb[:, 0:128], rhs=b3[:, ch, 1:3, :].rearrange("p a b -> p (a b)"), start=first, stop=False),
                          reads=["FRI", b3k], writes=[pxk])
                    em.op("pe", lambda e, ch=ch: e.matmul(px[:, ch, :, :].rearrange("p a b -> p (a b)"), lhsT=(nFIb[:] if conj else FRIb[:, 128:256]), rhs=b3[:, ch, 0:2, :].rearrange("p a b -> p (a b)"), start=False, stop=last),
                          reads=["FRI", "nFI", b3k], writes=[pxk])

            npair = 512 if 31 not in skip else 4
            for pr in range(npair):
                c0 = pr * 2
                bf_, bfk = stage1_tw(HF, c0, FRIb, "FRI", TWt, "TW")
                px = pX[cnt["x"] % 2]; pxk = f"pX{cnt['x'] % 2}"; cnt["x"] += 1
                bb_, bbk = stage1_tw(HF, 1024 + c0, FRnIb, "FRnI", TWct, "TWc")
                for ch in range(2):
                    o = px[:, ch, :, :].rearrange("p a b -> p (a b)")
                    em.op("pe", lambda e, o=o, ch=ch, bf_=bf_: e.matmul(o, lhsT=FRIb[:, 0:128], rhs=bf_[:, ch, 1:3, :].rearrange("p a b -> p (a b)"), start=True, stop=False), reads=["FRI", bfk], writes=[pxk])
                    em.op("pe", lambda e, o=o, ch=ch, bf_=bf_: e.matmul(o, lhsT=FRIb[:, 128:256], rhs=bf_[:, ch, 0:2, :].rearrange("p a b -> p (a b)"), start=False, stop=False), reads=["FRI", bfk], writes=[pxk])
                    em.op("pe", lambda e, o=o, ch=ch, bb_=bb_: e.matmul(o, lhsT=FRIb[:, 0:128], rhs=bb_[:, ch, 1:3, :].rearrange("p a b -> p (a b)"), start=False, stop=False), reads=["FRI", bbk], writes=[pxk])
                    em.op("pe", lambda e, o=o, ch=ch, bb_=bb_: e.matmul(o, lhsT=nFIb[:], rhs=bb_[:, ch, 0:2, :].rearrange("p a b -> p (a b)"), start=False, stop=True), reads=["nFI", bbk], writes=[pxk])
                kf_, kfk = KFs.nxt()
                em.op("act", lambda e, kf_=kf_, px=px: e.activation(out=kf_[:], in_=px[:], func=AF.Copy), reads=[pxk], writes=[kfk])
                em.dma("sp", lambda e, kf_=kf_, c0=c0: e.dma_start(out=KF[c0:c0 + 2].rearrange("c k a b -> k c a b"), in_=kf_[:]), reads=[kfk], writes=[f"KF{pr}"])
            for pr in range(npair):
                c0 = pr * 2
                b3, b3k = stage1_tw(UT, c0, FRIb, "FRI", TWt, "TW")
                px = pX[cnt["x"] % 2]; pxk = f"pX{cnt['x'] % 2}"; cnt["x"] += 1
                for ch in range(2):
                    o = px[:, ch, :, :].rearrange("p a b -> p (a b)")
                    em.op("pe", lambda e, o=o, ch=ch, b3=b3: e.matmul(o, lhsT=FRIb[:, 0:128], rhs=b3[:, ch, 1:3, :].rearrange("p a b -> p (a b)"), start=True, stop=False), reads=["FRI", b3k], writes=[pxk])
                    em.op("pe", lambda e, o=o, ch=ch, b3=b3: e.matmul(o, lhsT=FRIb[:, 128:256], rhs=b3[:, ch, 0:2, :].rearrange("p a b -> p (a b)"), start=False, stop=True), reads=["FRI", b3k], writes=[pxk])
                kf_, kfk = KFs.nxt()
                em.dma("sp", lambda e, kf_=kf_, c0=c0: e.dma_start(out=kf_[:], in_=KF[c0:c0 + 2].rearrange("c k a b -> k c a b")), reads=[f"KF{pr}"], writes=[kfk])
                p1, p1k = P1.nxt(); p2, p2k = P2.nxt(); y2, y2k = Y2.nxt()
                em.op("dve", lambda e, p1=p1, px=px, kf_=kf_: e.tensor_tensor(out=p1[:], in0=px[:], in1=kf_[:, :, 0:1, :].to_broadcast([128, 2, 2, 128]), op=ALU.mult), reads=[pxk, kfk], writes=[p1k])
                em.op("dve", lambda e, p2=p2, px=px, kf_=kf_: e.tensor_tensor(out=p2[:], in0=px[:, :, ::-1, :], in1=kf_[:, :, 1:2, :].to_broadcast([128, 2, 2, 128]), op=ALU.mult), reads=[pxk, kfk], writes=[p2k])
                em.op("pool", lambda e, y2=y2, p1=p1, p2=p2: e.tensor_tensor(out=y2[:, :, 0, :], in0=p1[:, :, 0, :], in1=p2[:, :, 0, :], op=ALU.subtract), reads=[p1k, p2k], writes=[y2k])
                em.op("pool", lambda e, y2=y2, p1=p1, p2=p2: e.tensor_tensor(out=y2[:, :, 1, :], in0=p1[:, :, 1, :], in1=p2[:, :, 1, :], op=ALU.add), reads=[p1k, p2k], writes=[y2k])
                pc = pC[cnt["c"] % 2]; pck = f"pC{cnt['c'] % 2}"; cnt["c"] += 1
                for ch in range(2):
                    o = pc[:, ch, :, :].rearrange("p a b -> p (a b)")
                    em.op("pe", lambda e, o=o, ch=ch, y2=y2: e.matmul(o, lhsT=y2[:, ch, 0, :], rhs=FRnIb[:], start=True, stop=False), reads=[y2k, "FRnI"], writes=[pck])
                    em.op("pe", lambda e, o=o, ch=ch, y2=y2: e.matmul(o, lhsT=y2[:, ch, 1, :], rhs=FIRb[:], start=False, stop=True), reads=[y2k, "FIR"], writes=[pck])
                p1, p1k = P1.nxt(); p2, p2k = P2.nxt(); d2, d2k = D2.nxt()
                cmul(pc, pck, TWct, "TWc", p1, p1k, p2, p2k)
                em.op("pool", lambda e, d2=d2, p1=p1, p2=p2: e.tensor_tensor(out=d2[:, 0, :, :], in0=p1[:, :, 0, :], in1=p2[:, :, 0, :], op=ALU.subtract), reads=[p1k, p2k], writes=[d2k])
                em.op("pool", lambda e, d2=d2, p1=p1, p2=p2: e.tensor_tensor(out=d2[:, 1, :, :], in0=p1[:, :, 1, :], in1=p2[:, :, 1, :], op=ALU.add), reads=[p1k, p2k], writes=[d2k])
                py = pY[0]
                em.op("pe", lambda e, d2=d2: e.matmul(py[:].rearrange("p a b -> p (a b)"), lhsT=FRIb[:, 0:64], rhs=d2[:, 0, :, :].rearrange("p a b -> p (a b)"), start=True, stop=False), reads=["FRI", d2k], writes=["pY"])
                em.op("pe", lambda e, d2=d2: e.matmul(py[:].rearrange("p a b -> p (a b)"), lhsT=FRIb[:, 128:192], rhs=d2[:, 1, :, :].rearrange("p a b -> p (a b)"), start=False, stop=True), reads=["FRI", d2k], writes=["pY"])
                yo, yok = YO.nxt()
                em.op("act", lambda e, yo=yo: e.activation(out=yo[:], in_=py[:], func=AF.Copy, scale=1.0 / 16384.0), reads=["pY"], writes=[yok])
                em.dma("sp", lambda e, yo=yo, c0=c0: e.dma_start(out=YCT[c0:c0 + 2, :].rearrange("c (a b) -> a c b", b=128), in_=yo[:]), reads=[yok], writes=["YCT"])
            em.flush()
        print("inst", em.n_inst, "waits", em.n_wait)
        if stop_after <= 3:
            return nc
        TK = 256
        NTK = (L // 2) // TK
        if 4 not in skip:
          with ExitStack() as es:
            sbf = lambda n, s, d: es.enter_context(nc.sbuf_tensor(uq(n), list(s), d))
            psf = lambda n, s, d: es.enter_context(nc.psum_tensor(uq(n), list(s), d))
            wa = sbf("wa", [128, 8, D], BF16); wb = sbf("wb", [128, 8, D], BF16); wo = sbf("wo", [128, 8, D], BF16)
            for wt_, src, nm in ((wa, w_branch_a, "wa"), (wb, w_branch_b, "wb"), (wo, w_out, "wo")):
                for k in range(8):
                    em.dma("pool", lambda e, wt_=wt_, src=src, k=k: e.dma_start(out=wt_[:, k, :], in_=src[k * 128:(k + 1) * 128, :]), writes=[nm])
            ones = sbf("ones", [128, 128], F32)
            em.op("dve", lambda e: e.memset(ones[:], 1.0), writes=["ones"])
            gcol = sbf("gcol", [128, 1], F32)
            em.dma("sp", lambda e: e.dma_start(out=gcol[:], in_=hgrn_norm_g.rearrange("o v -> v o"), allow_slow_non_contiguous=True), writes=["gcol"])
            ot = Rot(sbf, "ot", 2, [128, 8, TK], F32)
            ogt = Rot(sbf, "ogt", 2, [128, 8, TK], BF16)
            sq = Rot(sbf, "sq", 2, [128, 8, TK], F32)
            rs = Rot(sbf, "rs", 2, [128, 2, TK], F32)
            tmpA = Rot(sbf, "tmpA", 2, [128, 2, TK], F32)
            At = Rot(sbf, "At", 2, [128, 8, TK], BF16)
            x0t = Rot(sbf, "x0t", 2, [128, 8, TK], F32)
            yct = Rot(sbf, "yct", 2, [128, 8, TK], F32)
            Bt = Rot(sbf, "Bt", 2, [128, 8, TK], BF16)
            gat = Rot(sbf, "gat", 2, [128, 8, 2, TK], BF16)
            tg = Rot(sbf, "tg", 2, [128, 2, TK], F32)
            mg = Rot(sbf, "mg", 2, [128, 8, TK], BF16)
            xt4 = Rot(sbf, "xt4", 2, [128, 2, D], F32)
            h1t = Rot(sbf, "h1t", 2, [128, 2, D], F32)
            pw = [psf(f"pw{i}", [128, 512], F32) for i in range(6)]
            pwi = [0]

            def npw():
                pwi[0] = (pwi[0] + 1) % 6
                return pw[pwi[0]], f"pw{pwi[0]}"

            for tk in range(NTK):
                tg0 = TOK0 + tk * TK
                o_, ok = ot.nxt(); og_, ogk = ogt.nxt(); s_, sk_ = sq.nxt(); a_, ak = At.nxt()
                em.dma("sp", lambda e, o_=o_, tk=tk: e.dma_start(out=o_[:], in_=OTh[:, :, tk * TK:(tk + 1) * TK].rearrange("h k t -> k h t")), writes=[ok])
                em.dma("sp", lambda e, og_=og_, tk=tk: e.dma_start(out=og_[:], in_=OGTh[:, tk * TK:(tk + 1) * TK].rearrange("(h k) t -> k h t", k=128)), writes=[ogk])
                em.op("act", lambda e, s_=s_, o_=o_: e.activation(out=s_[:], in_=o_[:], func=AF.Square), reads=[ok], writes=[sk_])
                for h2 in range(4):
                    p_, pk = npw()
                    for hh in range(2):
                        h = h2 * 2 + hh
                        em.op("pe", lambda e, p_=p_, s_=s_, h=h, hh=hh: e.matmul(p_[:, hh * TK:(hh + 1) * TK], lhsT=ones[:], rhs=s_[:, h, :], start=True, stop=True), reads=["ones", sk_], writes=[pk])
                    r_, rk = rs.nxt(); t_, tk_ = tmpA.nxt()
                    em.op("act", lambda e, r_=r_, p_=p_: e.activation(out=r_[:].rearrange("p a b -> p (a b)"), in_=p_[:], func=AF.Ln, scale=1.0 / 128, bias=EPS), reads=[pk], writes=[rk])
                    em.op("act", lambda e, r_=r_: e.activation(out=r_[:], in_=r_[:], func=AF.Exp, scale=-0.5), reads=[rk], writes=[rk])
                    em.op("dve", lambda e, t_=t_, o_=o_, r_=r_, h2=h2: e.scalar_tensor_tensor(out=t_[:], in0=o_[:, h2 * 2:h2 * 2 + 2, :], scalar=gcol[:, 0:1], in1=r_[:], op0=ALU.mult, op1=ALU.mult), reads=[ok, rk, "gcol"], writes=[tk_])
                    em.op("pool", lambda e, a_=a_, t_=t_, og_=og_, h2=h2: e.tensor_tensor(out=a_[:, h2 * 2:h2 * 2 + 2, :], in0=t_[:], in1=og_[:, h2 * 2:h2 * 2 + 2, :], op=ALU.mult), reads=[tk_, ogk], writes=[ak])
                if "AT" in dbg:
                    em.dma("sp", lambda e, a_=a_, tk=tk: e.dma_start(out=AT[:, tk * TK:(tk + 1) * TK].rearrange("(h k) t -> k h t", k=128), in_=a_[:]), reads=[ak], writes=["AT"])
                x0_, x0k = x0t.nxt(); yc_, yck = yct.nxt(); b_, bk = Bt.nxt(); ga_, gak = gat.nxt()
                em.dma("sp", lambda e, x0_=x0_, tk=tk: e.dma_start(out=x0_[:], in_=X0Th[:, tk * TK:(tk + 1) * TK].rearrange("(h k) t -> k h t", k=128)), writes=[x0k])
                em.dma("sp", lambda e, yc_=yc_, tk=tk: e.dma_start(out=yc_[:], in_=YCTh[:, tk * TK:(tk + 1) * TK].rearrange("(h k) t -> k h t", k=128)), writes=[yck])
                for a2 in range(2):
                    em.dma("sp", lambda e, ga_=ga_, tk=tk, a2=a2: e.dma_start(out=ga_[:, :, a2, :], in_=GTh[a2 * 1024:(a2 + 1) * 1024, tk * TK:(tk + 1) * TK].rearrange("(h k) t -> k h t", k=128)), writes=[gak])
                em.op("pool", lambda e, b_=b_, x0_=x0_, yc_=yc_: e.tensor_tensor(out=b_[:], in0=x0_[:], in1=yc_[:], op=ALU.mult), reads=[x0k, yck], writes=[bk])
                m_, mk_ = mg.nxt()
                for db in range(8):
                    p_, pk = npw()
                    for k in range(8):
                        em.op("pe", lambda e, p_=p_, k=k, db=db, a_=a_: e.matmul(p_[:, 0:TK], lhsT=wa[:, k, db * 128:(db + 1) * 128], rhs=a_[:, k, :], start=(k == 0), stop=(k == 7)), reads=["wa", ak], writes=[pk])
                    for k in range(8):
                        em.op("pe", lambda e, p_=p_, k=k, db=db, b_=b_: e.matmul(p_[:, TK:2 * TK], lhsT=wb[:, k, db * 128:(db + 1) * 128], rhs=b_[:, k, :], start=(k == 0), stop=(k == 7)), reads=["wb", bk], writes=[pk])
                    t_, tk_ = tg.nxt()
                    em.op("dve", lambda e, t_=t_, p_=p_, ga_=ga_, db=db: e.tensor_tensor(out=t_[:].rearrange("p a b -> p (a b)"), in0=p_[:], in1=ga_[:, db, :, :].rearrange("p a b -> p (a b)"), op=ALU.mult), reads=[pk, gak], writes=[tk_])
                    em.op("pool", lambda e, m_=m_, t_=t_, db=db: e.tensor_tensor(out=m_[:, db, :], in0=t_[:, 0, :], in1=t_[:, 1, :], op=ALU.add), reads=[tk_], writes=[mk_])
                if "MG" in dbg:
                    em.dma("sp", lambda e, m_=m_, tk=tk: e.dma_start(out=MG[:, tk * TK:(tk + 1) * TK].rearrange("(h k) t -> k h t", k=128), in_=m_[:]), reads=[mk_], writes=["MGd"])
                x_, xk = xt4.nxt(); h_, hk = h1t.nxt()
                em.dma("sp", lambda e, x_=x_, tk=tk: e.dma_start(out=x_[:], in_=xh[tk * TK:(tk + 1) * TK, :].rearrange("(s p) d -> p s d", p=128)), writes=[xk])
                for s in range(2):
                    for hf in range(2):
                        p_, pk = npw()
                        for k in range(8):
                            em.op("pe", lambda e, p_=p_, k=k, s=s, hf=hf, m_=m_: e.matmul(p_[:], lhsT=m_[:, k, s * 128:(s + 1) * 128], rhs=wo[:, k, hf * 512:(hf + 1) * 512], start=(k == 0), stop=(k == 7)), reads=["wo", mk_], writes=[pk])
                        em.op("dve", lambda e, h_=h_, p_=p_, x_=x_, s=s, hf=hf: e.tensor_tensor(out=h_[:, s, hf * 512:(hf + 1) * 512], in0=p_[:], in1=x_[:, s, hf * 512:(hf + 1) * 512], op=ALU.add), reads=[pk, xk], writes=[hk])
                em.dma("sp", lambda e, h_=h_, tk=tk: e.dma_start(out=H1[tk * TK:(tk + 1) * TK, :].rearrange("(s p) d -> p s d", p=128), in_=h_[:]), reads=[hk], writes=["H1d"])
            em.flush()
        print("inst", em.n_inst, "waits", em.n_wait)
        if stop_after <= 4:
            return nc
        if 5 not in skip:
          with ExitStack() as es:
            sbf = lambda n, s, d: es.enter_context(nc.sbuf_tensor(uq(n), list(s), d))
            psf = lambda n, s, d: es.enter_context(nc.psum_tensor(uq(n), list(s), d))
            idf5 = sbf("idf5", [128, 128], F32); idb5 = sbf("idb5", [128, 128], BF16)
            em.dma("sp", lambda e: e.dma_start(out=idf5[:], in_=ident), writes=["idf5"])
            em.op("dve", lambda e: e.tensor_copy(out=idb5[:], in_=idf5[:]), reads=["idf5"], writes=["idb5"])
            for r in range(16):
                em.dma("pool", lambda e, r=r: e.dma_start(out=VBF[r * 1024:(r + 1) * 1024, :], in_=peer_v[r * 1024:(r + 1) * 1024, :]), writes=["VBF"])
            urow = Rot(sbf, "urow", 3, [128, D], BF16)
            uts = Rot(sbf, "uts", 3, [128, 8, 128], BF16)
            pU = [psf(f"pU{i}", [128, 8, 128], BF16) for i in range(2)]
            nj = 128 if 51 not in skip else 2
            for j in range(nj):
                u_, uk = urow.nxt(); t_, tk_ = uts.nxt()
                em.dma("pool", lambda e, u_=u_, j=j: e.dma_start(out=u_[:], in_=peer_u.rearrange("(i j) d -> j i d", j=128)[j]), writes=[uk])
                p_ = pU[j % 2]; pk = f"pU{j % 2}"
                for k in range(8):
                    em.op("pe", lambda e, p_=p_, u_=u_, k=k: e.transpose(out=p_[:, k, :], in_=u_[:, k * 128:(k + 1) * 128], identity=idb5[:]), reads=[uk, "idb5"], writes=[pk])
                em.op("act" if j % 2 else "dve", lambda e, p_=p_, t_=t_, j=j: (e.activation(out=t_[:], in_=p_[:], func=AF.Copy) if j % 2 else e.tensor_copy(out=t_[:], in_=p_[:])), reads=[pk], writes=[tk_])
                em.dma("sp", lambda e, t_=t_, j=j: e.dma_start(out=UTS[j], in_=t_[:]), reads=[tk_], writes=["UTS"])
            em.flush()

          with ExitStack() as es:
            sbf = lambda n, s, d: es.enter_context(nc.sbuf_tensor(uq(n), list(s), d))
            psf = lambda n, s, d: es.enter_context(nc.psum_tensor(uq(n), list(s), d))
            wq = sbf("wq", [128, 8, 2048], BF16)
            for k in range(8):
                em.dma("pool", lambda e, k=k: e.dma_start(out=wq[:, k, :], in_=peer_w_q[k * 128:(k + 1) * 128, :]), writes=["wq"])
            idf = sbf("idf", [128, 128], F32)
            em.dma("sp", lambda e: e.dma_start(out=idf[:], in_=ident), writes=["idf"])
            iot = sbf("iot", [128, 128], F32)
            em.dma("sp", lambda e: e.dma_start(out=iot[:], in_=iota), writes=["iot"])
            gff = sbf("gff", [128, D], F32); gfin = sbf("gfin", [128, D], F32)
            em.dma("sp", lambda e: e.dma_start(out=gff[:], in_=norm_ffn_g.partition_broadcast(128)), writes=["gff"])
            em.dma("sp", lambda e: e.dma_start(out=gfin[:], in_=norm_final_g.partition_broadcast(128)), writes=["gfin"])
            skT = sbf("skT", [128, 16, 128], BF16)
            h1 = Rot(sbf, "h1", 1, [128, 2, D], F32)
            sqj = sbf("sqj5", [128, D], F32)
            ss5 = sbf("ss5", [128, 2], F32); rstd5 = sbf("rstd5", [128, 2], F32)
            xn2 = sbf("xn2", [128, 2, D], F32)
            xn2T = sbf("xn2T", [128, 8, TK], BF16)
            qT = sbf("qT", [128, 16, TK], BF16)
            scr = sbf("scr", [128, 16, 128], F32)
            skf = scr
            em.dma("sp", lambda e: e.dma_start(out=skf[:], in_=peer_sk.rearrange("j n c -> n j c")), writes=["scr"])
            scr2 = scr
            vals = sbf("vals", [128, 16, 16], F32)
            idxu = sbf("idxu", [128, 16, 16], U32)
            idxf = sbf("idxf", [128, 16, 16], F32)
            Cg = sbf("Cg", [128, 8, 256], F32); Cg2 = Cg
            cv = sbf("cv", [128, 8, 16], F32)
            posu = sbf("posu", [128, 8, 16], U32); pa_u = sbf("pa_u", [128, 8, 16], U32); pb_u = sbf("pb_u", [128, 8, 16], U32)
            paf = sbf("paf", [128, 8, 16], F32); pbf = sbf("pbf", [128, 8, 16], F32)
            eq = sbf("eq", [128, 8, 16, 16], F32)
            ik = sbf("ik", [128, 8, 16], F32); jk = sbf("jk", [128, 8, 16], F32)
            ee = sbf("ee", [128, 8, 16], F32); zz = sbf("zz", [128, 8], F32); gg = sbf("gg", [128, 8, 16], F32)
            ikT = sbf("ikT", [128, TK], F32); jkT = sbf("jkT", [128, TK], F32); gT = sbf("gT", [128, TK], F32)
            Lt = Rot(sbf, "Lt", 4, [128, 128], BF16); Rt = Rot(sbf, "Rt", 4, [128, 128], BF16)
            Gs = sbf("Gs", [128, TK, 128], BF16)
            utj = Rot(sbf, "utj", 3, [128, 8, 128], BF16); vj = Rot(sbf, "vj", 3, [128, D], BF16)
            gx = Rot(sbf, "gx", 2, [128, TK], F32); gx2 = Rot(sbf, "gx2", 2, [128, TK], F32); gin = Rot(sbf, "gin", 2, [128, TK], F32)
            gsg = Rot(sbf, "gsg", 2, [128, TK], F32); gact = Rot(sbf, "gact", 2, [128, TK], F32)
            ATj = Rot(sbf, "ATj", 3, [128, TK], BF16)

            acc = [psf(f"acc{i}", [128, 512], F32) for i in range(4)]
            pw = [psf(f"pw{i}", [128, 512], F32) for i in range(4)]
            pwi = [0]

            def npw():
                pwi[0] = (pwi[0] + 1) % 4
                return pw[pwi[0]], f"pw{pwi[0]}"

            for j4 in range(4):
                p_, pk = npw()
                for jj in range(4):
                    j = j4 * 4 + jj
                    em.op("pe", lambda e, p_=p_, jj=jj, j=j: e.transpose(out=p_[:, jj * 128:(jj + 1) * 128], in_=skf[:, j, :], identity=idf[:]), reads=["scr", "idf"], writes=[pk])
                em.op("dve", lambda e, p_=p_, j4=j4: e.tensor_copy(out=skT[:, j4 * 4:(j4 + 1) * 4, :].rearrange("p a b -> p (a b)"), in_=p_[:]), reads=[pk], writes=["skT"])

            ntk = NTK if 52 not in skip else 1
            import os
            P5STOP = int(os.environ.get("P5STOP", "99"))
            for tk in range(ntk):
                h_, hk = h1.nxt()
                em.dma("sp", lambda e, h_=h_, tk=tk: e.dma_start(out=h_[:], in_=H1[tk * TK:(tk + 1) * TK, :].rearrange("(s p) d -> p s d", p=128)), writes=[hk])
                for s in range(2):
                    em.op("act", lambda e, h_=h_, s=s: e.activation(out=sqj[:], in_=h_[:, s, :], func=AF.Square, accum_out=ss5[:, s:s + 1]), reads=[hk], writes=["sqj5", "ss5"])
                em.op("act", lambda e: e.activation(out=rstd5[:], in_=ss5[:], func=AF.Ln, scale=1.0 / D, bias=EPS), reads=["ss5"], writes=["rstd5"])
                em.op("act", lambda e: e.activation(out=rstd5[:], in_=rstd5[:], func=AF.Exp, scale=-0.5), reads=["rstd5"], writes=["rstd5"])
                for s in range(2):
                    em.op("dve", lambda e, h_=h_, s=s: e.scalar_tensor_tensor(out=xn2[:, s, :], in0=h_[:, s, :], scalar=rstd5[:, s:s + 1], in1=gff[:], op0=ALU.mult, op1=ALU.mult), reads=[hk, "rstd5", "gff"], writes=[f"xn2{s}"])
                    for k4 in range(2):
                        p_, pk = npw()
                        for kk in range(4):
                            k = k4 * 4 + kk
                            em.op("pe", lambda e, p_=p_, kk=kk, k=k, s=s: e.transpose(out=p_[:, kk * 128:(kk + 1) * 128], in_=xn2[:, s, k * 128:(k + 1) * 128], identity=idf[:]), reads=[f"xn2{s}", "idf"], writes=[pk])
                        em.op("act", lambda e, p_=p_, k4=k4, s=s: e.activation(out=xn2T[:, k4 * 4:(k4 + 1) * 4, s * 128:(s + 1) * 128], in_=p_[:].rearrange("p (a b) -> p a b", a=4), func=AF.Copy), reads=[pk], writes=["xn2T"])
                if P5STOP <= 1:
                    continue
                for j2 in range(8):
                    p_, pk = npw()
                    for jj in range(2):
                        j = j2 * 2 + jj
                        for k in range(8):
                            em.op("pe", lambda e, p_=p_, jj=jj, j=j, k=k: e.matmul(p_[:, jj * TK:(jj + 1) * TK], lhsT=wq[:, k, j * 128:(j + 1) * 128], rhs=xn2T[:, k, :], start=(k == 0), stop=(k == 7)), reads=["wq", "xn2T"], writes=[pk])
                    em.op("dve", lambda e, p_=p_, j2=j2: e.tensor_copy(out=qT[:, j2 * 2:j2 * 2 + 2, :].rearrange("p a b -> p (a b)"), in_=p_[:]), reads=[pk], writes=["qT"])
                if P5STOP <= 2:
                    continue
                for s in range(2):
                    for j4 in range(4):
                        p_, pk = npw()
                        for jj in range(4):
                            j = j4 * 4 + jj
                            em.op("pe", lambda e, p_=p_, jj=jj, j=j, s=s: e.matmul(p_[:, jj * 128:(jj + 1) * 128], lhsT=qT[:, j, s * 128:(s + 1) * 128], rhs=skT[:, j, :], start=True, stop=True), reads=["qT", "skT"], writes=[pk])
                        em.op("act", lambda e, p_=p_, j4=j4: e.activation(out=scr[:, j4 * 4:(j4 + 1) * 4, :].rearrange("p a b -> p (a b)"), in_=p_[:], func=AF.Copy), reads=[pk], writes=["scr"])
                    for j in range(16):
                        em.op("dve", lambda e, j=j: e.max(out=vals[:, j, 0:8], in_=scr[:, j, :]), reads=["scr"], writes=["vals"])
                        em.op("dve", lambda e, j=j: e.max_index(out=idxu[:, j, 0:8], in_max=vals[:, j, 0:8], in_values=scr[:, j, :]), reads=["scr", "vals"], writes=["idxu"])
                        em.op("dve", lambda e, j=j: e.match_replace(out=scr2[:, j, :], in_to_replace=vals[:, j, 0:8], in_values=scr[:, j, :], imm_value=-1e30), reads=["scr", "vals"], writes=["scr"])
                        em.op("dve", lambda e, j=j: e.max(out=vals[:, j, 8:16], in_=scr2[:, j, :]), reads=["scr"], writes=["vals"])
                        em.op("dve", lambda e, j=j: e.max_index(out=idxu[:, j, 8:16], in_max=vals[:, j, 8:16], in_values=scr2[:, j, :]), reads=["scr", "vals"], writes=["idxu"])
                    em.op("dve", lambda e: e.tensor_copy(out=idxf[:], in_=idxu[:]), reads=["idxu"], writes=["idxf"])
                    v4 = vals[:].rearrange("p (h t) a -> p h t a", t=2)
                    i4 = idxf[:].rearrange("p (h t) a -> p h t a", t=2)
                    em.op("dve", lambda e, v4=v4: e.tensor_tensor(out=Cg[:].rearrange("p h (a b) -> p h a b", b=16), in0=v4[:, :, 0, :].unsqueeze(3).to_broadcast([128, 8, 16, 16]), in1=v4[:, :, 1, :].unsqueeze(2).to_broadcast([128, 8, 16, 16]), op=ALU.add), reads=["vals"], writes=["Cg"])
                    for h in range(8):
                        em.op("dve", lambda e, h=h: e.max(out=cv[:, h, 0:8], in_=Cg[:, h, :]), reads=["Cg"], writes=["cv"])
                        em.op("dve", lambda e, h=h: e.max_index(out=posu[:, h, 0:8], in_max=cv[:, h, 0:8], in_values=Cg[:, h, :]), reads=["Cg", "cv"], writes=["posu"])
                        em.op("dve", lambda e, h=h: e.match_replace(out=Cg2[:, h, :], in_to_replace=cv[:, h, 0:8], in_values=Cg[:, h, :], imm_value=-1e30), reads=["Cg", "cv"], writes=["Cg"])
                        em.op("dve", lambda e, h=h: e.max(out=cv[:, h, 8:16], in_=Cg2[:, h, :]), reads=["Cg"], writes=["cv"])
                        em.op("dve", lambda e, h=h: e.max_index(out=posu[:, h, 8:16], in_max=cv[:, h, 8:16], in_values=Cg2[:, h, :]), reads=["Cg", "cv"], writes=["posu"])
                    em.op("dve", lambda e: e.tensor_single_scalar(out=pa_u[:], in_=posu[:], scalar=4, op=ALU.logical_shift_right), reads=["posu"], writes=["pa_u"])
                    em.op("dve", lambda e: e.tensor_single_scalar(out=pb_u[:], in_=posu[:], scalar=15, op=ALU.bitwise_and), reads=["posu"], writes=["pb_u"])
                    em.op("dve", lambda e: e.tensor_copy(out=paf[:], in_=pa_u[:]), reads=["pa_u"], writes=["paf"])
                    em.op("dve", lambda e: e.tensor_copy(out=pbf[:], in_=pb_u[:]), reads=["pb_u"], writes=["pbf"])
                    io16 = iot[:, 0:16].unsqueeze(1).unsqueeze(1).to_broadcast([128, 8, 16, 16])
                    for (pf, pfk, plane, dst, dstk) in ((paf, "paf", 0, ik, "ik"), (pbf, "pbf", 1, jk, "jk")):
                        em.op("dve", lambda e, pf=pf: e.tensor_tensor(out=eq[:], in0=pf[:].unsqueeze(3).to_broadcast([128, 8, 16, 16]), in1=io16, op=ALU.is_equal), reads=[pfk, "iot"], writes=["eq"])
                        em.op("dve", lambda e, plane=plane, i4=i4: e.tensor_tensor(out=eq[:], in0=eq[:], in1=i4[:, :, plane, :].unsqueeze(2).to_broadcast([128, 8, 16, 16]), op=ALU.mult), reads=["eq", "idxf"], writes=["eq"])
                        em.op("dve", lambda e, dst=dst: e.tensor_reduce(out=dst[:], in_=eq[:], axis=AX.X, op=ALU.add), reads=["eq"], writes=[dstk])
                    em.op("dve", lambda e: e.tensor_tensor(out=ee[:], in0=cv[:], in1=cv[:, :, 0:1].to_broadcast([128, 8, 16]), op=ALU.subtract), reads=["cv"], writes=["ee"])
                    em.op("act", lambda e: e.activation(out=ee[:], in_=ee[:], func=AF.Exp), reads=["ee"], writes=["ee"])
                    em.op("dve", lambda e: e.tensor_reduce(out=zz[:], in_=ee[:], axis=AX.X, op=ALU.add), reads=["ee"], writes=["zz"])
                    em.op("dve", lambda e: e.reciprocal(out=zz[:], in_=zz[:]), reads=["zz"], writes=["zz"])
                    em.op("dve", lambda e: e.tensor_tensor(out=gg[:], in0=ee[:], in1=zz[:].unsqueeze(2).to_broadcast([128, 8, 16]), op=ALU.mult), reads=["ee", "zz"], writes=["gg"])
                    p_, pk = npw()
                    for n_, (src, srck) in enumerate(((ik, "ik"), (jk, "jk"), (gg, "gg"))):
                        em.op("pe", lambda e, p_=p_, n_=n_, src=src: e.transpose(out=p_[:, n_ * 128:(n_ + 1) * 128], in_=src[:].rearrange("p h k -> p (h k)"), identity=idf[:]), reads=[srck, "idf"], writes=[pk])
                    em.op("dve", lambda e, p_=p_, s=s: e.tensor_copy(out=ikT[:, s * 128:(s + 1) * 128], in_=p_[:, 0:128]), reads=[pk], writes=["ikT"])
                    em.op("dve", lambda e, p_=p_, s=s: e.tensor_copy(out=jkT[:, s * 128:(s + 1) * 128], in_=p_[:, 128:256]), reads=[pk], writes=["jkT"])
                    em.op("dve", lambda e, p_=p_, s=s: e.tensor_copy(out=gT[:, s * 128:(s + 1) * 128], in_=p_[:, 256:384]), reads=[pk], writes=["gT"])
                if P5STOP <= 3:
                    continue
                for t4 in range(TK // 4):
                    p_, pk = npw()
                    for tq in range(4):
                        t = t4 * 4 + tq
                        l_, lk = Lt.nxt(); r_, rk = Rt.nxt()
                        em.op("dve" if t % 2 else "pool", lambda e, l_=l_, t=t: e.tensor_scalar(out=l_[:], in0=iot[:], scalar1=ikT[:, t:t + 1], scalar2=gT[:, t:t + 1], op0=ALU.is_equal, op1=ALU.mult), reads=["iot", "ikT", "gT"], writes=[lk])
                        em.op("pool" if t % 2 else "dve", lambda e, r_=r_, t=t: e.tensor_scalar(out=r_[:], in0=iot[:], scalar1=jkT[:, t:t + 1], scalar2=None, op0=ALU.is_equal), reads=["iot", "jkT"], writes=[rk])
                        em.op("pe", lambda e, p_=p_, tq=tq, l_=l_, r_=r_: e.matmul(p_[:, tq * 128:(tq + 1) * 128], lhsT=l_[:], rhs=r_[:], start=True, stop=True), reads=[lk, rk], writes=[pk])
                    em.op("act", lambda e, p_=p_, t4=t4: e.activation(out=Gs[:, t4 * 4:(t4 + 1) * 4, :].rearrange("p a b -> p (a b)"), in_=p_[:], func=AF.Copy), reads=[pk], writes=["Gs"])
                if P5STOP <= 4:
                    continue
                for j in range(128):
                    u_, uk = utj.nxt(); v_, vk = vj.nxt()
                    em.dma("sp", lambda e, u_=u_, j=j: e.dma_start(out=u_[:], in_=UTS[j]), writes=[uk])
                    em.dma("sp", lambda e, v_=v_, j=j: e.dma_start(out=v_[:], in_=VBF.rearrange("(i j) d -> j i d", j=128)[j]), writes=[vk])
                    p_, pk = npw()
                    for k in range(8):
                        em.op("pe", lambda e, p_=p_, k=k, u_=u_: e.matmul(p_[:, 0:TK], lhsT=u_[:, k, :], rhs=xn2T[:, k, :], start=(k == 0), stop=(k == 7)), reads=[uk, "xn2T"], writes=[pk])
                    x_, xk = gx.nxt(); x2_, x2k = gx2.nxt(); gi_, gik = gin.nxt(); sg_, sgk = gsg.nxt(); ga_, gak = gact.nxt(); a_, ak = ATj.nxt()
                    em.op("act", lambda e, x_=x_, p_=p_: e.activation(out=x_[:], in_=p_[:, 0:TK], func=AF.Copy), reads=[pk], writes=[xk])
                    em.op("act", lambda e, x2_=x2_, p_=p_: e.activation(out=x2_[:], in_=p_[:, 0:TK], func=AF.Square), reads=[pk], writes=[x2k])
                    em.op("pool", lambda e, x2_=x2_: e.tensor_scalar(out=x2_[:], in0=x2_[:], scalar1=0.044715, scalar2=1.0, op0=ALU.mult, op1=ALU.add), reads=[x2k], writes=[x2k])
                    em.op("pool", lambda e, gi_=gi_, x2_=x2_, x_=x_: e.tensor_tensor(out=gi_[:], in0=x2_[:], in1=x_[:], op=ALU.mult), reads=[x2k, xk], writes=[gik])
                    em.op("act", lambda e, sg_=sg_, gi_=gi_: e.activation(out=sg_[:], in_=gi_[:], func=AF.Sigmoid, scale=1.5957691216), reads=[gik], writes=[sgk])
                    em.op("dve", lambda e, ga_=ga_, sg_=sg_, x_=x_: e.tensor_tensor(out=ga_[:], in0=sg_[:], in1=x_[:], op=ALU.mult), reads=[sgk, xk], writes=[gak])
                    em.op("dve", lambda e, a_=a_, ga_=ga_, j=j: e.tensor_tensor(out=a_[:], in0=ga_[:], in1=Gs[:, :, j], op=ALU.mult), reads=[gak, "Gs"], writes=[ak])
                    for s in range(2):
                        for hf in range(2):
                            em.op("pe", lambda e, s=s, hf=hf, a_=a_, v_=v_, j=j: e.matmul(acc[s * 2 + hf][:], lhsT=a_[:, s * 128:(s + 1) * 128], rhs=v_[:, hf * 512:(hf + 1) * 512], start=(j == 0), stop=(j == 127)), reads=[ak, vk], writes=[f"acc{s * 2 + hf}"])
                if P5STOP <= 5:
                    continue
                for s in range(2):
                    for hf in range(2):
                        em.op("dve", lambda e, s=s, hf=hf, h_=h_: e.tensor_tensor(out=h_[:, s, hf * 512:(hf + 1) * 512], in0=acc[s * 2 + hf][:], in1=h_[:, s, hf * 512:(hf + 1) * 512], op=ALU.add), reads=[f"acc{s * 2 + hf}", hk], writes=[hk])
                for s in range(2):
                    em.op("act", lambda e, s=s, h_=h_: e.activation(out=sqj[:], in_=h_[:, s, :], func=AF.Square, accum_out=ss5[:, s:s + 1]), reads=[hk], writes=["sqj5", "ss5"])
                em.op("act", lambda e: e.activation(out=rstd5[:], in_=ss5[:], func=AF.Ln, scale=1.0 / D, bias=EPS), reads=["ss5"], writes=["rstd5"])
                em.op("act", lambda e: e.activation(out=rstd5[:], in_=rstd5[:], func=AF.Exp, scale=-0.5), reads=["rstd5"], writes=["rstd5"])
                for s in range(2):
                    em.op("dve", lambda e, s=s, h_=h_: e.scalar_tensor_tensor(out=xn2[:, s, :], in0=h_[:, s, :], scalar=rstd5[:, s:s + 1], in1=gfin[:], op0=ALU.mult, op1=ALU.mult), reads=[hk, "rstd5", "gfin"], writes=[f"xn2{s}"])
                em.dma("sp", lambda e, tk=tk: e.dma_start(out=out[tk * TK:(tk + 1) * TK, :].rearrange("(s p) d -> p s d", p=128), in_=xn2[:]), reads=["xn20", "xn21"], writes=[f"out{tk}"])
            em.flush()
        print("inst", em.n_inst, "waits", em.n_wait)
    return nc


_IN_NAMES = None


def core_inputs(inp, core):
    b, g = core // 2, core % 2
    m = dict(host_consts())
    xb = inp["x"][b]
    w_in = inp["w_in"][0]
    conv_w = inp["hyena_conv_w"][0]
    w3 = inp["filt_w3"][0]
    dec = inp["filt_decay"].reshape(2048)
    if g == 1:
        xb = xb[::-1]
        w_in = np.concatenate([w_in[:, 0:1024], w_in[:, 2048:3072], w_in[:, 1024:2048], w_in[:, 3072:]], axis=1)
        conv_w = conv_w[::-1]
        w3 = np.concatenate([w3[:, 1024:], w3[:, :1024]], axis=1)
        dec = np.concatenate([dec[1024:], dec[:1024]])
    m["x"] = np.ascontiguousarray(xb)
    m["xh"] = np.ascontiguousarray(xb[:L // 2])
    sel = np.zeros((128, 2), np.float32); sel[:, g] = 1.0
    m["sel"] = sel
    m["norm_mix_g"] = np.ascontiguousarray(inp["norm_mix_g"].reshape(1, D))
    m["w_in"] = np.ascontiguousarray(w_in)
    m["hgrn_lb_logits"] = np.ascontiguousarray(inp["hgrn_lb_logits"])
    m["filt_w1"] = np.ascontiguousarray(inp["filt_w1"][0]); m["filt_w2"] = np.ascontiguousarray(inp["filt_w2"][0])
    m["filt_w3"] = np.ascontiguousarray(w3)
    m["filt_vec"] = np.ascontiguousarray(np.stack([inp["filt_b1"][0], inp["filt_freq1"][0], inp["filt_b2"][0], inp["filt_freq2"][0]], 0))
    m["filt_decay"] = np.ascontiguousarray(dec.reshape(1, 2048))
    m["hyena_bias"] = np.ascontiguousarray(inp["hyena_bias"].reshape(1, 1024))
    m["conv_w"] = np.ascontiguousarray(conv_w); m["conv_b"] = np.ascontiguousarray(inp["hyena_conv_b"].reshape(1, 3072))
    m["hgrn_norm_g"] = np.ascontiguousarray(inp["hgrn_norm_g"].reshape(1, 128))
    for k in ("w_branch_a", "w_branch_b", "w_out", "peer_w_q", "peer_u", "peer_v"):
        m[k] = np.ascontiguousarray(inp[k][0])
    m["norm_ffn_g"] = np.ascontiguousarray(inp["norm_ffn_g"].reshape(1, D)); m["norm_final_g"] = np.ascontiguousarray(inp["norm_final_g"].reshape(1, D))
    m["peer_sk"] = np.ascontiguousarray(inp["peer_subkeys"][0].reshape(16, 128, 128))
    return m


def kernel(**inputs):
    inp = {k: np.asarray(v) for k, v in inputs.items()}
    nc = build()
    in_maps = [core_inputs(inp, c) for c in range(8)]
    res = run_bass_kernel_spmd(nc, in_maps, core_ids=list(range(8)))
    out = np.zeros((4, L, D), np.float32)
    for c in range(8):
        b, g = c // 2, c % 2
        r = np.asarray(res.results[c]["out"])
        if g == 0:
            out[b, :L // 2] = r
        else:
            out[b, L // 2:] = r[::-1]
    return out
```

```python
import math
import numpy as np
from contextlib import ExitStack
import concourse.bass as bass
import concourse.mybir as mybir
from concourse.bass_utils import run_bass_kernel_spmd

F32 = mybir.dt.float32
BF16 = mybir.dt.bfloat16
U32 = mybir.dt.uint32
ALU = mybir.AluOpType
AF = mybir.ActivationFunctionType
AX = mybir.AxisListType

L = 8192
D = 1024
NCOL = 10240
TT = 512
NTILE = L // TT
EPS = 1e-6

ENGS = ("pe", "act", "dve", "pool", "sp")
ENGMAP = {"pe": "tensor", "act": "scalar", "dve": "vector", "pool": "gpsimd", "sp": "sync"}
SEM_LIMIT = 30000
NDMA = 32


class Em:
    def __init__(self, nc, es):
        self.nc = nc
        self.es = es
        self.q = {e: [] for e in ENGS}
        self.sems = {e: [es.enter_context(nc.semaphore(f"s_{e}_0"))] for e in ENGS}
        self.cnt = {e: 0 for e in ENGS}
        self.dsem = [es.enter_context(nc.semaphore(f"d_{i}")) for i in range(NDMA)]
        self.dcnt = [0] * NDMA
        self.dnext = 0
        self.waited = {e: {} for e in ENGS}
        self.lastw = {}
        self.readers = {}
        self.n_inst = 0
        self.n_wait = 0
        self.pending = []

    def _tok_new(self, eng):
        if self.cnt[eng] >= SEM_LIMIT:
            self.sems[eng].append(
                self.es.enter_context(self.nc.semaphore(f"s_{eng}_{len(self.sems[eng])}")))
            self.cnt[eng] = 0
        self.cnt[eng] += 1
        return (self.sems[eng][-1], self.cnt[eng])

    NOKEYS = frozenset(["WIN", "QD0", "QD1", "KD0", "KD1", "KDTM0", "KDTM1", "VT", "OGT", "HYT", "GT", "HF", "UT",
                        "X0T", "YCT", "OT", "AT", "DEC0", "DEC1", "UTS", "VBF", "UBF"])

    def _deps(self, reads, writes):
        reads = [k for k in reads if k not in self.NOKEYS]
        writes = [k for k in writes if k not in self.NOKEYS]
        deps = []
        for k in reads:
            lw = self.lastw.get(k)
            if lw is not None:
                deps.append(lw)
        for k in writes:
            lw = self.lastw.get(k)
            if lw is not None:
                deps.append(lw)
            deps.extend(self.readers.get(k, ()))
        return deps

    def _emit_waits(self, eng, deps, skip_sems=()):
        w = self.waited[eng]
        need = {}
        for (sem, val) in deps:
            sid = id(sem)
            if sid in skip_sems:
                continue
            if w.get(sid, 0) >= val:
                continue
            if sid not in need or need[sid][1] < val:
                need[sid] = (sem, val)
        for sid, (sem, val) in need.items():
            w[sid] = val
            self.q[eng].append(("wait", sem, val))
            self.n_wait += 1

    def _record(self, tok, reads, writes):
        reads = [k for k in reads if k not in self.NOKEYS]
        writes = [k for k in writes if k not in self.NOKEYS]
        for k in reads:
            self.readers.setdefault(k, []).append(tok)
        for k in writes:
            self.lastw[k] = tok
            self.readers[k] = []

    NO_SELF_WAIT = ("pe",)
    STORE_DELAY = 48

    def _pending_tick(self, reads, writes, force=False):
        if not self.pending:
            return
        keep = []
        ws = set(writes)
        rs = set(reads)
        for p in self.pending:
            p[0] -= 1
            if force or p[0] <= 0 or (ws and (ws.intersection(p[3]) or ws.intersection(p[4]))) or (rs and rs.intersection(p[4])):
                self._dma_now(p[1], p[2], p[3], p[4])
            else:
                keep.append(p)
        self.pending = keep

    def store(self, eng, fn, reads=(), writes=()):
        self.pending.append([self.STORE_DELAY, eng, fn, list(reads), list(writes)])

    def op(self, eng, fn, reads=(), writes=()):
        self._pending_tick(reads, writes)
        deps = self._deps(reads, writes)
        skip = tuple(id(s) for s in self.sems[eng]) if eng in self.NO_SELF_WAIT else ()
        self._emit_waits(eng, deps, skip)
        tok = self._tok_new(eng)
        self.q[eng].append(("op", fn, tok, 1))
        self._record(tok, reads, writes)
        self.n_inst += 1
        return tok

    def dma(self, eng, fn, reads=(), writes=()):
        self._pending_tick(reads, writes)
        self._dma_now(eng, fn, reads, writes)

    def _dma_now(self, eng, fn, reads=(), writes=()):
        deps = self._deps(reads, writes)
        slot = self.dnext
        self.dnext = (self.dnext + 1) % NDMA
        if self.dcnt[slot] > 0:
            deps.append((self.dsem[slot], self.dcnt[slot]))
        self._emit_waits(eng, deps)
        self.dcnt[slot] += 16
        tok = (self.dsem[slot], self.dcnt[slot])
        self.q[eng].append(("op", fn, tok, 16))
        self._record(tok, reads, writes)
        self.n_inst += 1
        return tok

    def flush(self):
        nc = self.nc
        self._pending_tick((), (), force=True)
        final = []
        for i in range(NDMA):
            if self.dcnt[i]:
                final.append((self.dsem[i], self.dcnt[i]))
        for e in ENGS:
            if self.cnt[e]:
                final.append((self.sems[e][-1], self.cnt[e]))
        self._emit_waits("sp", final)
        with nc.Block() as block:
            for e in ENGS:
                items = self.q[e]

                def body(engine, items=items):
                    for it in items:
                        if it[0] == "wait":
                            engine.wait_ge(it[1], it[2])
                        else:
                            it[1](engine).then_inc(it[2][0], it[3])
                getattr(block, ENGMAP[e])(body)
        self.q = {e: [] for e in ENGS}
        self.lastw = {}
        self.readers = {}


def run_pipeline(gens, max_new_per_step=1, max_active=16):
    it = iter(gens)
    active = []
    exhausted = False
    while True:
        if not exhausted and len(active) < max_active:
            try:
                active.append(next(it))
            except StopIteration:
                exhausted = True
        if not active:
            if exhausted:
                break
            continue
        nxt = []
        for g in active:
            try:
                next(g)
                nxt.append(g)
            except StopIteration:
                pass
        active = nxt


class Rot:
    def __init__(self, sbf, name, n, shape, dt):
        self.t = [sbf(f"{name}{i}", shape, dt) for i in range(n)]
        self.k = [f"{name}{i}" for i in range(n)]
        self.i = -1

    def nxt(self):
        self.i = (self.i + 1) % len(self.t)
        return self.t[self.i], self.k[self.i]


def host_consts():
    c = {}
    c["ident"] = np.eye(128, dtype=np.float32)
    s = np.arange(64)
    mf = (s[:, None] <= s[None, :]).astype(np.float32)
    mb = (s[:, None] >= s[None, :]).astype(np.float32)
    c["maskf"] = np.ascontiguousarray(np.broadcast_to(mf[:, None, :], (64, 8, 64))).reshape(64, 512)
    c["maskb"] = np.ascontiguousarray(np.broadcast_to(mb[:, None, :], (64, 8, 64))).reshape(64, 512)
    t = np.arange(TT)
    rf = np.ones((128, TT), np.float32); rf[:, t % 64 == 0] = 0
    rb = np.ones((128, TT), np.float32); rb[:, t % 64 == 63] = 0
    c["rmf"] = rf
    c["rmb"] = rb
    n = np.arange(128, dtype=np.float64)
    ang = 2 * np.pi * np.outer(n, n) / 128.0
    Fr = np.cos(ang); Fi = -np.sin(ang)
    c["FRI"] = np.concatenate([Fr, Fi], 1).astype(np.float32)
    c["FRnI"] = np.concatenate([Fr, -Fi], 1).astype(np.float32)
    c["FIR"] = np.concatenate([Fi, Fr], 1).astype(np.float32)
    c["nFI"] = (-Fi).astype(np.float32)
    angt = 2 * np.pi * np.outer(n, n) / 16384.0
    Tr = np.cos(angt); Ti = -np.sin(angt)
    c["TW"] = np.concatenate([Tr, Ti], 1).astype(np.float32)
    c["TWc"] = np.concatenate([Tr, -Ti], 1).astype(np.float32)
    pos = np.arange(L, dtype=np.float32)
    tpos = pos / np.float32(L - 1)
    bands = np.linspace(1e-4, 15, 16, dtype=np.float32)
    angz = (np.float32(2.0 * math.pi / L) * pos[:, None]) * bands[None, :]
    z = np.concatenate([tpos[:, None], np.cos(angz), -np.sin(angz)], -1).astype(np.float32)
    c["zT"] = np.ascontiguousarray(z.T)
    c["tpos"] = np.ascontiguousarray(tpos.reshape(1, L))
    c["iota"] = np.ascontiguousarray(np.broadcast_to(np.arange(128, dtype=np.float32)[None, :], (128, 128)))
    return c


def build(stop_after=99, dbg=(), skip=(), ext_in=()):
    nc = bass.Bass("TRN2", target_bir_lowering=False)
    _uqc = [0]

    def uq(n):
        _uqc[0] += 1
        return f"{n}_u{_uqc[0]}"
    EI = dict(kind="ExternalInput")
    def din(name, shape, dt=F32):
        return nc.dram_tensor(name, list(shape), dt, **EI).ap()
    def dscr(name, shape, dt):
        kind = "ExternalOutput" if name in dbg else ("ExternalInput" if name in ext_in else "Internal")
        return nc.dram_tensor(name, list(shape), dt, kind=kind).ap()

    x = din("x", [L, D])
    xh = din("xh", [L // 2, D])
    norm_mix_g = din("norm_mix_g", [1, D])
    w_in = din("w_in", [D, NCOL])
    lbl = din("hgrn_lb_logits", [2, D])
    ident = din("ident", [128, 128])
    maskf = din("maskf", [64, 512]); maskb = din("maskb", [64, 512])
    rmf = din("rmf", [128, TT]); rmb = din("rmb", [128, TT])
    out = nc.dram_tensor("out", [L // 2, D], F32, kind="ExternalOutput").ap()

    WIN = dscr("WIN", [D, NCOL], BF16)
    QD = [dscr(f"QD{d}", [8, 128, L], BF16) for d in range(2)]
    KD = [dscr(f"KD{d}", [8, 128, L], BF16) for d in range(2)]
    KDTM = [dscr(f"KDTM{d}", [L, D], BF16) for d in range(2)]
    DEC = [dscr(f"DEC{d}", [128, 8, 128], F32) for d in range(2)]
    VT = dscr("VT", [L, D], BF16)
    OGT = dscr("OGT", [D, L], BF16)
    HYT = dscr("HYT", [3 * D, L], F32)
    GT = dscr("GT", [2 * D, L], BF16)
    OF = dscr("OF", [8, 128, L], F32)
    OT = dscr("OT", [8, 128, L], F32)
    filt_w1 = din("filt_w1", [33, 64]); filt_w2 = din("filt_w2", [64, 64]); filt_w3 = din("filt_w3", [64, 2048])
    filt_vec = din("filt_vec", [4, 64])
    filt_decay = din("filt_decay", [1, 2048]); hyena_bias = din("hyena_bias", [1, 1024])
    conv_w = din("conv_w", [3, 3072]); conv_b = din("conv_b", [1, 3072])
    zT = din("zT", [33, L]); tpos = din("tpos", [1, L]); sel = din("sel", [128, 2])
    FRI = din("FRI", [128, 256]); FRnI = din("FRnI", [128, 256]); FIR = din("FIR", [128, 256]); nFI = din("nFI", [128, 128])
    TW = din("TW", [128, 256]); TWc = din("TWc", [128, 256])
    HF = dscr("HF", [2048, L], BF16)
    UT = dscr("UT", [D, L], BF16)
    X0T = dscr("X0T", [D, L], F32)
    KF = dscr("KF", [D, 128, 2, 128], F32)
    YCT = dscr("YCT", [D, L], F32)
    hgrn_norm_g = din("hgrn_norm_g", [1, 128])
    w_branch_a = din("w_branch_a", [D, D]); w_branch_b = din("w_branch_b", [D, D]); w_out = din("w_out", [D, D])
    norm_ffn_g = din("norm_ffn_g", [1, D]); norm_final_g = din("norm_final_g", [1, D])
    peer_w_q = din("peer_w_q", [D, 2048]); peer_sk = din("peer_sk", [16, 128, 128])
    peer_u = din("peer_u", [16384, D]); peer_v = din("peer_v", [16384, D])
    iota = din("iota", [128, 128])
    AT = dscr("AT", [D, L // 2], BF16) if "AT" in dbg else None
    MG = dscr("MG", [D, L // 2], BF16) if "MG" in dbg else None
    PEERO = dscr("PEERO", [L // 2, D], F32) if "PEERO" in dbg else None
    H1 = dscr("H1", [L // 2, D], F32)
    VBF = dscr("VBF", [16384, D], BF16)
    UTS = dscr("UTS", [128, 128, 8, 128], BF16)
    TOK0 = 0
    OTh = OT[:, :, 0:L // 2]; OGTh = OGT[:, 0:L // 2]; X0Th = X0T[:, 0:L // 2]; YCTh = YCT[:, 0:L // 2]; GTh = GT[:, 0:L // 2]

    es0 = ExitStack()
    with es0:
        em = Em(nc, es0)

        for r in range(8):
            em.dma("pool", lambda e, r=r: e.dma_start(out=WIN[r * 128:(r + 1) * 128, :], in_=w_in[r * 128:(r + 1) * 128, :]),
                   writes=["WIN"])
        em.flush()
        if stop_after <= 0:
            return nc

        if 1 not in skip:
          with ExitStack() as es:
              sbf = lambda n, s, d: es.enter_context(nc.sbuf_tensor(uq(n), list(s), d))
              psf = lambda n, s, d: es.enter_context(nc.psum_tensor(uq(n), list(s), d))
              gt = sbf("gt", [128, D], F32)
              idf = sbf("idf", [128, 128], F32)
              idb = sbf("idb", [128, 128], BF16)
              lb2 = sbf("lb2", [128, 2, 8], F32)
              lbt = sbf("lbt", [128, 8], F32)
              olt = sbf("olt", [128, 8], F32)
              nolt = sbf("nolt", [128, 8], F32)
              rmf_t = sbf("rmf_t", [128, TT], F32)
              rmb_t = sbf("rmb_t", [128, TT], F32)
              dec_t = [sbf(f"dec_t{d}", [128, 8, 128], F32) for d in range(2)]
              xt = sbf("xt", [128, 4, D], F32)
              sqj = sbf("sqj", [128, D], F32)
              ss = sbf("ss", [128, 4], F32)
              rstd = sbf("rstd", [128, 4], F32)
              xn = sbf("xn", [128, 4, D], BF16)
              xnT = Rot(sbf, "xnT", 2, [128, 8, TT], BF16)
              wg = Rot(sbf, "wg", 2, [128, 8, 1024], BF16)
              QS = sbf("QS", [128, 8, TT], F32)
              SG = Rot(sbf, "SG", 2, [128, TT], F32)
              KK = Rot(sbf, "KK", 2, [128, TT], F32)
              LF = Rot(sbf, "LF", 2, [128, TT], F32)
              BB = Rot(sbf, "BB", 2, [128, TT], F32)
              E1 = Rot(sbf, "E1", 2, [128, TT], F32)
              E2 = Rot(sbf, "E2", 2, [128, TT], F32)
              QDs = Rot(sbf, "QDs", 2, [128, TT], BF16)
              KDs = Rot(sbf, "KDs", 2, [128, TT], BF16)
              KTs = Rot(sbf, "KTs", 2, [128, 4, 128], BF16)
              OB = Rot(sbf, "OB", 3, [128, TT], BF16)
              OFt = Rot(sbf, "OFt", 3, [128, TT], F32)
              pT = psf("pT", [128, 8, 128], BF16)
              pM = [psf(f"pM{i}", [128, TT], F32) for i in range(4)]
              pK = [psf(f"pK{i}", [128, 4, 128], BF16) for i in range(2)]
              pmi = [0]

              em.dma("sp", lambda e: e.dma_start(out=gt[:], in_=norm_mix_g.partition_broadcast(128)), writes=["gt"])
              em.dma("sp", lambda e: e.dma_start(out=idf[:], in_=ident), writes=["idf"])
              em.dma("sp", lambda e: e.dma_start(out=rmf_t[:], in_=rmf), writes=["rmf"])
              em.dma("sp", lambda e: e.dma_start(out=rmb_t[:], in_=rmb), writes=["rmb"])
              em.dma("sp", lambda e: e.dma_start(out=lb2[:], in_=lbl.rearrange("t (h k) -> k t h", k=128), allow_slow_non_contiguous=True), writes=["lb2"])
              em.op("dve", lambda e: e.tensor_copy(out=idb[:], in_=idf[:]), reads=["idf"], writes=["idb"])
              em.op("dve", lambda e: e.tensor_tensor(out=lbt[:], in0=lb2[:, 1, :], in1=lb2[:, 0, :], op=ALU.subtract), reads=["lb2"], writes=["lbt"])
              em.op("act", lambda e: e.activation(out=lbt[:], in_=lbt[:], func=AF.Exp), reads=["lbt"], writes=["lbt"])
              em.op("dve", lambda e: e.tensor_scalar(out=lbt[:], in0=lbt[:], scalar1=1.0, scalar2=None, op0=ALU.add), reads=["lbt"], writes=["lbt"])
              em.op("dve", lambda e: e.reciprocal(out=lbt[:], in_=lbt[:]), reads=["lbt"], writes=["lbt"])
              em.op("dve", lambda e: e.tensor_scalar(out=olt[:], in0=lbt[:], scalar1=-1.0, scalar2=1.0, op0=ALU.mult, op1=ALU.add), reads=["lbt"], writes=["olt"])
              em.op("dve", lambda e: e.tensor_scalar(out=nolt[:], in0=olt[:], scalar1=-1.0, scalar2=None, op0=ALU.mult), reads=["olt"], writes=["nolt"])

              def next_pm():
                  pmi[0] = (pmi[0] + 1) % 4
                  return pM[pmi[0]], f"pM{pmi[0]}"

              ntile = NTILE if stop_after > 1 else 1
              for tt in range(ntile):
                  t0 = tt * TT
                  em.dma("sp", lambda e, t0=t0: e.dma_start(out=xt[:], in_=x[t0:t0 + TT, :].rearrange("(s p) d -> p s d", p=128)), writes=["xt"])
                  for s in range(4):
                      em.op("act", lambda e, s=s: e.activation(out=sqj[:], in_=xt[:, s, :], func=AF.Square, accum_out=ss[:, s:s + 1]),
                            reads=["xt"], writes=["sqj", "ss"])
                  em.op("act", lambda e: e.activation(out=rstd[:], in_=ss[:], func=AF.Ln, scale=1.0 / D, bias=EPS), reads=["ss"], writes=["rstd"])
                  em.op("act", lambda e: e.activation(out=rstd[:], in_=rstd[:], func=AF.Exp, scale=-0.5), reads=["rstd"], writes=["rstd"])
                  xT, xTk = xnT.nxt()
                  for s in range(4):
                      em.op("dve", lambda e, s=s: e.scalar_tensor_tensor(out=xn[:, s, :], in0=xt[:, s, :], scalar=rstd[:, s:s + 1], in1=gt[:], op0=ALU.mult, op1=ALU.mult),
                            reads=["xt", "rstd", "gt"], writes=[f"xn{s}"])
                      for k in range(8):
                          em.op("pe", lambda e, s=s, k=k: e.transpose(out=pT[:, k, :], in_=xn[:, s, k * 128:(k + 1) * 128], identity=idb[:]),
                                reads=[f"xn{s}", "idb"], writes=["pT"])
                      em.op("act" if s % 2 else "dve", lambda e, s=s, xT=xT: (e.activation(out=xT[:, :, s * 128:(s + 1) * 128], in_=pT[:], func=AF.Copy) if s % 2 else e.tensor_copy(out=xT[:, :, s * 128:(s + 1) * 128], in_=pT[:])),
                            reads=["pT"], writes=[xTk])
                  for g in range(10):
                      w, wk = wg.nxt()
                      em.dma("sp", lambda e, g=g, w=w: e.dma_start(out=w[:], in_=WIN[:, g * 1024:(g + 1) * 1024].rearrange("(k p) c -> p k c", p=128)),
                             reads=["WIN"], writes=[wk])
                      if g == 3:
                          for s in range(4):
                              for hf in range(2):
                                  pm, pmk = next_pm()
                                  for k in range(8):
                                      em.op("pe", lambda e, pm=pm, k=k, s=s, hf=hf, w=w, xT=xT: e.matmul(pm[:], lhsT=xT[:, k, s * 128:(s + 1) * 128], rhs=w[:, k, hf * 512:(hf + 1) * 512], start=(k == 0), stop=(k == 7)),
                                            reads=[xTk, wk], writes=[pmk])
                                  ob, obk = OB.nxt()
                                  em.op("dve", lambda e, pm=pm, ob=ob: e.tensor_copy(out=ob[:], in_=pm[:]), reads=[pmk], writes=[obk])
                                  em.store("sp", lambda e, ob=ob, s=s, hf=hf, t0=t0: e.dma_start(out=VT[t0 + s * 128:t0 + (s + 1) * 128, hf * 512:(hf + 1) * 512], in_=ob[:]),
                                         reads=[obk], writes=["VT"])
                          continue
                      for cb in range(8):
                          pm, pmk = next_pm()
                          for k in range(8):
                              em.op("pe", lambda e, pm=pm, k=k, cb=cb, w=w, xT=xT: e.matmul(pm[:], lhsT=w[:, k, cb * 128:(cb + 1) * 128], rhs=xT[:, k, :], start=(k == 0), stop=(k == 7)),
                                    reads=[xTk, wk], writes=[pmk])
                          if g == 0:
                              em.op("act", lambda e, pm=pm, cb=cb: e.activation(out=QS[:, cb, :], in_=pm[:], func=AF.Silu), reads=[pmk], writes=[f"QS{cb}"])
                          elif g in (1, 2):
                              d = g - 1
                              h = cb
                              sg, sgk = SG.nxt(); kk, kkk = KK.nxt(); lf, lfk = LF.nxt(); bb, bbk = BB.nxt()
                              e1, e1k = E1.nxt(); e2, e2k = E2.nxt(); qd, qdk = QDs.nxt(); kd, kdk = KDs.nxt()
                              kt, ktk = KTs.nxt()
                              em.op("act", lambda e, pm=pm, sg=sg: e.activation(out=sg[:], in_=pm[:], func=AF.Sigmoid), reads=[pmk], writes=[sgk])
                              em.op("dve", lambda e, sg=sg, kk=kk, h=h: e.tensor_scalar(out=kk[:], in0=sg[:], scalar1=nolt[:, h:h + 1], scalar2=olt[:, h:h + 1], op0=ALU.mult, op1=ALU.add),
                                    reads=[sgk, "nolt", "olt"], writes=[kkk])
                              em.op("act", lambda e, sg=sg, lf=lf, h=h: e.activation(out=lf[:], in_=sg[:], func=AF.Ln, scale=olt[:, h:h + 1], bias=lbt[:, h:h + 1]),
                                    reads=[sgk, "olt", "lbt"], writes=[lfk])
                              if d == 0:
                                  em.op("dve", lambda e, bb=bb, lf=lf: e.tensor_tensor_scan(out=bb[:], data0=rmf_t[:], data1=lf[:], initial=0.0, op0=ALU.mult, op1=ALU.add),
                                        reads=[lfk, "rmf"], writes=[bbk])
                              else:
                                  em.op("dve", lambda e, bb=bb, lf=lf: e.tensor_tensor_scan(out=bb[:, ::-1], data0=rmb_t[:, ::-1], data1=lf[:, ::-1], initial=0.0, op0=ALU.mult, op1=ALU.add),
                                        reads=[lfk, "rmb"], writes=[bbk])
                              em.op("act", lambda e, bb=bb, e1=e1: e.activation(out=e1[:], in_=bb[:], func=AF.Exp), reads=[bbk], writes=[e1k])
                              em.op("act", lambda e, bb=bb, e2=e2: e.activation(out=e2[:], in_=bb[:], func=AF.Exp, scale=-1.0), reads=[bbk], writes=[e2k])
                              em.op("pool", lambda e, qd=qd, e1=e1, h=h: e.tensor_tensor(out=qd[:], in0=QS[:, h, :], in1=e1[:], op=ALU.mult), reads=[f"QS{h}", e1k], writes=[qdk])
                              em.op("pool", lambda e, kd=kd, e2=e2, kk=kk: e.tensor_tensor(out=kd[:], in0=kk[:], in1=e2[:], op=ALU.mult), reads=[kkk, e2k], writes=[kdk])
                              off = 63 if d == 0 else 0
                              em.op("dve", lambda e, e1=e1, d=d, h=h, tt=tt, off=off: e.tensor_copy(out=dec_t[d][:, h, tt * 8:(tt + 1) * 8], in_=e1[:, off::64]),
                                    reads=[e1k], writes=[f"dec{d}"])
                              em.store("sp", lambda e, qd=qd, d=d, h=h, t0=t0: e.dma_start(out=QD[d][h, :, t0:t0 + TT], in_=qd[:]), reads=[qdk], writes=[f"QD{d}"])
                              em.store("sp", lambda e, kd=kd, d=d, h=h, t0=t0: e.dma_start(out=KD[d][h, :, t0:t0 + TT], in_=kd[:]), reads=[kdk], writes=[f"KD{d}"])
                              pk = pK[h % 2]; pkk = f"pK{h % 2}"
                              for s in range(4):
                                  em.op("pe", lambda e, pk=pk, kd=kd, s=s: e.transpose(out=pk[:, s, :], in_=kd[:, s * 128:(s + 1) * 128], identity=idb[:]),
                                        reads=[kdk, "idb"], writes=[pkk])
                              em.op("dve", lambda e, pk=pk, kt=kt: e.tensor_copy(out=kt[:], in_=pk[:]), reads=[pkk], writes=[ktk])
                              em.store("sp", lambda e, kt=kt, d=d, h=h, t0=t0: e.dma_start(out=KDTM[d][t0:t0 + TT, h * 128:(h + 1) * 128].rearrange("(s p) k -> p s k", p=128), in_=kt[:]),
                                     reads=[ktk], writes=[f"KDTM{d}"])
                          elif g == 4:
                              ob, obk = OB.nxt()
                              em.op("act", lambda e, pm=pm, ob=ob: e.activation(out=ob[:], in_=pm[:], func=AF.Silu), reads=[pmk], writes=[obk])
                              em.store("sp", lambda e, ob=ob, cb=cb, t0=t0: e.dma_start(out=OGT[cb * 128:(cb + 1) * 128, t0:t0 + TT], in_=ob[:]), reads=[obk], writes=["OGT"])
                          elif g in (5, 6, 7):
                              of_, ofk = OFt.nxt()
                              r0 = (g - 5) * 1024 + cb * 128
                              em.op("dve", lambda e, pm=pm, of_=of_: e.tensor_copy(out=of_[:], in_=pm[:]), reads=[pmk], writes=[ofk])
                              em.store("sp", lambda e, of_=of_, r0=r0, t0=t0: e.dma_start(out=HYT[r0:r0 + 128, t0:t0 + TT], in_=of_[:]), reads=[ofk], writes=["HYT"])
                          else:
                              ob, obk = OB.nxt()
                              r0 = (g - 8) * 1024 + cb * 128
                              em.op("act", lambda e, pm=pm, ob=ob: e.activation(out=ob[:], in_=pm[:], func=AF.Sigmoid), reads=[pmk], writes=[obk])
                              em.store("sp", lambda e, ob=ob, r0=r0, t0=t0: e.dma_start(out=GT[r0:r0 + 128, t0:t0 + TT], in_=ob[:]), reads=[obk], writes=["GT"])
              for d in range(2):
                  em.store("sp", lambda e, d=d: e.dma_start(out=DEC[d], in_=dec_t[d][:]), reads=[f"dec{d}"], writes=[f"DEC{d}"])
              em.flush()
        if stop_after <= 1:
            print("inst", em.n_inst, "waits", em.n_wait)
            return nc

        if 2 not in skip:
          with ExitStack() as es:
              sbf = lambda n, s, d: es.enter_context(nc.sbuf_tensor(uq(n), list(s), d))
              psf = lambda n, s, d: es.enter_context(nc.psum_tensor(uq(n), list(s), d))
              mk = [sbf("mkf", [64, 512], F32), sbf("mkb", [64, 512], F32)]
              dect = [sbf(f"dect{d}", [128, 8, 128], F32) for d in range(2)]
              S = sbf("S", [128, 8, 128], F32)
              Sb = sbf("Sb", [128, 8, 128], BF16)
              tmpS = sbf("tmpS", [128, 8, 128], F32)
              qdt = Rot(sbf, "qdt", 2, [128, 8, TT], BF16)
              kdt = Rot(sbf, "kdt", 2, [128, 8, TT], BF16)
              ktm = Rot(sbf, "ktm", 2, [64, 8, D], BF16)
              vtm = Rot(sbf, "vtm", 2, [64, 8, D], BF16)
              scb = Rot(sbf, "scb", 2, [64, 8, 64], BF16)
              Ot = Rot(sbf, "Ot", 2, [128, 8, TT], F32)
              Of = Rot(sbf, "Of", 2, [128, 8, TT], F32)
              pS = [psf(f"pS{i}", [64, 8, 64], F32) for i in range(2)]
              pO = [psf(f"pO{i}", [128, 8, 64], F32) for i in range(2)]
              pP = [psf(f"pP{i}", [128, 4, 128], F32) for i in range(2)]
              em.dma("sp", lambda e: e.dma_start(out=mk[0][:], in_=maskf), writes=["mk0"])
              em.dma("sp", lambda e: e.dma_start(out=mk[1][:], in_=maskb), writes=["mk1"])
              for d in range(2):
                  em.dma("sp", lambda e, d=d: e.dma_start(out=dect[d][:], in_=DEC[d]), writes=[f"dect{d}"])
              for d in range(2):
                  em.op("pool", lambda e: e.memset(S[:], 0.0), writes=["S"])
                  em.op("pool", lambda e: e.memset(Sb[:], 0.0), writes=["Sb"])
                  tiles = range(NTILE) if d == 0 else range(NTILE - 1, -1, -1)
                  for tt in tiles:
                      t0 = tt * TT
                      q_, qk = qdt.nxt(); k_, kk_ = kdt.nxt(); kt_, ktk = ktm.nxt(); v_, vk = vtm.nxt()
                      o_, ok = Ot.nxt()
                      em.dma("sp", lambda e, q_=q_, d=d, t0=t0: e.dma_start(out=q_[:], in_=QD[d][:, :, t0:t0 + TT].rearrange("h k t -> k h t")), writes=[qk])
                      em.dma("sp", lambda e, k_=k_, d=d, t0=t0: e.dma_start(out=k_[:], in_=KD[d][:, :, t0:t0 + TT].rearrange("h k t -> k h t")), writes=[kk_])
                      em.dma("sp", lambda e, kt_=kt_, d=d, t0=t0: e.dma_start(out=kt_[:], in_=KDTM[d][t0:t0 + TT, :].rearrange("(c s) k -> s c k", s=64)), writes=[ktk])
                      em.dma("sp", lambda e, v_=v_, t0=t0: e.dma_start(out=v_[:], in_=VT[t0:t0 + TT, :].rearrange("(c s) k -> s c k", s=64)), writes=[vk])
                      if d == 1:
                          f_, fk = Of.nxt()
                          em.dma("sp", lambda e, f_=f_, t0=t0: e.dma_start(out=f_[:], in_=OF[:, :, t0:t0 + TT].rearrange("h k t -> k h t")), reads=[f"OF{tt}"], writes=[fk])
                      chunks = range(8) if d == 0 else range(7, -1, -1)
                      for c in chunks:
                          gc = tt * 8 + c
                          cs = slice(c * 64, (c + 1) * 64)
                          ps_ = pS[gc % 2]; psk = f"pS{gc % 2}"
                          po_ = pO[gc % 2]; pok = f"pO{gc % 2}"
                          for h in range(8):
                              em.op("pe", lambda e, ps_=ps_, h=h, k_=k_, q_=q_, cs=cs: e.matmul(ps_[:, h, :], lhsT=k_[:, h, cs], rhs=q_[:, h, cs], start=True, stop=True),
                                    reads=[kk_, qk], writes=[psk])
                          sb_, sbk = scb.nxt()
                          em.op("dve", lambda e, sb_=sb_, ps_=ps_, d=d: e.tensor_tensor(out=sb_[:], in0=ps_[:], in1=mk[d][:].rearrange("p (h t) -> p h t", h=8), op=ALU.mult),
                                reads=[psk, f"mk{d}"], writes=[sbk])
                          for h in range(8):
                              hs = slice(h * 128, (h + 1) * 128)
                              em.op("pe", lambda e, po_=po_, h=h, v_=v_, sb_=sb_, c=c, hs=hs: e.matmul(po_[:, h, :], lhsT=v_[:, c, hs], rhs=sb_[:, h, :], start=True, stop=False),
                                    reads=[vk, sbk], writes=[pok])
                              em.op("pe", lambda e, po_=po_, h=h, q_=q_, cs=cs: e.matmul(po_[:, h, :], lhsT=Sb[:, h, :], rhs=q_[:, h, cs], start=False, stop=True),
                                    reads=["Sb", qk], writes=[pok])
                          if d == 0:
                              em.op("act", lambda e, o_=o_, po_=po_, cs=cs: e.activation(out=o_[:, :, cs], in_=po_[:], func=AF.Copy), reads=[pok], writes=[ok])
                          else:
                              em.op("dve", lambda e, o_=o_, po_=po_, f_=f_, cs=cs: e.tensor_tensor(out=o_[:, :, cs], in0=po_[:], in1=f_[:, :, cs], op=ALU.add), reads=[pok, fk], writes=[ok])
                          decb = dect[d][:, :, gc:gc + 1]
                          for hh in range(2):
                              pp = pP[hh]; ppk = f"pP{hh}"
                              for h4 in range(4):
                                  h = hh * 4 + h4
                                  hs = slice(h * 128, (h + 1) * 128)
                                  em.op("pe", lambda e, pp=pp, h4=h4, kt_=kt_, v_=v_, c=c, hs=hs: e.matmul(pp[:, h4, :], lhsT=kt_[:, c, hs], rhs=v_[:, c, hs], start=True, stop=True),
                                        reads=[ktk, vk], writes=[ppk])
                              h4s = slice(hh * 4, hh * 4 + 4)
                              em.op("dve", lambda e, pp=pp, h4s=h4s, decb=decb: e.tensor_tensor(out=tmpS[:, h4s, :], in0=pp[:], in1=decb[:, h4s, :].to_broadcast([128, 4, 128]), op=ALU.mult),
                                    reads=[ppk, f"dect{d}"], writes=[f"tmpS{hh}"])
                          em.op("pool", lambda e, decb=decb: e.tensor_tensor(out=S[:], in0=S[:], in1=decb.to_broadcast([128, 8, 128]), op=ALU.mult),
                                reads=["S", f"dect{d}"], writes=["S"])
                          em.op("pool", lambda e: e.tensor_tensor(out=S[:], in0=S[:], in1=tmpS[:], op=ALU.add), reads=["S", "tmpS0", "tmpS1"], writes=["S"])
                          em.op("act", lambda e: e.activation(out=Sb[:], in_=S[:], func=AF.Copy), reads=["S"], writes=["Sb"])
                      dst = OF if d == 0 else OT
                      em.store("sp", lambda e, o_=o_, dst=dst, t0=t0: e.dma_start(out=dst[:, :, t0:t0 + TT].rearrange("h k t -> k h t"), in_=o_[:]), reads=[ok], writes=[f"OF{tt}" if d == 0 else "OT"])
              em.flush()
        print("inst", em.n_inst, "waits", em.n_wait)
        if stop_after <= 2:
            return nc
        if 3 not in skip:
          with ExitStack() as es:
            sbf = lambda n, s, d: es.enter_context(nc.sbuf_tensor(uq(n), list(s), d))
            psf = lambda n, s, d: es.enter_context(nc.psum_tensor(uq(n), list(s), d))
            w1t = sbf("w1t", [33, 64], F32); w2t = sbf("w2t", [64, 64], F32); w3t = sbf("w3t", [64, 2048], F32)
            fb = sbf("fb", [64, 4], F32)
            fs = sbf("fs", [64, 4], F32)
            dcy = sbf("dcy", [128, 16], F32)
            hbias = sbf("hbias", [128, 8], F32)
            tpb = sbf("tpb", [128, TT], F32)
            zt = Rot(sbf, "zt", 2, [33, TT], F32)
            ya = Rot(sbf, "ya", 2, [64, TT], F32)
            yb_ = Rot(sbf, "yb_", 2, [64, TT], F32)
            hd1 = Rot(sbf, "hd1", 2, [64, TT], F32)
            hd2 = Rot(sbf, "hd2", 2, [64, TT], F32)
            wn = Rot(sbf, "wn", 2, [128, TT], F32)
            fo = Rot(sbf, "fo", 2, [128, TT], F32)
            fob = Rot(sbf, "fob", 3, [128, TT], BF16)
            pF = [psf(f"pF{i}", [128, TT], F32) for i in range(3)]
            lag0 = sbf("lag0", [128, 16], F32); lagc = sbf("lagc", [128, 16], F32); lagb = sbf("lagb", [128, 16], BF16)
            selt = sbf("selt", [128, 2], F32)
            em.dma("sp", lambda e: e.dma_start(out=selt[:], in_=sel), writes=["selt"])
            em.dma("sp", lambda e: e.dma_start(out=w1t[:], in_=filt_w1), writes=["w1t"])
            em.dma("sp", lambda e: e.dma_start(out=w2t[:], in_=filt_w2), writes=["w2t"])
            em.dma("sp", lambda e: e.dma_start(out=w3t[:], in_=filt_w3), writes=["w3t"])
            em.dma("sp", lambda e: e.dma_start(out=fb[:], in_=filt_vec.rearrange("j k -> k j"), allow_slow_non_contiguous=True), writes=["fb"])
            em.dma("sp", lambda e: e.dma_start(out=dcy[:], in_=filt_decay.rearrange("o (b p) -> p (o b)", p=128), allow_slow_non_contiguous=True), writes=["dcy"])
            em.dma("sp", lambda e: e.dma_start(out=hbias[:], in_=hyena_bias.rearrange("o (b p) -> p (o b)", p=128), allow_slow_non_contiguous=True), writes=["hbias"])
            dcn = sbf("dcn", [128, 16], F32)
            em.op("dve", lambda e: e.tensor_scalar(out=dcn[:], in0=dcy[:], scalar1=-1.0, scalar2=None, op0=ALU.mult), reads=["dcy"], writes=["dcn"])
            em.op("dve", lambda e: e.tensor_tensor(out=dcy[:], in0=dcy[:], in1=dcn[:], op=ALU.min), reads=["dcy", "dcn"], writes=["dcy"])
            I2P = 1.0 / (2.0 * math.pi)
            for j in range(2):
                em.op("dve", lambda e, j=j: e.tensor_scalar(out=fs[:, 2 * j:2 * j + 1], in0=fb[:, 2 * j + 1:2 * j + 2], scalar1=I2P, scalar2=None, op0=ALU.mult), reads=["fb"], writes=["fs"])
                em.op("dve", lambda e, j=j: e.tensor_tensor(out=fs[:, 2 * j + 1:2 * j + 2], in0=fs[:, 2 * j:2 * j + 1], in1=fb[:, 2 * j:2 * j + 1], op=ALU.mult), reads=["fb", "fs"], writes=["fs"])
            MAGIC = 12582912.0

            def sin_layer(pm, pmk, j, hd, hdk):
                a, ak = ya.nxt(); b_, bk = yb_.nxt()
                em.op("dve", lambda e: e.tensor_scalar(out=a[:], in0=pm[0:64, :], scalar1=fs[:, 2 * j:2 * j + 1], scalar2=fs[:, 2 * j + 1:2 * j + 2], op0=ALU.mult, op1=ALU.add), reads=[pmk, "fs"], writes=[ak])
                em.op("dve", lambda e: e.tensor_scalar(out=b_[:], in0=a[:], scalar1=MAGIC, scalar2=None, op0=ALU.add), reads=[ak], writes=[bk])
                em.op("dve", lambda e: e.tensor_scalar(out=b_[:], in0=b_[:], scalar1=MAGIC, scalar2=None, op0=ALU.subtract), reads=[bk], writes=[bk])
                em.op("dve", lambda e: e.tensor_tensor(out=a[:], in0=a[:], in1=b_[:], op=ALU.subtract), reads=[ak, bk], writes=[ak])
                em.op("dve", lambda e: e.tensor_scalar(out=a[:], in0=a[:], scalar1=-0.499999, scalar2=0.499999, op0=ALU.max, op1=ALU.min), reads=[ak], writes=[ak])
                em.op("act", lambda e: e.activation(out=hd[:], in_=a[:], func=AF.Sin, scale=2.0 * math.pi), reads=[ak], writes=[hdk])

            pfi = 0
            for tt in range(NTILE if "3a" not in skip else 0):
                t0 = tt * TT
                z_, zk = zt.nxt()
                em.dma("sp", lambda e, z_=z_, t0=t0: e.dma_start(out=z_[:], in_=zT[:, t0:t0 + TT]), writes=[zk])
                em.dma("sp", lambda e, t0=t0: e.dma_start(out=tpb[:], in_=tpos[:, t0:t0 + TT].partition_broadcast(128)), writes=["tpb"])
                pm = pF[pfi % 3]; pmk = f"pF{pfi % 3}"; pfi += 1
                em.op("pe", lambda e, pm=pm, z_=z_: e.matmul(pm[0:64, :], lhsT=w1t[:], rhs=z_[:], start=True, stop=True), reads=["w1t", zk], writes=[pmk])
                h1_, h1k = hd1.nxt()
                sin_layer(pm, pmk, 0, h1_, h1k)
                pm = pF[pfi % 3]; pmk = f"pF{pfi % 3}"; pfi += 1
                em.op("pe", lambda e, pm=pm, h1_=h1_: e.matmul(pm[0:64, :], lhsT=w2t[:], rhs=h1_[:], start=True, stop=True), reads=["w2t", h1k], writes=[pmk])
                h2_, h2k = hd2.nxt()
                sin_layer(pm, pmk, 1, h2_, h2k)
                for cb in range(16):
                    pm = pF[pfi % 3]; pmk = f"pF{pfi % 3}"; pfi += 1
                    em.op("pe", lambda e, pm=pm, h2_=h2_, cb=cb: e.matmul(pm[:], lhsT=w3t[:, cb * 128:(cb + 1) * 128], rhs=h2_[:], start=True, stop=True), reads=["w3t", h2k], writes=[pmk])
                    w_, wk_ = wn.nxt(); f_, fk_ = fo.nxt(); fb_, fbk = fob.nxt()
                    em.op("act", lambda e, w_=w_, cb=cb: e.activation(out=w_[:], in_=tpb[:], func=AF.Exp, scale=dcy[:, cb:cb + 1]), reads=["tpb", "dcy"], writes=[wk_])
                    em.op("dve", lambda e, f_=f_, pm=pm, w_=w_: e.tensor_tensor(out=f_[:], in0=pm[:], in1=w_[:], op=ALU.mult), reads=[pmk, wk_], writes=[fk_])
                    if tt == 0:
                        em.op("dve", lambda e, f_=f_, cb=cb: e.tensor_copy(out=lag0[:, cb:cb + 1], in_=f_[:, 0:1]), reads=[fk_], writes=["lag0"])
                    em.op("pool", lambda e, fb_=fb_, f_=f_: e.tensor_copy(out=fb_[:], in_=f_[:]), reads=[fk_], writes=[fbk])
                    em.store("sp", lambda e, fb_=fb_, cb=cb, t0=t0: e.dma_start(out=HF[cb * 128:(cb + 1) * 128, t0:t0 + TT], in_=fb_[:]), reads=[fbk], writes=[f"HF0_{cb}" if tt == 0 else "HF"])
                if tt == 0:
                    em.op("dve", lambda e: e.memset(lagc[:], 0.0), writes=["lagc"])
                    em.op("dve", lambda e: e.tensor_scalar(out=lagc[:, 0:8], in0=lag0[:, 0:8], scalar1=selt[:, 0:1], scalar2=None, op0=ALU.mult), reads=["lag0", "selt"], writes=["lagc"])
                    em.op("dve", lambda e: e.scalar_tensor_tensor(out=lagc[:, 0:8], in0=lag0[:, 8:16], scalar=selt[:, 1:2], in1=lagc[:, 0:8], op0=ALU.mult, op1=ALU.add), reads=["lag0", "selt", "lagc"], writes=["lagc"])
                    em.op("dve", lambda e: e.tensor_tensor(out=lagc[:, 0:8], in0=lagc[:, 0:8], in1=hbias[:], op=ALU.add), reads=["lagc", "hbias"], writes=["lagc"])
                    em.op("dve", lambda e: e.tensor_copy(out=lagb[:], in_=lagc[:]), reads=["lagc"], writes=["lagb"])
                    em.dma("sp", lambda e: e.dma_start(out=HF.rearrange("(b p) t -> p b t", p=128)[:, :, 0:1], in_=lagb[:].unsqueeze(2), allow_slow_non_contiguous=True),
                           reads=["lagb"], writes=[f"HF0_{c_}" for c_ in range(16)])
            em.flush()

          with ExitStack() as es:
            sbf = lambda n, s, d: es.enter_context(nc.sbuf_tensor(uq(n), list(s), d))
            PW = 2048
            cw = sbf("cw", [128, 3, 24], F32)
            cbias = sbf("cbias", [128, 24], F32)
            hyin = [Rot(sbf, f"hyin{j}", 2, [128, PW + 2], F32) for j in range(3)]
            cv = [Rot(sbf, f"cv{j}", 2, [128, PW], F32) for j in range(3)]
            ub = Rot(sbf, "ub", 2, [128, PW], BF16)
            em.dma("sp", lambda e: e.dma_start(out=cw[:], in_=conv_w.rearrange("j (b p) -> p j b", p=128), allow_slow_non_contiguous=True), writes=["cw"])
            em.dma("sp", lambda e: e.dma_start(out=cbias[:], in_=conv_b.rearrange("o (b p) -> p (o b)", p=128), allow_slow_non_contiguous=True), writes=["cbias"])
            for cb in range(8 if "3b" not in skip else 0):
                for pc in range(L // PW):
                    t0 = pc * PW
                    outs = []
                    for j in range(3):
                        hy_, hyk = hyin[j].nxt(); c_, ck = cv[j].nxt()
                        blk = j * 8 + cb
                        r0 = blk * 128
                        lo = max(t0 - 1, 0); hi = min(t0 + PW + 1, L)
                        if t0 == 0:
                            em.op("pool", lambda e, hy_=hy_: e.memset(hy_[:, 0:1], 0.0), writes=[hyk])
                        if t0 + PW == L:
                            em.op("pool", lambda e, hy_=hy_: e.memset(hy_[:, PW + 1:PW + 2], 0.0), writes=[hyk])
                        o0 = lo - (t0 - 1)
                        em.dma("sp", lambda e, hy_=hy_, r0=r0, lo=lo, hi=hi, o0=o0: e.dma_start(out=hy_[:, o0:o0 + hi - lo], in_=HYT[r0:r0 + 128, lo:hi]), writes=[hyk])
                        eng = "dve"
                        em.op(eng, lambda e, c_=c_, hy_=hy_, blk=blk: e.tensor_scalar(out=c_[:], in0=hy_[:, 1:PW + 1], scalar1=cw[:, 1, blk:blk + 1], scalar2=cbias[:, blk:blk + 1], op0=ALU.mult, op1=ALU.add), reads=[hyk, "cw", "cbias"], writes=[ck])
                        em.op(eng, lambda e, c_=c_, hy_=hy_, blk=blk: e.scalar_tensor_tensor(out=c_[:], in0=hy_[:, 0:PW], scalar=cw[:, 0, blk:blk + 1], in1=c_[:], op0=ALU.mult, op1=ALU.add), reads=[hyk, "cw", ck], writes=[ck])
                        em.op(eng, lambda e, c_=c_, hy_=hy_, blk=blk: e.scalar_tensor_tensor(out=c_[:], in0=hy_[:, 2:PW + 2], scalar=cw[:, 2, blk:blk + 1], in1=c_[:], op0=ALU.mult, op1=ALU.add), reads=[hyk, "cw", ck], writes=[ck])
                        outs.append((c_, ck))
                    u_, uk = ub.nxt()
                    em.op("pool", lambda e, u_=u_, a=outs[2][0], b=outs[1][0]: e.tensor_tensor(out=u_[:], in0=a[:], in1=b[:], op=ALU.mult), reads=[outs[2][1], outs[1][1]], writes=[uk])
                    em.store("sp", lambda e, u_=u_, cb=cb, t0=t0: e.dma_start(out=UT[cb * 128:(cb + 1) * 128, t0:t0 + PW], in_=u_[:]), reads=[uk], writes=["UT"])
                    em.store("sp", lambda e, a=outs[0][0], cb=cb, t0=t0: e.dma_start(out=X0T[cb * 128:(cb + 1) * 128, t0:t0 + PW], in_=a[:]), reads=[outs[0][1]], writes=["X0T"])
            em.flush()

          with ExitStack() as es:
            sbf = lambda n, s, d: es.enter_context(nc.sbuf_tensor(uq(n), list(s), d))
            psf = lambda n, s, d: es.enter_context(nc.psum_tensor(uq(n), list(s), d))
            cst = sbf("cst", [128, 256], F32)
            FRIb = sbf("FRIb", [128, 256], BF16); FRnIb = sbf("FRnIb", [128, 256], BF16)
            FIRb = sbf("FIRb", [128, 256], BF16); nFIb = sbf("nFIb", [128, 128], BF16)
            TWt = sbf("TWt", [128, 2, 128], F32); TWct = sbf("TWct", [128, 2, 128], F32)
            for nm, src, dst in (("FRI", FRI, FRIb), ("FRnI", FRnI, FRnIb), ("FIR", FIR, FIRb)):
                em.dma("sp", lambda e, src=src: e.dma_start(out=cst[:], in_=src), writes=["cst"])
                em.op("dve", lambda e, dst=dst: e.tensor_copy(out=dst[:], in_=cst[:]), reads=["cst"], writes=[nm])
            em.dma("sp", lambda e: e.dma_start(out=cst[:, 0:128], in_=nFI), writes=["cst"])
            em.op("dve", lambda e: e.tensor_copy(out=nFIb[:], in_=cst[:, 0:128]), reads=["cst"], writes=["nFI"])
            em.dma("sp", lambda e: e.dma_start(out=TWt[:].rearrange("p a b -> p (a b)"), in_=TW), writes=["TW"])
            em.dma("sp", lambda e: e.dma_start(out=TWct[:].rearrange("p a b -> p (a b)"), in_=TWc), writes=["TWc"])
            Min = Rot(sbf, "Min", 8, [64, 2, 128], BF16)
            P1 = Rot(sbf, "P1", 6, [128, 2, 2, 128], F32)
            P2 = Rot(sbf, "P2", 6, [128, 2, 2, 128], F32)
            B2 = Rot(sbf, "B2", 8, [128, 2, 2, 128], BF16)
            Y2 = Rot(sbf, "Y2", 4, [128, 2, 2, 128], BF16)
            D2 = Rot(sbf, "D2", 4, [128, 2, 2, 128], BF16)
            KFs = Rot(sbf, "KFs", 5, [128, 2, 2, 128], F32)
            YO = Rot(sbf, "YO", 3, [64, 2, 128], F32)
            pq = [psf(f"pq{i}", [128, 2, 2, 128], F32) for i in range(8)]
            pqk = [f"pq{i}" for i in range(8)]

            def bc4(t3, ri):
                return t3[:, ri:ri + 1, :].unsqueeze(1).to_broadcast([128, 2, 2, 128])

            def cmul(src, srck, tw, twk, p1, p1k, p2, p2k):
                em.op("dve", lambda e: e.tensor_tensor(out=p1[:], in0=src[:], in1=bc4(tw, 0), op=ALU.mult), reads=[srck, twk], writes=[p1k])
                em.op("dve", lambda e: e.tensor_tensor(out=p2[:], in0=src[:, :, ::-1, :], in1=bc4(tw, 1), op=ALU.mult), reads=[srck, twk], writes=[p2k])

            def flat(ap3):
                return ap3.rearrange("p a b -> p (a b)")

            def st2(o, ok_, b2, b2k, ch, conj, first, last):
                fi = nFIb if conj else FRIb[:, 128:256]
                nfi = FRIb[:, 128:256] if conj else nFIb
                em.op("pe", lambda e: e.matmul(flat(o), lhsT=FRIb[:, 0:128], rhs=flat(b2[:, ch, :, :]), start=first, stop=False), reads=["FRI", b2k], writes=[ok_])
                em.op("pe", lambda e: e.matmul(o[:, 0, :], lhsT=nfi[:] if conj is False else nfi, rhs=b2[:, ch, 1, :], start=False, stop=False), reads=["FRI", "nFI", b2k], writes=[ok_])
                em.op("pe", lambda e: e.matmul(o[:, 1, :], lhsT=fi[:] if conj else fi, rhs=b2[:, ch, 0, :], start=False, stop=last), reads=["FRI", "nFI", b2k], writes=[ok_])

            npair = 512 if 31 not in skip else 4

            def stage1_gen(src_dram, c0, rhs1, rhs1k, tw, twk, pa, pak, res):
                m_, mk_ = Min.nxt()
                em.dma("sp", lambda e: e.dma_start(out=m_[:], in_=src_dram[c0:c0 + 2, :].rearrange("c (a b) -> a c b", b=128)), writes=[mk_])
                yield
                for ch in range(2):
                    em.op("pe", lambda e, ch=ch: e.matmul(flat(pa[:, ch, :, :]), lhsT=m_[:, ch, :], rhs=rhs1[0:64, :], start=True, stop=True), reads=[mk_, rhs1k], writes=[pak])
                yield
                p1, p1k = P1.nxt(); p2, p2k = P2.nxt(); b2, b2k = B2.nxt()
                cmul(pa, pak, tw, twk, p1, p1k, p2, p2k)
                yield
                em.op("pool", lambda e: e.tensor_tensor(out=b2[:, :, 0, :], in0=p1[:, :, 0, :], in1=p2[:, :, 0, :], op=ALU.subtract), reads=[p1k, p2k], writes=[b2k])
                em.op("pool", lambda e: e.tensor_tensor(out=b2[:, :, 1, :], in0=p1[:, :, 1, :], in1=p2[:, :, 1, :], op=ALU.add), reads=[p1k, p2k], writes=[b2k])
                res.append((b2, b2k))
                yield

            def kf_gen(pr):
                c0 = pr * 2
                rf, rb = [], []
                pa0 = pq[(pr % 2) * 2]; pa0k = pqk[(pr % 2) * 2]
                pa1 = pq[(pr % 2) * 2 + 1]; pa1k = pqk[(pr % 2) * 2 + 1]
                g1 = stage1_gen(HF, c0, FRIb, "FRI", TWt, "TW", pa0, pa0k, rf)
                g2 = stage1_gen(HF, 1024 + c0, FRnIb, "FRnI", TWct, "TWc", pa1, pa1k, rb)
                for _ in range(4):
                    next(g1); next(g2)
                    yield
                bf_, bfk = rf[0]; bb_, bbk = rb[0]
                px = pq[4 + pr % 3]; pxk = pqk[4 + pr % 3]
                for ch in range(2):
                    st2(px[:, ch, :, :], pxk, bf_, bfk, ch, False, True, False)
                    st2(px[:, ch, :, :], pxk, bb_, bbk, ch, True, False, True)
                yield
                kf_, kfk = KFs.nxt()
                em.op("act", lambda e: e.activation(out=kf_[:], in_=px[:], func=AF.Copy), reads=[pxk], writes=[kfk])
                em.store("sp", lambda e: e.dma_start(out=KF[c0:c0 + 2].rearrange("c k a b -> k c a b"), in_=kf_[:]), reads=[kfk], writes=[f"KF{pr}"])

            def data_gen(pr):
                c0 = pr * 2
                kf_, kfk = KFs.nxt()
                em.dma("sp", lambda e: e.dma_start(out=kf_[:], in_=KF[c0:c0 + 2].rearrange("c k a b -> k c a b")), reads=[f"KF{pr}"], writes=[kfk])
                rf = []
                pa = pq[pr % 2]; pak = pqk[pr % 2]
                g1 = stage1_gen(UT, c0, FRIb, "FRI", TWt, "TW", pa, pak, rf)
                for _ in range(4):
                    next(g1)
                    yield
                b2, b2k = rf[0]
                px = pq[2 + pr % 2]; pxk = pqk[2 + pr % 2]
                for ch in range(2):
                    st2(px[:, ch, :, :], pxk, b2, b2k, ch, False, True, True)
                yield
                p1, p1k = P1.nxt(); p2, p2k = P2.nxt(); y2, y2k = Y2.nxt()
                em.op("dve", lambda e: e.tensor_tensor(out=p1[:], in0=px[:], in1=kf_[:, :, 0:1, :].to_broadcast([128, 2, 2, 128]), op=ALU.mult), reads=[pxk, kfk], writes=[p1k])
                em.op("dve", lambda e: e.tensor_tensor(out=p2[:], in0=px[:, :, ::-1, :], in1=kf_[:, :, 1:2, :].to_broadcast([128, 2, 2, 128]), op=ALU.mult), reads=[pxk, kfk], writes=[p2k])
                yield
                em.op("pool", lambda e: e.tensor_tensor(out=y2[:, :, 0, :], in0=p1[:, :, 0, :], in1=p2[:, :, 0, :], op=ALU.subtract), reads=[p1k, p2k], writes=[y2k])
                em.op("pool", lambda e: e.tensor_tensor(out=y2[:, :, 1, :], in0=p1[:, :, 1, :], in1=p2[:, :, 1, :], op=ALU.add), reads=[p1k, p2k], writes=[y2k])
                yield
                pc = pq[4 + pr % 2]; pck = pqk[4 + pr % 2]
                for ch in range(2):
                    o = flat(pc[:, ch, :, :])
                    em.op("pe", lambda e, o=o, ch=ch: e.matmul(o, lhsT=y2[:, ch, 0, :], rhs=FRnIb[:], start=True, stop=False), reads=[y2k, "FRnI"], writes=[pck])
                    em.op("pe", lambda e, o=o, ch=ch: e.matmul(o, lhsT=y2[:, ch, 1, :], rhs=FIRb[:], start=False, stop=True), reads=[y2k, "FIR"], writes=[pck])
                yield
                p1b, p1bk = P1.nxt(); p2b, p2bk = P2.nxt(); d2, d2k = D2.nxt()
                cmul(pc, pck, TWct, "TWc", p1b, p1bk, p2b, p2bk)
                yield
                em.op("pool", lambda e: e.tensor_tensor(out=d2[:, 0, :, :], in0=p1b[:, :, 0, :], in1=p2b[:, :, 0, :], op=ALU.subtract), reads=[p1bk, p2bk], writes=[d2k])
                em.op("pool", lambda e: e.tensor_tensor(out=d2[:, 1, :, :], in0=p1b[:, :, 1, :], in1=p2b[:, :, 1, :], op=ALU.add), reads=[p1bk, p2bk], writes=[d2k])
                yield
                py = pq[6 + pr % 2][0:64, 0, :, :]; pyk = pqk[6 + pr % 2]
                em.op("pe", lambda e: e.matmul(flat(py), lhsT=FRIb[:, 0:64], rhs=flat(d2[:, 0, :, :]), start=True, stop=False), reads=["FRI", d2k], writes=[pyk])
                em.op("pe", lambda e: e.matmul(flat(py), lhsT=FRIb[:, 128:192], rhs=flat(d2[:, 1, :, :]), start=False, stop=True), reads=["FRI", d2k], writes=[pyk])
                yield
                yo, yok = YO.nxt()
                em.op("act", lambda e: e.activation(out=yo[:], in_=py, func=AF.Copy, scale=1.0 / 16384.0), reads=[pyk], writes=[yok])
                em.store("sp", lambda e: e.dma_start(out=YCT[c0:c0 + 2, :].rearrange("c (a b) -> a c b", b=128), in_=yo[:]), reads=[yok], writes=["YCT"])

            run_pipeline(kf_gen(pr) for pr in range(npair))
            run_pipeline(data_gen(pr) for pr in range(npair))
            em.flush()
        print("inst", em.n_inst, "waits", em.n_wait)
        if stop_after <= 3:
            return nc
        TK = 256
        NTK = (L // 2) // TK
        if 4 not in skip:
          with ExitStack() as es:
            sbf = lambda n, s, d: es.enter_context(nc.sbuf_tensor(uq(n), list(s), d))
            psf = lambda n, s, d: es.enter_context(nc.psum_tensor(uq(n), list(s), d))
            wa = sbf("wa", [128, 8, D], BF16); wb = sbf("wb", [128, 8, D], BF16); wo = sbf("wo", [128, 8, D], BF16)
            for wt_, src, nm in ((wa, w_branch_a, "wa"), (wb, w_branch_b, "wb"), (wo, w_out, "wo")):
                for k in range(8):
                    em.dma("pool", lambda e, wt_=wt_, src=src, k=k: e.dma_start(out=wt_[:, k, :], in_=src[k * 128:(k + 1) * 128, :]), writes=[nm])
            ones = sbf("ones", [128, 128], F32)
            em.op("dve", lambda e: e.memset(ones[:], 1.0), writes=["ones"])
            gcol = sbf("gcol", [128, 1], F32)
            em.dma("sp", lambda e: e.dma_start(out=gcol[:], in_=hgrn_norm_g.rearrange("o v -> v o"), allow_slow_non_contiguous=True), writes=["gcol"])
            ot = Rot(sbf, "ot", 2, [128, 8, TK], F32)
            ogt = Rot(sbf, "ogt", 2, [128, 8, TK], BF16)
            sq = Rot(sbf, "sq", 2, [128, 8, TK], F32)
            rs = Rot(sbf, "rs", 2, [128, 2, TK], F32)
            tmpA = Rot(sbf, "tmpA", 2, [128, 2, TK], F32)
            At = Rot(sbf, "At", 2, [128, 8, TK], BF16)
            x0t = Rot(sbf, "x0t", 2, [128, 8, TK], F32)
            yct = Rot(sbf, "yct", 2, [128, 8, TK], F32)
            Bt = Rot(sbf, "Bt", 2, [128, 8, TK], BF16)
            gat = Rot(sbf, "gat", 2, [128, 8, 2, TK], BF16)
            tg = Rot(sbf, "tg", 2, [128, 2, TK], F32)
            mg = Rot(sbf, "mg", 2, [128, 8, TK], BF16)
            xt4 = Rot(sbf, "xt4", 2, [128, 2, D], F32)
            h1t = Rot(sbf, "h1t", 2, [128, 2, D], F32)
            pw = [psf(f"pw{i}", [128, 512], F32) for i in range(6)]
            pwi = [0]

            def npw():
                pwi[0] = (pwi[0] + 1) % 6
                return pw[pwi[0]], f"pw{pwi[0]}"

            for tk in range(NTK):
                tg0 = TOK0 + tk * TK
                o_, ok = ot.nxt(); og_, ogk = ogt.nxt(); s_, sk_ = sq.nxt(); a_, ak = At.nxt()
                em.dma("sp", lambda e, o_=o_, tk=tk: e.dma_start(out=o_[:], in_=OTh[:, :, tk * TK:(tk + 1) * TK].rearrange("h k t -> k h t")), writes=[ok])
                em.dma("sp", lambda e, og_=og_, tk=tk: e.dma_start(out=og_[:], in_=OGTh[:, tk * TK:(tk + 1) * TK].rearrange("(h k) t -> k h t", k=128)), writes=[ogk])
                em.op("act", lambda e, s_=s_, o_=o_: e.activation(out=s_[:], in_=o_[:], func=AF.Square), reads=[ok], writes=[sk_])
                for h2 in range(4):
                    p_, pk = npw()
                    for hh in range(2):
                        h = h2 * 2 + hh
                        em.op("pe", lambda e, p_=p_, s_=s_, h=h, hh=hh: e.matmul(p_[:, hh * TK:(hh + 1) * TK], lhsT=ones[:], rhs=s_[:, h, :], start=True, stop=True), reads=["ones", sk_], writes=[pk])
                    r_, rk = rs.nxt(); t_, tk_ = tmpA.nxt()
                    em.op("act", lambda e, r_=r_, p_=p_: e.activation(out=r_[:].rearrange("p a b -> p (a b)"), in_=p_[:], func=AF.Ln, scale=1.0 / 128, bias=EPS), reads=[pk], writes=[rk])
                    em.op("act", lambda e, r_=r_: e.activation(out=r_[:], in_=r_[:], func=AF.Exp, scale=-0.5), reads=[rk], writes=[rk])
                    em.op("dve", lambda e, t_=t_, o_=o_, r_=r_, h2=h2: e.scalar_tensor_tensor(out=t_[:], in0=o_[:, h2 * 2:h2 * 2 + 2, :], scalar=gcol[:, 0:1], in1=r_[:], op0=ALU.mult, op1=ALU.mult), reads=[ok, rk, "gcol"], writes=[tk_])
                    em.op("pool", lambda e, a_=a_, t_=t_, og_=og_, h2=h2: e.tensor_tensor(out=a_[:, h2 * 2:h2 * 2 + 2, :], in0=t_[:], in1=og_[:, h2 * 2:h2 * 2 + 2, :], op=ALU.mult), reads=[tk_, ogk], writes=[ak])
                if "AT" in dbg:
                    em.store("sp", lambda e, a_=a_, tk=tk: e.dma_start(out=AT[:, tk * TK:(tk + 1) * TK].rearrange("(h k) t -> k h t", k=128), in_=a_[:]), reads=[ak], writes=["AT"])
                x0_, x0k = x0t.nxt(); yc_, yck = yct.nxt(); b_, bk = Bt.nxt(); ga_, gak = gat.nxt()
                em.dma("sp", lambda e, x0_=x0_, tk=tk: e.dma_start(out=x0_[:], in_=X0Th[:, tk * TK:(tk + 1) * TK].rearrange("(h k) t -> k h t", k=128)), writes=[x0k])
                em.dma("sp", lambda e, yc_=yc_, tk=tk: e.dma_start(out=yc_[:], in_=YCTh[:, tk * TK:(tk + 1) * TK].rearrange("(h k) t -> k h t", k=128)), writes=[yck])
                for a2 in range(2):
                    em.dma("sp", lambda e, ga_=ga_, tk=tk, a2=a2: e.dma_start(out=ga_[:, :, a2, :], in_=GTh[a2 * 1024:(a2 + 1) * 1024, tk * TK:(tk + 1) * TK].rearrange("(h k) t -> k h t", k=128)), writes=[gak])
                em.op("pool", lambda e, b_=b_, x0_=x0_, yc_=yc_: e.tensor_tensor(out=b_[:], in0=x0_[:], in1=yc_[:], op=ALU.mult), reads=[x0k, yck], writes=[bk])
                m_, mk_ = mg.nxt()
                for db in range(8):
                    p_, pk = npw()
                    for k in range(8):
                        em.op("pe", lambda e, p_=p_, k=k, db=db, a_=a_: e.matmul(p_[:, 0:TK], lhsT=wa[:, k, db * 128:(db + 1) * 128], rhs=a_[:, k, :], start=(k == 0), stop=(k == 7)), reads=["wa", ak], writes=[pk])
                    for k in range(8):
                        em.op("pe", lambda e, p_=p_, k=k, db=db, b_=b_: e.matmul(p_[:, TK:2 * TK], lhsT=wb[:, k, db * 128:(db + 1) * 128], rhs=b_[:, k, :], start=(k == 0), stop=(k == 7)), reads=["wb", bk], writes=[pk])
                    t_, tk_ = tg.nxt()
                    em.op("dve", lambda e, t_=t_, p_=p_, ga_=ga_, db=db: e.tensor_tensor(out=t_[:].rearrange("p a b -> p (a b)"), in0=p_[:], in1=ga_[:, db, :, :].rearrange("p a b -> p (a b)"), op=ALU.mult), reads=[pk, gak], writes=[tk_])
                    em.op("pool", lambda e, m_=m_, t_=t_, db=db: e.tensor_tensor(out=m_[:, db, :], in0=t_[:, 0, :], in1=t_[:, 1, :], op=ALU.add), reads=[tk_], writes=[mk_])
                if "MG" in dbg:
                    em.store("sp", lambda e, m_=m_, tk=tk: e.dma_start(out=MG[:, tk * TK:(tk + 1) * TK].rearrange("(h k) t -> k h t", k=128), in_=m_[:]), reads=[mk_], writes=["MGd"])
                x_, xk = xt4.nxt(); h_, hk = h1t.nxt()
                em.dma("sp", lambda e, x_=x_, tk=tk: e.dma_start(out=x_[:], in_=xh[tk * TK:(tk + 1) * TK, :].rearrange("(s p) d -> p s d", p=128)), writes=[xk])
                for s in range(2):
                    for hf in range(2):
                        p_, pk = npw()
                        for k in range(8):
                            em.op("pe", lambda e, p_=p_, k=k, s=s, hf=hf, m_=m_: e.matmul(p_[:], lhsT=m_[:, k, s * 128:(s + 1) * 128], rhs=wo[:, k, hf * 512:(hf + 1) * 512], start=(k == 0), stop=(k == 7)), reads=["wo", mk_], writes=[pk])
                        em.op("dve", lambda e, h_=h_, p_=p_, x_=x_, s=s, hf=hf: e.tensor_tensor(out=h_[:, s, hf * 512:(hf + 1) * 512], in0=p_[:], in1=x_[:, s, hf * 512:(hf + 1) * 512], op=ALU.add), reads=[pk, xk], writes=[hk])
                em.store("sp", lambda e, h_=h_, tk=tk: e.dma_start(out=H1[tk * TK:(tk + 1) * TK, :].rearrange("(s p) d -> p s d", p=128), in_=h_[:]), reads=[hk], writes=["H1d"])
            em.flush()
        print("inst", em.n_inst, "waits", em.n_wait)
        if stop_after <= 4:
            return nc
        if 5 not in skip:
          with ExitStack() as es:
            sbf = lambda n, s, d: es.enter_context(nc.sbuf_tensor(uq(n), list(s), d))
            psf = lambda n, s, d: es.enter_context(nc.psum_tensor(uq(n), list(s), d))
            idf5 = sbf("idf5", [128, 128], F32); idb5 = sbf("idb5", [128, 128], BF16)
            em.dma("sp", lambda e: e.dma_start(out=idf5[:], in_=ident), writes=["idf5"])
            em.op("dve", lambda e: e.tensor_copy(out=idb5[:], in_=idf5[:]), reads=["idf5"], writes=["idb5"])
            for r in range(16):
                em.dma("pool", lambda e, r=r: e.dma_start(out=VBF[r * 1024:(r + 1) * 1024, :], in_=peer_v[r * 1024:(r + 1) * 1024, :]), writes=["VBF"])
            urow = Rot(sbf, "urow", 3, [128, D], BF16)
            uts = Rot(sbf, "uts", 3, [128, 8, 128], BF16)
            pU = [psf(f"pU{i}", [128, 8, 128], BF16) for i in range(2)]
            nj = 128 if 51 not in skip else 2
            for j in range(nj):
                u_, uk = urow.nxt(); t_, tk_ = uts.nxt()
                em.dma("pool", lambda e, u_=u_, j=j: e.dma_start(out=u_[:], in_=peer_u.rearrange("(i j) d -> j i d", j=128)[j]), writes=[uk])
                p_ = pU[j % 2]; pk = f"pU{j % 2}"
                for k in range(8):
                    em.op("pe", lambda e, p_=p_, u_=u_, k=k: e.transpose(out=p_[:, k, :], in_=u_[:, k * 128:(k + 1) * 128], identity=idb5[:]), reads=[uk, "idb5"], writes=[pk])
                em.op("act" if j % 2 else "dve", lambda e, p_=p_, t_=t_, j=j: (e.activation(out=t_[:], in_=p_[:], func=AF.Copy) if j % 2 else e.tensor_copy(out=t_[:], in_=p_[:])), reads=[pk], writes=[tk_])
                em.store("sp", lambda e, t_=t_, j=j: e.dma_start(out=UTS[j], in_=t_[:]), reads=[tk_], writes=["UTS"])
            em.flush()

          with ExitStack() as es:
            sbf = lambda n, s, d: es.enter_context(nc.sbuf_tensor(uq(n), list(s), d))
            psf = lambda n, s, d: es.enter_context(nc.psum_tensor(uq(n), list(s), d))
            wq = sbf("wq", [128, 8, 2048], BF16)
            for k in range(8):
                em.dma("pool", lambda e, k=k: e.dma_start(out=wq[:, k, :], in_=peer_w_q[k * 128:(k + 1) * 128, :]), writes=["wq"])
            idf = sbf("idf", [128, 128], F32)
            em.dma("sp", lambda e: e.dma_start(out=idf[:], in_=ident), writes=["idf"])
            iot = sbf("iot", [128, 128], F32)
            em.dma("sp", lambda e: e.dma_start(out=iot[:], in_=iota), writes=["iot"])
            gff = sbf("gff", [128, D], F32); gfin = sbf("gfin", [128, D], F32)
            em.dma("sp", lambda e: e.dma_start(out=gff[:], in_=norm_ffn_g.partition_broadcast(128)), writes=["gff"])
            em.dma("sp", lambda e: e.dma_start(out=gfin[:], in_=norm_final_g.partition_broadcast(128)), writes=["gfin"])
            skT = sbf("skT", [128, 16, 128], BF16)
            h1 = Rot(sbf, "h1", 1, [128, 2, D], F32)
            ss5 = sbf("ss5", [128, 2], F32); rstd5 = sbf("rstd5", [128, 2], F32)
            xn2 = sbf("xn2", [128, 2, D], F32)
            sqj = xn2[:, 1, :]
            xn2T = sbf("xn2T", [128, 8, TK], BF16)
            qT = sbf("qT", [128, 16, TK], BF16)
            scr = sbf("scr", [128, 16, 128], F32)
            skf = scr
            em.dma("sp", lambda e: e.dma_start(out=skf[:], in_=peer_sk.rearrange("j n c -> n j c")), writes=["scr"])
            scr2 = scr
            vals = sbf("vals", [128, 16, 16], F32)
            idxu = sbf("idxu", [128, 16, 16], U32)
            idxf = sbf("idxf", [128, 16, 16], F32)
            Cg = sbf("Cg", [128, 8, 256], F32); Cg2 = Cg
            cv = sbf("cv", [128, 8, 16], F32)
            posu = sbf("posu", [128, 8, 16], U32); pa_u = sbf("pa_u", [128, 8, 16], U32); pb_u = sbf("pb_u", [128, 8, 16], U32)
            paf = sbf("paf", [128, 8, 16], F32); pbf = sbf("pbf", [128, 8, 16], F32)
            eq = scr[:].rearrange("p a b -> p (a b)").rearrange("p (h k a) -> p h k a", h=8, k=16)
            ik = sbf("ik", [128, 8, 16], F32); jk = sbf("jk", [128, 8, 16], F32)
            ee = sbf("ee", [128, 8, 16], F32); zz = sbf("zz", [128, 8], F32); gg = sbf("gg", [128, 8, 16], F32)
            ikT = sbf("ikT", [128, TK], F32); jkT = sbf("jkT", [128, TK], F32); gT = sbf("gT", [128, TK], F32)
            Lt = Rot(sbf, "Lt", 12, [128, 128], BF16); Rt = Rot(sbf, "Rt", 12, [128, 128], BF16)
            Gs = sbf("Gs", [128, TK, 128], BF16)
            utj = Rot(sbf, "utj", 6, [128, 8, 128], BF16); vj = Rot(sbf, "vj", 6, [128, D], BF16)
            gact = Rot(sbf, "gact", 4, [128, TK], F32)
            ATj = Rot(sbf, "ATj", 4, [128, TK], BF16)

            acc = [psf(f"acc{i}", [128, 512], F32) for i in range(4)]
            pw = [psf(f"pw{i}", [128, 512], F32) for i in range(4)]
            pwi = [0]

            def npw():
                pwi[0] = (pwi[0] + 1) % 4
                return pw[pwi[0]], f"pw{pwi[0]}"

            for j4 in range(4):
                p_, pk = npw()
                for jj in range(4):
                    j = j4 * 4 + jj
                    em.op("pe", lambda e, p_=p_, jj=jj, j=j: e.transpose(out=p_[:, jj * 128:(jj + 1) * 128], in_=skf[:, j, :], identity=idf[:]), reads=["scr", "idf"], writes=[pk])
                em.op("dve", lambda e, p_=p_, j4=j4: e.tensor_copy(out=skT[:, j4 * 4:(j4 + 1) * 4, :].rearrange("p a b -> p (a b)"), in_=p_[:]), reads=[pk], writes=["skT"])

            ntk = NTK if 52 not in skip else 1
            import os
            P5STOP = int(os.environ.get("P5STOP", "99"))
            for tk in range(ntk):
                h_, hk = h1.nxt()
                em.dma("sp", lambda e, h_=h_, tk=tk: e.dma_start(out=h_[:], in_=H1[tk * TK:(tk + 1) * TK, :].rearrange("(s p) d -> p s d", p=128)), writes=[hk])
                for s in range(2):
                    em.op("act", lambda e, h_=h_, s=s: e.activation(out=sqj, in_=h_[:, s, :], func=AF.Square, accum_out=ss5[:, s:s + 1]), reads=[hk], writes=["xn21", "ss5"])
                em.op("act", lambda e: e.activation(out=rstd5[:], in_=ss5[:], func=AF.Ln, scale=1.0 / D, bias=EPS), reads=["ss5"], writes=["rstd5"])
                em.op("act", lambda e: e.activation(out=rstd5[:], in_=rstd5[:], func=AF.Exp, scale=-0.5), reads=["rstd5"], writes=["rstd5"])
                for s in range(2):
                    em.op("dve", lambda e, h_=h_, s=s: e.scalar_tensor_tensor(out=xn2[:, s, :], in0=h_[:, s, :], scalar=rstd5[:, s:s + 1], in1=gff[:], op0=ALU.mult, op1=ALU.mult), reads=[hk, "rstd5", "gff"], writes=[f"xn2{s}"])
                    for k4 in range(2):
                        p_, pk = npw()
                        for kk in range(4):
                            k = k4 * 4 + kk
                            em.op("pe", lambda e, p_=p_, kk=kk, k=k, s=s: e.transpose(out=p_[:, kk * 128:(kk + 1) * 128], in_=xn2[:, s, k * 128:(k + 1) * 128], identity=idf[:]), reads=[f"xn2{s}", "idf"], writes=[pk])
                        em.op("act", lambda e, p_=p_, k4=k4, s=s: e.activation(out=xn2T[:, k4 * 4:(k4 + 1) * 4, s * 128:(s + 1) * 128], in_=p_[:].rearrange("p (a b) -> p a b", a=4), func=AF.Copy), reads=[pk], writes=["xn2T"])
                if P5STOP <= 1:
                    continue
                for j2 in range(8):
                    p_, pk = npw()
                    for jj in range(2):
                        j = j2 * 2 + jj
                        for k in range(8):
                            em.op("pe", lambda e, p_=p_, jj=jj, j=j, k=k: e.matmul(p_[:, jj * TK:(jj + 1) * TK], lhsT=wq[:, k, j * 128:(j + 1) * 128], rhs=xn2T[:, k, :], start=(k == 0), stop=(k == 7)), reads=["wq", "xn2T"], writes=[pk])
                    em.op("dve", lambda e, p_=p_, j2=j2: e.tensor_copy(out=qT[:, j2 * 2:j2 * 2 + 2, :].rearrange("p a b -> p (a b)"), in_=p_[:]), reads=[pk], writes=["qT"])
                if P5STOP <= 2:
                    continue
                for s in range(2):
                    for j4 in range(4):
                        p_, pk = npw()
                        for jj in range(4):
                            j = j4 * 4 + jj
                            em.op("pe", lambda e, p_=p_, jj=jj, j=j, s=s: e.matmul(p_[:, jj * 128:(jj + 1) * 128], lhsT=qT[:, j, s * 128:(s + 1) * 128], rhs=skT[:, j, :], start=True, stop=True), reads=["qT", "skT"], writes=[pk])
                        em.op("act", lambda e, p_=p_, j4=j4: e.activation(out=scr[:, j4 * 4:(j4 + 1) * 4, :].rearrange("p a b -> p (a b)"), in_=p_[:], func=AF.Copy), reads=[pk], writes=["scr"])
                    for j in range(16):
                        em.op("dve", lambda e, j=j: e.max(out=vals[:, j, 0:8], in_=scr[:, j, :]), reads=["scr"], writes=["vals"])
                        em.op("dve", lambda e, j=j: e.max_index(out=idxu[:, j, 0:8], in_max=vals[:, j, 0:8], in_values=scr[:, j, :]), reads=["scr", "vals"], writes=["idxu"])
                        em.op("dve", lambda e, j=j: e.match_replace(out=scr2[:, j, :], in_to_replace=vals[:, j, 0:8], in_values=scr[:, j, :], imm_value=-1e30), reads=["scr", "vals"], writes=["scr"])
                        em.op("dve", lambda e, j=j: e.max(out=vals[:, j, 8:16], in_=scr2[:, j, :]), reads=["scr"], writes=["vals"])
                        em.op("dve", lambda e, j=j: e.max_index(out=idxu[:, j, 8:16], in_max=vals[:, j, 8:16], in_values=scr2[:, j, :]), reads=["scr", "vals"], writes=["idxu"])
                    em.op("dve", lambda e: e.tensor_copy(out=idxf[:], in_=idxu[:]), reads=["idxu"], writes=["idxf"])
                    v4 = vals[:].rearrange("p (h t) a -> p h t a", t=2)
                    i4 = idxf[:].rearrange("p (h t) a -> p h t a", t=2)
                    em.op("dve", lambda e, v4=v4: e.tensor_tensor(out=Cg[:].rearrange("p h (a b) -> p h a b", b=16), in0=v4[:, :, 0, :].unsqueeze(3).to_broadcast([128, 8, 16, 16]), in1=v4[:, :, 1, :].unsqueeze(2).to_broadcast([128, 8, 16, 16]), op=ALU.add), reads=["vals"], writes=["Cg"])
                    for h in range(8):
                        em.op("dve", lambda e, h=h: e.max(out=cv[:, h, 0:8], in_=Cg[:, h, :]), reads=["Cg"], writes=["cv"])
                        em.op("dve", lambda e, h=h: e.max_index(out=posu[:, h, 0:8], in_max=cv[:, h, 0:8], in_values=Cg[:, h, :]), reads=["Cg", "cv"], writes=["posu"])
                        em.op("dve", lambda e, h=h: e.match_replace(out=Cg2[:, h, :], in_to_replace=cv[:, h, 0:8], in_values=Cg[:, h, :], imm_value=-1e30), reads=["Cg", "cv"], writes=["Cg"])
                        em.op("dve", lambda e, h=h: e.max(out=cv[:, h, 8:16], in_=Cg2[:, h, :]), reads=["Cg"], writes=["cv"])
                        em.op("dve", lambda e, h=h: e.max_index(out=posu[:, h, 8:16], in_max=cv[:, h, 8:16], in_values=Cg2[:, h, :]), reads=["Cg", "cv"], writes=["posu"])
                    em.op("dve", lambda e: e.tensor_single_scalar(out=pa_u[:], in_=posu[:], scalar=4, op=ALU.logical_shift_right), reads=["posu"], writes=["pa_u"])
                    em.op("dve", lambda e: e.tensor_single_scalar(out=pb_u[:], in_=posu[:], scalar=15, op=ALU.bitwise_and), reads=["posu"], writes=["pb_u"])
                    em.op("dve", lambda e: e.tensor_copy(out=paf[:], in_=pa_u[:]), reads=["pa_u"], writes=["paf"])
                    em.op("dve", lambda e: e.tensor_copy(out=pbf[:], in_=pb_u[:]), reads=["pb_u"], writes=["pbf"])
                    io16 = iot[:, 0:16].unsqueeze(1).unsqueeze(1).to_broadcast([128, 8, 16, 16])
                    for (pf, pfk, plane, dst, dstk) in ((paf, "paf", 0, ik, "ik"), (pbf, "pbf", 1, jk, "jk")):
                        em.op("dve", lambda e, pf=pf: e.tensor_tensor(out=eq, in0=pf[:].unsqueeze(3).to_broadcast([128, 8, 16, 16]), in1=io16, op=ALU.is_equal), reads=[pfk, "iot"], writes=["scr"])
                        em.op("dve", lambda e, plane=plane, i4=i4: e.tensor_tensor(out=eq, in0=eq, in1=i4[:, :, plane, :].unsqueeze(2).to_broadcast([128, 8, 16, 16]), op=ALU.mult), reads=["scr", "idxf"], writes=["scr"])
                        em.op("dve", lambda e, dst=dst: e.tensor_reduce(out=dst[:], in_=eq, axis=AX.X, op=ALU.add), reads=["scr"], writes=[dstk])
                    em.op("dve", lambda e: e.tensor_tensor(out=ee[:], in0=cv[:], in1=cv[:, :, 0:1].to_broadcast([128, 8, 16]), op=ALU.subtract), reads=["cv"], writes=["ee"])
                    em.op("act", lambda e: e.activation(out=ee[:], in_=ee[:], func=AF.Exp), reads=["ee"], writes=["ee"])
                    em.op("dve", lambda e: e.tensor_reduce(out=zz[:], in_=ee[:], axis=AX.X, op=ALU.add), reads=["ee"], writes=["zz"])
                    em.op("dve", lambda e: e.reciprocal(out=zz[:], in_=zz[:]), reads=["zz"], writes=["zz"])
                    em.op("dve", lambda e: e.tensor_tensor(out=gg[:], in0=ee[:], in1=zz[:].unsqueeze(2).to_broadcast([128, 8, 16]), op=ALU.mult), reads=["ee", "zz"], writes=["gg"])
                    p_, pk = npw()
                    for n_, (src, srck) in enumerate(((ik, "ik"), (jk, "jk"), (gg, "gg"))):
                        em.op("pe", lambda e, p_=p_, n_=n_, src=src: e.transpose(out=p_[:, n_ * 128:(n_ + 1) * 128], in_=src[:].rearrange("p h k -> p (h k)"), identity=idf[:]), reads=[srck, "idf"], writes=[pk])
                    em.op("dve", lambda e, p_=p_, s=s: e.tensor_copy(out=ikT[:, s * 128:(s + 1) * 128], in_=p_[:, 0:128]), reads=[pk], writes=["ikT"])
                    em.op("dve", lambda e, p_=p_, s=s: e.tensor_copy(out=jkT[:, s * 128:(s + 1) * 128], in_=p_[:, 128:256]), reads=[pk], writes=["jkT"])
                    em.op("dve", lambda e, p_=p_, s=s: e.tensor_copy(out=gT[:, s * 128:(s + 1) * 128], in_=p_[:, 256:384]), reads=[pk], writes=["gT"])
                if P5STOP <= 3:
                    continue
                def g_gen(t4):
                    lr = []
                    for tq in range(4):
                        t = t4 * 4 + tq
                        l_, lk = Lt.nxt(); r_, rk = Rt.nxt()
                        em.op("dve" if t % 2 else "pool", lambda e, l_=l_, t=t: e.tensor_scalar(out=l_[:], in0=iot[:], scalar1=ikT[:, t:t + 1], scalar2=gT[:, t:t + 1], op0=ALU.is_equal, op1=ALU.mult), reads=["iot", "ikT", "gT"], writes=[lk])
                        em.op("pool" if t % 2 else "dve", lambda e, r_=r_, t=t: e.tensor_scalar(out=r_[:], in0=iot[:], scalar1=jkT[:, t:t + 1], scalar2=None, op0=ALU.is_equal), reads=["iot", "jkT"], writes=[rk])
                        lr.append((l_, lk, r_, rk))
                    yield
                    p_, pk = npw()
                    for tq in range(4):
                        l_, lk, r_, rk = lr[tq]
                        em.op("pe", lambda e, tq=tq, l_=l_, r_=r_: e.matmul(p_[:, tq * 128:(tq + 1) * 128], lhsT=l_[:], rhs=r_[:], start=True, stop=True), reads=[lk, rk], writes=[pk])
                    yield
                    em.op("act", lambda e: e.activation(out=Gs[:, t4 * 4:(t4 + 1) * 4, :].rearrange("p a b -> p (a b)"), in_=p_[:], func=AF.Copy), reads=[pk], writes=["Gs"])
                run_pipeline(g_gen(t4) for t4 in range(TK // 4))
                if P5STOP <= 4:
                    continue
                def dense_gen(j):
                    u_, uk = utj.nxt(); v_, vk = vj.nxt()
                    em.dma("sp", lambda e: e.dma_start(out=u_[:], in_=UTS[j]), writes=[uk])
                    em.dma("sp", lambda e: e.dma_start(out=v_[:], in_=VBF.rearrange("(i j) d -> j i d", j=128)[j]), writes=[vk])
                    yield
                    yield
                    p_, pk = npw()
                    for k in range(8):
                        em.op("pe", lambda e, k=k: e.matmul(p_[:, 0:TK], lhsT=u_[:, k, :], rhs=xn2T[:, k, :], start=(k == 0), stop=(k == 7)), reads=[uk, "xn2T"], writes=[pk])
                    yield
                    ga_, gak = gact.nxt(); a_, ak = ATj.nxt()
                    em.op("act", lambda e: e.activation(out=ga_[:], in_=p_[:, 0:TK], func=AF.Gelu_apprx_tanh), reads=[pk], writes=[gak])
                    yield
                    em.op("dve" if j % 2 else "pool", lambda e: e.tensor_tensor(out=a_[:], in0=ga_[:], in1=Gs[:, :, j], op=ALU.mult), reads=[gak, "Gs"], writes=[ak])
                    yield
                    for s in range(2):
                        for hf in range(2):
                            em.op("pe", lambda e, s=s, hf=hf: e.matmul(acc[s * 2 + hf][:], lhsT=a_[:, s * 128:(s + 1) * 128], rhs=v_[:, hf * 512:(hf + 1) * 512], start=(j == 0), stop=(j == 127)), reads=[ak, vk], writes=[f"acc{s * 2 + hf}"])
                run_pipeline(dense_gen(j) for j in range(128))
                if P5STOP <= 5:
                    continue
                for s in range(2):
                    for hf in range(2):
                        em.op("dve", lambda e, s=s, hf=hf, h_=h_: e.tensor_tensor(out=h_[:, s, hf * 512:(hf + 1) * 512], in0=acc[s * 2 + hf][:], in1=h_[:, s, hf * 512:(hf + 1) * 512], op=ALU.add), reads=[f"acc{s * 2 + hf}", hk], writes=[hk])
                for s in range(2):
                    em.op("act", lambda e, s=s, h_=h_: e.activation(out=sqj, in_=h_[:, s, :], func=AF.Square, accum_out=ss5[:, s:s + 1]), reads=[hk], writes=["xn21", "ss5"])
                em.op("act", lambda e: e.activation(out=rstd5[:], in_=ss5[:], func=AF.Ln, scale=1.0 / D, bias=EPS), reads=["ss5"], writes=["rstd5"])
                em.op("act", lambda e: e.activation(out=rstd5[:], in_=rstd5[:], func=AF.Exp, scale=-0.5), reads=["rstd5"], writes=["rstd5"])
                for s in range(2):
                    em.op("dve", lambda e, s=s, h_=h_: e.scalar_tensor_tensor(out=xn2[:, s, :], in0=h_[:, s, :], scalar=rstd5[:, s:s + 1], in1=gfin[:], op0=ALU.mult, op1=ALU.mult), reads=[hk, "rstd5", "gfin"], writes=[f"xn2{s}"])
                em.store("sp", lambda e, tk=tk: e.dma_start(out=out[tk * TK:(tk + 1) * TK, :].rearrange("(s p) d -> p s d", p=128), in_=xn2[:]), reads=["xn20", "xn21"], writes=[f"out{tk}"])
            em.flush()
        print("inst", em.n_inst, "waits", em.n_wait)
    return nc


_IN_NAMES = None


def core_inputs(inp, core):
    b, g = core // 2, core % 2
    m = dict(host_consts())
    xb = inp["x"][b]
    w_in = inp["w_in"][0]
    conv_w = inp["hyena_conv_w"][0]
    w3 = inp["filt_w3"][0]
    dec = inp["filt_decay"].reshape(2048)
    if g == 1:
        xb = xb[::-1]
        w_in = np.concatenate([w_in[:, 0:1024], w_in[:, 2048:3072], w_in[:, 1024:2048], w_in[:, 3072:]], axis=1)
        conv_w = conv_w[::-1]
        w3 = np.concatenate([w3[:, 1024:], w3[:, :1024]], axis=1)
        dec = np.concatenate([dec[1024:], dec[:1024]])
    m["x"] = np.ascontiguousarray(xb)
    m["xh"] = np.ascontiguousarray(xb[:L // 2])
    sel = np.zeros((128, 2), np.float32); sel[:, g] = 1.0
    m["sel"] = sel
    m["norm_mix_g"] = np.ascontiguousarray(inp["norm_mix_g"].reshape(1, D))
    m["w_in"] = np.ascontiguousarray(w_in)
    m["hgrn_lb_logits"] = np.ascontiguousarray(inp["hgrn_lb_logits"])
    m["filt_w1"] = np.ascontiguousarray(inp["filt_w1"][0]); m["filt_w2"] = np.ascontiguousarray(inp["filt_w2"][0])
    m["filt_w3"] = np.ascontiguousarray(w3)
    m["filt_vec"] = np.ascontiguousarray(np.stack([inp["filt_b1"][0], inp["filt_freq1"][0], inp["filt_b2"][0], inp["filt_freq2"][0]], 0))
    m["filt_decay"] = np.ascontiguousarray(dec.reshape(1, 2048))
    m["hyena_bias"] = np.ascontiguousarray(inp["hyena_bias"].reshape(1, 1024))
    m["conv_w"] = np.ascontiguousarray(conv_w); m["conv_b"] = np.ascontiguousarray(inp["hyena_conv_b"].reshape(1, 3072))
    m["hgrn_norm_g"] = np.ascontiguousarray(inp["hgrn_norm_g"].reshape(1, 128))
    for k in ("w_branch_a", "w_branch_b", "w_out", "peer_w_q", "peer_u", "peer_v"):
        m[k] = np.ascontiguousarray(inp[k][0])
    m["norm_ffn_g"] = np.ascontiguousarray(inp["norm_ffn_g"].reshape(1, D)); m["norm_final_g"] = np.ascontiguousarray(inp["norm_final_g"].reshape(1, D))
    m["peer_sk"] = np.ascontiguousarray(inp["peer_subkeys"][0].reshape(16, 128, 128))
    return m


def kernel(**inputs):
    inp = {k: np.asarray(v) for k, v in inputs.items()}
    nc = build()
    in_maps = [core_inputs(inp, c) for c in range(8)]
    res = run_bass_kernel_spmd(nc, in_maps, core_ids=list(range(8)))
    out = np.zeros((4, L, D), np.float32)
    for c in range(8):
        b, g = c // 2, c % 2
        r = np.asarray(res.results[c]["out"])
        if g == 0:
            out[b, :L // 2] = r
        else:
            out[b, L // 2:] = r[::-1]
    return out
```

```python
import math
import numpy as np
from contextlib import ExitStack
import concourse.bass as bass
import concourse.mybir as mybir
from concourse.bass_utils import run_bass_kernel_spmd

F32 = mybir.dt.float32
BF16 = mybir.dt.bfloat16
U32 = mybir.dt.uint32
ALU = mybir.AluOpType
AF = mybir.ActivationFunctionType
AX = mybir.AxisListType

L = 8192
D = 1024
NCOL = 10240
TT = 512
NTILE = L // TT
EPS = 1e-6

ENGS = ("pe", "act", "dve", "pool", "sp")
ENGMAP = {"pe": "tensor", "act": "scalar", "dve": "vector", "pool": "gpsimd", "sp": "sync"}
SEM_LIMIT = 30000
NDMA = 32


class Em:
    def __init__(self, nc, es):
        self.nc = nc
        self.es = es
        self.q = {e: [] for e in ENGS}
        self.sems = {e: [es.enter_context(nc.semaphore(f"s_{e}_0"))] for e in ENGS}
        self.cnt = {e: 0 for e in ENGS}
        self.dsem = [es.enter_context(nc.semaphore(f"d_{i}")) for i in range(NDMA)]
        self.dcnt = [0] * NDMA
        self.dnext = 0
        self.waited = {e: {} for e in ENGS}
        self.lastw = {}
        self.readers = {}
        self.n_inst = 0
        self.n_wait = 0
        self.pending = []

    def _tok_new(self, eng):
        if self.cnt[eng] >= SEM_LIMIT:
            self.sems[eng].append(
                self.es.enter_context(self.nc.semaphore(f"s_{eng}_{len(self.sems[eng])}")))
            self.cnt[eng] = 0
        self.cnt[eng] += 1
        return (self.sems[eng][-1], self.cnt[eng])

    NOKEYS = frozenset(["WIN", "QD0", "QD1", "KD0", "KD1", "KDTM0", "KDTM1", "VT", "OGT", "HYT", "GT", "HF", "UT",
                        "X0T", "YCT", "OT", "AT", "DEC0", "DEC1", "UTS", "VBF", "UBF"])

    def _deps(self, reads, writes):
        reads = [k for k in reads if k not in self.NOKEYS]
        writes = [k for k in writes if k not in self.NOKEYS]
        deps = []
        for k in reads:
            lw = self.lastw.get(k)
            if lw is not None:
                deps.append(lw)
        for k in writes:
            lw = self.lastw.get(k)
            if lw is not None:
                deps.append(lw)
            deps.extend(self.readers.get(k, ()))
        return deps

    def _emit_waits(self, eng, deps, skip_sems=()):
        w = self.waited[eng]
        need = {}
        for (sem, val) in deps:
            sid = id(sem)
            if sid in skip_sems:
                continue
            if w.get(sid, 0) >= val:
                continue
            if sid not in need or need[sid][1] < val:
                need[sid] = (sem, val)
        for sid, (sem, val) in need.items():
            w[sid] = val
            self.q[eng].append(("wait", sem, val))
            self.n_wait += 1

    def _record(self, tok, reads, writes):
        reads = [k for k in reads if k not in self.NOKEYS]
        writes = [k for k in writes if k not in self.NOKEYS]
        for k in reads:
            self.readers.setdefault(k, []).append(tok)
        for k in writes:
            self.lastw[k] = tok
            self.readers[k] = []

    NO_SELF_WAIT = ("pe",)
    STORE_DELAY = 48

    def _pending_tick(self, reads, writes, force=False):
        if not self.pending:
            return
        keep = []
        ws = set(writes)
        rs = set(reads)
        for p in self.pending:
            p[0] -= 1
            if force or p[0] <= 0 or (ws and (ws.intersection(p[3]) or ws.intersection(p[4]))) or (rs and rs.intersection(p[4])):
                self._dma_now(p[1], p[2], p[3], p[4])
            else:
                keep.append(p)
        self.pending = keep

    def store(self, eng, fn, reads=(), writes=()):
        self.pending.append([self.STORE_DELAY, eng, fn, list(reads), list(writes)])

    def op(self, eng, fn, reads=(), writes=()):
        self._pending_tick(reads, writes)
        deps = self._deps(reads, writes)
        skip = tuple(id(s) for s in self.sems[eng]) if eng in self.NO_SELF_WAIT else ()
        self._emit_waits(eng, deps, skip)
        tok = self._tok_new(eng)
        self.q[eng].append(("op", fn, tok, 1))
        self._record(tok, reads, writes)
        self.n_inst += 1
        return tok

    def dma(self, eng, fn, reads=(), writes=()):
        self._pending_tick(reads, writes)
        self._dma_now(eng, fn, reads, writes)

    def _dma_now(self, eng, fn, reads=(), writes=()):
        deps = self._deps(reads, writes)
        slot = self.dnext
        self.dnext = (self.dnext + 1) % NDMA
        if self.dcnt[slot] > 0:
            deps.append((self.dsem[slot], self.dcnt[slot]))
        self._emit_waits(eng, deps)
        self.dcnt[slot] += 16
        tok = (self.dsem[slot], self.dcnt[slot])
        self.q[eng].append(("op", fn, tok, 16))
        self._record(tok, reads, writes)
        self.n_inst += 1
        return tok

    def flush(self):
        nc = self.nc
        self._pending_tick((), (), force=True)
        final = []
        for i in range(NDMA):
            if self.dcnt[i]:
                final.append((self.dsem[i], self.dcnt[i]))
        for e in ENGS:
            if self.cnt[e]:
                final.append((self.sems[e][-1], self.cnt[e]))
        self._emit_waits("sp", final)
        with nc.Block() as block:
            for e in ENGS:
                items = self.q[e]

                def body(engine, items=items):
                    for it in items:
                        if it[0] == "wait":
                            engine.wait_ge(it[1], it[2])
                        else:
                            it[1](engine).then_inc(it[2][0], it[3])
                getattr(block, ENGMAP[e])(body)
        self.q = {e: [] for e in ENGS}
        self.lastw = {}
        self.readers = {}


def run_pipeline(gens, max_new_per_step=1, max_active=16):
    it = iter(gens)
    active = []
    exhausted = False
    while True:
        if not exhausted and len(active) < max_active:
            try:
                active.append(next(it))
            except StopIteration:
                exhausted = True
        if not active:
            if exhausted:
                break
            continue
        nxt = []
        for g in active:
            try:
                next(g)
                nxt.append(g)
            except StopIteration:
                pass
        active = nxt


class Rot:
    def __init__(self, sbf, name, n, shape, dt):
        self.t = [sbf(f"{name}{i}", shape, dt) for i in range(n)]
        self.k = [f"{name}{i}" for i in range(n)]
        self.i = -1

    def nxt(self):
        self.i = (self.i + 1) % len(self.t)
        return self.t[self.i], self.k[self.i]


def host_consts():
    c = {}
    c["ident"] = np.eye(128, dtype=np.float32)
    s = np.arange(64)
    mf = (s[:, None] <= s[None, :]).astype(np.float32)
    mb = (s[:, None] >= s[None, :]).astype(np.float32)
    c["maskf"] = np.ascontiguousarray(np.broadcast_to(mf[:, None, :], (64, 8, 64))).reshape(64, 512)
    c["maskb"] = np.ascontiguousarray(np.broadcast_to(mb[:, None, :], (64, 8, 64))).reshape(64, 512)
    t = np.arange(TT)
    rf = np.ones((128, TT), np.float32); rf[:, t % 64 == 0] = 0
    rb = np.ones((128, TT), np.float32); rb[:, t % 64 == 63] = 0
    c["rmf"] = rf
    c["rmb"] = rb
    n = np.arange(128, dtype=np.float64)
    ang = 2 * np.pi * np.outer(n, n) / 128.0
    Fr = np.cos(ang); Fi = -np.sin(ang)
    c["FRI"] = np.concatenate([Fr, Fi], 1).astype(np.float32)
    c["FRnI"] = np.concatenate([Fr, -Fi], 1).astype(np.float32)
    c["FIR"] = np.concatenate([Fi, Fr], 1).astype(np.float32)
    c["nFI"] = (-Fi).astype(np.float32)
    angt = 2 * np.pi * np.outer(n, n) / 16384.0
    Tr = np.cos(angt); Ti = -np.sin(angt)
    c["TW"] = np.concatenate([Tr, Ti], 1).astype(np.float32)
    c["TWc"] = np.concatenate([Tr, -Ti], 1).astype(np.float32)
    pos = np.arange(L, dtype=np.float32)
    tpos = pos / np.float32(L - 1)
    bands = np.linspace(1e-4, 15, 16, dtype=np.float32)
    angz = (np.float32(2.0 * math.pi / L) * pos[:, None]) * bands[None, :]
    z = np.concatenate([tpos[:, None], np.cos(angz), -np.sin(angz)], -1).astype(np.float32)
    c["zT"] = np.ascontiguousarray(z.T)
    c["tpos"] = np.ascontiguousarray(tpos.reshape(1, L))
    c["iota"] = np.ascontiguousarray(np.broadcast_to(np.arange(128, dtype=np.float32)[None, :], (128, 128)))
    return c


def build(stop_after=99, dbg=(), skip=(), ext_in=()):
    nc = bass.Bass("TRN2", target_bir_lowering=False)
    _uqc = [0]

    def uq(n):
        _uqc[0] += 1
        return f"{n}_u{_uqc[0]}"
    EI = dict(kind="ExternalInput")
    def din(name, shape, dt=F32):
        return nc.dram_tensor(name, list(shape), dt, **EI).ap()
    def dscr(name, shape, dt):
        kind = "ExternalOutput" if name in dbg else ("ExternalInput" if name in ext_in else "Internal")
        return nc.dram_tensor(name, list(shape), dt, kind=kind).ap()

    x = din("x", [L, D])
    xh = din("xh", [L // 2, D])
    norm_mix_g = din("norm_mix_g", [1, D])
    w_in = din("w_in", [D, NCOL])
    lbl = din("hgrn_lb_logits", [2, D])
    ident = din("ident", [128, 128])
    maskf = din("maskf", [64, 512]); maskb = din("maskb", [64, 512])
    rmf = din("rmf", [128, TT]); rmb = din("rmb", [128, TT])
    out = nc.dram_tensor("out", [L // 2, D], F32, kind="ExternalOutput").ap()

    WIN = dscr("WIN", [D, NCOL], BF16)
    QD = [dscr(f"QD{d}", [8, 128, L], BF16) for d in range(2)]
    KD = [dscr(f"KD{d}", [8, 128, L], BF16) for d in range(2)]
    KDTM = [dscr(f"KDTM{d}", [L, D], BF16) for d in range(2)]
    DEC = [dscr(f"DEC{d}", [128, 8, 128], F32) for d in range(2)]
    VT = dscr("VT", [L, D], BF16)
    OGT = dscr("OGT", [D, L], BF16)
    HYT = dscr("HYT", [3 * D, L], F32)
    GT = dscr("GT", [2 * D, L], BF16)
    OF = dscr("OF", [8, 128, L], F32)
    OT = dscr("OT", [8, 128, L], F32)
    filt_w1 = din("filt_w1", [33, 64]); filt_w2 = din("filt_w2", [64, 64]); filt_w3 = din("filt_w3", [64, 2048])
    filt_vec = din("filt_vec", [4, 64])
    filt_decay = din("filt_decay", [1, 2048]); hyena_bias = din("hyena_bias", [1, 1024])
    conv_w = din("conv_w", [3, 3072]); conv_b = din("conv_b", [1, 3072])
    zT = din("zT", [33, L]); tpos = din("tpos", [1, L]); sel = din("sel", [128, 2])
    FRI = din("FRI", [128, 256]); FRnI = din("FRnI", [128, 256]); FIR = din("FIR", [128, 256]); nFI = din("nFI", [128, 128])
    TW = din("TW", [128, 256]); TWc = din("TWc", [128, 256])
    HF = dscr("HF", [2048, L], BF16)
    UT = dscr("UT", [D, L], BF16)
    X0T = dscr("X0T", [D, L], F32)
    KF = dscr("KF", [D, 128, 2, 128], F32)
    YCT = dscr("YCT", [D, L], F32)
    hgrn_norm_g = din("hgrn_norm_g", [1, 128])
    w_branch_a = din("w_branch_a", [D, D]); w_branch_b = din("w_branch_b", [D, D]); w_out = din("w_out", [D, D])
    norm_ffn_g = din("norm_ffn_g", [1, D]); norm_final_g = din("norm_final_g", [1, D])
    peer_w_q = din("peer_w_q", [D, 2048]); peer_sk = din("peer_sk", [16, 128, 128])
    peer_u = din("peer_u", [16384, D]); peer_v = din("peer_v", [16384, D])
    iota = din("iota", [128, 128])
    AT = dscr("AT", [D, L // 2], BF16) if "AT" in dbg else None
    MG = dscr("MG", [D, L // 2], BF16) if "MG" in dbg else None
    PEERO = dscr("PEERO", [L // 2, D], F32) if "PEERO" in dbg else None
    H1 = dscr("H1", [L // 2, D], F32)
    VBF = dscr("VBF", [16384, D], BF16)
    UTS = dscr("UTS", [128, 128, 8, 128], BF16)
    TOK0 = 0
    OTh = OT[:, :, 0:L // 2]; OGTh = OGT[:, 0:L // 2]; X0Th = X0T[:, 0:L // 2]; YCTh = YCT[:, 0:L // 2]; GTh = GT[:, 0:L // 2]

    es0 = ExitStack()
    with es0:
        em = Em(nc, es0)

        for r in range(8):
            em.dma("pool", lambda e, r=r: e.dma_start(out=WIN[r * 128:(r + 1) * 128, :], in_=w_in[r * 128:(r + 1) * 128, :]),
                   writes=["WIN"])
        em.flush()
        if stop_after <= 0:
            return nc

        if 1 not in skip:
          with ExitStack() as es:
              sbf = lambda n, s, d: es.enter_context(nc.sbuf_tensor(uq(n), list(s), d))
              psf = lambda n, s, d: es.enter_context(nc.psum_tensor(uq(n), list(s), d))
              gt = sbf("gt", [128, D], F32)
              idf = sbf("idf", [128, 128], F32)
              idb = sbf("idb", [128, 128], BF16)
              lb2 = sbf("lb2", [128, 2, 8], F32)
              lbt = sbf("lbt", [128, 8], F32)
              olt = sbf("olt", [128, 8], F32)
              nolt = sbf("nolt", [128, 8], F32)
              rmf_t = sbf("rmf_t", [128, TT], F32)
              rmb_t = sbf("rmb_t", [128, TT], F32)
              dec_t = [sbf(f"dec_t{d}", [128, 8, 128], F32) for d in range(2)]
              xt = sbf("xt", [128, 4, D], F32)
              sqj = sbf("sqj", [128, D], F32)
              ss = sbf("ss", [128, 4], F32)
              rstd = sbf("rstd", [128, 4], F32)
              xn = sbf("xn", [128, 4, D], BF16)
              xnT = Rot(sbf, "xnT", 2, [128, 8, TT], BF16)
              wg = Rot(sbf, "wg", 3, [128, 8, 1024], BF16)
              QS = sbf("QS", [128, 8, TT], F32)
              SG = Rot(sbf, "SG", 3, [128, TT], F32)
              KK = Rot(sbf, "KK", 6, [128, TT], F32)
              LF = Rot(sbf, "LF", 3, [128, TT], F32)
              BB = Rot(sbf, "BB", 3, [128, TT], F32)
              E1 = Rot(sbf, "E1", 3, [128, TT], F32)
              E2 = Rot(sbf, "E2", 3, [128, TT], F32)
              QDs = Rot(sbf, "QDs", 4, [128, TT], BF16)
              KDs = Rot(sbf, "KDs", 5, [128, TT], BF16)
              KTs = Rot(sbf, "KTs", 3, [128, 4, 128], BF16)
              OB = Rot(sbf, "OB", 3, [128, TT], BF16)
              OFt = Rot(sbf, "OFt", 3, [128, TT], F32)
              pTs = [psf(f"pT{i}", [128, 8, 128], BF16) for i in range(2)]
              pM = [psf(f"pM{i}", [128, TT], F32) for i in range(4)]
              pK = [psf(f"pK{i}", [128, 4, 128], BF16) for i in range(2)]
              pmi = [0]

              em.dma("sp", lambda e: e.dma_start(out=gt[:], in_=norm_mix_g.partition_broadcast(128)), writes=["gt"])
              em.dma("sp", lambda e: e.dma_start(out=idf[:], in_=ident), writes=["idf"])
              em.dma("sp", lambda e: e.dma_start(out=rmf_t[:], in_=rmf), writes=["rmf"])
              em.dma("sp", lambda e: e.dma_start(out=rmb_t[:], in_=rmb), writes=["rmb"])
              em.dma("sp", lambda e: e.dma_start(out=lb2[:], in_=lbl.rearrange("t (h k) -> k t h", k=128), allow_slow_non_contiguous=True), writes=["lb2"])
              em.op("dve", lambda e: e.tensor_copy(out=idb[:], in_=idf[:]), reads=["idf"], writes=["idb"])
              em.op("dve", lambda e: e.tensor_tensor(out=lbt[:], in0=lb2[:, 1, :], in1=lb2[:, 0, :], op=ALU.subtract), reads=["lb2"], writes=["lbt"])
              em.op("act", lambda e: e.activation(out=lbt[:], in_=lbt[:], func=AF.Exp), reads=["lbt"], writes=["lbt"])
              em.op("dve", lambda e: e.tensor_scalar(out=lbt[:], in0=lbt[:], scalar1=1.0, scalar2=None, op0=ALU.add), reads=["lbt"], writes=["lbt"])
              em.op("dve", lambda e: e.reciprocal(out=lbt[:], in_=lbt[:]), reads=["lbt"], writes=["lbt"])
              em.op("dve", lambda e: e.tensor_scalar(out=olt[:], in0=lbt[:], scalar1=-1.0, scalar2=1.0, op0=ALU.mult, op1=ALU.add), reads=["lbt"], writes=["olt"])
              em.op("dve", lambda e: e.tensor_scalar(out=nolt[:], in0=olt[:], scalar1=-1.0, scalar2=None, op0=ALU.mult), reads=["olt"], writes=["nolt"])

              def next_pm():
                  pmi[0] = (pmi[0] + 1) % 4
                  return pM[pmi[0]], f"pM{pmi[0]}"

              ntile = NTILE if stop_after > 1 else 1
              HALF = NTILE // 2
              wcur = {}

              def prologue_gen(tt):
                  t0 = tt * TT
                  em.dma("sp", lambda e: e.dma_start(out=xt[:], in_=x[t0:t0 + TT, :].rearrange("(s p) d -> p s d", p=128)), writes=["xt"])
                  yield
                  for s in range(4):
                      em.op("act", lambda e, s=s: e.activation(out=sqj[:], in_=xt[:, s, :], func=AF.Square, accum_out=ss[:, s:s + 1]),
                            reads=["xt"], writes=["sqj", "ss"])
                  em.op("act", lambda e: e.activation(out=rstd[:], in_=ss[:], func=AF.Ln, scale=1.0 / D, bias=EPS), reads=["ss"], writes=["rstd"])
                  em.op("act", lambda e: e.activation(out=rstd[:], in_=rstd[:], func=AF.Exp, scale=-0.5), reads=["rstd"], writes=["rstd"])
                  yield
                  xT, xTk = xnT.nxt()
                  wcur[("xT", tt)] = (xT, xTk)
                  for s in range(4):
                      em.op("dve", lambda e, s=s: e.scalar_tensor_tensor(out=xn[:, s, :], in0=xt[:, s, :], scalar=rstd[:, s:s + 1], in1=gt[:], op0=ALU.mult, op1=ALU.mult),
                            reads=["xt", "rstd", "gt"], writes=[f"xn{s}"])
                  yield
                  for s in range(4):
                      pt_ = pTs[s % 2]; ptk = f"pT{s % 2}"
                      for k in range(8):
                          em.op("pe", lambda e, s=s, k=k, pt_=pt_: e.transpose(out=pt_[:, k, :], in_=xn[:, s, k * 128:(k + 1) * 128], identity=idb[:]),
                                reads=[f"xn{s}", "idb"], writes=[ptk])
                      em.op("act" if s % 2 else "dve", lambda e, s=s, pt_=pt_: (e.activation(out=xT[:, :, s * 128:(s + 1) * 128], in_=pt_[:], func=AF.Copy) if s % 2 else e.tensor_copy(out=xT[:, :, s * 128:(s + 1) * 128], in_=pt_[:])),
                            reads=[ptk], writes=[xTk])
                  yield

              def wload_gen(tt, g):
                  w, wk = wg.nxt()
                  wcur[(tt, g)] = (w, wk)
                  em.dma("sp", lambda e: e.dma_start(out=w[:], in_=WIN[:, g * 1024:(g + 1) * 1024].rearrange("(k p) c -> p k c", p=128)), writes=[wk])
                  yield

              def vblock_gen(tt, s, hf):
                  t0 = tt * TT
                  w, wk = wcur[(tt, 3)]; xT, xTk = wcur[("xT", tt)]
                  pm, pmk = next_pm()
                  for k in range(8):
                      em.op("pe", lambda e, k=k: e.matmul(pm[:], lhsT=xT[:, k, s * 128:(s + 1) * 128], rhs=w[:, k, hf * 512:(hf + 1) * 512], start=(k == 0), stop=(k == 7)),
                            reads=[xTk, wk], writes=[pmk])
                  yield
                  ob, obk = OB.nxt()
                  em.op("dve", lambda e: e.tensor_copy(out=ob[:], in_=pm[:]), reads=[pmk], writes=[obk])
                  em.store("sp", lambda e: e.dma_start(out=VT[t0 + s * 128:t0 + (s + 1) * 128, hf * 512:(hf + 1) * 512], in_=ob[:]), reads=[obk], writes=["VT"])

              def block_gen(tt, g, cb):
                  t0 = tt * TT
                  full = tt < HALF
                  w, wk = wcur[(tt, g)]; xT, xTk = wcur[("xT", tt)]
                  pm, pmk = next_pm()
                  for k in range(8):
                      em.op("pe", lambda e, k=k: e.matmul(pm[:], lhsT=w[:, k, cb * 128:(cb + 1) * 128], rhs=xT[:, k, :], start=(k == 0), stop=(k == 7)),
                            reads=[xTk, wk], writes=[pmk])
                  yield
                  if g == 0:
                      em.op("act", lambda e: e.activation(out=QS[:, cb, :], in_=pm[:], func=AF.Silu), reads=[pmk], writes=[f"QS{cb}"])
                  elif g in (1, 2):
                      d = g - 1
                      h = cb
                      sg, sgk = SG.nxt(); kk, kkk = KK.nxt(); lf, lfk = LF.nxt(); bb, bbk = BB.nxt()
                      e1, e1k = E1.nxt(); e2, e2k = E2.nxt(); qd, qdk = QDs.nxt(); kd, kdk = KDs.nxt()
                      kt, ktk = KTs.nxt()
                      em.op("act", lambda e: e.activation(out=sg[:], in_=pm[:], func=AF.Sigmoid), reads=[pmk], writes=[sgk])
                      yield
                      em.op("dve", lambda e: e.tensor_scalar(out=kk[:], in0=sg[:], scalar1=nolt[:, h:h + 1], scalar2=olt[:, h:h + 1], op0=ALU.mult, op1=ALU.add),
                            reads=[sgk, "nolt", "olt"], writes=[kkk])
                      em.op("act", lambda e: e.activation(out=lf[:], in_=sg[:], func=AF.Ln, scale=olt[:, h:h + 1], bias=lbt[:, h:h + 1]),
                            reads=[sgk, "olt", "lbt"], writes=[lfk])
                      yield
                      if d == 0:
                          em.op("dve", lambda e: e.tensor_tensor_scan(out=bb[:], data0=rmf_t[:], data1=lf[:], initial=0.0, op0=ALU.mult, op1=ALU.add),
                                reads=[lfk, "rmf"], writes=[bbk])
                      else:
                          em.op("dve", lambda e: e.tensor_tensor_scan(out=bb[:, ::-1], data0=rmb_t[:, ::-1], data1=lf[:, ::-1], initial=0.0, op0=ALU.mult, op1=ALU.add),
                                reads=[lfk, "rmb"], writes=[bbk])
                      yield
                      em.op("act", lambda e: e.activation(out=e1[:], in_=bb[:], func=AF.Exp), reads=[bbk], writes=[e1k])
                      em.op("act", lambda e: e.activation(out=e2[:], in_=bb[:], func=AF.Exp, scale=-1.0), reads=[bbk], writes=[e2k])
                      yield
                      if full:
                          em.op("pool", lambda e: e.tensor_tensor(out=qd[:], in0=QS[:, h, :], in1=e1[:], op=ALU.mult), reads=[f"QS{h}", e1k], writes=[qdk])
                      em.op("pool", lambda e: e.tensor_tensor(out=kd[:], in0=kk[:], in1=e2[:], op=ALU.mult), reads=[kkk, e2k], writes=[kdk])
                      off = 63 if d == 0 else 0
                      em.op("dve", lambda e: e.tensor_copy(out=dec_t[d][:, h, tt * 8:(tt + 1) * 8], in_=e1[:, off::64]),
                            reads=[e1k], writes=[f"dec{d}"])
                      if full:
                          em.store("sp", lambda e: e.dma_start(out=QD[d][h, :, t0:t0 + TT], in_=qd[:]), reads=[qdk], writes=[f"QD{d}"])
                          em.store("sp", lambda e: e.dma_start(out=KD[d][h, :, t0:t0 + TT], in_=kd[:]), reads=[kdk], writes=[f"KD{d}"])
                      yield
                      pk = pK[h % 2]; pkk = f"pK{h % 2}"
                      for s in range(4):
                          em.op("pe", lambda e, s=s: e.transpose(out=pk[:, s, :], in_=kd[:, s * 128:(s + 1) * 128], identity=idb[:]),
                                reads=[kdk, "idb"], writes=[pkk])
                      yield
                      em.op("dve", lambda e: e.tensor_copy(out=kt[:], in_=pk[:]), reads=[pkk], writes=[ktk])
                      em.store("sp", lambda e: e.dma_start(out=KDTM[d][t0:t0 + TT, h * 128:(h + 1) * 128].rearrange("(s p) k -> p s k", p=128), in_=kt[:]),
                               reads=[ktk], writes=[f"KDTM{d}"])
                  elif g == 4:
                      ob, obk = OB.nxt()
                      em.op("act", lambda e: e.activation(out=ob[:], in_=pm[:], func=AF.Silu), reads=[pmk], writes=[obk])
                      em.store("sp", lambda e: e.dma_start(out=OGT[cb * 128:(cb + 1) * 128, t0:t0 + TT], in_=ob[:]), reads=[obk], writes=["OGT"])
                  elif g in (5, 6, 7):
                      of_, ofk = OFt.nxt()
                      r0 = (g - 5) * 1024 + cb * 128
                      em.op("dve" if cb % 2 else "act", lambda e: (e.tensor_copy(out=of_[:], in_=pm[:]) if cb % 2 else e.activation(out=of_[:], in_=pm[:], func=AF.Copy)), reads=[pmk], writes=[ofk])
                      em.store("sp", lambda e: e.dma_start(out=HYT[r0:r0 + 128, t0:t0 + TT], in_=of_[:]), reads=[ofk], writes=["HYT"])
                  else:
                      ob, obk = OB.nxt()
                      r0 = (g - 8) * 1024 + cb * 128
                      em.op("act", lambda e: e.activation(out=ob[:], in_=pm[:], func=AF.Sigmoid), reads=[pmk], writes=[obk])
                      em.store("sp", lambda e: e.dma_start(out=GT[r0:r0 + 128, t0:t0 + TT], in_=ob[:]), reads=[obk], writes=["GT"])

              def p1_items():
                  yield prologue_gen(0)
                  for tt in range(ntile):
                      groups = list(range(10)) if tt < HALF else [2, 3, 5, 6, 7]
                      yield wload_gen(tt, groups[0])
                      for gi, g in enumerate(groups):
                          if gi + 1 < len(groups):
                              yield wload_gen(tt, groups[gi + 1])
                          if gi == len(groups) // 2 and tt + 1 < ntile:
                              yield prologue_gen(tt + 1)
                          if g == 3:
                              for s in range(4):
                                  for hf in range(2):
                                      yield vblock_gen(tt, s, hf)
                          else:
                              for cb in range(8):
                                  yield block_gen(tt, g, cb)
              run_pipeline(p1_items())
              for d in range(2):
                  em.store("sp", lambda e, d=d: e.dma_start(out=DEC[d], in_=dec_t[d][:]), reads=[f"dec{d}"], writes=[f"DEC{d}"])
              em.flush()
        if stop_after <= 1:
            print("inst", em.n_inst, "waits", em.n_wait)
            return nc

        if 2 not in skip:
          with ExitStack() as es:
              sbf = lambda n, s, d: es.enter_context(nc.sbuf_tensor(uq(n), list(s), d))
              psf = lambda n, s, d: es.enter_context(nc.psum_tensor(uq(n), list(s), d))
              mk = [sbf("mkf", [64, 512], F32), sbf("mkb", [64, 512], F32)]
              dect = [sbf(f"dect{d}", [128, 8, 128], F32) for d in range(2)]
              S = sbf("S", [128, 8, 128], F32)
              Sb = sbf("Sb", [128, 8, 128], BF16)
              tmpS = sbf("tmpS", [128, 8, 128], F32)
              qdt = Rot(sbf, "qdt", 2, [128, 8, TT], BF16)
              kdt = Rot(sbf, "kdt", 2, [128, 8, TT], BF16)
              ktm = Rot(sbf, "ktm", 2, [64, 8, D], BF16)
              vtm = Rot(sbf, "vtm", 2, [64, 8, D], BF16)
              scb = Rot(sbf, "scb", 3, [64, 8, 64], BF16)
              Ot = Rot(sbf, "Ot", 2, [128, 8, TT], F32)
              Of = Rot(sbf, "Of", 3, [128, 8, TT], F32)
              pS = [psf(f"pS{i}", [64, 8, 64], F32) for i in range(2)]
              pO = [psf(f"pO{i}", [128, 8, 64], F32) for i in range(2)]
              pP = [psf(f"pP{i}", [128, 4, 128], F32) for i in range(4)]
              em.dma("sp", lambda e: e.dma_start(out=mk[0][:], in_=maskf), writes=["mk0"])
              em.dma("sp", lambda e: e.dma_start(out=mk[1][:], in_=maskb), writes=["mk1"])
              for d in range(2):
                  em.dma("sp", lambda e, d=d: e.dma_start(out=dect[d][:], in_=DEC[d]), writes=[f"dect{d}"])
              for d in range(2):
                  em.op("pool", lambda e: e.memset(S[:], 0.0), writes=["S"])
                  em.op("pool", lambda e: e.memset(Sb[:], 0.0), writes=["Sb"])
                  HALF = NTILE // 2
                  tiles = list(range(HALF)) if d == 0 else list(range(NTILE - 1, -1, -1))
                  tl = {}

                  def tload_gen(tt, d=d, tl=tl):
                      t0 = tt * TT
                      full = tt < HALF
                      kt_, ktk = ktm.nxt(); v_, vk = vtm.nxt()
                      ent = dict(kt=(kt_, ktk), v=(v_, vk))
                      em.dma("sp", lambda e: e.dma_start(out=kt_[:], in_=KDTM[d][t0:t0 + TT, :].rearrange("(c s) k -> s c k", s=64)), writes=[ktk])
                      em.dma("sp", lambda e: e.dma_start(out=v_[:], in_=VT[t0:t0 + TT, :].rearrange("(c s) k -> s c k", s=64)), writes=[vk])
                      if full:
                          q_, qk = qdt.nxt(); k_, kk_ = kdt.nxt(); o_, ok = Ot.nxt()
                          ent.update(q=(q_, qk), k=(k_, kk_), o=(o_, ok))
                          em.dma("sp", lambda e: e.dma_start(out=q_[:], in_=QD[d][:, :, t0:t0 + TT].rearrange("h k t -> k h t")), writes=[qk])
                          em.dma("sp", lambda e: e.dma_start(out=k_[:], in_=KD[d][:, :, t0:t0 + TT].rearrange("h k t -> k h t")), writes=[kk_])
                          if d == 1:
                              f_, fk = Of.nxt()
                              ent.update(f=(f_, fk))
                              em.dma("sp", lambda e: e.dma_start(out=f_[:], in_=OF[:, :, t0:t0 + TT].rearrange("h k t -> k h t")), reads=[f"OF{tt}"], writes=[fk])
                      tl[tt] = ent
                      yield

                  def chunk_gen(tt, c, last, d=d, tl=tl):
                      t0 = tt * TT
                      full = tt < HALF
                      ent = tl[tt]
                      kt_, ktk = ent["kt"]; v_, vk = ent["v"]
                      gc = tt * 8 + c
                      cs = slice(c * 64, (c + 1) * 64)
                      decb = dect[d][:, :, gc:gc + 1]
                      if full:
                          q_, qk = ent["q"]; k_, kk_ = ent["k"]; o_, ok = ent["o"]
                          ps_ = pS[gc % 2]; psk = f"pS{gc % 2}"
                          po_ = pO[gc % 2]; pok = f"pO{gc % 2}"
                          for h in range(8):
                              em.op("pe", lambda e, h=h: e.matmul(ps_[:, h, :], lhsT=k_[:, h, cs], rhs=q_[:, h, cs], start=True, stop=True),
                                    reads=[kk_, qk], writes=[psk])
                      pps = []
                      for hh in range(2):
                          pp = pP[(gc % 2) * 2 + hh]; ppk = f"pP{(gc % 2) * 2 + hh}"
                          pps.append((pp, ppk))
                          for h4 in range(4):
                              h = hh * 4 + h4
                              hs = slice(h * 128, (h + 1) * 128)
                              em.op("pe", lambda e, pp=pp, h4=h4, hs=hs: e.matmul(pp[:, h4, :], lhsT=kt_[:, c, hs], rhs=v_[:, c, hs], start=True, stop=True),
                                    reads=[ktk, vk], writes=[ppk])
                      yield
                      if full:
                          sb_, sbk = scb.nxt()
                          em.op("dve", lambda e: e.tensor_tensor(out=sb_[:], in0=ps_[:], in1=mk[d][:].rearrange("p (h t) -> p h t", h=8), op=ALU.mult),
                                reads=[psk, f"mk{d}"], writes=[sbk])
                          for h in range(8):
                              hs = slice(h * 128, (h + 1) * 128)
                              em.op("pe", lambda e, h=h, hs=hs: e.matmul(po_[:, h, :], lhsT=v_[:, c, hs], rhs=sb_[:, h, :], start=True, stop=False),
                                    reads=[vk, sbk], writes=[pok])
                              em.op("pe", lambda e, h=h: e.matmul(po_[:, h, :], lhsT=Sb[:, h, :], rhs=q_[:, h, cs], start=False, stop=True),
                                    reads=["Sb", qk], writes=[pok])
                      for hh in range(2):
                          pp, ppk = pps[hh]
                          h4s = slice(hh * 4, hh * 4 + 4)
                          em.op("dve", lambda e, pp=pp, h4s=h4s: e.tensor_tensor(out=S[:, h4s, :], in0=pp[:], in1=S[:, h4s, :], op=ALU.add),
                                reads=[ppk, "S"], writes=["S"])
                      yield
                      em.op("dve", lambda e: e.tensor_tensor(out=S[:], in0=S[:], in1=decb.to_broadcast([128, 8, 128]), op=ALU.mult),
                            reads=["S", f"dect{d}"], writes=["S"])
                      em.op("act", lambda e: e.activation(out=Sb[:], in_=S[:], func=AF.Copy), reads=["S"], writes=["Sb"])
                      if full:
                          if d == 0:
                              em.op("act", lambda e: e.activation(out=o_[:, :, cs], in_=po_[:], func=AF.Copy), reads=[pok], writes=[ok])
                          else:
                              f_, fk = ent["f"]
                              em.op("pool" if False else "dve", lambda e: e.tensor_tensor(out=o_[:, :, cs], in0=po_[:], in1=f_[:, :, cs], op=ALU.add), reads=[pok, fk], writes=[ok])
                          if last:
                              dst = OF if d == 0 else OT
                              em.store("sp", lambda e: e.dma_start(out=dst[:, :, t0:t0 + TT].rearrange("h k t -> k h t"), in_=o_[:]), reads=[ok], writes=[f"OF{tt}" if d == 0 else "OT"])

                  def p2_items(d=d, tiles=tiles):
                      yield tload_gen(tiles[0])
                      for ti, tt in enumerate(tiles):
                          if ti + 1 < len(tiles):
                              yield tload_gen(tiles[ti + 1])
                          chunks = list(range(8)) if d == 0 else list(range(7, -1, -1))
                          for ci, c in enumerate(chunks):
                              yield chunk_gen(tt, c, ci == 7)
                  run_pipeline(p2_items())
              em.flush()
        print("inst", em.n_inst, "waits", em.n_wait)
        if stop_after <= 2:
            return nc
        if 3 not in skip:
          with ExitStack() as es:
            sbf = lambda n, s, d: es.enter_context(nc.sbuf_tensor(uq(n), list(s), d))
            psf = lambda n, s, d: es.enter_context(nc.psum_tensor(uq(n), list(s), d))
            w1t = sbf("w1t", [33, 64], F32); w2t = sbf("w2t", [64, 64], F32); w3t = sbf("w3t", [64, 2048], F32)
            fb = sbf("fb", [64, 4], F32)
            fs = sbf("fs", [64, 4], F32)
            dcy = sbf("dcy", [128, 16], F32)
            hbias = sbf("hbias", [128, 8], F32)
            tpb = sbf("tpb", [128, TT], F32)
            zt = Rot(sbf, "zt", 2, [33, TT], F32)
            ya = Rot(sbf, "ya", 2, [64, TT], F32)
            yb_ = Rot(sbf, "yb_", 2, [64, TT], F32)
            hd1 = Rot(sbf, "hd1", 2, [64, TT], F32)
            hd2 = Rot(sbf, "hd2", 2, [64, TT], F32)
            wn = Rot(sbf, "wn", 2, [128, TT], F32)
            fo = Rot(sbf, "fo", 2, [128, TT], F32)
            fob = Rot(sbf, "fob", 3, [128, TT], BF16)
            pF = [psf(f"pF{i}", [128, TT], F32) for i in range(3)]
            lag0 = sbf("lag0", [128, 16], F32); lagc = sbf("lagc", [128, 16], F32); lagb = sbf("lagb", [128, 16], BF16)
            selt = sbf("selt", [128, 2], F32)
            em.dma("sp", lambda e: e.dma_start(out=selt[:], in_=sel), writes=["selt"])
            em.dma("sp", lambda e: e.dma_start(out=w1t[:], in_=filt_w1), writes=["w1t"])
            em.dma("sp", lambda e: e.dma_start(out=w2t[:], in_=filt_w2), writes=["w2t"])
            em.dma("sp", lambda e: e.dma_start(out=w3t[:], in_=filt_w3), writes=["w3t"])
            em.dma("sp", lambda e: e.dma_start(out=fb[:], in_=filt_vec.rearrange("j k -> k j"), allow_slow_non_contiguous=True), writes=["fb"])
            em.dma("sp", lambda e: e.dma_start(out=dcy[:], in_=filt_decay.rearrange("o (b p) -> p (o b)", p=128), allow_slow_non_contiguous=True), writes=["dcy"])
            em.dma("sp", lambda e: e.dma_start(out=hbias[:], in_=hyena_bias.rearrange("o (b p) -> p (o b)", p=128), allow_slow_non_contiguous=True), writes=["hbias"])
            dcn = sbf("dcn", [128, 16], F32)
            em.op("dve", lambda e: e.tensor_scalar(out=dcn[:], in0=dcy[:], scalar1=-1.0, scalar2=None, op0=ALU.mult), reads=["dcy"], writes=["dcn"])
            em.op("dve", lambda e: e.tensor_tensor(out=dcy[:], in0=dcy[:], in1=dcn[:], op=ALU.min), reads=["dcy", "dcn"], writes=["dcy"])
            I2P = 1.0 / (2.0 * math.pi)
            for j in range(2):
                em.op("dve", lambda e, j=j: e.tensor_scalar(out=fs[:, 2 * j:2 * j + 1], in0=fb[:, 2 * j + 1:2 * j + 2], scalar1=I2P, scalar2=None, op0=ALU.mult), reads=["fb"], writes=["fs"])
                em.op("dve", lambda e, j=j: e.tensor_tensor(out=fs[:, 2 * j + 1:2 * j + 2], in0=fs[:, 2 * j:2 * j + 1], in1=fb[:, 2 * j:2 * j + 1], op=ALU.mult), reads=["fb", "fs"], writes=["fs"])
            MAGIC = 12582912.0

            def sin_layer(pm, pmk, j, hd, hdk):
                a, ak = ya.nxt(); b_, bk = yb_.nxt()
                em.op("dve", lambda e: e.tensor_scalar(out=a[:], in0=pm[0:64, :], scalar1=fs[:, 2 * j:2 * j + 1], scalar2=fs[:, 2 * j + 1:2 * j + 2], op0=ALU.mult, op1=ALU.add), reads=[pmk, "fs"], writes=[ak])
                em.op("dve", lambda e: e.tensor_scalar(out=b_[:], in0=a[:], scalar1=MAGIC, scalar2=None, op0=ALU.add), reads=[ak], writes=[bk])
                em.op("dve", lambda e: e.tensor_scalar(out=b_[:], in0=b_[:], scalar1=MAGIC, scalar2=None, op0=ALU.subtract), reads=[bk], writes=[bk])
                em.op("dve", lambda e: e.tensor_tensor(out=a[:], in0=a[:], in1=b_[:], op=ALU.subtract), reads=[ak, bk], writes=[ak])
                em.op("dve", lambda e: e.tensor_scalar(out=a[:], in0=a[:], scalar1=-0.499999, scalar2=0.499999, op0=ALU.max, op1=ALU.min), reads=[ak], writes=[ak])
                em.op("act", lambda e: e.activation(out=hd[:], in_=a[:], func=AF.Sin, scale=2.0 * math.pi), reads=[ak], writes=[hdk])

            pfi = 0
            for tt in range(NTILE if "3a" not in skip else 0):
                t0 = tt * TT
                z_, zk = zt.nxt()
                em.dma("sp", lambda e, z_=z_, t0=t0: e.dma_start(out=z_[:], in_=zT[:, t0:t0 + TT]), writes=[zk])
                em.dma("sp", lambda e, t0=t0: e.dma_start(out=tpb[:], in_=tpos[:, t0:t0 + TT].partition_broadcast(128)), writes=["tpb"])
                pm = pF[pfi % 3]; pmk = f"pF{pfi % 3}"; pfi += 1
                em.op("pe", lambda e, pm=pm, z_=z_: e.matmul(pm[0:64, :], lhsT=w1t[:], rhs=z_[:], start=True, stop=True), reads=["w1t", zk], writes=[pmk])
                h1_, h1k = hd1.nxt()
                sin_layer(pm, pmk, 0, h1_, h1k)
                pm = pF[pfi % 3]; pmk = f"pF{pfi % 3}"; pfi += 1
                em.op("pe", lambda e, pm=pm, h1_=h1_: e.matmul(pm[0:64, :], lhsT=w2t[:], rhs=h1_[:], start=True, stop=True), reads=["w2t", h1k], writes=[pmk])
                h2_, h2k = hd2.nxt()
                sin_layer(pm, pmk, 1, h2_, h2k)
                for cb in range(16):
                    pm = pF[pfi % 3]; pmk = f"pF{pfi % 3}"; pfi += 1
                    em.op("pe", lambda e, pm=pm, h2_=h2_, cb=cb: e.matmul(pm[:], lhsT=w3t[:, cb * 128:(cb + 1) * 128], rhs=h2_[:], start=True, stop=True), reads=["w3t", h2k], writes=[pmk])
                    w_, wk_ = wn.nxt(); f_, fk_ = fo.nxt(); fb_, fbk = fob.nxt()
                    em.op("act", lambda e, w_=w_, cb=cb: e.activation(out=w_[:], in_=tpb[:], func=AF.Exp, scale=dcy[:, cb:cb + 1]), reads=["tpb", "dcy"], writes=[wk_])
                    em.op("dve", lambda e, f_=f_, pm=pm, w_=w_: e.tensor_tensor(out=f_[:], in0=pm[:], in1=w_[:], op=ALU.mult), reads=[pmk, wk_], writes=[fk_])
                    if tt == 0:
                        em.op("dve", lambda e, f_=f_, cb=cb: e.tensor_copy(out=lag0[:, cb:cb + 1], in_=f_[:, 0:1]), reads=[fk_], writes=["lag0"])
                    em.op("pool", lambda e, fb_=fb_, f_=f_: e.tensor_copy(out=fb_[:], in_=f_[:]), reads=[fk_], writes=[fbk])
                    em.store("sp", lambda e, fb_=fb_, cb=cb, t0=t0: e.dma_start(out=HF[cb * 128:(cb + 1) * 128, t0:t0 + TT], in_=fb_[:]), reads=[fbk], writes=[f"HF0_{cb}" if tt == 0 else "HF"])
                if tt == 0:
                    em.op("dve", lambda e: e.memset(lagc[:], 0.0), writes=["lagc"])
                    em.op("dve", lambda e: e.tensor_scalar(out=lagc[:, 0:8], in0=lag0[:, 0:8], scalar1=selt[:, 0:1], scalar2=None, op0=ALU.mult), reads=["lag0", "selt"], writes=["lagc"])
                    em.op("dve", lambda e: e.scalar_tensor_tensor(out=lagc[:, 0:8], in0=lag0[:, 8:16], scalar=selt[:, 1:2], in1=lagc[:, 0:8], op0=ALU.mult, op1=ALU.add), reads=["lag0", "selt", "lagc"], writes=["lagc"])
                    em.op("dve", lambda e: e.tensor_tensor(out=lagc[:, 0:8], in0=lagc[:, 0:8], in1=hbias[:], op=ALU.add), reads=["lagc", "hbias"], writes=["lagc"])
                    em.op("dve", lambda e: e.tensor_copy(out=lagb[:], in_=lagc[:]), reads=["lagc"], writes=["lagb"])
                    em.dma("sp", lambda e: e.dma_start(out=HF.rearrange("(b p) t -> p b t", p=128)[:, :, 0:1], in_=lagb[:].unsqueeze(2), allow_slow_non_contiguous=True),
                           reads=["lagb"], writes=[f"HF0_{c_}" for c_ in range(16)])
            em.flush()

          with ExitStack() as es:
            sbf = lambda n, s, d: es.enter_context(nc.sbuf_tensor(uq(n), list(s), d))
            PW = 2048
            cw = sbf("cw", [128, 3, 24], F32)
            cbias = sbf("cbias", [128, 24], F32)
            hyin = [Rot(sbf, f"hyin{j}", 2, [128, PW + 2], F32) for j in range(3)]
            cv = [Rot(sbf, f"cv{j}", 2, [128, PW], F32) for j in range(3)]
            ub = Rot(sbf, "ub", 2, [128, PW], BF16)
            em.dma("sp", lambda e: e.dma_start(out=cw[:], in_=conv_w.rearrange("j (b p) -> p j b", p=128), allow_slow_non_contiguous=True), writes=["cw"])
            em.dma("sp", lambda e: e.dma_start(out=cbias[:], in_=conv_b.rearrange("o (b p) -> p (o b)", p=128), allow_slow_non_contiguous=True), writes=["cbias"])
            for cb in range(8 if "3b" not in skip else 0):
                for pc in range(L // PW):
                    t0 = pc * PW
                    outs = []
                    for j in range(3):
                        hy_, hyk = hyin[j].nxt(); c_, ck = cv[j].nxt()
                        blk = j * 8 + cb
                        r0 = blk * 128
                        lo = max(t0 - 1, 0); hi = min(t0 + PW + 1, L)
                        if t0 == 0:
                            em.op("pool", lambda e, hy_=hy_: e.memset(hy_[:, 0:1], 0.0), writes=[hyk])
                        if t0 + PW == L:
                            em.op("pool", lambda e, hy_=hy_: e.memset(hy_[:, PW + 1:PW + 2], 0.0), writes=[hyk])
                        o0 = lo - (t0 - 1)
                        em.dma("sp", lambda e, hy_=hy_, r0=r0, lo=lo, hi=hi, o0=o0: e.dma_start(out=hy_[:, o0:o0 + hi - lo], in_=HYT[r0:r0 + 128, lo:hi]), writes=[hyk])
                        eng = "dve"
                        em.op(eng, lambda e, c_=c_, hy_=hy_, blk=blk: e.tensor_scalar(out=c_[:], in0=hy_[:, 1:PW + 1], scalar1=cw[:, 1, blk:blk + 1], scalar2=cbias[:, blk:blk + 1], op0=ALU.mult, op1=ALU.add), reads=[hyk, "cw", "cbias"], writes=[ck])
                        em.op(eng, lambda e, c_=c_, hy_=hy_, blk=blk: e.scalar_tensor_tensor(out=c_[:], in0=hy_[:, 0:PW], scalar=cw[:, 0, blk:blk + 1], in1=c_[:], op0=ALU.mult, op1=ALU.add), reads=[hyk, "cw", ck], writes=[ck])
                        em.op(eng, lambda e, c_=c_, hy_=hy_, blk=blk: e.scalar_tensor_tensor(out=c_[:], in0=hy_[:, 2:PW + 2], scalar=cw[:, 2, blk:blk + 1], in1=c_[:], op0=ALU.mult, op1=ALU.add), reads=[hyk, "cw", ck], writes=[ck])
                        outs.append((c_, ck))
                    u_, uk = ub.nxt()
                    em.op("pool", lambda e, u_=u_, a=outs[2][0], b=outs[1][0]: e.tensor_tensor(out=u_[:], in0=a[:], in1=b[:], op=ALU.mult), reads=[outs[2][1], outs[1][1]], writes=[uk])
                    em.store("sp", lambda e, u_=u_, cb=cb, t0=t0: e.dma_start(out=UT[cb * 128:(cb + 1) * 128, t0:t0 + PW], in_=u_[:]), reads=[uk], writes=["UT"])
                    em.store("sp", lambda e, a=outs[0][0], cb=cb, t0=t0: e.dma_start(out=X0T[cb * 128:(cb + 1) * 128, t0:t0 + PW], in_=a[:]), reads=[outs[0][1]], writes=["X0T"])
            em.flush()

          with ExitStack() as es:
            sbf = lambda n, s, d: es.enter_context(nc.sbuf_tensor(uq(n), list(s), d))
            psf = lambda n, s, d: es.enter_context(nc.psum_tensor(uq(n), list(s), d))
            cst = sbf("cst", [128, 256], F32)
            FRIb = sbf("FRIb", [128, 256], BF16); FRnIb = sbf("FRnIb", [128, 256], BF16)
            FIRb = sbf("FIRb", [128, 256], BF16); nFIb = sbf("nFIb", [128, 128], BF16)
            TWt = sbf("TWt", [128, 2, 128], F32); TWct = sbf("TWct", [128, 2, 128], F32)
            for nm, src, dst in (("FRI", FRI, FRIb), ("FRnI", FRnI, FRnIb), ("FIR", FIR, FIRb)):
                em.dma("sp", lambda e, src=src: e.dma_start(out=cst[:], in_=src), writes=["cst"])
                em.op("dve", lambda e, dst=dst: e.tensor_copy(out=dst[:], in_=cst[:]), reads=["cst"], writes=[nm])
            em.dma("sp", lambda e: e.dma_start(out=cst[:, 0:128], in_=nFI), writes=["cst"])
            em.op("dve", lambda e: e.tensor_copy(out=nFIb[:], in_=cst[:, 0:128]), reads=["cst"], writes=["nFI"])
            em.dma("sp", lambda e: e.dma_start(out=TWt[:].rearrange("p a b -> p (a b)"), in_=TW), writes=["TW"])
            em.dma("sp", lambda e: e.dma_start(out=TWct[:].rearrange("p a b -> p (a b)"), in_=TWc), writes=["TWc"])
            Min = Rot(sbf, "Min", 8, [64, 2, 128], BF16)
            P1 = Rot(sbf, "P1", 6, [128, 2, 2, 128], F32)
            P2 = Rot(sbf, "P2", 6, [128, 2, 2, 128], F32)
            B2 = Rot(sbf, "B2", 8, [128, 2, 2, 128], BF16)
            Y2 = Rot(sbf, "Y2", 4, [128, 2, 2, 128], BF16)
            D2 = Rot(sbf, "D2", 4, [128, 2, 2, 128], BF16)
            KFs = Rot(sbf, "KFs", 6, [128, 2, 2, 128], F32)
            YO = Rot(sbf, "YO", 3, [64, 2, 128], F32)
            pq = [psf(f"pq{i}", [128, 2, 2, 128], F32) for i in range(8)]
            pqk = [f"pq{i}" for i in range(8)]

            def bc4(t3, ri):
                return t3[:, ri:ri + 1, :].unsqueeze(1).to_broadcast([128, 2, 2, 128])

            def cmul(src, srck, tw, twk, p1, p1k, p2, p2k):
                em.op("dve", lambda e: e.tensor_tensor(out=p1[:], in0=src[:], in1=bc4(tw, 0), op=ALU.mult), reads=[srck, twk], writes=[p1k])
                em.op("dve", lambda e: e.tensor_tensor(out=p2[:], in0=src[:, :, ::-1, :], in1=bc4(tw, 1), op=ALU.mult), reads=[srck, twk], writes=[p2k])

            def flat(ap3):
                return ap3.rearrange("p a b -> p (a b)")

            def st2(o, ok_, b2, b2k, ch, conj, first, last):
                fi = nFIb if conj else FRIb[:, 128:256]
                nfi = FRIb[:, 128:256] if conj else nFIb
                em.op("pe", lambda e: e.matmul(flat(o), lhsT=FRIb[:, 0:128], rhs=flat(b2[:, ch, :, :]), start=first, stop=False), reads=["FRI", b2k], writes=[ok_])
                em.op("pe", lambda e: e.matmul(o[:, 0, :], lhsT=nfi[:] if conj is False else nfi, rhs=b2[:, ch, 1, :], start=False, stop=False), reads=["FRI", "nFI", b2k], writes=[ok_])
                em.op("pe", lambda e: e.matmul(o[:, 1, :], lhsT=fi[:] if conj else fi, rhs=b2[:, ch, 0, :], start=False, stop=last), reads=["FRI", "nFI", b2k], writes=[ok_])

            npair = 512 if 31 not in skip else 4

            def stage1_gen(src_dram, c0, rhs1, rhs1k, tw, twk, pa, pak, res):
                m_, mk_ = Min.nxt()
                em.dma("sp", lambda e: e.dma_start(out=m_[:], in_=src_dram[c0:c0 + 2, :].rearrange("c (a b) -> a c b", b=128)), writes=[mk_])
                yield
                for ch in range(2):
                    em.op("pe", lambda e, ch=ch: e.matmul(flat(pa[:, ch, :, :]), lhsT=m_[:, ch, :], rhs=rhs1[0:64, :], start=True, stop=True), reads=[mk_, rhs1k], writes=[pak])
                yield
                p1, p1k = P1.nxt(); p2, p2k = P2.nxt(); b2, b2k = B2.nxt()
                cmul(pa, pak, tw, twk, p1, p1k, p2, p2k)
                yield
                em.op("pool", lambda e: e.tensor_tensor(out=b2[:, :, 0, :], in0=p1[:, :, 0, :], in1=p2[:, :, 0, :], op=ALU.subtract), reads=[p1k, p2k], writes=[b2k])
                em.op("pool", lambda e: e.tensor_tensor(out=b2[:, :, 1, :], in0=p1[:, :, 1, :], in1=p2[:, :, 1, :], op=ALU.add), reads=[p1k, p2k], writes=[b2k])
                res.append((b2, b2k))
                yield

            def kf_gen(pr):
                c0 = pr * 2
                rf, rb = [], []
                pa0 = pq[(pr % 2) * 2]; pa0k = pqk[(pr % 2) * 2]
                pa1 = pq[(pr % 2) * 2 + 1]; pa1k = pqk[(pr % 2) * 2 + 1]
                g1 = stage1_gen(HF, c0, FRIb, "FRI", TWt, "TW", pa0, pa0k, rf)
                g2 = stage1_gen(HF, 1024 + c0, FRnIb, "FRnI", TWct, "TWc", pa1, pa1k, rb)
                for _ in range(4):
                    next(g1); next(g2)
                    yield
                bf_, bfk = rf[0]; bb_, bbk = rb[0]
                px = pq[4 + pr % 3]; pxk = pqk[4 + pr % 3]
                for ch in range(2):
                    st2(px[:, ch, :, :], pxk, bf_, bfk, ch, False, True, False)
                    st2(px[:, ch, :, :], pxk, bb_, bbk, ch, True, False, True)
                yield
                kf_, kfk = KFs.nxt()
                em.op("act", lambda e: e.activation(out=kf_[:], in_=px[:], func=AF.Copy), reads=[pxk], writes=[kfk])
                em.store("sp", lambda e: e.dma_start(out=KF[c0:c0 + 2].rearrange("c k a b -> k c a b"), in_=kf_[:]), reads=[kfk], writes=[f"KF{pr}"])

            def data_gen(pr):
                c0 = pr * 2
                kf_, kfk = KFs.nxt()
                em.dma("sp", lambda e: e.dma_start(out=kf_[:], in_=KF[c0:c0 + 2].rearrange("c k a b -> k c a b")), reads=[f"KF{pr}"], writes=[kfk])
                rf = []
                pa = pq[pr % 2]; pak = pqk[pr % 2]
                g1 = stage1_gen(UT, c0, FRIb, "FRI", TWt, "TW", pa, pak, rf)
                for _ in range(4):
                    next(g1)
                    yield
                b2, b2k = rf[0]
                px = pq[2 + pr % 2]; pxk = pqk[2 + pr % 2]
                for ch in range(2):
                    st2(px[:, ch, :, :], pxk, b2, b2k, ch, False, True, True)
                yield
                p1, p1k = P1.nxt(); p2, p2k = P2.nxt(); y2, y2k = Y2.nxt()
                em.op("dve", lambda e: e.tensor_tensor(out=p1[:], in0=px[:], in1=kf_[:, :, 0:1, :].to_broadcast([128, 2, 2, 128]), op=ALU.mult), reads=[pxk, kfk], writes=[p1k])
                em.op("dve", lambda e: e.tensor_tensor(out=p2[:], in0=px[:, :, ::-1, :], in1=kf_[:, :, 1:2, :].to_broadcast([128, 2, 2, 128]), op=ALU.mult), reads=[pxk, kfk], writes=[p2k])
                yield
                em.op("pool", lambda e: e.tensor_tensor(out=y2[:, :, 0, :], in0=p1[:, :, 0, :], in1=p2[:, :, 0, :], op=ALU.subtract), reads=[p1k, p2k], writes=[y2k])
                em.op("pool", lambda e: e.tensor_tensor(out=y2[:, :, 1, :], in0=p1[:, :, 1, :], in1=p2[:, :, 1, :], op=ALU.add), reads=[p1k, p2k], writes=[y2k])
                yield
                pc = pq[4 + pr % 2]; pck = pqk[4 + pr % 2]
                for ch in range(2):
                    o = flat(pc[:, ch, :, :])
                    em.op("pe", lambda e, o=o, ch=ch: e.matmul(o, lhsT=y2[:, ch, 0, :], rhs=FRnIb[:], start=True, stop=False), reads=[y2k, "FRnI"], writes=[pck])
                    em.op("pe", lambda e, o=o, ch=ch: e.matmul(o, lhsT=y2[:, ch, 1, :], rhs=FIRb[:], start=False, stop=True), reads=[y2k, "FIR"], writes=[pck])
                yield
                p1b, p1bk = P1.nxt(); p2b, p2bk = P2.nxt(); d2, d2k = D2.nxt()
                cmul(pc, pck, TWct, "TWc", p1b, p1bk, p2b, p2bk)
                yield
                em.op("pool", lambda e: e.tensor_tensor(out=d2[:, 0, :, :], in0=p1b[:, :, 0, :], in1=p2b[:, :, 0, :], op=ALU.subtract), reads=[p1bk, p2bk], writes=[d2k])
                em.op("pool", lambda e: e.tensor_tensor(out=d2[:, 1, :, :], in0=p1b[:, :, 1, :], in1=p2b[:, :, 1, :], op=ALU.add), reads=[p1bk, p2bk], writes=[d2k])
                yield
                py = pq[6 + pr % 2][0:64, 0, :, :]; pyk = pqk[6 + pr % 2]
                em.op("pe", lambda e: e.matmul(flat(py), lhsT=FRIb[:, 0:64], rhs=flat(d2[:, 0, :, :]), start=True, stop=False), reads=["FRI", d2k], writes=[pyk])
                em.op("pe", lambda e: e.matmul(flat(py), lhsT=FRIb[:, 128:192], rhs=flat(d2[:, 1, :, :]), start=False, stop=True), reads=["FRI", d2k], writes=[pyk])
                yield
                yo, yok = YO.nxt()
                em.op("act", lambda e: e.activation(out=yo[:], in_=py, func=AF.Copy, scale=1.0 / 16384.0), reads=[pyk], writes=[yok])
                em.store("sp", lambda e: e.dma_start(out=YCT[c0:c0 + 2, :].rearrange("c (a b) -> a c b", b=128), in_=yo[:]), reads=[yok], writes=["YCT"])

            run_pipeline(kf_gen(pr) for pr in range(npair))
            run_pipeline(data_gen(pr) for pr in range(npair))
            em.flush()
        print("inst", em.n_inst, "waits", em.n_wait)
        if stop_after <= 3:
            return nc
        TK = 256
        NTK = (L // 2) // TK
        if 4 not in skip:
          with ExitStack() as es:
            sbf = lambda n, s, d: es.enter_context(nc.sbuf_tensor(uq(n), list(s), d))
            psf = lambda n, s, d: es.enter_context(nc.psum_tensor(uq(n), list(s), d))
            wa = sbf("wa", [128, 8, D], BF16); wb = sbf("wb", [128, 8, D], BF16); wo = sbf("wo", [128, 8, D], BF16)
            for wt_, src, nm in ((wa, w_branch_a, "wa"), (wb, w_branch_b, "wb"), (wo, w_out, "wo")):
                for k in range(8):
                    em.dma("pool", lambda e, wt_=wt_, src=src, k=k: e.dma_start(out=wt_[:, k, :], in_=src[k * 128:(k + 1) * 128, :]), writes=[nm])
            ones = sbf("ones", [128, 128], F32)
            em.op("dve", lambda e: e.memset(ones[:], 1.0), writes=["ones"])
            gcol = sbf("gcol", [128, 1], F32)
            em.dma("sp", lambda e: e.dma_start(out=gcol[:], in_=hgrn_norm_g.rearrange("o v -> v o"), allow_slow_non_contiguous=True), writes=["gcol"])
            ot = Rot(sbf, "ot", 2, [128, 8, TK], F32)
            ogt = Rot(sbf, "ogt", 2, [128, 8, TK], BF16)
            sq = Rot(sbf, "sq", 2, [128, 8, TK], F32)
            rs = Rot(sbf, "rs", 2, [128, 2, TK], F32)
            tmpA = Rot(sbf, "tmpA", 2, [128, 2, TK], F32)
            At = Rot(sbf, "At", 2, [128, 8, TK], BF16)
            x0t = Rot(sbf, "x0t", 2, [128, 8, TK], F32)
            yct = Rot(sbf, "yct", 2, [128, 8, TK], F32)
            Bt = Rot(sbf, "Bt", 2, [128, 8, TK], BF16)
            gat = Rot(sbf, "gat", 2, [128, 8, 2, TK], BF16)
            tg = Rot(sbf, "tg", 2, [128, 2, TK], F32)
            mg = Rot(sbf, "mg", 2, [128, 8, TK], BF16)
            xt4 = Rot(sbf, "xt4", 2, [128, 2, D], F32)
            h1t = Rot(sbf, "h1t", 2, [128, 2, D], F32)
            pw = [psf(f"pw{i}", [128, 512], F32) for i in range(6)]
            pwi = [0]

            def npw():
                pwi[0] = (pwi[0] + 1) % 6
                return pw[pwi[0]], f"pw{pwi[0]}"

            for tk in range(NTK):
                tg0 = TOK0 + tk * TK
                o_, ok = ot.nxt(); og_, ogk = ogt.nxt(); s_, sk_ = sq.nxt(); a_, ak = At.nxt()
                em.dma("sp", lambda e, o_=o_, tk=tk: e.dma_start(out=o_[:], in_=OTh[:, :, tk * TK:(tk + 1) * TK].rearrange("h k t -> k h t")), writes=[ok])
                em.dma("sp", lambda e, og_=og_, tk=tk: e.dma_start(out=og_[:], in_=OGTh[:, tk * TK:(tk + 1) * TK].rearrange("(h k) t -> k h t", k=128)), writes=[ogk])
                em.op("act", lambda e, s_=s_, o_=o_: e.activation(out=s_[:], in_=o_[:], func=AF.Square), reads=[ok], writes=[sk_])
                for h2 in range(4):
                    p_, pk = npw()
                    for hh in range(2):
                        h = h2 * 2 + hh
                        em.op("pe", lambda e, p_=p_, s_=s_, h=h, hh=hh: e.matmul(p_[:, hh * TK:(hh + 1) * TK], lhsT=ones[:], rhs=s_[:, h, :], start=True, stop=True), reads=["ones", sk_], writes=[pk])
                    r_, rk = rs.nxt(); t_, tk_ = tmpA.nxt()
                    em.op("act", lambda e, r_=r_, p_=p_: e.activation(out=r_[:].rearrange("p a b -> p (a b)"), in_=p_[:], func=AF.Ln, scale=1.0 / 128, bias=EPS), reads=[pk], writes=[rk])
                    em.op("act", lambda e, r_=r_: e.activation(out=r_[:], in_=r_[:], func=AF.Exp, scale=-0.5), reads=[rk], writes=[rk])
                    em.op("dve", lambda e, t_=t_, o_=o_, r_=r_, h2=h2: e.scalar_tensor_tensor(out=t_[:], in0=o_[:, h2 * 2:h2 * 2 + 2, :], scalar=gcol[:, 0:1], in1=r_[:], op0=ALU.mult, op1=ALU.mult), reads=[ok, rk, "gcol"], writes=[tk_])
                    em.op("pool", lambda e, a_=a_, t_=t_, og_=og_, h2=h2: e.tensor_tensor(out=a_[:, h2 * 2:h2 * 2 + 2, :], in0=t_[:], in1=og_[:, h2 * 2:h2 * 2 + 2, :], op=ALU.mult), reads=[tk_, ogk], writes=[ak])
                if "AT" in dbg:
                    em.store("sp", lambda e, a_=a_, tk=tk: e.dma_start(out=AT[:, tk * TK:(tk + 1) * TK].rearrange("(h k) t -> k h t", k=128), in_=a_[:]), reads=[ak], writes=["AT"])
                x0_, x0k = x0t.nxt(); yc_, yck = yct.nxt(); b_, bk = Bt.nxt(); ga_, gak = gat.nxt()
                em.dma("sp", lambda e, x0_=x0_, tk=tk: e.dma_start(out=x0_[:], in_=X0Th[:, tk * TK:(tk + 1) * TK].rearrange("(h k) t -> k h t", k=128)), writes=[x0k])
                em.dma("sp", lambda e, yc_=yc_, tk=tk: e.dma_start(out=yc_[:], in_=YCTh[:, tk * TK:(tk + 1) * TK].rearrange("(h k) t -> k h t", k=128)), writes=[yck])
                for a2 in range(2):
                    em.dma("sp", lambda e, ga_=ga_, tk=tk, a2=a2: e.dma_start(out=ga_[:, :, a2, :], in_=GTh[a2 * 1024:(a2 + 1) * 1024, tk * TK:(tk + 1) * TK].rearrange("(h k) t -> k h t", k=128)), writes=[gak])
                em.op("pool", lambda e, b_=b_, x0_=x0_, yc_=yc_: e.tensor_tensor(out=b_[:], in0=x0_[:], in1=yc_[:], op=ALU.mult), reads=[x0k, yck], writes=[bk])
                m_, mk_ = mg.nxt()
                for db in range(8):
                    p_, pk = npw()
                    for k in range(8):
                        em.op("pe", lambda e, p_=p_, k=k, db=db, a_=a_: e.matmul(p_[:, 0:TK], lhsT=wa[:, k, db * 128:(db + 1) * 128], rhs=a_[:, k, :], start=(k == 0), stop=(k == 7)), reads=["wa", ak], writes=[pk])
                    for k in range(8):
                        em.op("pe", lambda e, p_=p_, k=k, db=db, b_=b_: e.matmul(p_[:, TK:2 * TK], lhsT=wb[:, k, db * 128:(db + 1) * 128], rhs=b_[:, k, :], start=(k == 0), stop=(k == 7)), reads=["wb", bk], writes=[pk])
                    t_, tk_ = tg.nxt()
                    em.op("dve", lambda e, t_=t_, p_=p_, ga_=ga_, db=db: e.tensor_tensor(out=t_[:].rearrange("p a b -> p (a b)"), in0=p_[:], in1=ga_[:, db, :, :].rearrange("p a b -> p (a b)"), op=ALU.mult), reads=[pk, gak], writes=[tk_])
                    em.op("pool", lambda e, m_=m_, t_=t_, db=db: e.tensor_tensor(out=m_[:, db, :], in0=t_[:, 0, :], in1=t_[:, 1, :], op=ALU.add), reads=[tk_], writes=[mk_])
                if "MG" in dbg:
                    em.store("sp", lambda e, m_=m_, tk=tk: e.dma_start(out=MG[:, tk * TK:(tk + 1) * TK].rearrange("(h k) t -> k h t", k=128), in_=m_[:]), reads=[mk_], writes=["MGd"])
                x_, xk = xt4.nxt(); h_, hk = h1t.nxt()
                em.dma("sp", lambda e, x_=x_, tk=tk: e.dma_start(out=x_[:], in_=xh[tk * TK:(tk + 1) * TK, :].rearrange("(s p) d -> p s d", p=128)), writes=[xk])
                for s in range(2):
                    for hf in range(2):
                        p_, pk = npw()
                        for k in range(8):
                            em.op("pe", lambda e, p_=p_, k=k, s=s, hf=hf, m_=m_: e.matmul(p_[:], lhsT=m_[:, k, s * 128:(s + 1) * 128], rhs=wo[:, k, hf * 512:(hf + 1) * 512], start=(k == 0), stop=(k == 7)), reads=["wo", mk_], writes=[pk])
                        em.op("dve", lambda e, h_=h_, p_=p_, x_=x_, s=s, hf=hf: e.tensor_tensor(out=h_[:, s, hf * 512:(hf + 1) * 512], in0=p_[:], in1=x_[:, s, hf * 512:(hf + 1) * 512], op=ALU.add), reads=[pk, xk], writes=[hk])
                em.store("sp", lambda e, h_=h_, tk=tk: e.dma_start(out=H1[tk * TK:(tk + 1) * TK, :].rearrange("(s p) d -> p s d", p=128), in_=h_[:]), reads=[hk], writes=["H1d"])
            em.flush()
        print("inst", em.n_inst, "waits", em.n_wait)
        if stop_after <= 4:
            return nc
        if 5 not in skip:
          with ExitStack() as es:
            sbf = lambda n, s, d: es.enter_context(nc.sbuf_tensor(uq(n), list(s), d))
            psf = lambda n, s, d: es.enter_context(nc.psum_tensor(uq(n), list(s), d))
            idf5 = sbf("idf5", [128, 128], F32); idb5 = sbf("idb5", [128, 128], BF16)
            em.dma("sp", lambda e: e.dma_start(out=idf5[:], in_=ident), writes=["idf5"])
            em.op("dve", lambda e: e.tensor_copy(out=idb5[:], in_=idf5[:]), reads=["idf5"], writes=["idb5"])
            for r in range(16):
                em.dma("pool", lambda e, r=r: e.dma_start(out=VBF[r * 1024:(r + 1) * 1024, :], in_=peer_v[r * 1024:(r + 1) * 1024, :]), writes=["VBF"])
            urow = Rot(sbf, "urow", 3, [128, D], BF16)
            uts = Rot(sbf, "uts", 3, [128, 8, 128], BF16)
            pU = [psf(f"pU{i}", [128, 8, 128], BF16) for i in range(2)]
            nj = 128 if 51 not in skip else 2
            for j in range(nj):
                u_, uk = urow.nxt(); t_, tk_ = uts.nxt()
                em.dma("pool", lambda e, u_=u_, j=j: e.dma_start(out=u_[:], in_=peer_u.rearrange("(i j) d -> j i d", j=128)[j]), writes=[uk])
                p_ = pU[j % 2]; pk = f"pU{j % 2}"
                for k in range(8):
                    em.op("pe", lambda e, p_=p_, u_=u_, k=k: e.transpose(out=p_[:, k, :], in_=u_[:, k * 128:(k + 1) * 128], identity=idb5[:]), reads=[uk, "idb5"], writes=[pk])
                em.op("act" if j % 2 else "dve", lambda e, p_=p_, t_=t_, j=j: (e.activation(out=t_[:], in_=p_[:], func=AF.Copy) if j % 2 else e.tensor_copy(out=t_[:], in_=p_[:])), reads=[pk], writes=[tk_])
                em.store("sp", lambda e, t_=t_, j=j: e.dma_start(out=UTS[j], in_=t_[:]), reads=[tk_], writes=["UTS"])
            em.flush()

          with ExitStack() as es:
            sbf = lambda n, s, d: es.enter_context(nc.sbuf_tensor(uq(n), list(s), d))
            psf = lambda n, s, d: es.enter_context(nc.psum_tensor(uq(n), list(s), d))
            wq = sbf("wq", [128, 8, 2048], BF16)
            for k in range(8):
                em.dma("pool", lambda e, k=k: e.dma_start(out=wq[:, k, :], in_=peer_w_q[k * 128:(k + 1) * 128, :]), writes=["wq"])
            idf = sbf("idf", [128, 128], F32)
            em.dma("sp", lambda e: e.dma_start(out=idf[:], in_=ident), writes=["idf"])
            iot = sbf("iot", [128, 128], F32)
            em.dma("sp", lambda e: e.dma_start(out=iot[:], in_=iota), writes=["iot"])
            gff = sbf("gff", [128, D], F32); gfin = sbf("gfin", [128, D], F32)
            em.dma("sp", lambda e: e.dma_start(out=gff[:], in_=norm_ffn_g.partition_broadcast(128)), writes=["gff"])
            em.dma("sp", lambda e: e.dma_start(out=gfin[:], in_=norm_final_g.partition_broadcast(128)), writes=["gfin"])
            skT = sbf("skT", [128, 16, 128], BF16)
            h1 = Rot(sbf, "h1", 1, [128, 2, D], F32)
            ss5 = sbf("ss5", [128, 2], F32); rstd5 = sbf("rstd5", [128, 2], F32)
            xn2 = sbf("xn2", [128, 2, D], F32)
            sqj = xn2[:, 1, :]
            xn2T = sbf("xn2T", [128, 8, TK], BF16)
            qT = sbf("qT", [128, 16, TK], BF16)
            scr = sbf("scr", [128, 16, 128], F32)
            skf = scr
            em.dma("sp", lambda e: e.dma_start(out=skf[:], in_=peer_sk.rearrange("j n c -> n j c")), writes=["scr"])
            scr2 = scr
            vals = sbf("vals", [128, 16, 16], F32)
            idxu = sbf("idxu", [128, 16, 16], U32)
            idxf = sbf("idxf", [128, 16, 16], F32)
            Cg = sbf("Cg", [128, 8, 256], F32); Cg2 = Cg
            cv = sbf("cv", [128, 8, 16], F32)
            posu = sbf("posu", [128, 8, 16], U32); pa_u = sbf("pa_u", [128, 8, 16], U32); pb_u = sbf("pb_u", [128, 8, 16], U32)
            paf = sbf("paf", [128, 8, 16], F32); pbf = sbf("pbf", [128, 8, 16], F32)
            eq = scr[:].rearrange("p a b -> p (a b)").rearrange("p (h k a) -> p h k a", h=8, k=16)
            ik = sbf("ik", [128, 8, 16], F32); jk = sbf("jk", [128, 8, 16], F32)
            ee = sbf("ee", [128, 8, 16], F32); zz = sbf("zz", [128, 8], F32); gg = sbf("gg", [128, 8, 16], F32)
            ikT = sbf("ikT", [128, TK], F32); jkT = sbf("jkT", [128, TK], F32); gT = sbf("gT", [128, TK], F32)
            njkT = sbf("njkT", [128, TK], F32)
            Ra = Rot(sbf, "Ra", 4, [128, 128], F32)
            Lt = Rot(sbf, "Lt", 12, [128, 128], BF16); Rt = Rot(sbf, "Rt", 12, [128, 128], BF16)
            Gs = sbf("Gs", [128, TK, 128], BF16)
            utj = Rot(sbf, "utj", 6, [128, 8, 128], BF16); vj = Rot(sbf, "vj", 6, [128, D], BF16)
            gact = Rot(sbf, "gact", 4, [128, TK], F32)
            ATj = Rot(sbf, "ATj", 4, [128, TK], BF16)

            acc = [psf(f"acc{i}", [128, 512], F32) for i in range(4)]
            pw = [psf(f"pw{i}", [128, 512], F32) for i in range(4)]
            pwi = [0]

            def npw():
                pwi[0] = (pwi[0] + 1) % 4
                return pw[pwi[0]], f"pw{pwi[0]}"

            for j4 in range(4):
                p_, pk = npw()
                for jj in range(4):
                    j = j4 * 4 + jj
                    em.op("pe", lambda e, p_=p_, jj=jj, j=j: e.transpose(out=p_[:, jj * 128:(jj + 1) * 128], in_=skf[:, j, :], identity=idf[:]), reads=["scr", "idf"], writes=[pk])
                em.op("dve", lambda e, p_=p_, j4=j4: e.tensor_copy(out=skT[:, j4 * 4:(j4 + 1) * 4, :].rearrange("p a b -> p (a b)"), in_=p_[:]), reads=[pk], writes=["skT"])

            ntk = NTK if 52 not in skip else 1
            import os
            P5STOP = int(os.environ.get("P5STOP", "99"))
            for tk in range(ntk):
                h_, hk = h1.nxt()
                em.dma("sp", lambda e, h_=h_, tk=tk: e.dma_start(out=h_[:], in_=H1[tk * TK:(tk + 1) * TK, :].rearrange("(s p) d -> p s d", p=128)), writes=[hk])
                for s in range(2):
                    em.op("act", lambda e, h_=h_, s=s: e.activation(out=sqj, in_=h_[:, s, :], func=AF.Square, accum_out=ss5[:, s:s + 1]), reads=[hk], writes=["xn21", "ss5"])
                em.op("act", lambda e: e.activation(out=rstd5[:], in_=ss5[:], func=AF.Ln, scale=1.0 / D, bias=EPS), reads=["ss5"], writes=["rstd5"])
                em.op("act", lambda e: e.activation(out=rstd5[:], in_=rstd5[:], func=AF.Exp, scale=-0.5), reads=["rstd5"], writes=["rstd5"])
                for s in range(2):
                    em.op("dve", lambda e, h_=h_, s=s: e.scalar_tensor_tensor(out=xn2[:, s, :], in0=h_[:, s, :], scalar=rstd5[:, s:s + 1], in1=gff[:], op0=ALU.mult, op1=ALU.mult), reads=[hk, "rstd5", "gff"], writes=[f"xn2{s}"])
                    for k4 in range(2):
                        p_, pk = npw()
                        for kk in range(4):
                            k = k4 * 4 + kk
                            em.op("pe", lambda e, p_=p_, kk=kk, k=k, s=s: e.transpose(out=p_[:, kk * 128:(kk + 1) * 128], in_=xn2[:, s, k * 128:(k + 1) * 128], identity=idf[:]), reads=[f"xn2{s}", "idf"], writes=[pk])
                        em.op("act", lambda e, p_=p_, k4=k4, s=s: e.activation(out=xn2T[:, k4 * 4:(k4 + 1) * 4, s * 128:(s + 1) * 128], in_=p_[:].rearrange("p (a b) -> p a b", a=4), func=AF.Copy), reads=[pk], writes=["xn2T"])
                if P5STOP <= 1:
                    continue
                for j2 in range(8):
                    p_, pk = npw()
                    for jj in range(2):
                        j = j2 * 2 + jj
                        for k in range(8):
                            em.op("pe", lambda e, p_=p_, jj=jj, j=j, k=k: e.matmul(p_[:, jj * TK:(jj + 1) * TK], lhsT=wq[:, k, j * 128:(j + 1) * 128], rhs=xn2T[:, k, :], start=(k == 0), stop=(k == 7)), reads=["wq", "xn2T"], writes=[pk])
                    em.op("dve", lambda e, p_=p_, j2=j2: e.tensor_copy(out=qT[:, j2 * 2:j2 * 2 + 2, :].rearrange("p a b -> p (a b)"), in_=p_[:]), reads=[pk], writes=["qT"])
                if P5STOP <= 2:
                    continue
                for s in range(2):
                    for j4 in range(4):
                        p_, pk = npw()
                        for jj in range(4):
                            j = j4 * 4 + jj
                            em.op("pe", lambda e, p_=p_, jj=jj, j=j, s=s: e.matmul(p_[:, jj * 128:(jj + 1) * 128], lhsT=qT[:, j, s * 128:(s + 1) * 128], rhs=skT[:, j, :], start=True, stop=True), reads=["qT", "skT"], writes=[pk])
                        em.op("act", lambda e, p_=p_, j4=j4: e.activation(out=scr[:, j4 * 4:(j4 + 1) * 4, :].rearrange("p a b -> p (a b)"), in_=p_[:], func=AF.Copy), reads=[pk], writes=["scr"])
                    for j in range(16):
                        em.op("dve", lambda e, j=j: e.max(out=vals[:, j, 0:8], in_=scr[:, j, :]), reads=["scr"], writes=["vals"])
                        em.op("dve", lambda e, j=j: e.max_index(out=idxu[:, j, 0:8], in_max=vals[:, j, 0:8], in_values=scr[:, j, :]), reads=["scr", "vals"], writes=["idxu"])
                        em.op("dve", lambda e, j=j: e.match_replace(out=scr2[:, j, :], in_to_replace=vals[:, j, 0:8], in_values=scr[:, j, :], imm_value=-1e30), reads=["scr", "vals"], writes=["scr"])
                        em.op("dve", lambda e, j=j: e.max(out=vals[:, j, 8:16], in_=scr2[:, j, :]), reads=["scr"], writes=["vals"])
                        em.op("dve", lambda e, j=j: e.max_index(out=idxu[:, j, 8:16], in_max=vals[:, j, 8:16], in_values=scr2[:, j, :]), reads=["scr", "vals"], writes=["idxu"])
                    em.op("dve", lambda e: e.tensor_copy(out=idxf[:], in_=idxu[:]), reads=["idxu"], writes=["idxf"])
                    v4 = vals[:].rearrange("p (h t) a -> p h t a", t=2)
                    i4 = idxf[:].rearrange("p (h t) a -> p h t a", t=2)
                    em.op("dve", lambda e, v4=v4: e.tensor_tensor(out=Cg[:].rearrange("p h (a b) -> p h a b", b=16), in0=v4[:, :, 0, :].unsqueeze(3).to_broadcast([128, 8, 16, 16]), in1=v4[:, :, 1, :].unsqueeze(2).to_broadcast([128, 8, 16, 16]), op=ALU.add), reads=["vals"], writes=["Cg"])
                    for h in range(8):
                        em.op("dve", lambda e, h=h: e.max(out=cv[:, h, 0:8], in_=Cg[:, h, :]), reads=["Cg"], writes=["cv"])
                        em.op("dve", lambda e, h=h: e.max_index(out=posu[:, h, 0:8], in_max=cv[:, h, 0:8], in_values=Cg[:, h, :]), reads=["Cg", "cv"], writes=["posu"])
                        em.op("dve", lambda e, h=h: e.match_replace(out=Cg2[:, h, :], in_to_replace=cv[:, h, 0:8], in_values=Cg[:, h, :], imm_value=-1e30), reads=["Cg", "cv"], writes=["Cg"])
                        em.op("dve", lambda e, h=h: e.max(out=cv[:, h, 8:16], in_=Cg2[:, h, :]), reads=["Cg"], writes=["cv"])
                        em.op("dve", lambda e, h=h: e.max_index(out=posu[:, h, 8:16], in_max=cv[:, h, 8:16], in_values=Cg2[:, h, :]), reads=["Cg", "cv"], writes=["posu"])
                    em.op("dve", lambda e: e.tensor_single_scalar(out=pa_u[:], in_=posu[:], scalar=4, op=ALU.logical_shift_right), reads=["posu"], writes=["pa_u"])
                    em.op("dve", lambda e: e.tensor_single_scalar(out=pb_u[:], in_=posu[:], scalar=15, op=ALU.bitwise_and), reads=["posu"], writes=["pb_u"])
                    em.op("dve", lambda e: e.tensor_copy(out=paf[:], in_=pa_u[:]), reads=["pa_u"], writes=["paf"])
                    em.op("dve", lambda e: e.tensor_copy(out=pbf[:], in_=pb_u[:]), reads=["pb_u"], writes=["pbf"])
                    io16 = iot[:, 0:16].unsqueeze(1).unsqueeze(1).to_broadcast([128, 8, 16, 16])
                    for (pf, pfk, plane, dst, dstk) in ((paf, "paf", 0, ik, "ik"), (pbf, "pbf", 1, jk, "jk")):
                        em.op("dve", lambda e, pf=pf: e.tensor_tensor(out=eq, in0=pf[:].unsqueeze(3).to_broadcast([128, 8, 16, 16]), in1=io16, op=ALU.is_equal), reads=[pfk, "iot"], writes=["scr"])
                        em.op("dve", lambda e, plane=plane, i4=i4: e.tensor_tensor(out=eq, in0=eq, in1=i4[:, :, plane, :].unsqueeze(2).to_broadcast([128, 8, 16, 16]), op=ALU.mult), reads=["scr", "idxf"], writes=["scr"])
                        em.op("dve", lambda e, dst=dst: e.tensor_reduce(out=dst[:], in_=eq, axis=AX.X, op=ALU.add), reads=["scr"], writes=[dstk])
                    em.op("dve", lambda e: e.tensor_tensor(out=ee[:], in0=cv[:], in1=cv[:, :, 0:1].to_broadcast([128, 8, 16]), op=ALU.subtract), reads=["cv"], writes=["ee"])
                    em.op("act", lambda e: e.activation(out=ee[:], in_=ee[:], func=AF.Exp), reads=["ee"], writes=["ee"])
                    em.op("dve", lambda e: e.tensor_reduce(out=zz[:], in_=ee[:], axis=AX.X, op=ALU.add), reads=["ee"], writes=["zz"])
                    em.op("dve", lambda e: e.reciprocal(out=zz[:], in_=zz[:]), reads=["zz"], writes=["zz"])
                    em.op("dve", lambda e: e.tensor_tensor(out=gg[:], in0=ee[:], in1=zz[:].unsqueeze(2).to_broadcast([128, 8, 16]), op=ALU.mult), reads=["ee", "zz"], writes=["gg"])
                    p_, pk = npw()
                    for n_, (src, srck) in enumerate(((ik, "ik"), (jk, "jk"), (gg, "gg"))):
                        em.op("pe", lambda e, p_=p_, n_=n_, src=src: e.transpose(out=p_[:, n_ * 128:(n_ + 1) * 128], in_=src[:].rearrange("p h k -> p (h k)"), identity=idf[:]), reads=[srck, "idf"], writes=[pk])
                    em.op("dve", lambda e, p_=p_, s=s: e.tensor_copy(out=ikT[:, s * 128:(s + 1) * 128], in_=p_[:, 0:128]), reads=[pk], writes=["ikT"])
                    em.op("dve", lambda e, p_=p_, s=s: e.tensor_copy(out=jkT[:, s * 128:(s + 1) * 128], in_=p_[:, 128:256]), reads=[pk], writes=["jkT"])
                    em.op("dve", lambda e, p_=p_, s=s: e.tensor_copy(out=gT[:, s * 128:(s + 1) * 128], in_=p_[:, 256:384]), reads=[pk], writes=["gT"])
                if P5STOP <= 3:
                    continue
                def g_gen(t4):
                    lr = []
                    for tq in range(4):
                        t = t4 * 4 + tq
                        l_, lk = Lt.nxt(); r_, rk = Rt.nxt()
                        em.op("dve", lambda e, l_=l_, t=t: e.tensor_scalar(out=l_[:], in0=iot[:], scalar1=ikT[:, t:t + 1], scalar2=gT[:, t:t + 1], op0=ALU.is_equal, op1=ALU.mult), reads=["iot", "ikT", "gT"], writes=[lk])
                        if t % 4 == 3:
                            em.op("dve", lambda e, r_=r_, t=t: e.tensor_scalar(out=r_[:], in0=iot[:], scalar1=jkT[:, t:t + 1], scalar2=None, op0=ALU.is_equal), reads=["iot", "jkT"], writes=[rk])
                        else:
                            ra_, rak = Ra.nxt()
                            em.op("act", lambda e, ra_=ra_, t=t: e.activation(out=ra_[:], in_=iot[:], func=AF.Abs, bias=njkT[:, t:t + 1]), reads=["iot", "njkT"], writes=[rak])
                            em.op("act", lambda e, ra_=ra_, r_=r_: e.activation(out=r_[:], in_=ra_[:], func=AF.Relu, scale=-1.0, bias=1.0), reads=[rak], writes=[rk])
                        lr.append((l_, lk, r_, rk))
                    yield
                    p_, pk = npw()
                    for tq in range(4):
                        l_, lk, r_, rk = lr[tq]
                        em.op("pe", lambda e, tq=tq, l_=l_, r_=r_: e.matmul(p_[:, tq * 128:(tq + 1) * 128], lhsT=l_[:], rhs=r_[:], start=True, stop=True), reads=[lk, rk], writes=[pk])
                    yield
                    em.op("dve", lambda e: e.tensor_copy(out=Gs[:, t4 * 4:(t4 + 1) * 4, :].rearrange("p a b -> p (a b)"), in_=p_[:]), reads=[pk], writes=["Gs"])
                em.op("dve", lambda e: e.tensor_scalar(out=njkT[:], in0=jkT[:], scalar1=-1.0, scalar2=None, op0=ALU.mult), reads=["jkT"], writes=["njkT"])
                run_pipeline(g_gen(t4) for t4 in range(TK // 4))
                if P5STOP <= 4:
                    continue
                def dense_gen(j):
                    u_, uk = utj.nxt(); v_, vk = vj.nxt()
                    em.dma("sp", lambda e: e.dma_start(out=u_[:], in_=UTS[j]), writes=[uk])
                    em.dma("sp", lambda e: e.dma_start(out=v_[:], in_=VBF.rearrange("(i j) d -> j i d", j=128)[j]), writes=[vk])
                    yield
                    yield
                    p_, pk = npw()
                    for k in range(8):
                        em.op("pe", lambda e, k=k: e.matmul(p_[:, 0:TK], lhsT=u_[:, k, :], rhs=xn2T[:, k, :], start=(k == 0), stop=(k == 7)), reads=[uk, "xn2T"], writes=[pk])
                    yield
                    ga_, gak = gact.nxt(); a_, ak = ATj.nxt()
                    em.op("act", lambda e: e.activation(out=ga_[:], in_=p_[:, 0:TK], func=AF.Gelu_apprx_tanh), reads=[pk], writes=[gak])
                    yield
                    em.op("dve" if j % 2 else "pool", lambda e: e.tensor_tensor(out=a_[:], in0=ga_[:], in1=Gs[:, :, j], op=ALU.mult), reads=[gak, "Gs"], writes=[ak])
                    yield
                    for s in range(2):
                        for hf in range(2):
                            em.op("pe", lambda e, s=s, hf=hf: e.matmul(acc[s * 2 + hf][:], lhsT=a_[:, s * 128:(s + 1) * 128], rhs=v_[:, hf * 512:(hf + 1) * 512], start=(j == 0), stop=(j == 127)), reads=[ak, vk], writes=[f"acc{s * 2 + hf}"])
                run_pipeline(dense_gen(j) for j in range(128))
                if P5STOP <= 5:
                    continue
                for s in range(2):
                    for hf in range(2):
                        em.op("dve", lambda e, s=s, hf=hf, h_=h_: e.tensor_tensor(out=h_[:, s, hf * 512:(hf + 1) * 512], in0=acc[s * 2 + hf][:], in1=h_[:, s, hf * 512:(hf + 1) * 512], op=ALU.add), reads=[f"acc{s * 2 + hf}", hk], writes=[hk])
                for s in range(2):
                    em.op("act", lambda e, s=s, h_=h_: e.activation(out=sqj, in_=h_[:, s, :], func=AF.Square, accum_out=ss5[:, s:s + 1]), reads=[hk], writes=["xn21", "ss5"])
                em.op("act", lambda e: e.activation(out=rstd5[:], in_=ss5[:], func=AF.Ln, scale=1.0 / D, bias=EPS), reads=["ss5"], writes=["rstd5"])
                em.op("act", lambda e: e.activation(out=rstd5[:], in_=rstd5[:], func=AF.Exp, scale=-0.5), reads=["rstd5"], writes=["rstd5"])
                for s in range(2):
                    em.op("dve", lambda e, s=s, h_=h_: e.scalar_tensor_tensor(out=xn2[:, s, :], in0=h_[:, s, :], scalar=rstd5[:, s:s + 1], in1=gfin[:], op0=ALU.mult, op1=ALU.mult), reads=[hk, "rstd5", "gfin"], writes=[f"xn2{s}"])
                em.store("sp", lambda e, tk=tk: e.dma_start(out=out[tk * TK:(tk + 1) * TK, :].rearrange("(s p) d -> p s d", p=128), in_=xn2[:]), reads=["xn20", "xn21"], writes=[f"out{tk}"])
            em.flush()
        print("inst", em.n_inst, "waits", em.n_wait)
    return nc


_IN_NAMES = None


def core_inputs(inp, core):
    b, g = core // 2, core % 2
    m = dict(host_consts())
    xb = inp["x"][b]
    w_in = inp["w_in"][0]
    conv_w = inp["hyena_conv_w"][0]
    w3 = inp["filt_w3"][0]
    dec = inp["filt_decay"].reshape(2048)
    if g == 1:
        xb = xb[::-1]
        w_in = np.concatenate([w_in[:, 0:1024], w_in[:, 2048:3072], w_in[:, 1024:2048], w_in[:, 3072:]], axis=1)
        conv_w = conv_w[::-1]
        w3 = np.concatenate([w3[:, 1024:], w3[:, :1024]], axis=1)
        dec = np.concatenate([dec[1024:], dec[:1024]])
    m["x"] = np.ascontiguousarray(xb)
    m["xh"] = np.ascontiguousarray(xb[:L // 2])
    sel = np.zeros((128, 2), np.float32); sel[:, g] = 1.0
    m["sel"] = sel
    m["norm_mix_g"] = np.ascontiguousarray(inp["norm_mix_g"].reshape(1, D))
    m["w_in"] = np.ascontiguousarray(w_in)
    m["hgrn_lb_logits"] = np.ascontiguousarray(inp["hgrn_lb_logits"])
    m["filt_w1"] = np.ascontiguousarray(inp["filt_w1"][0]); m["filt_w2"] = np.ascontiguousarray(inp["filt_w2"][0])
    m["filt_w3"] = np.ascontiguousarray(w3)
    m["filt_vec"] = np.ascontiguousarray(np.stack([inp["filt_b1"][0], inp["filt_freq1"][0], inp["filt_b2"][0], inp["filt_freq2"][0]], 0))
    m["filt_decay"] = np.ascontiguousarray(dec.reshape(1, 2048))
    m["hyena_bias"] = np.ascontiguousarray(inp["hyena_bias"].reshape(1, 1024))
    m["conv_w"] = np.ascontiguousarray(conv_w); m["conv_b"] = np.ascontiguousarray(inp["hyena_conv_b"].reshape(1, 3072))
    m["hgrn_norm_g"] = np.ascontiguousarray(inp["hgrn_norm_g"].reshape(1, 128))
    for k in ("w_branch_a", "w_branch_b", "w_out", "peer_w_q", "peer_u", "peer_v"):
        m[k] = np.ascontiguousarray(inp[k][0])
    m["norm_ffn_g"] = np.ascontiguousarray(inp["norm_ffn_g"].reshape(1, D)); m["norm_final_g"] = np.ascontiguousarray(inp["norm_final_g"].reshape(1, D))
    m["peer_sk"] = np.ascontiguousarray(inp["peer_subkeys"][0].reshape(16, 128, 128))
    return m


def kernel(**inputs):
    inp = {k: np.asarray(v) for k, v in inputs.items()}
    nc = build()
    in_maps = [core_inputs(inp, c) for c in range(8)]
    res = run_bass_kernel_spmd(nc, in_maps, core_ids=list(range(8)))
    out = np.zeros((4, L, D), np.float32)
    for c in range(8):
        b, g = c // 2, c % 2
        r = np.asarray(res.results[c]["out"])
        if g == 0:
            out[b, :L // 2] = r
        else:
            out[b, L // 2:] = r[::-1]
    return out
```

```python
import math
import numpy as np
from contextlib import ExitStack
import concourse.bass as bass
import concourse.mybir as mybir
from concourse.bass_utils import run_bass_kernel_spmd

F32 = mybir.dt.float32
BF16 = mybir.dt.bfloat16
U32 = mybir.dt.uint32
ALU = mybir.AluOpType
AF = mybir.ActivationFunctionType
AX = mybir.AxisListType

L = 8192
D = 1024
NCOL = 10240
TT = 512
NTILE = L // TT
EPS = 1e-6

ENGS = ("pe", "act", "dve", "pool", "sp")
ENGMAP = {"pe": "tensor", "act": "scalar", "dve": "vector", "pool": "gpsimd", "sp": "sync"}
SEM_LIMIT = 30000
NDMA = 32


class Em:
    def __init__(self, nc, es):
        self.nc = nc
        self.es = es
        self.q = {e: [] for e in ENGS}
        self.sems = {e: [es.enter_context(nc.semaphore(f"s_{e}_0"))] for e in ENGS}
        self.cnt = {e: 0 for e in ENGS}
        self.dsem = [es.enter_context(nc.semaphore(f"d_{i}")) for i in range(NDMA)]
        self.dcnt = [0] * NDMA
        self.dnext = 0
        self.waited = {e: {} for e in ENGS}
        self.lastw = {}
        self.readers = {}
        self.n_inst = 0
        self.n_wait = 0
        self.pending = []

    def _tok_new(self, eng):
        if self.cnt[eng] >= SEM_LIMIT:
            self.sems[eng].append(
                self.es.enter_context(self.nc.semaphore(f"s_{eng}_{len(self.sems[eng])}")))
            self.cnt[eng] = 0
        self.cnt[eng] += 1
        return (self.sems[eng][-1], self.cnt[eng])

    NOKEYS = frozenset(["WIN", "QD0", "QD1", "KD0", "KD1", "KDTM0", "KDTM1", "VT", "OGT", "HYT", "GT", "HF", "UT",
                        "X0T", "YCT", "OT", "AT", "DEC0", "DEC1", "UTS", "VBF", "UBF"])

    def _deps(self, reads, writes):
        reads = [k for k in reads if k not in self.NOKEYS]
        writes = [k for k in writes if k not in self.NOKEYS]
        deps = []
        for k in reads:
            lw = self.lastw.get(k)
            if lw is not None:
                deps.append(lw)
        for k in writes:
            lw = self.lastw.get(k)
            if lw is not None:
                deps.append(lw)
            deps.extend(self.readers.get(k, ()))
        return deps

    def _emit_waits(self, eng, deps, skip_sems=()):
        w = self.waited[eng]
        need = {}
        for (sem, val) in deps:
            sid = id(sem)
            if sid in skip_sems:
                continue
            if w.get(sid, 0) >= val:
                continue
            if sid not in need or need[sid][1] < val:
                need[sid] = (sem, val)
        for sid, (sem, val) in need.items():
            w[sid] = val
            self.q[eng].append(("wait", sem, val))
            self.n_wait += 1

    def _record(self, tok, reads, writes):
        reads = [k for k in reads if k not in self.NOKEYS]
        writes = [k for k in writes if k not in self.NOKEYS]
        for k in reads:
            self.readers.setdefault(k, []).append(tok)
        for k in writes:
            self.lastw[k] = tok
            self.readers[k] = []

    NO_SELF_WAIT = ("pe",)
    STORE_DELAY = 48

    def _pending_tick(self, reads, writes, force=False):
        if not self.pending:
            return
        keep = []
        ws = set(writes)
        rs = set(reads)
        for p in self.pending:
            p[0] -= 1
            if force or p[0] <= 0 or (ws and (ws.intersection(p[3]) or ws.intersection(p[4]))) or (rs and rs.intersection(p[4])):
                self._dma_now(p[1], p[2], p[3], p[4])
            else:
                keep.append(p)
        self.pending = keep

    def store(self, eng, fn, reads=(), writes=()):
        self.pending.append([self.STORE_DELAY, eng, fn, list(reads), list(writes)])

    def op(self, eng, fn, reads=(), writes=()):
        self._pending_tick(reads, writes)
        deps = self._deps(reads, writes)
        skip = tuple(id(s) for s in self.sems[eng]) if eng in self.NO_SELF_WAIT else ()
        self._emit_waits(eng, deps, skip)
        tok = self._tok_new(eng)
        self.q[eng].append(("op", fn, tok, 1))
        self._record(tok, reads, writes)
        self.n_inst += 1
        return tok

    def dma(self, eng, fn, reads=(), writes=()):
        self._pending_tick(reads, writes)
        self._dma_now(eng, fn, reads, writes)

    def _dma_now(self, eng, fn, reads=(), writes=()):
        deps = self._deps(reads, writes)
        slot = self.dnext
        self.dnext = (self.dnext + 1) % NDMA
        if self.dcnt[slot] > 0:
            deps.append((self.dsem[slot], self.dcnt[slot]))
        self._emit_waits(eng, deps)
        self.dcnt[slot] += 16
        tok = (self.dsem[slot], self.dcnt[slot])
        self.q[eng].append(("op", fn, tok, 16))
        self._record(tok, reads, writes)
        self.n_inst += 1
        return tok

    def flush(self):
        nc = self.nc
        self._pending_tick((), (), force=True)
        final = []
        for i in range(NDMA):
            if self.dcnt[i]:
                final.append((self.dsem[i], self.dcnt[i]))
        for e in ENGS:
            if self.cnt[e]:
                final.append((self.sems[e][-1], self.cnt[e]))
        self._emit_waits("sp", final)
        with nc.Block() as block:
            for e in ENGS:
                items = self.q[e]

                def body(engine, items=items):
                    for it in items:
                        if it[0] == "wait":
                            engine.wait_ge(it[1], it[2])
                        else:
                            it[1](engine).then_inc(it[2][0], it[3])
                getattr(block, ENGMAP[e])(body)
        self.q = {e: [] for e in ENGS}
        self.lastw = {}
        self.readers = {}


def run_pipeline(gens, max_new_per_step=1, max_active=16):
    it = iter(gens)
    active = []
    exhausted = False
    while True:
        if not exhausted and len(active) < max_active:
            try:
                active.append(next(it))
            except StopIteration:
                exhausted = True
        if not active:
            if exhausted:
                break
            continue
        nxt = []
        for g in active:
            try:
                next(g)
                nxt.append(g)
            except StopIteration:
                pass
        active = nxt


class Rot:
    def __init__(self, sbf, name, n, shape, dt):
        self.t = [sbf(f"{name}{i}", shape, dt) for i in range(n)]
        self.k = [f"{name}{i}" for i in range(n)]
        self.i = -1

    def nxt(self):
        self.i = (self.i + 1) % len(self.t)
        return self.t[self.i], self.k[self.i]


def host_consts():
    c = {}
    c["ident"] = np.eye(128, dtype=np.float32)
    s = np.arange(64)
    mf = (s[:, None] <= s[None, :]).astype(np.float32)
    mb = (s[:, None] >= s[None, :]).astype(np.float32)
    c["maskf"] = np.ascontiguousarray(np.broadcast_to(mf[:, None, :], (64, 8, 64))).reshape(64, 512)
    c["maskb"] = np.ascontiguousarray(np.broadcast_to(mb[:, None, :], (64, 8, 64))).reshape(64, 512)
    t = np.arange(TT)
    rf = np.ones((128, TT), np.float32); rf[:, t % 64 == 0] = 0
    rb = np.ones((128, TT), np.float32); rb[:, t % 64 == 63] = 0
    c["rmf"] = rf
    c["rmb"] = rb
    n = np.arange(128, dtype=np.float64)
    ang = 2 * np.pi * np.outer(n, n) / 128.0
    Fr = np.cos(ang); Fi = -np.sin(ang)
    c["FRI"] = np.concatenate([Fr, Fi], 1).astype(np.float32)
    c["FRnI"] = np.concatenate([Fr, -Fi], 1).astype(np.float32)
    c["FIR"] = np.concatenate([Fi, Fr], 1).astype(np.float32)
    c["nFI"] = (-Fi).astype(np.float32)
    angt = 2 * np.pi * np.outer(n, n) / 16384.0
    Tr = np.cos(angt); Ti = -np.sin(angt)
    c["TW"] = np.concatenate([Tr, Ti], 1).astype(np.float32)
    c["TWc"] = np.concatenate([Tr, -Ti], 1).astype(np.float32)
    pos = np.arange(L, dtype=np.float32)
    tpos = pos / np.float32(L - 1)
    bands = np.linspace(1e-4, 15, 16, dtype=np.float32)
    angz = (np.float32(2.0 * math.pi / L) * pos[:, None]) * bands[None, :]
    z = np.concatenate([tpos[:, None], np.cos(angz), -np.sin(angz)], -1).astype(np.float32)
    c["zT"] = np.ascontiguousarray(z.T)
    c["tpos"] = np.ascontiguousarray(tpos.reshape(1, L))
    c["iota"] = np.ascontiguousarray(np.broadcast_to(np.arange(128, dtype=np.float32)[None, :], (128, 128)))
    return c


def build(stop_after=99, dbg=(), skip=(), ext_in=()):
    nc = bass.Bass("TRN2", target_bir_lowering=False)
    _uqc = [0]

    def uq(n):
        _uqc[0] += 1
        return f"{n}_u{_uqc[0]}"
    EI = dict(kind="ExternalInput")
    def din(name, shape, dt=F32):
        return nc.dram_tensor(name, list(shape), dt, **EI).ap()
    def dscr(name, shape, dt):
        kind = "ExternalOutput" if name in dbg else ("ExternalInput" if name in ext_in else "Internal")
        return nc.dram_tensor(name, list(shape), dt, kind=kind).ap()

    x = din("x", [L, D])
    xh = din("xh", [L // 2, D])
    norm_mix_g = din("norm_mix_g", [1, D])
    w_in = din("w_in", [D, NCOL])
    lbl = din("hgrn_lb_logits", [2, D])
    ident = din("ident", [128, 128])
    maskf = din("maskf", [64, 512]); maskb = din("maskb", [64, 512])
    rmf = din("rmf", [128, TT]); rmb = din("rmb", [128, TT])
    out = nc.dram_tensor("out", [L // 2, D], F32, kind="ExternalOutput").ap()

    WIN = dscr("WIN", [D, NCOL], BF16)
    QD = [dscr(f"QD{d}", [8, 128, L], BF16) for d in range(2)]
    KD = [dscr(f"KD{d}", [8, 128, L], BF16) for d in range(2)]
    KDTM = [dscr(f"KDTM{d}", [L, D], BF16) for d in range(2)]
    DEC = [dscr(f"DEC{d}", [128, 8, 128], F32) for d in range(2)]
    VT = dscr("VT", [L, D], BF16)
    OGT = dscr("OGT", [D, L], BF16)
    HYT = dscr("HYT", [3 * D, L], F32)
    GT = dscr("GT", [2 * D, L], BF16)
    OF = dscr("OF", [8, 128, L], F32)
    OT = dscr("OT", [8, 128, L], F32)
    filt_w1 = din("filt_w1", [33, 64]); filt_w2 = din("filt_w2", [64, 64]); filt_w3 = din("filt_w3", [64, 2048])
    filt_vec = din("filt_vec", [4, 64])
    filt_decay = din("filt_decay", [1, 2048]); hyena_bias = din("hyena_bias", [1, 1024])
    conv_w = din("conv_w", [3, 3072]); conv_b = din("conv_b", [1, 3072])
    zT = din("zT", [33, L]); tpos = din("tpos", [1, L]); sel = din("sel", [128, 2])
    FRI = din("FRI", [128, 256]); FRnI = din("FRnI", [128, 256]); FIR = din("FIR", [128, 256]); nFI = din("nFI", [128, 128])
    TW = din("TW", [128, 256]); TWc = din("TWc", [128, 256])
    HF = dscr("HF", [2048, L], BF16)
    UT = dscr("UT", [D, L], BF16)
    X0T = dscr("X0T", [D, L], F32)
    KF = dscr("KF", [D, 128, 2, 128], F32)
    YCT = dscr("YCT", [D, L], F32)
    hgrn_norm_g = din("hgrn_norm_g", [1, 128])
    w_branch_a = din("w_branch_a", [D, D]); w_branch_b = din("w_branch_b", [D, D]); w_out = din("w_out", [D, D])
    norm_ffn_g = din("norm_ffn_g", [1, D]); norm_final_g = din("norm_final_g", [1, D])
    peer_w_q = din("peer_w_q", [D, 2048]); peer_sk = din("peer_sk", [16, 128, 128])
    peer_u = din("peer_u", [16384, D]); peer_v = din("peer_v", [16384, D])
    iota = din("iota", [128, 128])
    AT = dscr("AT", [D, L // 2], BF16) if "AT" in dbg else None
    MG = dscr("MG", [D, L // 2], BF16) if "MG" in dbg else None
    PEERO = dscr("PEERO", [L // 2, D], F32) if "PEERO" in dbg else None
    H1 = dscr("H1", [L // 2, D], F32)
    VBF = dscr("VBF", [16384, D], BF16)
    UTS = dscr("UTS", [128, 128, 8, 128], BF16)
    TOK0 = 0
    OTh = OT[:, :, 0:L // 2]; OGTh = OGT[:, 0:L // 2]; X0Th = X0T[:, 0:L // 2]; YCTh = YCT[:, 0:L // 2]; GTh = GT[:, 0:L // 2]

    es0 = ExitStack()
    with es0:
        em = Em(nc, es0)

        for r in range(8):
            em.dma("pool", lambda e, r=r: e.dma_start(out=WIN[r * 128:(r + 1) * 128, :], in_=w_in[r * 128:(r + 1) * 128, :]),
                   writes=["WIN"])
        em.flush()
        if stop_after <= 0:
            return nc

        if 1 not in skip:
          with ExitStack() as es:
              sbf = lambda n, s, d: es.enter_context(nc.sbuf_tensor(uq(n), list(s), d))
              psf = lambda n, s, d: es.enter_context(nc.psum_tensor(uq(n), list(s), d))
              gt = sbf("gt", [128, D], F32)
              idf = sbf("idf", [128, 128], F32)
              idb = sbf("idb", [128, 128], BF16)
              lb2 = sbf("lb2", [128, 2, 8], F32)
              lbt = sbf("lbt", [128, 8], F32)
              olt = sbf("olt", [128, 8], F32)
              nolt = sbf("nolt", [128, 8], F32)
              rmf_t = sbf("rmf_t", [128, TT], F32)
              rmb_t = sbf("rmb_t", [128, TT], F32)
              dec_t = [sbf(f"dec_t{d}", [128, 8, 128], F32) for d in range(2)]
              xt = sbf("xt", [128, 4, D], F32)
              sqj = sbf("sqj", [128, D], F32)
              ss = sbf("ss", [128, 4], F32)
              rstd = sbf("rstd", [128, 4], F32)
              xn = sbf("xn", [128, 4, D], BF16)
              xnT = Rot(sbf, "xnT", 2, [128, 8, TT], BF16)
              wg = Rot(sbf, "wg", 3, [128, 8, 1024], BF16)
              QS = sbf("QS", [128, 8, TT], F32)
              SG = Rot(sbf, "SG", 3, [128, TT], F32)
              KK = Rot(sbf, "KK", 6, [128, TT], F32)
              LF = Rot(sbf, "LF", 3, [128, TT], F32)
              BB = Rot(sbf, "BB", 3, [128, TT], F32)
              E1 = Rot(sbf, "E1", 3, [128, TT], F32)
              E2 = Rot(sbf, "E2", 3, [128, TT], F32)
              QDs = Rot(sbf, "QDs", 4, [128, TT], BF16)
              KDs = Rot(sbf, "KDs", 5, [128, TT], BF16)
              KTs = Rot(sbf, "KTs", 3, [128, 4, 128], BF16)
              OB = Rot(sbf, "OB", 3, [128, TT], BF16)
              OFt = Rot(sbf, "OFt", 3, [128, TT], F32)
              pTs = [psf(f"pT{i}", [128, 8, 128], BF16) for i in range(2)]
              pM = [psf(f"pM{i}", [128, TT], F32) for i in range(4)]
              pK = [psf(f"pK{i}", [128, 4, 128], BF16) for i in range(2)]
              pmi = [0]

              em.dma("sp", lambda e: e.dma_start(out=gt[:], in_=norm_mix_g.partition_broadcast(128)), writes=["gt"])
              em.dma("sp", lambda e: e.dma_start(out=idf[:], in_=ident), writes=["idf"])
              em.dma("sp", lambda e: e.dma_start(out=rmf_t[:], in_=rmf), writes=["rmf"])
              em.dma("sp", lambda e: e.dma_start(out=rmb_t[:], in_=rmb), writes=["rmb"])
              em.dma("sp", lambda e: e.dma_start(out=lb2[:], in_=lbl.rearrange("t (h k) -> k t h", k=128), allow_slow_non_contiguous=True), writes=["lb2"])
              em.op("dve", lambda e: e.tensor_copy(out=idb[:], in_=idf[:]), reads=["idf"], writes=["idb"])
              em.op("dve", lambda e: e.tensor_tensor(out=lbt[:], in0=lb2[:, 1, :], in1=lb2[:, 0, :], op=ALU.subtract), reads=["lb2"], writes=["lbt"])
              em.op("act", lambda e: e.activation(out=lbt[:], in_=lbt[:], func=AF.Exp), reads=["lbt"], writes=["lbt"])
              em.op("dve", lambda e: e.tensor_scalar(out=lbt[:], in0=lbt[:], scalar1=1.0, scalar2=None, op0=ALU.add), reads=["lbt"], writes=["lbt"])
              em.op("dve", lambda e: e.reciprocal(out=lbt[:], in_=lbt[:]), reads=["lbt"], writes=["lbt"])
              em.op("dve", lambda e: e.tensor_scalar(out=olt[:], in0=lbt[:], scalar1=-1.0, scalar2=1.0, op0=ALU.mult, op1=ALU.add), reads=["lbt"], writes=["olt"])
              em.op("dve", lambda e: e.tensor_scalar(out=nolt[:], in0=olt[:], scalar1=-1.0, scalar2=None, op0=ALU.mult), reads=["olt"], writes=["nolt"])

              def next_pm():
                  pmi[0] = (pmi[0] + 1) % 4
                  return pM[pmi[0]], f"pM{pmi[0]}"

              ntile = NTILE if stop_after > 1 else 1
              HALF = NTILE // 2
              wcur = {}

              def prologue_gen(tt):
                  t0 = tt * TT
                  em.dma("sp", lambda e: e.dma_start(out=xt[:], in_=x[t0:t0 + TT, :].rearrange("(s p) d -> p s d", p=128)), writes=["xt"])
                  yield
                  for s in range(4):
                      em.op("act", lambda e, s=s: e.activation(out=sqj[:], in_=xt[:, s, :], func=AF.Square, accum_out=ss[:, s:s + 1]),
                            reads=["xt"], writes=["sqj", "ss"])
                  em.op("act", lambda e: e.activation(out=rstd[:], in_=ss[:], func=AF.Ln, scale=1.0 / D, bias=EPS), reads=["ss"], writes=["rstd"])
                  em.op("act", lambda e: e.activation(out=rstd[:], in_=rstd[:], func=AF.Exp, scale=-0.5), reads=["rstd"], writes=["rstd"])
                  yield
                  xT, xTk = xnT.nxt()
                  wcur[("xT", tt)] = (xT, xTk)
                  for s in range(4):
                      em.op("dve", lambda e, s=s: e.scalar_tensor_tensor(out=xn[:, s, :], in0=xt[:, s, :], scalar=rstd[:, s:s + 1], in1=gt[:], op0=ALU.mult, op1=ALU.mult),
                            reads=["xt", "rstd", "gt"], writes=[f"xn{s}"])
                  yield
                  for s in range(4):
                      pt_ = pTs[s % 2]; ptk = f"pT{s % 2}"
                      for k in range(8):
                          em.op("pe", lambda e, s=s, k=k, pt_=pt_: e.transpose(out=pt_[:, k, :], in_=xn[:, s, k * 128:(k + 1) * 128], identity=idb[:]),
                                reads=[f"xn{s}", "idb"], writes=[ptk])
                      em.op("act" if s % 2 else "dve", lambda e, s=s, pt_=pt_: (e.activation(out=xT[:, :, s * 128:(s + 1) * 128], in_=pt_[:], func=AF.Copy) if s % 2 else e.tensor_copy(out=xT[:, :, s * 128:(s + 1) * 128], in_=pt_[:])),
                            reads=[ptk], writes=[xTk])
                  yield

              def wload_gen(tt, g):
                  w, wk = wg.nxt()
                  wcur[(tt, g)] = (w, wk)
                  em.dma("sp", lambda e: e.dma_start(out=w[:], in_=WIN[:, g * 1024:(g + 1) * 1024].rearrange("(k p) c -> p k c", p=128)), writes=[wk])
                  yield

              def vblock_gen(tt, s, hf):
                  t0 = tt * TT
                  w, wk = wcur[(tt, 3)]; xT, xTk = wcur[("xT", tt)]
                  pm, pmk = next_pm()
                  for k in range(8):
                      em.op("pe", lambda e, k=k: e.matmul(pm[:], lhsT=xT[:, k, s * 128:(s + 1) * 128], rhs=w[:, k, hf * 512:(hf + 1) * 512], start=(k == 0), stop=(k == 7)),
                            reads=[xTk, wk], writes=[pmk])
                  yield
                  ob, obk = OB.nxt()
                  em.op("dve", lambda e: e.tensor_copy(out=ob[:], in_=pm[:]), reads=[pmk], writes=[obk])
                  em.store("sp", lambda e: e.dma_start(out=VT[t0 + s * 128:t0 + (s + 1) * 128, hf * 512:(hf + 1) * 512], in_=ob[:]), reads=[obk], writes=["VT"])

              def block_gen(tt, g, cb):
                  t0 = tt * TT
                  full = tt < HALF
                  w, wk = wcur[(tt, g)]; xT, xTk = wcur[("xT", tt)]
                  pm, pmk = next_pm()
                  for k in range(8):
                      em.op("pe", lambda e, k=k: e.matmul(pm[:], lhsT=w[:, k, cb * 128:(cb + 1) * 128], rhs=xT[:, k, :], start=(k == 0), stop=(k == 7)),
                            reads=[xTk, wk], writes=[pmk])
                  yield
                  if g == 0:
                      em.op("act", lambda e: e.activation(out=QS[:, cb, :], in_=pm[:], func=AF.Silu), reads=[pmk], writes=[f"QS{cb}"])
                  elif g in (1, 2):
                      d = g - 1
                      h = cb
                      sg, sgk = SG.nxt(); kk, kkk = KK.nxt(); lf, lfk = LF.nxt(); bb, bbk = BB.nxt()
                      e1, e1k = E1.nxt(); e2, e2k = E2.nxt(); qd, qdk = QDs.nxt(); kd, kdk = KDs.nxt()
                      kt, ktk = KTs.nxt()
                      em.op("act", lambda e: e.activation(out=sg[:], in_=pm[:], func=AF.Sigmoid), reads=[pmk], writes=[sgk])
                      yield
                      em.op("dve", lambda e: e.tensor_scalar(out=kk[:], in0=sg[:], scalar1=nolt[:, h:h + 1], scalar2=olt[:, h:h + 1], op0=ALU.mult, op1=ALU.add),
                            reads=[sgk, "nolt", "olt"], writes=[kkk])
                      em.op("act", lambda e: e.activation(out=lf[:], in_=sg[:], func=AF.Ln, scale=olt[:, h:h + 1], bias=lbt[:, h:h + 1]),
                            reads=[sgk, "olt", "lbt"], writes=[lfk])
                      yield
                      if d == 0:
                          em.op("dve", lambda e: e.tensor_tensor_scan(out=bb[:], data0=rmf_t[:], data1=lf[:], initial=0.0, op0=ALU.mult, op1=ALU.add),
                                reads=[lfk, "rmf"], writes=[bbk])
                      else:
                          em.op("dve", lambda e: e.tensor_tensor_scan(out=bb[:, ::-1], data0=rmb_t[:, ::-1], data1=lf[:, ::-1], initial=0.0, op0=ALU.mult, op1=ALU.add),
                                reads=[lfk, "rmb"], writes=[bbk])
                      yield
                      em.op("act", lambda e: e.activation(out=e1[:], in_=bb[:], func=AF.Exp), reads=[bbk], writes=[e1k])
                      em.op("act", lambda e: e.activation(out=e2[:], in_=bb[:], func=AF.Exp, scale=-1.0), reads=[bbk], writes=[e2k])
                      yield
                      if full:
                          em.op("pool", lambda e: e.tensor_tensor(out=qd[:], in0=QS[:, h, :], in1=e1[:], op=ALU.mult), reads=[f"QS{h}", e1k], writes=[qdk])
                      em.op("pool", lambda e: e.tensor_tensor(out=kd[:], in0=kk[:], in1=e2[:], op=ALU.mult), reads=[kkk, e2k], writes=[kdk])
                      off = 63 if d == 0 else 0
                      em.op("dve", lambda e: e.tensor_copy(out=dec_t[d][:, h, tt * 8:(tt + 1) * 8], in_=e1[:, off::64]),
                            reads=[e1k], writes=[f"dec{d}"])
                      if full:
                          em.store("sp", lambda e: e.dma_start(out=QD[d][h, :, t0:t0 + TT], in_=qd[:]), reads=[qdk], writes=[f"QD{d}"])
                          em.store("sp", lambda e: e.dma_start(out=KD[d][h, :, t0:t0 + TT], in_=kd[:]), reads=[kdk], writes=[f"KD{d}"])
                      yield
                      pk = pK[h % 2]; pkk = f"pK{h % 2}"
                      for s in range(4):
                          em.op("pe", lambda e, s=s: e.transpose(out=pk[:, s, :], in_=kd[:, s * 128:(s + 1) * 128], identity=idb[:]),
                                reads=[kdk, "idb"], writes=[pkk])
                      yield
                      em.op("dve", lambda e: e.tensor_copy(out=kt[:], in_=pk[:]), reads=[pkk], writes=[ktk])
                      em.store("sp", lambda e: e.dma_start(out=KDTM[d][t0:t0 + TT, h * 128:(h + 1) * 128].rearrange("(s p) k -> p s k", p=128), in_=kt[:]),
                               reads=[ktk], writes=[f"KDTM{d}"])
                  elif g == 4:
                      ob, obk = OB.nxt()
                      em.op("act", lambda e: e.activation(out=ob[:], in_=pm[:], func=AF.Silu), reads=[pmk], writes=[obk])
                      em.store("sp", lambda e: e.dma_start(out=OGT[cb * 128:(cb + 1) * 128, t0:t0 + TT], in_=ob[:]), reads=[obk], writes=["OGT"])
                  elif g in (5, 6, 7):
                      of_, ofk = OFt.nxt()
                      r0 = (g - 5) * 1024 + cb * 128
                      em.op("dve" if cb % 2 else "act", lambda e: (e.tensor_copy(out=of_[:], in_=pm[:]) if cb % 2 else e.activation(out=of_[:], in_=pm[:], func=AF.Copy)), reads=[pmk], writes=[ofk])
                      em.store("sp", lambda e: e.dma_start(out=HYT[r0:r0 + 128, t0:t0 + TT], in_=of_[:]), reads=[ofk], writes=["HYT"])
                  else:
                      ob, obk = OB.nxt()
                      r0 = (g - 8) * 1024 + cb * 128
                      em.op("act", lambda e: e.activation(out=ob[:], in_=pm[:], func=AF.Sigmoid), reads=[pmk], writes=[obk])
                      em.store("sp", lambda e: e.dma_start(out=GT[r0:r0 + 128, t0:t0 + TT], in_=ob[:]), reads=[obk], writes=["GT"])

              def p1_items():
                  yield prologue_gen(0)
                  for tt in range(ntile):
                      groups = list(range(10)) if tt < HALF else [2, 3, 5, 6, 7]
                      yield wload_gen(tt, groups[0])
                      for gi, g in enumerate(groups):
                          if gi + 1 < len(groups):
                              yield wload_gen(tt, groups[gi + 1])
                          if gi == len(groups) // 2 and tt + 1 < ntile:
                              yield prologue_gen(tt + 1)
                          if g == 3:
                              for s in range(4):
                                  for hf in range(2):
                                      yield vblock_gen(tt, s, hf)
                          else:
                              for cb in range(8):
                                  yield block_gen(tt, g, cb)
              run_pipeline(p1_items())
              for d in range(2):
                  em.store("sp", lambda e, d=d: e.dma_start(out=DEC[d], in_=dec_t[d][:]), reads=[f"dec{d}"], writes=[f"DEC{d}"])
              em.flush()
        if stop_after <= 1:
            print("inst", em.n_inst, "waits", em.n_wait)
            return nc

        if 2 not in skip:
          with ExitStack() as es:
              sbf = lambda n, s, d: es.enter_context(nc.sbuf_tensor(uq(n), list(s), d))
              psf = lambda n, s, d: es.enter_context(nc.psum_tensor(uq(n), list(s), d))
              mk = [sbf("mkf", [64, 512], F32), sbf("mkb", [64, 512], F32)]
              dect = [sbf(f"dect{d}", [128, 8, 128], F32) for d in range(2)]
              S = sbf("S", [128, 8, 128], F32)
              Sb = sbf("Sb", [128, 8, 128], BF16)
              tmpS = sbf("tmpS", [128, 8, 128], F32)
              qdt = Rot(sbf, "qdt", 2, [128, 8, TT], BF16)
              kdt = Rot(sbf, "kdt", 2, [128, 8, TT], BF16)
              ktm = Rot(sbf, "ktm", 2, [64, 8, D], BF16)
              vtm = Rot(sbf, "vtm", 2, [64, 8, D], BF16)
              scb = Rot(sbf, "scb", 3, [64, 8, 64], BF16)
              Ot = Rot(sbf, "Ot", 2, [128, 8, TT], F32)
              Of = Rot(sbf, "Of", 3, [128, 8, TT], F32)
              pS = [psf(f"pS{i}", [64, 8, 64], F32) for i in range(2)]
              pO = [psf(f"pO{i}", [128, 8, 64], F32) for i in range(2)]
              pP = [psf(f"pP{i}", [128, 4, 128], F32) for i in range(4)]
              em.dma("sp", lambda e: e.dma_start(out=mk[0][:], in_=maskf), writes=["mk0"])
              em.dma("sp", lambda e: e.dma_start(out=mk[1][:], in_=maskb), writes=["mk1"])
              for d in range(2):
                  em.dma("sp", lambda e, d=d: e.dma_start(out=dect[d][:], in_=DEC[d]), writes=[f"dect{d}"])
              for d in range(2):
                  em.op("pool", lambda e: e.memset(S[:], 0.0), writes=["S"])
                  em.op("pool", lambda e: e.memset(Sb[:], 0.0), writes=["Sb"])
                  HALF = NTILE // 2
                  tiles = list(range(HALF)) if d == 0 else list(range(NTILE - 1, -1, -1))
                  tl = {}

                  def tload_gen(tt, d=d, tl=tl):
                      t0 = tt * TT
                      full = tt < HALF
                      kt_, ktk = ktm.nxt(); v_, vk = vtm.nxt()
                      ent = dict(kt=(kt_, ktk), v=(v_, vk))
                      em.dma("sp", lambda e: e.dma_start(out=kt_[:], in_=KDTM[d][t0:t0 + TT, :].rearrange("(c s) k -> s c k", s=64)), writes=[ktk])
                      em.dma("sp", lambda e: e.dma_start(out=v_[:], in_=VT[t0:t0 + TT, :].rearrange("(c s) k -> s c k", s=64)), writes=[vk])
                      if full:
                          q_, qk = qdt.nxt(); k_, kk_ = kdt.nxt(); o_, ok = Ot.nxt()
                          ent.update(q=(q_, qk), k=(k_, kk_), o=(o_, ok))
                          em.dma("sp", lambda e: e.dma_start(out=q_[:], in_=QD[d][:, :, t0:t0 + TT].rearrange("h k t -> k h t")), writes=[qk])
                          em.dma("sp", lambda e: e.dma_start(out=k_[:], in_=KD[d][:, :, t0:t0 + TT].rearrange("h k t -> k h t")), writes=[kk_])
                          if d == 1:
                              f_, fk = Of.nxt()
                              ent.update(f=(f_, fk))
                              em.dma("sp", lambda e: e.dma_start(out=f_[:], in_=OF[:, :, t0:t0 + TT].rearrange("h k t -> k h t")), reads=[f"OF{tt}"], writes=[fk])
                      tl[tt] = ent
                      yield

                  def chunk_gen(tt, c, last, d=d, tl=tl):
                      t0 = tt * TT
                      full = tt < HALF
                      ent = tl[tt]
                      kt_, ktk = ent["kt"]; v_, vk = ent["v"]
                      gc = tt * 8 + c
                      cs = slice(c * 64, (c + 1) * 64)
                      decb = dect[d][:, :, gc:gc + 1]
                      if full:
                          q_, qk = ent["q"]; k_, kk_ = ent["k"]; o_, ok = ent["o"]
                          ps_ = pS[gc % 2]; psk = f"pS{gc % 2}"
                          po_ = pO[gc % 2]; pok = f"pO{gc % 2}"
                          for h in range(8):
                              em.op("pe", lambda e, h=h: e.matmul(ps_[:, h, :], lhsT=k_[:, h, cs], rhs=q_[:, h, cs], start=True, stop=True),
                                    reads=[kk_, qk], writes=[psk])
                      pps = []
                      for hh in range(2):
                          pp = pP[(gc % 2) * 2 + hh]; ppk = f"pP{(gc % 2) * 2 + hh}"
                          pps.append((pp, ppk))
                          for h4 in range(4):
                              h = hh * 4 + h4
                              hs = slice(h * 128, (h + 1) * 128)
                              em.op("pe", lambda e, pp=pp, h4=h4, hs=hs: e.matmul(pp[:, h4, :], lhsT=kt_[:, c, hs], rhs=v_[:, c, hs], start=True, stop=True),
                                    reads=[ktk, vk], writes=[ppk])
                      yield
                      if full:
                          sb_, sbk = scb.nxt()
                          em.op("dve", lambda e: e.tensor_tensor(out=sb_[:], in0=ps_[:], in1=mk[d][:].rearrange("p (h t) -> p h t", h=8), op=ALU.mult),
                                reads=[psk, f"mk{d}"], writes=[sbk])
                          for h in range(8):
                              hs = slice(h * 128, (h + 1) * 128)
                              em.op("pe", lambda e, h=h, hs=hs: e.matmul(po_[:, h, :], lhsT=v_[:, c, hs], rhs=sb_[:, h, :], start=True, stop=False),
                                    reads=[vk, sbk], writes=[pok])
                              em.op("pe", lambda e, h=h: e.matmul(po_[:, h, :], lhsT=Sb[:, h, :], rhs=q_[:, h, cs], start=False, stop=True),
                                    reads=["Sb", qk], writes=[pok])
                      for hh in range(2):
                          pp, ppk = pps[hh]
                          h4s = slice(hh * 4, hh * 4 + 4)
                          em.op("dve", lambda e, pp=pp, h4s=h4s: e.tensor_tensor(out=S[:, h4s, :], in0=pp[:], in1=S[:, h4s, :], op=ALU.add),
                                reads=[ppk, "S"], writes=["S"])
                      yield
                      em.op("dve", lambda e: e.tensor_tensor(out=S[:], in0=S[:], in1=decb.to_broadcast([128, 8, 128]), op=ALU.mult),
                            reads=["S", f"dect{d}"], writes=["S"])
                      em.op("act", lambda e: e.activation(out=Sb[:], in_=S[:], func=AF.Copy), reads=["S"], writes=["Sb"])
                      if full:
                          if d == 0:
                              em.op("act", lambda e: e.activation(out=o_[:, :, cs], in_=po_[:], func=AF.Copy), reads=[pok], writes=[ok])
                          else:
                              f_, fk = ent["f"]
                              em.op("pool" if False else "dve", lambda e: e.tensor_tensor(out=o_[:, :, cs], in0=po_[:], in1=f_[:, :, cs], op=ALU.add), reads=[pok, fk], writes=[ok])
                          if last:
                              dst = OF if d == 0 else OT
                              em.store("sp", lambda e: e.dma_start(out=dst[:, :, t0:t0 + TT].rearrange("h k t -> k h t"), in_=o_[:]), reads=[ok], writes=[f"OF{tt}" if d == 0 else "OT"])

                  def p2_items(d=d, tiles=tiles):
                      yield tload_gen(tiles[0])
                      for ti, tt in enumerate(tiles):
                          if ti + 1 < len(tiles):
                              yield tload_gen(tiles[ti + 1])
                          chunks = list(range(8)) if d == 0 else list(range(7, -1, -1))
                          for ci, c in enumerate(chunks):
                              yield chunk_gen(tt, c, ci == 7)
                  run_pipeline(p2_items())
              em.flush()
        print("inst", em.n_inst, "waits", em.n_wait)
        if stop_after <= 2:
            return nc
        if 3 not in skip:
          with ExitStack() as es:
            sbf = lambda n, s, d: es.enter_context(nc.sbuf_tensor(uq(n), list(s), d))
            psf = lambda n, s, d: es.enter_context(nc.psum_tensor(uq(n), list(s), d))
            w1t = sbf("w1t", [33, 64], F32); w2t = sbf("w2t", [64, 64], F32); w3t = sbf("w3t", [64, 2048], F32)
            fb = sbf("fb", [64, 4], F32)
            fs = sbf("fs", [64, 4], F32)
            dcy = sbf("dcy", [128, 16], F32)
            hbias = sbf("hbias", [128, 8], F32)
            tpb = sbf("tpb", [128, TT], F32)
            zt = Rot(sbf, "zt", 2, [33, TT], F32)
            ya = Rot(sbf, "ya", 2, [64, TT], F32)
            yb_ = Rot(sbf, "yb_", 2, [64, TT], F32)
            hd1 = Rot(sbf, "hd1", 2, [64, TT], F32)
            hd2 = Rot(sbf, "hd2", 2, [64, TT], F32)
            wn = Rot(sbf, "wn", 2, [128, TT], F32)
            fo = Rot(sbf, "fo", 2, [128, TT], F32)
            fob = Rot(sbf, "fob", 3, [128, TT], BF16)
            pF = [psf(f"pF{i}", [128, TT], F32) for i in range(3)]
            lag0 = sbf("lag0", [128, 16], F32); lagc = sbf("lagc", [128, 16], F32); lagb = sbf("lagb", [128, 16], BF16)
            selt = sbf("selt", [128, 2], F32)
            em.dma("sp", lambda e: e.dma_start(out=selt[:], in_=sel), writes=["selt"])
            em.dma("sp", lambda e: e.dma_start(out=w1t[:], in_=filt_w1), writes=["w1t"])
            em.dma("sp", lambda e: e.dma_start(out=w2t[:], in_=filt_w2), writes=["w2t"])
            em.dma("sp", lambda e: e.dma_start(out=w3t[:], in_=filt_w3), writes=["w3t"])
            em.dma("sp", lambda e: e.dma_start(out=fb[:], in_=filt_vec.rearrange("j k -> k j"), allow_slow_non_contiguous=True), writes=["fb"])
            em.dma("sp", lambda e: e.dma_start(out=dcy[:], in_=filt_decay.rearrange("o (b p) -> p (o b)", p=128), allow_slow_non_contiguous=True), writes=["dcy"])
            em.dma("sp", lambda e: e.dma_start(out=hbias[:], in_=hyena_bias.rearrange("o (b p) -> p (o b)", p=128), allow_slow_non_contiguous=True), writes=["hbias"])
            dcn = sbf("dcn", [128, 16], F32)
            em.op("dve", lambda e: e.tensor_scalar(out=dcn[:], in0=dcy[:], scalar1=-1.0, scalar2=None, op0=ALU.mult), reads=["dcy"], writes=["dcn"])
            em.op("dve", lambda e: e.tensor_tensor(out=dcy[:], in0=dcy[:], in1=dcn[:], op=ALU.min), reads=["dcy", "dcn"], writes=["dcy"])
            I2P = 1.0 / (2.0 * math.pi)
            for j in range(2):
                em.op("dve", lambda e, j=j: e.tensor_scalar(out=fs[:, 2 * j:2 * j + 1], in0=fb[:, 2 * j + 1:2 * j + 2], scalar1=I2P, scalar2=None, op0=ALU.mult), reads=["fb"], writes=["fs"])
                em.op("dve", lambda e, j=j: e.tensor_tensor(out=fs[:, 2 * j + 1:2 * j + 2], in0=fs[:, 2 * j:2 * j + 1], in1=fb[:, 2 * j:2 * j + 1], op=ALU.mult), reads=["fb", "fs"], writes=["fs"])
            MAGIC = 12582912.0

            def sin_layer(pm, pmk, j, hd, hdk):
                a, ak = ya.nxt(); b_, bk = yb_.nxt()
                em.op("dve", lambda e: e.tensor_scalar(out=a[:], in0=pm[0:64, :], scalar1=fs[:, 2 * j:2 * j + 1], scalar2=fs[:, 2 * j + 1:2 * j + 2], op0=ALU.mult, op1=ALU.add), reads=[pmk, "fs"], writes=[ak])
                em.op("dve", lambda e: e.tensor_scalar(out=b_[:], in0=a[:], scalar1=MAGIC, scalar2=None, op0=ALU.add), reads=[ak], writes=[bk])
                em.op("dve", lambda e: e.tensor_scalar(out=b_[:], in0=b_[:], scalar1=MAGIC, scalar2=None, op0=ALU.subtract), reads=[bk], writes=[bk])
                em.op("dve", lambda e: e.tensor_tensor(out=a[:], in0=a[:], in1=b_[:], op=ALU.subtract), reads=[ak, bk], writes=[ak])
                em.op("dve", lambda e: e.tensor_scalar(out=a[:], in0=a[:], scalar1=-0.499999, scalar2=0.499999, op0=ALU.max, op1=ALU.min), reads=[ak], writes=[ak])
                em.op("act", lambda e: e.activation(out=hd[:], in_=a[:], func=AF.Sin, scale=2.0 * math.pi), reads=[ak], writes=[hdk])

            pfi = 0
            for tt in range(NTILE if "3a" not in skip else 0):
                t0 = tt * TT
                z_, zk = zt.nxt()
                em.dma("sp", lambda e, z_=z_, t0=t0: e.dma_start(out=z_[:], in_=zT[:, t0:t0 + TT]), writes=[zk])
                em.dma("sp", lambda e, t0=t0: e.dma_start(out=tpb[:], in_=tpos[:, t0:t0 + TT].partition_broadcast(128)), writes=["tpb"])
                pm = pF[pfi % 3]; pmk = f"pF{pfi % 3}"; pfi += 1
                em.op("pe", lambda e, pm=pm, z_=z_: e.matmul(pm[0:64, :], lhsT=w1t[:], rhs=z_[:], start=True, stop=True), reads=["w1t", zk], writes=[pmk])
                h1_, h1k = hd1.nxt()
                sin_layer(pm, pmk, 0, h1_, h1k)
                pm = pF[pfi % 3]; pmk = f"pF{pfi % 3}"; pfi += 1
                em.op("pe", lambda e, pm=pm, h1_=h1_: e.matmul(pm[0:64, :], lhsT=w2t[:], rhs=h1_[:], start=True, stop=True), reads=["w2t", h1k], writes=[pmk])
                h2_, h2k = hd2.nxt()
                sin_layer(pm, pmk, 1, h2_, h2k)
                for cb in range(16):
                    pm = pF[pfi % 3]; pmk = f"pF{pfi % 3}"; pfi += 1
                    em.op("pe", lambda e, pm=pm, h2_=h2_, cb=cb: e.matmul(pm[:], lhsT=w3t[:, cb * 128:(cb + 1) * 128], rhs=h2_[:], start=True, stop=True), reads=["w3t", h2k], writes=[pmk])
                    w_, wk_ = wn.nxt(); f_, fk_ = fo.nxt(); fb_, fbk = fob.nxt()
                    em.op("act", lambda e, w_=w_, cb=cb: e.activation(out=w_[:], in_=tpb[:], func=AF.Exp, scale=dcy[:, cb:cb + 1]), reads=["tpb", "dcy"], writes=[wk_])
                    em.op("dve", lambda e, f_=f_, pm=pm, w_=w_: e.tensor_tensor(out=f_[:], in0=pm[:], in1=w_[:], op=ALU.mult), reads=[pmk, wk_], writes=[fk_])
                    if tt == 0:
                        em.op("dve", lambda e, f_=f_, cb=cb: e.tensor_copy(out=lag0[:, cb:cb + 1], in_=f_[:, 0:1]), reads=[fk_], writes=["lag0"])
                    em.op("pool", lambda e, fb_=fb_, f_=f_: e.tensor_copy(out=fb_[:], in_=f_[:]), reads=[fk_], writes=[fbk])
                    em.store("sp", lambda e, fb_=fb_, cb=cb, t0=t0: e.dma_start(out=HF[cb * 128:(cb + 1) * 128, t0:t0 + TT], in_=fb_[:]), reads=[fbk], writes=[f"HF0_{cb}" if tt == 0 else "HF"])
                if tt == 0:
                    em.op("dve", lambda e: e.memset(lagc[:], 0.0), writes=["lagc"])
                    em.op("dve", lambda e: e.tensor_scalar(out=lagc[:, 0:8], in0=lag0[:, 0:8], scalar1=selt[:, 0:1], scalar2=None, op0=ALU.mult), reads=["lag0", "selt"], writes=["lagc"])
                    em.op("dve", lambda e: e.scalar_tensor_tensor(out=lagc[:, 0:8], in0=lag0[:, 8:16], scalar=selt[:, 1:2], in1=lagc[:, 0:8], op0=ALU.mult, op1=ALU.add), reads=["lag0", "selt", "lagc"], writes=["lagc"])
                    em.op("dve", lambda e: e.tensor_tensor(out=lagc[:, 0:8], in0=lagc[:, 0:8], in1=hbias[:], op=ALU.add), reads=["lagc", "hbias"], writes=["lagc"])
                    em.op("dve", lambda e: e.tensor_copy(out=lagb[:], in_=lagc[:]), reads=["lagc"], writes=["lagb"])
                    em.dma("sp", lambda e: e.dma_start(out=HF.rearrange("(b p) t -> p b t", p=128)[:, :, 0:1], in_=lagb[:].unsqueeze(2), allow_slow_non_contiguous=True),
                           reads=["lagb"], writes=[f"HF0_{c_}" for c_ in range(16)])
            em.flush()

          with ExitStack() as es:
            sbf = lambda n, s, d: es.enter_context(nc.sbuf_tensor(uq(n), list(s), d))
            PW = 2048
            cw = sbf("cw", [128, 3, 24], F32)
            cbias = sbf("cbias", [128, 24], F32)
            hyin = [Rot(sbf, f"hyin{j}", 2, [128, PW + 2], F32) for j in range(3)]
            cv = [Rot(sbf, f"cv{j}", 2, [128, PW], F32) for j in range(3)]
            ub = Rot(sbf, "ub", 2, [128, PW], BF16)
            em.dma("sp", lambda e: e.dma_start(out=cw[:], in_=conv_w.rearrange("j (b p) -> p j b", p=128), allow_slow_non_contiguous=True), writes=["cw"])
            em.dma("sp", lambda e: e.dma_start(out=cbias[:], in_=conv_b.rearrange("o (b p) -> p (o b)", p=128), allow_slow_non_contiguous=True), writes=["cbias"])
            for cb in range(8 if "3b" not in skip else 0):
                for pc in range(L // PW):
                    t0 = pc * PW
                    outs = []
                    for j in range(3):
                        hy_, hyk = hyin[j].nxt(); c_, ck = cv[j].nxt()
                        blk = j * 8 + cb
                        r0 = blk * 128
                        lo = max(t0 - 1, 0); hi = min(t0 + PW + 1, L)
                        if t0 == 0:
                            em.op("pool", lambda e, hy_=hy_: e.memset(hy_[:, 0:1], 0.0), writes=[hyk])
                        if t0 + PW == L:
                            em.op("pool", lambda e, hy_=hy_: e.memset(hy_[:, PW + 1:PW + 2], 0.0), writes=[hyk])
                        o0 = lo - (t0 - 1)
                        em.dma("sp", lambda e, hy_=hy_, r0=r0, lo=lo, hi=hi, o0=o0: e.dma_start(out=hy_[:, o0:o0 + hi - lo], in_=HYT[r0:r0 + 128, lo:hi]), writes=[hyk])
                        eng = "dve"
                        em.op(eng, lambda e, c_=c_, hy_=hy_, blk=blk: e.tensor_scalar(out=c_[:], in0=hy_[:, 1:PW + 1], scalar1=cw[:, 1, blk:blk + 1], scalar2=cbias[:, blk:blk + 1], op0=ALU.mult, op1=ALU.add), reads=[hyk, "cw", "cbias"], writes=[ck])
                        em.op(eng, lambda e, c_=c_, hy_=hy_, blk=blk: e.scalar_tensor_tensor(out=c_[:], in0=hy_[:, 0:PW], scalar=cw[:, 0, blk:blk + 1], in1=c_[:], op0=ALU.mult, op1=ALU.add), reads=[hyk, "cw", ck], writes=[ck])
                        em.op(eng, lambda e, c_=c_, hy_=hy_, blk=blk: e.scalar_tensor_tensor(out=c_[:], in0=hy_[:, 2:PW + 2], scalar=cw[:, 2, blk:blk + 1], in1=c_[:], op0=ALU.mult, op1=ALU.add), reads=[hyk, "cw", ck], writes=[ck])
                        outs.append((c_, ck))
                    u_, uk = ub.nxt()
                    em.op("pool", lambda e, u_=u_, a=outs[2][0], b=outs[1][0]: e.tensor_tensor(out=u_[:], in0=a[:], in1=b[:], op=ALU.mult), reads=[outs[2][1], outs[1][1]], writes=[uk])
                    em.store("sp", lambda e, u_=u_, cb=cb, t0=t0: e.dma_start(out=UT[cb * 128:(cb + 1) * 128, t0:t0 + PW], in_=u_[:]), reads=[uk], writes=["UT"])
                    em.store("sp", lambda e, a=outs[0][0], cb=cb, t0=t0: e.dma_start(out=X0T[cb * 128:(cb + 1) * 128, t0:t0 + PW], in_=a[:]), reads=[outs[0][1]], writes=["X0T"])
            em.flush()

          with ExitStack() as es:
            sbf = lambda n, s, d: es.enter_context(nc.sbuf_tensor(uq(n), list(s), d))
            psf = lambda n, s, d: es.enter_context(nc.psum_tensor(uq(n), list(s), d))
            cst = sbf("cst", [128, 256], F32)
            FRIb = sbf("FRIb", [128, 256], BF16); FRnIb = sbf("FRnIb", [128, 256], BF16)
            FIRb = sbf("FIRb", [128, 256], BF16); nFIb = sbf("nFIb", [128, 128], BF16)
            TWt = sbf("TWt", [128, 2, 128], F32); TWct = sbf("TWct", [128, 2, 128], F32)
            for nm, src, dst in (("FRI", FRI, FRIb), ("FRnI", FRnI, FRnIb), ("FIR", FIR, FIRb)):
                em.dma("sp", lambda e, src=src: e.dma_start(out=cst[:], in_=src), writes=["cst"])
                em.op("dve", lambda e, dst=dst: e.tensor_copy(out=dst[:], in_=cst[:]), reads=["cst"], writes=[nm])
            em.dma("sp", lambda e: e.dma_start(out=cst[:, 0:128], in_=nFI), writes=["cst"])
            em.op("dve", lambda e: e.tensor_copy(out=nFIb[:], in_=cst[:, 0:128]), reads=["cst"], writes=["nFI"])
            em.dma("sp", lambda e: e.dma_start(out=TWt[:].rearrange("p a b -> p (a b)"), in_=TW), writes=["TW"])
            em.dma("sp", lambda e: e.dma_start(out=TWct[:].rearrange("p a b -> p (a b)"), in_=TWc), writes=["TWc"])
            Min = Rot(sbf, "Min", 8, [64, 2, 128], BF16)
            P1 = Rot(sbf, "P1", 6, [128, 2, 2, 128], F32)
            P2 = Rot(sbf, "P2", 6, [128, 2, 2, 128], F32)
            B2 = Rot(sbf, "B2", 8, [128, 2, 2, 128], BF16)
            Y2 = Rot(sbf, "Y2", 4, [128, 2, 2, 128], BF16)
            D2 = Rot(sbf, "D2", 4, [128, 2, 2, 128], BF16)
            KFs = Rot(sbf, "KFs", 6, [128, 2, 2, 128], F32)
            YO = Rot(sbf, "YO", 3, [64, 2, 128], F32)
            pq = [psf(f"pq{i}", [128, 2, 2, 128], F32) for i in range(8)]
            pqk = [f"pq{i}" for i in range(8)]

            def bc4(t3, ri):
                return t3[:, ri:ri + 1, :].unsqueeze(1).to_broadcast([128, 2, 2, 128])

            def cmul(src, srck, tw, twk, p1, p1k, p2, p2k):
                em.op("dve", lambda e: e.tensor_tensor(out=p1[:], in0=src[:], in1=bc4(tw, 0), op=ALU.mult), reads=[srck, twk], writes=[p1k])
                em.op("dve", lambda e: e.tensor_tensor(out=p2[:], in0=src[:, :, ::-1, :], in1=bc4(tw, 1), op=ALU.mult), reads=[srck, twk], writes=[p2k])

            def flat(ap3):
                return ap3.rearrange("p a b -> p (a b)")

            def st2(o, ok_, b2, b2k, ch, conj, first, last):
                fi = nFIb if conj else FRIb[:, 128:256]
                nfi = FRIb[:, 128:256] if conj else nFIb
                em.op("pe", lambda e: e.matmul(flat(o), lhsT=FRIb[:, 0:128], rhs=flat(b2[:, ch, :, :]), start=first, stop=False), reads=["FRI", b2k], writes=[ok_])
                em.op("pe", lambda e: e.matmul(o[:, 0, :], lhsT=nfi[:] if conj is False else nfi, rhs=b2[:, ch, 1, :], start=False, stop=False), reads=["FRI", "nFI", b2k], writes=[ok_])
                em.op("pe", lambda e: e.matmul(o[:, 1, :], lhsT=fi[:] if conj else fi, rhs=b2[:, ch, 0, :], start=False, stop=last), reads=["FRI", "nFI", b2k], writes=[ok_])

            npair = 512 if 31 not in skip else 4

            def stage1_gen(src_dram, c0, rhs1, rhs1k, tw, twk, pa, pak, res):
                m_, mk_ = Min.nxt()
                em.dma("sp", lambda e: e.dma_start(out=m_[:], in_=src_dram[c0:c0 + 2, :].rearrange("c (a b) -> a c b", b=128)), writes=[mk_])
                yield
                for ch in range(2):
                    em.op("pe", lambda e, ch=ch: e.matmul(flat(pa[:, ch, :, :]), lhsT=m_[:, ch, :], rhs=rhs1[0:64, :], start=True, stop=True), reads=[mk_, rhs1k], writes=[pak])
                yield
                p1, p1k = P1.nxt(); p2, p2k = P2.nxt(); b2, b2k = B2.nxt()
                cmul(pa, pak, tw, twk, p1, p1k, p2, p2k)
                yield
                em.op("pool", lambda e: e.tensor_tensor(out=b2[:, :, 0, :], in0=p1[:, :, 0, :], in1=p2[:, :, 0, :], op=ALU.subtract), reads=[p1k, p2k], writes=[b2k])
                em.op("pool", lambda e: e.tensor_tensor(out=b2[:, :, 1, :], in0=p1[:, :, 1, :], in1=p2[:, :, 1, :], op=ALU.add), reads=[p1k, p2k], writes=[b2k])
                res.append((b2, b2k))
                yield

            def kf_gen(pr):
                c0 = pr * 2
                rf, rb = [], []
                pa0 = pq[(pr % 2) * 2]; pa0k = pqk[(pr % 2) * 2]
                pa1 = pq[(pr % 2) * 2 + 1]; pa1k = pqk[(pr % 2) * 2 + 1]
                g1 = stage1_gen(HF, c0, FRIb, "FRI", TWt, "TW", pa0, pa0k, rf)
                g2 = stage1_gen(HF, 1024 + c0, FRnIb, "FRnI", TWct, "TWc", pa1, pa1k, rb)
                for _ in range(4):
                    next(g1); next(g2)
                    yield
                bf_, bfk = rf[0]; bb_, bbk = rb[0]
                px = pq[4 + pr % 3]; pxk = pqk[4 + pr % 3]
                for ch in range(2):
                    st2(px[:, ch, :, :], pxk, bf_, bfk, ch, False, True, False)
                    st2(px[:, ch, :, :], pxk, bb_, bbk, ch, True, False, True)
                yield
                kf_, kfk = KFs.nxt()
                em.op("act", lambda e: e.activation(out=kf_[:], in_=px[:], func=AF.Copy), reads=[pxk], writes=[kfk])
                em.store("sp", lambda e: e.dma_start(out=KF[c0:c0 + 2].rearrange("c k a b -> k c a b"), in_=kf_[:]), reads=[kfk], writes=[f"KF{pr}"])

            def data_gen(pr):
                c0 = pr * 2
                kf_, kfk = KFs.nxt()
                em.dma("sp", lambda e: e.dma_start(out=kf_[:], in_=KF[c0:c0 + 2].rearrange("c k a b -> k c a b")), reads=[f"KF{pr}"], writes=[kfk])
                rf = []
                pa = pq[pr % 2]; pak = pqk[pr % 2]
                g1 = stage1_gen(UT, c0, FRIb, "FRI", TWt, "TW", pa, pak, rf)
                for _ in range(4):
                    next(g1)
                    yield
                b2, b2k = rf[0]
                px = pq[2 + pr % 2]; pxk = pqk[2 + pr % 2]
                for ch in range(2):
                    st2(px[:, ch, :, :], pxk, b2, b2k, ch, False, True, True)
                yield
                p1, p1k = P1.nxt(); p2, p2k = P2.nxt(); y2, y2k = Y2.nxt()
                em.op("dve", lambda e: e.tensor_tensor(out=p1[:], in0=px[:], in1=kf_[:, :, 0:1, :].to_broadcast([128, 2, 2, 128]), op=ALU.mult), reads=[pxk, kfk], writes=[p1k])
                em.op("dve", lambda e: e.tensor_tensor(out=p2[:], in0=px[:, :, ::-1, :], in1=kf_[:, :, 1:2, :].to_broadcast([128, 2, 2, 128]), op=ALU.mult), reads=[pxk, kfk], writes=[p2k])
                yield
                em.op("pool", lambda e: e.tensor_tensor(out=y2[:, :, 0, :], in0=p1[:, :, 0, :], in1=p2[:, :, 0, :], op=ALU.subtract), reads=[p1k, p2k], writes=[y2k])
                em.op("pool", lambda e: e.tensor_tensor(out=y2[:, :, 1, :], in0=p1[:, :, 1, :], in1=p2[:, :, 1, :], op=ALU.add), reads=[p1k, p2k], writes=[y2k])
                yield
                pc = pq[4 + pr % 2]; pck = pqk[4 + pr % 2]
                for ch in range(2):
                    o = flat(pc[:, ch, :, :])
                    em.op("pe", lambda e, o=o, ch=ch: e.matmul(o, lhsT=y2[:, ch, 0, :], rhs=FRnIb[:], start=True, stop=False), reads=[y2k, "FRnI"], writes=[pck])
                    em.op("pe", lambda e, o=o, ch=ch: e.matmul(o, lhsT=y2[:, ch, 1, :], rhs=FIRb[:], start=False, stop=True), reads=[y2k, "FIR"], writes=[pck])
                yield
                p1b, p1bk = P1.nxt(); p2b, p2bk = P2.nxt(); d2, d2k = D2.nxt()
                cmul(pc, pck, TWct, "TWc", p1b, p1bk, p2b, p2bk)
                yield
                em.op("pool", lambda e: e.tensor_tensor(out=d2[:, 0, :, :], in0=p1b[:, :, 0, :], in1=p2b[:, :, 0, :], op=ALU.subtract), reads=[p1bk, p2bk], writes=[d2k])
                em.op("pool", lambda e: e.tensor_tensor(out=d2[:, 1, :, :], in0=p1b[:, :, 1, :], in1=p2b[:, :, 1, :], op=ALU.add), reads=[p1bk, p2bk], writes=[d2k])
                yield
                py = pq[6 + pr % 2][0:64, 0, :, :]; pyk = pqk[6 + pr % 2]
                em.op("pe", lambda e: e.matmul(flat(py), lhsT=FRIb[:, 0:64], rhs=flat(d2[:, 0, :, :]), start=True, stop=False), reads=["FRI", d2k], writes=[pyk])
                em.op("pe", lambda e: e.matmul(flat(py), lhsT=FRIb[:, 128:192], rhs=flat(d2[:, 1, :, :]), start=False, stop=True), reads=["FRI", d2k], writes=[pyk])
                yield
                yo, yok = YO.nxt()
                em.op("act", lambda e: e.activation(out=yo[:], in_=py, func=AF.Copy, scale=1.0 / 16384.0), reads=[pyk], writes=[yok])
                em.store("sp", lambda e: e.dma_start(out=YCT[c0:c0 + 2, :].rearrange("c (a b) -> a c b", b=128), in_=yo[:]), reads=[yok], writes=["YCT"])

            run_pipeline(kf_gen(pr) for pr in range(npair))
            run_pipeline(data_gen(pr) for pr in range(npair))
            em.flush()
        print("inst", em.n_inst, "waits", em.n_wait)
        if stop_after <= 3:
            return nc
        TK = 256
        NTK = (L // 2) // TK
        if 4 not in skip:
          with ExitStack() as es:
            sbf = lambda n, s, d: es.enter_context(nc.sbuf_tensor(uq(n), list(s), d))
            psf = lambda n, s, d: es.enter_context(nc.psum_tensor(uq(n), list(s), d))
            wa = sbf("wa", [128, 8, D], BF16); wb = sbf("wb", [128, 8, D], BF16); wo = sbf("wo", [128, 8, D], BF16)
            for wt_, src, nm in ((wa, w_branch_a, "wa"), (wb, w_branch_b, "wb"), (wo, w_out, "wo")):
                for k in range(8):
                    em.dma("pool", lambda e, wt_=wt_, src=src, k=k: e.dma_start(out=wt_[:, k, :], in_=src[k * 128:(k + 1) * 128, :]), writes=[nm])
            ones = sbf("ones", [128, 128], F32)
            em.op("dve", lambda e: e.memset(ones[:], 1.0), writes=["ones"])
            gcol = sbf("gcol", [128, 1], F32)
            em.dma("sp", lambda e: e.dma_start(out=gcol[:], in_=hgrn_norm_g.rearrange("o v -> v o"), allow_slow_non_contiguous=True), writes=["gcol"])
            ot = Rot(sbf, "ot", 2, [128, 8, TK], F32)
            ogt = Rot(sbf, "ogt", 2, [128, 8, TK], BF16)
            sq = Rot(sbf, "sq", 2, [128, 8, TK], F32)
            rs = Rot(sbf, "rs", 2, [128, 2, TK], F32)
            tmpA = Rot(sbf, "tmpA", 2, [128, 2, TK], F32)
            At = Rot(sbf, "At", 2, [128, 8, TK], BF16)
            x0t = Rot(sbf, "x0t", 2, [128, 8, TK], F32)
            yct = Rot(sbf, "yct", 2, [128, 8, TK], F32)
            Bt = Rot(sbf, "Bt", 2, [128, 8, TK], BF16)
            gat = Rot(sbf, "gat", 2, [128, 8, 2, TK], BF16)
            tg = Rot(sbf, "tg", 2, [128, 2, TK], F32)
            mg = Rot(sbf, "mg", 2, [128, 8, TK], BF16)
            xt4 = Rot(sbf, "xt4", 2, [128, 2, D], F32)
            h1t = Rot(sbf, "h1t", 2, [128, 2, D], F32)
            pw = [psf(f"pw{i}", [128, 512], F32) for i in range(6)]
            pwi = [0]

            def npw():
                pwi[0] = (pwi[0] + 1) % 6
                return pw[pwi[0]], f"pw{pwi[0]}"

            for tk in range(NTK):
                tg0 = TOK0 + tk * TK
                o_, ok = ot.nxt(); og_, ogk = ogt.nxt(); s_, sk_ = sq.nxt(); a_, ak = At.nxt()
                em.dma("sp", lambda e, o_=o_, tk=tk: e.dma_start(out=o_[:], in_=OTh[:, :, tk * TK:(tk + 1) * TK].rearrange("h k t -> k h t")), writes=[ok])
                em.dma("sp", lambda e, og_=og_, tk=tk: e.dma_start(out=og_[:], in_=OGTh[:, tk * TK:(tk + 1) * TK].rearrange("(h k) t -> k h t", k=128)), writes=[ogk])
                em.op("act", lambda e, s_=s_, o_=o_: e.activation(out=s_[:], in_=o_[:], func=AF.Square), reads=[ok], writes=[sk_])
                for h2 in range(4):
                    p_, pk = npw()
                    for hh in range(2):
                        h = h2 * 2 + hh
                        em.op("pe", lambda e, p_=p_, s_=s_, h=h, hh=hh: e.matmul(p_[:, hh * TK:(hh + 1) * TK], lhsT=ones[:], rhs=s_[:, h, :], start=True, stop=True), reads=["ones", sk_], writes=[pk])
                    r_, rk = rs.nxt(); t_, tk_ = tmpA.nxt()
                    em.op("act", lambda e, r_=r_, p_=p_: e.activation(out=r_[:].rearrange("p a b -> p (a b)"), in_=p_[:], func=AF.Ln, scale=1.0 / 128, bias=EPS), reads=[pk], writes=[rk])
                    em.op("act", lambda e, r_=r_: e.activation(out=r_[:], in_=r_[:], func=AF.Exp, scale=-0.5), reads=[rk], writes=[rk])
                    em.op("dve", lambda e, t_=t_, o_=o_, r_=r_, h2=h2: e.scalar_tensor_tensor(out=t_[:], in0=o_[:, h2 * 2:h2 * 2 + 2, :], scalar=gcol[:, 0:1], in1=r_[:], op0=ALU.mult, op1=ALU.mult), reads=[ok, rk, "gcol"], writes=[tk_])
                    em.op("pool", lambda e, a_=a_, t_=t_, og_=og_, h2=h2: e.tensor_tensor(out=a_[:, h2 * 2:h2 * 2 + 2, :], in0=t_[:], in1=og_[:, h2 * 2:h2 * 2 + 2, :], op=ALU.mult), reads=[tk_, ogk], writes=[ak])
                if "AT" in dbg:
                    em.store("sp", lambda e, a_=a_, tk=tk: e.dma_start(out=AT[:, tk * TK:(tk + 1) * TK].rearrange("(h k) t -> k h t", k=128), in_=a_[:]), reads=[ak], writes=["AT"])
                x0_, x0k = x0t.nxt(); yc_, yck = yct.nxt(); b_, bk = Bt.nxt(); ga_, gak = gat.nxt()
                em.dma("sp", lambda e, x0_=x0_, tk=tk: e.dma_start(out=x0_[:], in_=X0Th[:, tk * TK:(tk + 1) * TK].rearrange("(h k) t -> k h t", k=128)), writes=[x0k])
                em.dma("sp", lambda e, yc_=yc_, tk=tk: e.dma_start(out=yc_[:], in_=YCTh[:, tk * TK:(tk + 1) * TK].rearrange("(h k) t -> k h t", k=128)), writes=[yck])
                for a2 in range(2):
                    em.dma("sp", lambda e, ga_=ga_, tk=tk, a2=a2: e.dma_start(out=ga_[:, :, a2, :], in_=GTh[a2 * 1024:(a2 + 1) * 1024, tk * TK:(tk + 1) * TK].rearrange("(h k) t -> k h t", k=128)), writes=[gak])
                em.op("pool", lambda e, b_=b_, x0_=x0_, yc_=yc_: e.tensor_tensor(out=b_[:], in0=x0_[:], in1=yc_[:], op=ALU.mult), reads=[x0k, yck], writes=[bk])
                m_, mk_ = mg.nxt()
                for db in range(8):
                    p_, pk = npw()
                    for k in range(8):
                        em.op("pe", lambda e, p_=p_, k=k, db=db, a_=a_: e.matmul(p_[:, 0:TK], lhsT=wa[:, k, db * 128:(db + 1) * 128], rhs=a_[:, k, :], start=(k == 0), stop=(k == 7)), reads=["wa", ak], writes=[pk])
                    for k in range(8):
                        em.op("pe", lambda e, p_=p_, k=k, db=db, b_=b_: e.matmul(p_[:, TK:2 * TK], lhsT=wb[:, k, db * 128:(db + 1) * 128], rhs=b_[:, k, :], start=(k == 0), stop=(k == 7)), reads=["wb", bk], writes=[pk])
                    t_, tk_ = tg.nxt()
                    em.op("dve", lambda e, t_=t_, p_=p_, ga_=ga_, db=db: e.tensor_tensor(out=t_[:].rearrange("p a b -> p (a b)"), in0=p_[:], in1=ga_[:, db, :, :].rearrange("p a b -> p (a b)"), op=ALU.mult), reads=[pk, gak], writes=[tk_])
                    em.op("pool", lambda e, m_=m_, t_=t_, db=db: e.tensor_tensor(out=m_[:, db, :], in0=t_[:, 0, :], in1=t_[:, 1, :], op=ALU.add), reads=[tk_], writes=[mk_])
                if "MG" in dbg:
                    em.store("sp", lambda e, m_=m_, tk=tk: e.dma_start(out=MG[:, tk * TK:(tk + 1) * TK].rearrange("(h k) t -> k h t", k=128), in_=m_[:]), reads=[mk_], writes=["MGd"])
                x_, xk = xt4.nxt(); h_, hk = h1t.nxt()
                em.dma("sp", lambda e, x_=x_, tk=tk: e.dma_start(out=x_[:], in_=xh[tk * TK:(tk + 1) * TK, :].rearrange("(s p) d -> p s d", p=128)), writes=[xk])
                for s in range(2):
                    for hf in range(2):
                        p_, pk = npw()
                        for k in range(8):
                            em.op("pe", lambda e, p_=p_, k=k, s=s, hf=hf, m_=m_: e.matmul(p_[:], lhsT=m_[:, k, s * 128:(s + 1) * 128], rhs=wo[:, k, hf * 512:(hf + 1) * 512], start=(k == 0), stop=(k == 7)), reads=["wo", mk_], writes=[pk])
                        em.op("dve", lambda e, h_=h_, p_=p_, x_=x_, s=s, hf=hf: e.tensor_tensor(out=h_[:, s, hf * 512:(hf + 1) * 512], in0=p_[:], in1=x_[:, s, hf * 512:(hf + 1) * 512], op=ALU.add), reads=[pk, xk], writes=[hk])
                em.store("sp", lambda e, h_=h_, tk=tk: e.dma_start(out=H1[tk * TK:(tk + 1) * TK, :].rearrange("(s p) d -> p s d", p=128), in_=h_[:]), reads=[hk], writes=["H1d"])
            em.flush()
        print("inst", em.n_inst, "waits", em.n_wait)
        if stop_after <= 4:
            return nc
        if 5 not in skip:
          with ExitStack() as es:
            sbf = lambda n, s, d: es.enter_context(nc.sbuf_tensor(uq(n), list(s), d))
            psf = lambda n, s, d: es.enter_context(nc.psum_tensor(uq(n), list(s), d))
            idf5 = sbf("idf5", [128, 128], F32); idb5 = sbf("idb5", [128, 128], BF16)
            em.dma("sp", lambda e: e.dma_start(out=idf5[:], in_=ident), writes=["idf5"])
            em.op("dve", lambda e: e.tensor_copy(out=idb5[:], in_=idf5[:]), reads=["idf5"], writes=["idb5"])
            for r in range(16):
                em.dma("pool", lambda e, r=r: e.dma_start(out=VBF[r * 1024:(r + 1) * 1024, :], in_=peer_v[r * 1024:(r + 1) * 1024, :]), writes=["VBF"])
            urow = Rot(sbf, "urow", 3, [128, D], BF16)
            uts = Rot(sbf, "uts", 3, [128, 8, 128], BF16)
            pU = [psf(f"pU{i}", [128, 8, 128], BF16) for i in range(2)]
            nj = 128 if 51 not in skip else 2
            for j in range(nj):
                u_, uk = urow.nxt(); t_, tk_ = uts.nxt()
                em.dma("pool", lambda e, u_=u_, j=j: e.dma_start(out=u_[:], in_=peer_u.rearrange("(i j) d -> j i d", j=128)[j]), writes=[uk])
                p_ = pU[j % 2]; pk = f"pU{j % 2}"
                for k in range(8):
                    em.op("pe", lambda e, p_=p_, u_=u_, k=k: e.transpose(out=p_[:, k, :], in_=u_[:, k * 128:(k + 1) * 128], identity=idb5[:]), reads=[uk, "idb5"], writes=[pk])
                em.op("act" if j % 2 else "dve", lambda e, p_=p_, t_=t_, j=j: (e.activation(out=t_[:], in_=p_[:], func=AF.Copy) if j % 2 else e.tensor_copy(out=t_[:], in_=p_[:])), reads=[pk], writes=[tk_])
                em.store("sp", lambda e, t_=t_, j=j: e.dma_start(out=UTS[j], in_=t_[:]), reads=[tk_], writes=["UTS"])
            em.flush()

          with ExitStack() as es:
            sbf = lambda n, s, d: es.enter_context(nc.sbuf_tensor(uq(n), list(s), d))
            psf = lambda n, s, d: es.enter_context(nc.psum_tensor(uq(n), list(s), d))
            wq = sbf("wq", [128, 8, 2048], BF16)
            for k in range(8):
                em.dma("pool", lambda e, k=k: e.dma_start(out=wq[:, k, :], in_=peer_w_q[k * 128:(k + 1) * 128, :]), writes=["wq"])
            idf = sbf("idf", [128, 128], F32)
            em.dma("sp", lambda e: e.dma_start(out=idf[:], in_=ident), writes=["idf"])
            iot = sbf("iot", [128, 128], F32)
            em.dma("sp", lambda e: e.dma_start(out=iot[:], in_=iota), writes=["iot"])
            gff = sbf("gff", [128, D], F32); gfin = sbf("gfin", [128, D], F32)
            em.dma("sp", lambda e: e.dma_start(out=gff[:], in_=norm_ffn_g.partition_broadcast(128)), writes=["gff"])
            em.dma("sp", lambda e: e.dma_start(out=gfin[:], in_=norm_final_g.partition_broadcast(128)), writes=["gfin"])
            skT = sbf("skT", [128, 16, 128], BF16)
            h1 = Rot(sbf, "h1", 2, [128, 2, D], F32)
            ss5 = sbf("ss5", [128, 2], F32); rstd5 = sbf("rstd5", [128, 2], F32)
            xn2 = sbf("xn2", [128, 2, D], F32)
            sqj = xn2[:, 1, :]
            xn2Ts = Rot(sbf, "xn2T", 2, [128, 8, TK], BF16)
            qT = sbf("qT", [128, 16, TK], BF16)
            scr = sbf("scr", [128, 16, 128], F32)
            skf = scr
            em.dma("sp", lambda e: e.dma_start(out=skf[:], in_=peer_sk.rearrange("j n c -> n j c")), writes=["scr"])
            scr2 = scr
            vals = sbf("vals", [128, 16, 16], F32)
            idxu = sbf("idxu", [128, 16, 16], U32)
            idxf = sbf("idxf", [128, 16, 16], F32)
            Cg = sbf("Cg", [128, 8, 256], F32); Cg2 = Cg
            cv = sbf("cv", [128, 8, 16], F32)
            posu = sbf("posu", [128, 8, 16], U32); pa_u = sbf("pa_u", [128, 8, 16], U32); pb_u = sbf("pb_u", [128, 8, 16], U32)
            paf = sbf("paf", [128, 8, 16], F32); pbf = sbf("pbf", [128, 8, 16], F32)
            eq = scr[:].rearrange("p a b -> p (a b)").rearrange("p (h k a) -> p h k a", h=8, k=16)
            ik = sbf("ik", [128, 8, 16], F32); jk = sbf("jk", [128, 8, 16], F32)
            ee = sbf("ee", [128, 8, 16], F32); zz = sbf("zz", [128, 8], F32); gg = sbf("gg", [128, 8, 16], F32)
            ikT = sbf("ikT", [128, TK], F32); jkT = sbf("jkT", [128, TK], F32); gT = sbf("gT", [128, TK], F32)
            njkT = sbf("njkT", [128, TK], F32)
            Ra = Rot(sbf, "Ra", 3, [128, 128], F32)
            Lt = Rot(sbf, "Lt", 8, [128, 128], BF16); Rt = Rot(sbf, "Rt", 8, [128, 128], BF16)
            Gs = sbf("Gs", [128, TK, 128], BF16)
            utj = Rot(sbf, "utj", 5, [128, 8, 128], BF16); vj = Rot(sbf, "vj", 5, [128, D], BF16)
            gact = Rot(sbf, "gact", 3, [128, TK], F32)
            ATj = Rot(sbf, "ATj", 3, [128, TK], BF16)

            acc = [psf(f"acc{i}", [128, 512], F32) for i in range(4)]
            pw = [psf(f"pw{i}", [128, 512], F32) for i in range(4)]
            pwi = [0]

            def npw():
                pwi[0] = (pwi[0] + 1) % 4
                return pw[pwi[0]], f"pw{pwi[0]}"
            pwd = [0]; pw2 = [0]

            def npw_d():
                pwd[0] = (pwd[0] + 1) % 2
                return pw[pwd[0]], f"pw{pwd[0]}"

            def npw2():
                pw2[0] = (pw2[0] + 1) % 2
                return pw[2 + pw2[0]], f"pw{2 + pw2[0]}"

            for j4 in range(4):
                p_, pk = npw()
                for jj in range(4):
                    j = j4 * 4 + jj
                    em.op("pe", lambda e, p_=p_, jj=jj, j=j: e.transpose(out=p_[:, jj * 128:(jj + 1) * 128], in_=skf[:, j, :], identity=idf[:]), reads=["scr", "idf"], writes=[pk])
                em.op("dve", lambda e, p_=p_, j4=j4: e.tensor_copy(out=skT[:, j4 * 4:(j4 + 1) * 4, :].rearrange("p a b -> p (a b)"), in_=p_[:]), reads=[pk], writes=["skT"])

            ntk = NTK if 52 not in skip else 1
            import os
            P5STOP = int(os.environ.get("P5STOP", "99"))
            def prep_gen(tk, st):
                thunks = []
                E_op = lambda *a_, **k_: thunks.append((em.op, a_, k_))
                E_dma = lambda *a_, **k_: thunks.append((em.dma, a_, k_))
                h_, hk = h1.nxt()
                xT2, xT2k = xn2Ts.nxt()
                st[tk] = (h_, hk, xT2, xT2k)
                E_dma("sp", lambda e, h_=h_, tk=tk: e.dma_start(out=h_[:], in_=H1[tk * TK:(tk + 1) * TK, :].rearrange("(s p) d -> p s d", p=128)), writes=[hk])
                for s in range(2):
                    E_op("act", lambda e, h_=h_, s=s: e.activation(out=sqj, in_=h_[:, s, :], func=AF.Square, accum_out=ss5[:, s:s + 1]), reads=[hk], writes=["xn21", "ss5"])
                E_op("act", lambda e: e.activation(out=rstd5[:], in_=ss5[:], func=AF.Ln, scale=1.0 / D, bias=EPS), reads=["ss5"], writes=["rstd5"])
                E_op("act", lambda e: e.activation(out=rstd5[:], in_=rstd5[:], func=AF.Exp, scale=-0.5), reads=["rstd5"], writes=["rstd5"])
                for s in range(2):
                    E_op("dve", lambda e, h_=h_, s=s: e.scalar_tensor_tensor(out=xn2[:, s, :], in0=h_[:, s, :], scalar=rstd5[:, s:s + 1], in1=gff[:], op0=ALU.mult, op1=ALU.mult), reads=[hk, "rstd5", "gff"], writes=[f"xn2{s}"])
                    for k4 in range(2):
                        p_, pk = npw2()
                        for kk in range(4):
                            k = k4 * 4 + kk
                            E_op("pe", lambda e, p_=p_, kk=kk, k=k, s=s: e.transpose(out=p_[:, kk * 128:(kk + 1) * 128], in_=xn2[:, s, k * 128:(k + 1) * 128], identity=idf[:]), reads=[f"xn2{s}", "idf"], writes=[pk])
                        E_op("act", lambda e, p_=p_, k4=k4, s=s: e.activation(out=xT2[:, k4 * 4:(k4 + 1) * 4, s * 128:(s + 1) * 128], in_=p_[:].rearrange("p (a b) -> p a b", a=4), func=AF.Copy), reads=[pk], writes=[xT2k])
                for j2 in range(8):
                    p_, pk = npw2()
                    for jj in range(2):
                        j = j2 * 2 + jj
                        for k in range(8):
                            E_op("pe", lambda e, p_=p_, jj=jj, j=j, k=k: e.matmul(p_[:, jj * TK:(jj + 1) * TK], lhsT=wq[:, k, j * 128:(j + 1) * 128], rhs=xT2[:, k, :], start=(k == 0), stop=(k == 7)), reads=["wq", xT2k], writes=[pk])
                    E_op("dve", lambda e, p_=p_, j2=j2: e.tensor_copy(out=qT[:, j2 * 2:j2 * 2 + 2, :].rearrange("p a b -> p (a b)"), in_=p_[:]), reads=[pk], writes=["qT"])
                for s in range(2):
                    for j4 in range(4):
                        p_, pk = npw2()
                        for jj in range(4):
                            j = j4 * 4 + jj
                            E_op("pe", lambda e, p_=p_, jj=jj, j=j, s=s: e.matmul(p_[:, jj * 128:(jj + 1) * 128], lhsT=qT[:, j, s * 128:(s + 1) * 128], rhs=skT[:, j, :], start=True, stop=True), reads=["qT", "skT"], writes=[pk])
                        E_op("act", lambda e, p_=p_, j4=j4: e.activation(out=scr[:, j4 * 4:(j4 + 1) * 4, :].rearrange("p a b -> p (a b)"), in_=p_[:], func=AF.Copy), reads=[pk], writes=["scr"])
                    for j in range(16):
                        E_op("dve", lambda e, j=j: e.max(out=vals[:, j, 0:8], in_=scr[:, j, :]), reads=["scr"], writes=["vals"])
                        E_op("dve", lambda e, j=j: e.max_index(out=idxu[:, j, 0:8], in_max=vals[:, j, 0:8], in_values=scr[:, j, :]), reads=["scr", "vals"], writes=["idxu"])
                        E_op("dve", lambda e, j=j: e.match_replace(out=scr2[:, j, :], in_to_replace=vals[:, j, 0:8], in_values=scr[:, j, :], imm_value=-1e30), reads=["scr", "vals"], writes=["scr"])
                        E_op("dve", lambda e, j=j: e.max(out=vals[:, j, 8:16], in_=scr2[:, j, :]), reads=["scr"], writes=["vals"])
                        E_op("dve", lambda e, j=j: e.max_index(out=idxu[:, j, 8:16], in_max=vals[:, j, 8:16], in_values=scr2[:, j, :]), reads=["scr", "vals"], writes=["idxu"])
                    E_op("dve", lambda e: e.tensor_copy(out=idxf[:], in_=idxu[:]), reads=["idxu"], writes=["idxf"])
                    v4 = vals[:].rearrange("p (h t) a -> p h t a", t=2)
                    i4 = idxf[:].rearrange("p (h t) a -> p h t a", t=2)
                    E_op("dve", lambda e, v4=v4: e.tensor_tensor(out=Cg[:].rearrange("p h (a b) -> p h a b", b=16), in0=v4[:, :, 0, :].unsqueeze(3).to_broadcast([128, 8, 16, 16]), in1=v4[:, :, 1, :].unsqueeze(2).to_broadcast([128, 8, 16, 16]), op=ALU.add), reads=["vals"], writes=["Cg"])
                    for h in range(8):
                        E_op("dve", lambda e, h=h: e.max(out=cv[:, h, 0:8], in_=Cg[:, h, :]), reads=["Cg"], writes=["cv"])
                        E_op("dve", lambda e, h=h: e.max_index(out=posu[:, h, 0:8], in_max=cv[:, h, 0:8], in_values=Cg[:, h, :]), reads=["Cg", "cv"], writes=["posu"])
                        E_op("dve", lambda e, h=h: e.match_replace(out=Cg2[:, h, :], in_to_replace=cv[:, h, 0:8], in_values=Cg[:, h, :], imm_value=-1e30), reads=["Cg", "cv"], writes=["Cg"])
                        E_op("dve", lambda e, h=h: e.max(out=cv[:, h, 8:16], in_=Cg2[:, h, :]), reads=["Cg"], writes=["cv"])
                        E_op("dve", lambda e, h=h: e.max_index(out=posu[:, h, 8:16], in_max=cv[:, h, 8:16], in_values=Cg2[:, h, :]), reads=["Cg", "cv"], writes=["posu"])
                    E_op("dve", lambda e: e.tensor_single_scalar(out=pa_u[:], in_=posu[:], scalar=4, op=ALU.logical_shift_right), reads=["posu"], writes=["pa_u"])
                    E_op("dve", lambda e: e.tensor_single_scalar(out=pb_u[:], in_=posu[:], scalar=15, op=ALU.bitwise_and), reads=["posu"], writes=["pb_u"])
                    E_op("dve", lambda e: e.tensor_copy(out=paf[:], in_=pa_u[:]), reads=["pa_u"], writes=["paf"])
                    E_op("dve", lambda e: e.tensor_copy(out=pbf[:], in_=pb_u[:]), reads=["pb_u"], writes=["pbf"])
                    io16 = iot[:, 0:16].unsqueeze(1).unsqueeze(1).to_broadcast([128, 8, 16, 16])
                    for (pf, pfk, plane, dst, dstk) in ((paf, "paf", 0, ik, "ik"), (pbf, "pbf", 1, jk, "jk")):
                        E_op("dve", lambda e, pf=pf: e.tensor_tensor(out=eq, in0=pf[:].unsqueeze(3).to_broadcast([128, 8, 16, 16]), in1=io16, op=ALU.is_equal), reads=[pfk, "iot"], writes=["scr"])
                        E_op("dve", lambda e, plane=plane, i4=i4: e.tensor_tensor(out=eq, in0=eq, in1=i4[:, :, plane, :].unsqueeze(2).to_broadcast([128, 8, 16, 16]), op=ALU.mult), reads=["scr", "idxf"], writes=["scr"])
                        E_op("dve", lambda e, dst=dst: e.tensor_reduce(out=dst[:], in_=eq, axis=AX.X, op=ALU.add), reads=["scr"], writes=[dstk])
                    E_op("dve", lambda e: e.tensor_tensor(out=ee[:], in0=cv[:], in1=cv[:, :, 0:1].to_broadcast([128, 8, 16]), op=ALU.subtract), reads=["cv"], writes=["ee"])
                    E_op("act", lambda e: e.activation(out=ee[:], in_=ee[:], func=AF.Exp), reads=["ee"], writes=["ee"])
                    E_op("dve", lambda e: e.tensor_reduce(out=zz[:], in_=ee[:], axis=AX.X, op=ALU.add), reads=["ee"], writes=["zz"])
                    E_op("dve", lambda e: e.reciprocal(out=zz[:], in_=zz[:]), reads=["zz"], writes=["zz"])
                    E_op("dve", lambda e: e.tensor_tensor(out=gg[:], in0=ee[:], in1=zz[:].unsqueeze(2).to_broadcast([128, 8, 16]), op=ALU.mult), reads=["ee", "zz"], writes=["gg"])
                    p_, pk = npw2()
                    for n_, (src, srck) in enumerate(((ik, "ik"), (jk, "jk"), (gg, "gg"))):
                        E_op("pe", lambda e, p_=p_, n_=n_, src=src: e.transpose(out=p_[:, n_ * 128:(n_ + 1) * 128], in_=src[:].rearrange("p h k -> p (h k)"), identity=idf[:]), reads=[srck, "idf"], writes=[pk])
                    E_op("dve", lambda e, p_=p_, s=s: e.tensor_copy(out=ikT[:, s * 128:(s + 1) * 128], in_=p_[:, 0:128]), reads=[pk], writes=["ikT"])
                    E_op("dve", lambda e, p_=p_, s=s: e.tensor_copy(out=jkT[:, s * 128:(s + 1) * 128], in_=p_[:, 128:256]), reads=[pk], writes=["jkT"])
                    E_op("dve", lambda e, p_=p_, s=s: e.tensor_copy(out=gT[:, s * 128:(s + 1) * 128], in_=p_[:, 256:384]), reads=[pk], writes=["gT"])
                for i_, (f_, a_, k_) in enumerate(thunks):
                    f_(*a_, **k_)
                    if i_ % 5 == 4:
                        yield

            sts = {}
            for _ in prep_gen(0, sts):
                pass
            for tk in range(ntk):
                h_, hk, cur_xT, cur_xTk = sts[tk]
                def g_gen(t4):
                    lr = []
                    for tq in range(4):
                        t = t4 * 4 + tq
                        l_, lk = Lt.nxt(); r_, rk = Rt.nxt()
                        em.op("dve", lambda e, l_=l_, t=t: e.tensor_scalar(out=l_[:], in0=iot[:], scalar1=ikT[:, t:t + 1], scalar2=gT[:, t:t + 1], op0=ALU.is_equal, op1=ALU.mult), reads=["iot", "ikT", "gT"], writes=[lk])
                        if t % 4 == 3:
                            em.op("dve", lambda e, r_=r_, t=t: e.tensor_scalar(out=r_[:], in0=iot[:], scalar1=jkT[:, t:t + 1], scalar2=None, op0=ALU.is_equal), reads=["iot", "jkT"], writes=[rk])
                        else:
                            ra_, rak = Ra.nxt()
                            em.op("act", lambda e, ra_=ra_, t=t: e.activation(out=ra_[:], in_=iot[:], func=AF.Abs, bias=njkT[:, t:t + 1]), reads=["iot", "njkT"], writes=[rak])
                            em.op("act", lambda e, ra_=ra_, r_=r_: e.activation(out=r_[:], in_=ra_[:], func=AF.Relu, scale=-1.0, bias=1.0), reads=[rak], writes=[rk])
                        lr.append((l_, lk, r_, rk))
                    yield
                    p_, pk = npw()
                    for tq in range(4):
                        l_, lk, r_, rk = lr[tq]
                        em.op("pe", lambda e, tq=tq, l_=l_, r_=r_: e.matmul(p_[:, tq * 128:(tq + 1) * 128], lhsT=l_[:], rhs=r_[:], start=True, stop=True), reads=[lk, rk], writes=[pk])
                    yield
                    em.op("dve", lambda e: e.tensor_copy(out=Gs[:, t4 * 4:(t4 + 1) * 4, :].rearrange("p a b -> p (a b)"), in_=p_[:]), reads=[pk], writes=["Gs"])
                em.op("dve", lambda e: e.tensor_scalar(out=njkT[:], in0=jkT[:], scalar1=-1.0, scalar2=None, op0=ALU.mult), reads=["jkT"], writes=["njkT"])
                run_pipeline(g_gen(t4) for t4 in range(TK // 4))
                def dense_gen(j, xT_=cur_xT, xTk_=cur_xTk):
                    u_, uk = utj.nxt(); v_, vk = vj.nxt()
                    em.dma("sp", lambda e: e.dma_start(out=u_[:], in_=UTS[j]), writes=[uk])
                    em.dma("sp", lambda e: e.dma_start(out=v_[:], in_=VBF.rearrange("(i j) d -> j i d", j=128)[j]), writes=[vk])
                    yield
                    yield
                    p_, pk = npw_d()
                    for k in range(8):
                        em.op("pe", lambda e, k=k: e.matmul(p_[:, 0:TK], lhsT=u_[:, k, :], rhs=xT_[:, k, :], start=(k == 0), stop=(k == 7)), reads=[uk, xTk_], writes=[pk])
                    yield
                    ga_, gak = gact.nxt(); a_, ak = ATj.nxt()
                    em.op("act", lambda e: e.activation(out=ga_[:], in_=p_[:, 0:TK], func=AF.Gelu_apprx_tanh), reads=[pk], writes=[gak])
                    yield
                    em.op("dve" if j % 2 else "pool", lambda e: e.tensor_tensor(out=a_[:], in0=ga_[:], in1=Gs[:, :, j], op=ALU.mult), reads=[gak, "Gs"], writes=[ak])
                    yield
                    for s in range(2):
                        for hf in range(2):
                            em.op("pe", lambda e, s=s, hf=hf: e.matmul(acc[s * 2 + hf][:], lhsT=a_[:, s * 128:(s + 1) * 128], rhs=v_[:, hf * 512:(hf + 1) * 512], start=(j == 0), stop=(j == 127)), reads=[ak, vk], writes=[f"acc{s * 2 + hf}"])
                import itertools
                nxt_prep = [prep_gen(tk + 1, sts)] if tk + 1 < ntk else []
                run_pipeline(itertools.chain((dense_gen(j) for j in range(128)), nxt_prep) if os.environ.get("NOOVL") else itertools.chain(nxt_prep, (dense_gen(j) for j in range(128))), max_active=24)
                for s in range(2):
                    for hf in range(2):
                        em.op("dve", lambda e, s=s, hf=hf, h_=h_: e.tensor_tensor(out=h_[:, s, hf * 512:(hf + 1) * 512], in0=acc[s * 2 + hf][:], in1=h_[:, s, hf * 512:(hf + 1) * 512], op=ALU.add), reads=[f"acc{s * 2 + hf}", hk], writes=[hk])
                for s in range(2):
                    em.op("act", lambda e, s=s, h_=h_: e.activation(out=sqj, in_=h_[:, s, :], func=AF.Square, accum_out=ss5[:, s:s + 1]), reads=[hk], writes=["xn21", "ss5"])
                em.op("act", lambda e: e.activation(out=rstd5[:], in_=ss5[:], func=AF.Ln, scale=1.0 / D, bias=EPS), reads=["ss5"], writes=["rstd5"])
                em.op("act", lambda e: e.activation(out=rstd5[:], in_=rstd5[:], func=AF.Exp, scale=-0.5), reads=["rstd5"], writes=["rstd5"])
                for s in range(2):
                    em.op("dve", lambda e, s=s, h_=h_: e.scalar_tensor_tensor(out=xn2[:, s, :], in0=h_[:, s, :], scalar=rstd5[:, s:s + 1], in1=gfin[:], op0=ALU.mult, op1=ALU.mult), reads=[hk, "rstd5", "gfin"], writes=[f"xn2{s}"])
                em.store("sp", lambda e, tk=tk: e.dma_start(out=out[tk * TK:(tk + 1) * TK, :].rearrange("(s p) d -> p s d", p=128), in_=xn2[:]), reads=["xn20", "xn21"], writes=[f"out{tk}"])
            em.flush()
        print("inst", em.n_inst, "waits", em.n_wait)
    return nc


_IN_NAMES = None


def core_inputs(inp, core):
    b, g = core // 2, core % 2
    m = dict(host_consts())
    xb = inp["x"][b]
    w_in = inp["w_in"][0]
    conv_w = inp["hyena_conv_w"][0]
    w3 = inp["filt_w3"][0]
    dec = inp["filt_decay"].reshape(2048)
    if g == 1:
        xb = xb[::-1]
        w_in = np.concatenate([w_in[:, 0:1024], w_in[:, 2048:3072], w_in[:, 1024:2048], w_in[:, 3072:]], axis=1)
        conv_w = conv_w[::-1]
        w3 = np.concatenate([w3[:, 1024:], w3[:, :1024]], axis=1)
        dec = np.concatenate([dec[1024:], dec[:1024]])
    m["x"] = np.ascontiguousarray(xb)
    m["xh"] = np.ascontiguousarray(xb[:L // 2])
    sel = np.zeros((128, 2), np.float32); sel[:, g] = 1.0
    m["sel"] = sel
    m["norm_mix_g"] = np.ascontiguousarray(inp["norm_mix_g"].reshape(1, D))
    m["w_in"] = np.ascontiguousarray(w_in)
    m["hgrn_lb_logits"] = np.ascontiguousarray(inp["hgrn_lb_logits"])
    m["filt_w1"] = np.ascontiguousarray(inp["filt_w1"][0]); m["filt_w2"] = np.ascontiguousarray(inp["filt_w2"][0])
    m["filt_w3"] = np.ascontiguousarray(w3)
    m["filt_vec"] = np.ascontiguousarray(np.stack([inp["filt_b1"][0], inp["filt_freq1"][0], inp["filt_b2"][0], inp["filt_freq2"][0]], 0))
    m["filt_decay"] = np.ascontiguousarray(dec.reshape(1, 2048))
    m["hyena_bias"] = np.ascontiguousarray(inp["hyena_bias"].reshape(1, 1024))
    m["conv_w"] = np.ascontiguousarray(conv_w); m["conv_b"] = np.ascontiguousarray(inp["hyena_conv_b"].reshape(1, 3072))
    m["hgrn_norm_g"] = np.ascontiguousarray(inp["hgrn_norm_g"].reshape(1, 128))
    for k in ("w_branch_a", "w_branch_b", "w_out", "peer_w_q", "peer_u", "peer_v"):
        m[k] = np.ascontiguousarray(inp[k][0])
    m["norm_ffn_g"] = np.ascontiguousarray(inp["norm_ffn_g"].reshape(1, D)); m["norm_final_g"] = np.ascontiguousarray(inp["norm_final_g"].reshape(1, D))
    m["peer_sk"] = np.ascontiguousarray(inp["peer_subkeys"][0].reshape(16, 128, 128))
    return m


def kernel(**inputs):
    inp = {k: np.asarray(v) for k, v in inputs.items()}
    nc = build()
    in_maps = [core_inputs(inp, c) for c in range(8)]
    res = run_bass_kernel_spmd(nc, in_maps, core_ids=list(range(8)))
    out = np.zeros((4, L, D), np.float32)
    for c in range(8):
        b, g = c // 2, c % 2
        r = np.asarray(res.results[c]["out"])
        if g == 0:
            out[b, :L // 2] = r
        else:
            out[b, L // 2:] = r[::-1]
    return out
```

```python
import math
import numpy as np
from contextlib import ExitStack
import concourse.bass as bass
import concourse.mybir as mybir
from concourse.bass_utils import run_bass_kernel_spmd

F32 = mybir.dt.float32
BF16 = mybir.dt.bfloat16
U32 = mybir.dt.uint32
ALU = mybir.AluOpType
AF = mybir.ActivationFunctionType
AX = mybir.AxisListType

L = 8192
D = 1024
NCOL = 10240
TT = 512
NTILE = L // TT
EPS = 1e-6

ENGS = ("pe", "act", "dve", "pool", "sp")
ENGMAP = {"pe": "tensor", "act": "scalar", "dve": "vector", "pool": "gpsimd", "sp": "sync"}
SEM_LIMIT = 30000
NDMA = 32


class Em:
    def __init__(self, nc, es):
        self.nc = nc
        self.es = es
        self.q = {e: [] for e in ENGS}
        self.sems = {e: [es.enter_context(nc.semaphore(f"s_{e}_0"))] for e in ENGS}
        self.cnt = {e: 0 for e in ENGS}
        self.dsem = [es.enter_context(nc.semaphore(f"d_{i}")) for i in range(NDMA)]
        self.dcnt = [0] * NDMA
        self.dnext = 0
        self.waited = {e: {} for e in ENGS}
        self.lastw = {}
        self.readers = {}
        self.n_inst = 0
        self.n_wait = 0
        self.pending = []

    def _tok_new(self, eng):
        if self.cnt[eng] >= SEM_LIMIT:
            self.sems[eng].append(
                self.es.enter_context(self.nc.semaphore(f"s_{eng}_{len(self.sems[eng])}")))
            self.cnt[eng] = 0
        self.cnt[eng] += 1
        return (self.sems[eng][-1], self.cnt[eng])

    NOKEYS = frozenset(["WIN", "QD0", "QD1", "KD0", "KD1", "KDTM0", "KDTM1", "VT", "OGT", "HYT", "GT", "HF", "UT",
                        "X0T", "YCT", "OT", "AT", "DEC0", "DEC1", "UTS", "VBF", "UBF"])

    def _deps(self, reads, writes):
        reads = [k for k in reads if k not in self.NOKEYS]
        writes = [k for k in writes if k not in self.NOKEYS]
        deps = []
        for k in reads:
            lw = self.lastw.get(k)
            if lw is not None:
                deps.append(lw)
        for k in writes:
            lw = self.lastw.get(k)
            if lw is not None:
                deps.append(lw)
            deps.extend(self.readers.get(k, ()))
        return deps

    def _emit_waits(self, eng, deps, skip_sems=()):
        w = self.waited[eng]
        need = {}
        for (sem, val) in deps:
            sid = id(sem)
            if sid in skip_sems:
                continue
            if w.get(sid, 0) >= val:
                continue
            if sid not in need or need[sid][1] < val:
                need[sid] = (sem, val)
        for sid, (sem, val) in need.items():
            w[sid] = val
            self.q[eng].append(("wait", sem, val))
            self.n_wait += 1

    def _record(self, tok, reads, writes):
        reads = [k for k in reads if k not in self.NOKEYS]
        writes = [k for k in writes if k not in self.NOKEYS]
        for k in reads:
            self.readers.setdefault(k, []).append(tok)
        for k in writes:
            self.lastw[k] = tok
            self.readers[k] = []

    NO_SELF_WAIT = ("pe",)
    STORE_DELAY = 48

    def _pending_tick(self, reads, writes, force=False):
        if not self.pending:
            return
        keep = []
        ws = set(writes)
        rs = set(reads)
        for p in self.pending:
            p[0] -= 1
            if force or p[0] <= 0 or (ws and (ws.intersection(p[3]) or ws.intersection(p[4]))) or (rs and rs.intersection(p[4])):
                self._dma_now(p[1], p[2], p[3], p[4])
            else:
                keep.append(p)
        self.pending = keep

    def store(self, eng, fn, reads=(), writes=()):
        self.pending.append([self.STORE_DELAY, eng, fn, list(reads), list(writes)])

    def op(self, eng, fn, reads=(), writes=()):
        self._pending_tick(reads, writes)
        deps = self._deps(reads, writes)
        skip = tuple(id(s) for s in self.sems[eng]) if eng in self.NO_SELF_WAIT else ()
        self._emit_waits(eng, deps, skip)
        tok = self._tok_new(eng)
        self.q[eng].append(("op", fn, tok, 1))
        self._record(tok, reads, writes)
        self.n_inst += 1
        return tok

    def dma(self, eng, fn, reads=(), writes=()):
        self._pending_tick(reads, writes)
        self._dma_now(eng, fn, reads, writes)

    def _dma_now(self, eng, fn, reads=(), writes=()):
        deps = self._deps(reads, writes)
        slot = self.dnext
        self.dnext = (self.dnext + 1) % NDMA
        if self.dcnt[slot] > 0:
            deps.append((self.dsem[slot], self.dcnt[slot]))
        self._emit_waits(eng, deps)
        self.dcnt[slot] += 16
        tok = (self.dsem[slot], self.dcnt[slot])
        self.q[eng].append(("op", fn, tok, 16))
        self._record(tok, reads, writes)
        self.n_inst += 1
        return tok

    def flush(self):
        nc = self.nc
        self._pending_tick((), (), force=True)
        final = []
        for i in range(NDMA):
            if self.dcnt[i]:
                final.append((self.dsem[i], self.dcnt[i]))
        for e in ENGS:
            if self.cnt[e]:
                final.append((self.sems[e][-1], self.cnt[e]))
        self._emit_waits("sp", final)
        with nc.Block() as block:
            for e in ENGS:
                items = self.q[e]

                def body(engine, items=items):
                    for it in items:
                        if it[0] == "wait":
                            engine.wait_ge(it[1], it[2])
                        else:
                            it[1](engine).then_inc(it[2][0], it[3])
                getattr(block, ENGMAP[e])(body)
        self.q = {e: [] for e in ENGS}
        self.lastw = {}
        self.readers = {}


def run_pipeline(gens, max_new_per_step=1, max_active=16):
    it = iter(gens)
    active = []
    exhausted = False
    while True:
        if not exhausted and len(active) < max_active:
            try:
                active.append(next(it))
            except StopIteration:
                exhausted = True
        if not active:
            if exhausted:
                break
            continue
        nxt = []
        for g in active:
            try:
                next(g)
                nxt.append(g)
            except StopIteration:
                pass
        active = nxt


class Rot:
    def __init__(self, sbf, name, n, shape, dt):
        self.t = [sbf(f"{name}{i}", shape, dt) for i in range(n)]
        self.k = [f"{name}{i}" for i in range(n)]
        self.i = -1

    def nxt(self):
        self.i = (self.i + 1) % len(self.t)
        return self.t[self.i], self.k[self.i]


def host_consts():
    c = {}
    c["ident"] = np.eye(128, dtype=np.float32)
    s = np.arange(64)
    mf = (s[:, None] <= s[None, :]).astype(np.float32)
    mb = (s[:, None] >= s[None, :]).astype(np.float32)
    c["maskf"] = np.ascontiguousarray(np.broadcast_to(mf[:, None, :], (64, 8, 64))).reshape(64, 512)
    c["maskb"] = np.ascontiguousarray(np.broadcast_to(mb[:, None, :], (64, 8, 64))).reshape(64, 512)
    t = np.arange(TT)
    rf = np.ones((128, TT), np.float32); rf[:, t % 64 == 0] = 0
    rb = np.ones((128, TT), np.float32); rb[:, t % 64 == 63] = 0
    c["rmf"] = rf
    c["rmb"] = rb
    n = np.arange(128, dtype=np.float64)
    ang = 2 * np.pi * np.outer(n, n) / 128.0
    Fr = np.cos(ang); Fi = -np.sin(ang)
    c["FRI"] = np.concatenate([Fr, Fi], 1).astype(np.float32)
    c["FRnI"] = np.concatenate([Fr, -Fi], 1).astype(np.float32)
    c["FIR"] = np.concatenate([Fi, Fr], 1).astype(np.float32)
    c["nFI"] = (-Fi).astype(np.float32)
    angt = 2 * np.pi * np.outer(n, n) / 16384.0
    Tr = np.cos(angt); Ti = -np.sin(angt)
    c["TW"] = np.concatenate([Tr, Ti], 1).astype(np.float32)
    c["TWc"] = np.concatenate([Tr, -Ti], 1).astype(np.float32)
    pos = np.arange(L, dtype=np.float32)
    tpos = pos / np.float32(L - 1)
    bands = np.linspace(1e-4, 15, 16, dtype=np.float32)
    angz = (np.float32(2.0 * math.pi / L) * pos[:, None]) * bands[None, :]
    z = np.concatenate([tpos[:, None], np.cos(angz), -np.sin(angz)], -1).astype(np.float32)
    c["zT"] = np.ascontiguousarray(z.T)
    c["tpos"] = np.ascontiguousarray(tpos.reshape(1, L))
    c["iota"] = np.ascontiguousarray(np.broadcast_to(np.arange(128, dtype=np.float32)[None, :], (128, 128)))
    return c


def build(stop_after=99, dbg=(), skip=(), ext_in=()):
    nc = bass.Bass("TRN2", target_bir_lowering=False)
    _uqc = [0]

    def uq(n):
        _uqc[0] += 1
        return f"{n}_u{_uqc[0]}"
    EI = dict(kind="ExternalInput")
    def din(name, shape, dt=F32):
        return nc.dram_tensor(name, list(shape), dt, **EI).ap()
    def dscr(name, shape, dt):
        kind = "ExternalOutput" if name in dbg else ("ExternalInput" if name in ext_in else "Internal")
        return nc.dram_tensor(name, list(shape), dt, kind=kind).ap()

    x = din("x", [L, D])
    xh = din("xh", [L // 2, D])
    norm_mix_g = din("norm_mix_g", [1, D])
    w_in = din("w_in", [D, NCOL])
    lbl = din("hgrn_lb_logits", [2, D])
    ident = din("ident", [128, 128])
    maskf = din("maskf", [64, 512]); maskb = din("maskb", [64, 512])
    rmf = din("rmf", [128, TT]); rmb = din("rmb", [128, TT])
    out = nc.dram_tensor("out", [L // 2, D], F32, kind="ExternalOutput").ap()

    WIN = dscr("WIN", [D, NCOL], BF16)
    QD = [dscr(f"QD{d}", [8, 128, L], BF16) for d in range(2)]
    KD = [dscr(f"KD{d}", [8, 128, L], BF16) for d in range(2)]
    KDTM = [dscr(f"KDTM{d}", [L, D], BF16) for d in range(2)]
    DEC = [dscr(f"DEC{d}", [128, 8, 128], F32) for d in range(2)]
    VT = dscr("VT", [L, D], BF16)
    OGT = dscr("OGT", [D, L], BF16)
    HYT = dscr("HYT", [3 * D, L], F32)
    GT = dscr("GT", [2 * D, L], BF16)
    OF = dscr("OF", [8, 128, L], F32)
    OT = dscr("OT", [8, 128, L], F32)
    filt_w1 = din("filt_w1", [33, 64]); filt_w2 = din("filt_w2", [64, 64]); filt_w3 = din("filt_w3", [64, 2048])
    filt_vec = din("filt_vec", [4, 64])
    filt_decay = din("filt_decay", [1, 2048]); hyena_bias = din("hyena_bias", [1, 1024])
    conv_w = din("conv_w", [3, 3072]); conv_b = din("conv_b", [1, 3072])
    zT = din("zT", [33, L]); tpos = din("tpos", [1, L]); sel = din("sel", [128, 2])
    FRI = din("FRI", [128, 256]); FRnI = din("FRnI", [128, 256]); FIR = din("FIR", [128, 256]); nFI = din("nFI", [128, 128])
    TW = din("TW", [128, 256]); TWc = din("TWc", [128, 256])
    HF = dscr("HF", [2048, L], BF16)
    UT = dscr("UT", [D, L], BF16)
    X0T = dscr("X0T", [D, L], F32)
    KF = dscr("KF", [D, 128, 2, 128], F32)
    YCT = dscr("YCT", [D, L], F32)
    hgrn_norm_g = din("hgrn_norm_g", [1, 128])
    w_branch_a = din("w_branch_a", [D, D]); w_branch_b = din("w_branch_b", [D, D]); w_out = din("w_out", [D, D])
    norm_ffn_g = din("norm_ffn_g", [1, D]); norm_final_g = din("norm_final_g", [1, D])
    peer_w_q = din("peer_w_q", [D, 2048]); peer_sk = din("peer_sk", [16, 128, 128])
    peer_u = din("peer_u", [16384, D]); peer_v = din("peer_v", [16384, D])
    iota = din("iota", [128, 128])
    AT = dscr("AT", [D, L // 2], BF16) if "AT" in dbg else None
    MG = dscr("MG", [D, L // 2], BF16) if "MG" in dbg else None
    PEERO = dscr("PEERO", [L // 2, D], F32) if "PEERO" in dbg else None
    H1 = dscr("H1", [L // 2, D], F32)
    VBF = dscr("VBF", [16384, D], BF16)
    UTS = dscr("UTS", [128, 128, 8, 128], BF16)
    TOK0 = 0
    OTh = OT[:, :, 0:L // 2]; OGTh = OGT[:, 0:L // 2]; X0Th = X0T[:, 0:L // 2]; YCTh = YCT[:, 0:L // 2]; GTh = GT[:, 0:L // 2]

    es0 = ExitStack()
    with es0:
        em = Em(nc, es0)

        for r in range(8):
            em.dma("pool", lambda e, r=r: e.dma_start(out=WIN[r * 128:(r + 1) * 128, :], in_=w_in[r * 128:(r + 1) * 128, :]),
                   writes=["WIN"])
        em.flush()
        if stop_after <= 0:
            return nc

        if 1 not in skip:
          with ExitStack() as es:
              sbf = lambda n, s, d: es.enter_context(nc.sbuf_tensor(uq(n), list(s), d))
              psf = lambda n, s, d: es.enter_context(nc.psum_tensor(uq(n), list(s), d))
              gt = sbf("gt", [128, D], F32)
              idf = sbf("idf", [128, 128], F32)
              idb = sbf("idb", [128, 128], BF16)
              lb2 = sbf("lb2", [128, 2, 8], F32)
              lbt = sbf("lbt", [128, 8], F32)
              olt = sbf("olt", [128, 8], F32)
              nolt = sbf("nolt", [128, 8], F32)
              rmf_t = sbf("rmf_t", [128, TT], F32)
              rmb_t = sbf("rmb_t", [128, TT], F32)
              dec_t = [sbf(f"dec_t{d}", [128, 8, 128], F32) for d in range(2)]
              xt = sbf("xt", [128, 4, D], F32)
              sqj = sbf("sqj", [128, D], F32)
              ss = sbf("ss", [128, 4], F32)
              rstd = sbf("rstd", [128, 4], F32)
              xn = sbf("xn", [128, 4, D], BF16)
              xnT = Rot(sbf, "xnT", 2, [128, 8, TT], BF16)
              wg = Rot(sbf, "wg", 3, [128, 8, 1024], BF16)
              QS = sbf("QS", [128, 8, TT], F32)
              SG = Rot(sbf, "SG", 3, [128, TT], F32)
              KK = Rot(sbf, "KK", 6, [128, TT], F32)
              LF = Rot(sbf, "LF", 3, [128, TT], F32)
              BB = Rot(sbf, "BB", 3, [128, TT], F32)
              E1 = Rot(sbf, "E1", 3, [128, TT], F32)
              E2 = Rot(sbf, "E2", 3, [128, TT], F32)
              QDs = Rot(sbf, "QDs", 4, [128, TT], BF16)
              KDs = Rot(sbf, "KDs", 5, [128, TT], BF16)
              KTs = Rot(sbf, "KTs", 3, [128, 4, 128], BF16)
              OB = Rot(sbf, "OB", 3, [128, TT], BF16)
              OFt = Rot(sbf, "OFt", 3, [128, TT], F32)
              pTs = [psf(f"pT{i}", [128, 8, 128], BF16) for i in range(2)]
              pM = [psf(f"pM{i}", [128, TT], F32) for i in range(4)]
              pK = [psf(f"pK{i}", [128, 4, 128], BF16) for i in range(2)]
              pmi = [0]

              em.dma("sp", lambda e: e.dma_start(out=gt[:], in_=norm_mix_g.partition_broadcast(128)), writes=["gt"])
              em.dma("sp", lambda e: e.dma_start(out=idf[:], in_=ident), writes=["idf"])
              em.dma("sp", lambda e: e.dma_start(out=rmf_t[:], in_=rmf), writes=["rmf"])
              em.dma("sp", lambda e: e.dma_start(out=rmb_t[:], in_=rmb), writes=["rmb"])
              em.dma("sp", lambda e: e.dma_start(out=lb2[:], in_=lbl.rearrange("t (h k) -> k t h", k=128), allow_slow_non_contiguous=True), writes=["lb2"])
              em.op("dve", lambda e: e.tensor_copy(out=idb[:], in_=idf[:]), reads=["idf"], writes=["idb"])
              em.op("dve", lambda e: e.tensor_tensor(out=lbt[:], in0=lb2[:, 1, :], in1=lb2[:, 0, :], op=ALU.subtract), reads=["lb2"], writes=["lbt"])
              em.op("act", lambda e: e.activation(out=lbt[:], in_=lbt[:], func=AF.Exp), reads=["lbt"], writes=["lbt"])
              em.op("dve", lambda e: e.tensor_scalar(out=lbt[:], in0=lbt[:], scalar1=1.0, scalar2=None, op0=ALU.add), reads=["lbt"], writes=["lbt"])
              em.op("dve", lambda e: e.reciprocal(out=lbt[:], in_=lbt[:]), reads=["lbt"], writes=["lbt"])
              em.op("dve", lambda e: e.tensor_scalar(out=olt[:], in0=lbt[:], scalar1=-1.0, scalar2=1.0, op0=ALU.mult, op1=ALU.add), reads=["lbt"], writes=["olt"])
              em.op("dve", lambda e: e.tensor_scalar(out=nolt[:], in0=olt[:], scalar1=-1.0, scalar2=None, op0=ALU.mult), reads=["olt"], writes=["nolt"])

              def next_pm():
                  pmi[0] = (pmi[0] + 1) % 4
                  return pM[pmi[0]], f"pM{pmi[0]}"

              ntile = NTILE if stop_after > 1 else 1
              HALF = NTILE // 2
              wcur = {}

              def prologue_gen(tt):
                  t0 = tt * TT
                  em.dma("sp", lambda e: e.dma_start(out=xt[:], in_=x[t0:t0 + TT, :].rearrange("(s p) d -> p s d", p=128)), writes=["xt"])
                  yield
                  for s in range(4):
                      em.op("act", lambda e, s=s: e.activation(out=sqj[:], in_=xt[:, s, :], func=AF.Square, accum_out=ss[:, s:s + 1]),
                            reads=["xt"], writes=["sqj", "ss"])
                  em.op("act", lambda e: e.activation(out=rstd[:], in_=ss[:], func=AF.Ln, scale=1.0 / D, bias=EPS), reads=["ss"], writes=["rstd"])
                  em.op("act", lambda e: e.activation(out=rstd[:], in_=rstd[:], func=AF.Exp, scale=-0.5), reads=["rstd"], writes=["rstd"])
                  yield
                  xT, xTk = xnT.nxt()
                  wcur[("xT", tt)] = (xT, xTk)
                  for s in range(4):
                      em.op("dve", lambda e, s=s: e.scalar_tensor_tensor(out=xn[:, s, :], in0=xt[:, s, :], scalar=rstd[:, s:s + 1], in1=gt[:], op0=ALU.mult, op1=ALU.mult),
                            reads=["xt", "rstd", "gt"], writes=[f"xn{s}"])
                  yield
                  for s in range(4):
                      pt_ = pTs[s % 2]; ptk = f"pT{s % 2}"
                      for k in range(8):
                          em.op("pe", lambda e, s=s, k=k, pt_=pt_: e.transpose(out=pt_[:, k, :], in_=xn[:, s, k * 128:(k + 1) * 128], identity=idb[:]),
                                reads=[f"xn{s}", "idb"], writes=[ptk])
                      em.op("act" if s % 2 else "dve", lambda e, s=s, pt_=pt_: (e.activation(out=xT[:, :, s * 128:(s + 1) * 128], in_=pt_[:], func=AF.Copy) if s % 2 else e.tensor_copy(out=xT[:, :, s * 128:(s + 1) * 128], in_=pt_[:])),
                            reads=[ptk], writes=[xTk])
                  yield

              def wload_gen(tt, g):
                  w, wk = wg.nxt()
                  wcur[(tt, g)] = (w, wk)
                  em.dma("sp", lambda e: e.dma_start(out=w[:], in_=WIN[:, g * 1024:(g + 1) * 1024].rearrange("(k p) c -> p k c", p=128)), writes=[wk])
                  yield

              def vblock_gen(tt, s, hf):
                  t0 = tt * TT
                  w, wk = wcur[(tt, 3)]; xT, xTk = wcur[("xT", tt)]
                  pm, pmk = next_pm()
                  for k in range(8):
                      em.op("pe", lambda e, k=k: e.matmul(pm[:], lhsT=xT[:, k, s * 128:(s + 1) * 128], rhs=w[:, k, hf * 512:(hf + 1) * 512], start=(k == 0), stop=(k == 7)),
                            reads=[xTk, wk], writes=[pmk])
                  yield
                  ob, obk = OB.nxt()
                  em.op("dve", lambda e: e.tensor_copy(out=ob[:], in_=pm[:]), reads=[pmk], writes=[obk])
                  em.store("sp", lambda e: e.dma_start(out=VT[t0 + s * 128:t0 + (s + 1) * 128, hf * 512:(hf + 1) * 512], in_=ob[:]), reads=[obk], writes=["VT"])

              def block_gen(tt, g, cb):
                  t0 = tt * TT
                  full = tt < HALF
                  w, wk = wcur[(tt, g)]; xT, xTk = wcur[("xT", tt)]
                  pm, pmk = next_pm()
                  for k in range(8):
                      em.op("pe", lambda e, k=k: e.matmul(pm[:], lhsT=w[:, k, cb * 128:(cb + 1) * 128], rhs=xT[:, k, :], start=(k == 0), stop=(k == 7)),
                            reads=[xTk, wk], writes=[pmk])
                  yield
                  if g == 0:
                      em.op("act", lambda e: e.activation(out=QS[:, cb, :], in_=pm[:], func=AF.Silu), reads=[pmk], writes=[f"QS{cb}"])
                  elif g in (1, 2):
                      d = g - 1
                      h = cb
                      sg, sgk = SG.nxt(); kk, kkk = KK.nxt(); lf, lfk = LF.nxt(); bb, bbk = BB.nxt()
                      e1, e1k = E1.nxt(); e2, e2k = E2.nxt(); qd, qdk = QDs.nxt(); kd, kdk = KDs.nxt()
                      kt, ktk = KTs.nxt()
                      em.op("act", lambda e: e.activation(out=sg[:], in_=pm[:], func=AF.Sigmoid), reads=[pmk], writes=[sgk])
                      yield
                      em.op("dve", lambda e: e.tensor_scalar(out=kk[:], in0=sg[:], scalar1=nolt[:, h:h + 1], scalar2=olt[:, h:h + 1], op0=ALU.mult, op1=ALU.add),
                            reads=[sgk, "nolt", "olt"], writes=[kkk])
                      em.op("act", lambda e: e.activation(out=lf[:], in_=sg[:], func=AF.Ln, scale=olt[:, h:h + 1], bias=lbt[:, h:h + 1]),
                            reads=[sgk, "olt", "lbt"], writes=[lfk])
                      yield
                      if d == 0:
                          em.op("dve", lambda e: e.tensor_tensor_scan(out=bb[:], data0=rmf_t[:], data1=lf[:], initial=0.0, op0=ALU.mult, op1=ALU.add),
                                reads=[lfk, "rmf"], writes=[bbk])
                      else:
                          em.op("dve", lambda e: e.tensor_tensor_scan(out=bb[:, ::-1], data0=rmb_t[:, ::-1], data1=lf[:, ::-1], initial=0.0, op0=ALU.mult, op1=ALU.add),
                                reads=[lfk, "rmb"], writes=[bbk])
                      yield
                      em.op("act", lambda e: e.activation(out=e1[:], in_=bb[:], func=AF.Exp), reads=[bbk], writes=[e1k])
                      em.op("act", lambda e: e.activation(out=e2[:], in_=bb[:], func=AF.Exp, scale=-1.0), reads=[bbk], writes=[e2k])
                      yield
                      if full:
                          em.op("pool", lambda e: e.tensor_tensor(out=qd[:], in0=QS[:, h, :], in1=e1[:], op=ALU.mult), reads=[f"QS{h}", e1k], writes=[qdk])
                      em.op("pool", lambda e: e.tensor_tensor(out=kd[:], in0=kk[:], in1=e2[:], op=ALU.mult), reads=[kkk, e2k], writes=[kdk])
                      off = 63 if d == 0 else 0
                      em.op("dve", lambda e: e.tensor_copy(out=dec_t[d][:, h, tt * 8:(tt + 1) * 8], in_=e1[:, off::64]),
                            reads=[e1k], writes=[f"dec{d}"])
                      if full:
                          em.store("sp", lambda e: e.dma_start(out=QD[d][h, :, t0:t0 + TT], in_=qd[:]), reads=[qdk], writes=[f"QD{d}"])
                          em.store("sp", lambda e: e.dma_start(out=KD[d][h, :, t0:t0 + TT], in_=kd[:]), reads=[kdk], writes=[f"KD{d}"])
                      yield
                      pk = pK[h % 2]; pkk = f"pK{h % 2}"
                      for s in range(4):
                          em.op("pe", lambda e, s=s: e.transpose(out=pk[:, s, :], in_=kd[:, s * 128:(s + 1) * 128], identity=idb[:]),
                                reads=[kdk, "idb"], writes=[pkk])
                      yield
                      em.op("dve", lambda e: e.tensor_copy(out=kt[:], in_=pk[:]), reads=[pkk], writes=[ktk])
                      em.store("sp", lambda e: e.dma_start(out=KDTM[d][t0:t0 + TT, h * 128:(h + 1) * 128].rearrange("(s p) k -> p s k", p=128), in_=kt[:]),
                               reads=[ktk], writes=[f"KDTM{d}"])
                  elif g == 4:
                      ob, obk = OB.nxt()
                      em.op("act", lambda e: e.activation(out=ob[:], in_=pm[:], func=AF.Silu), reads=[pmk], writes=[obk])
                      em.store("sp", lambda e: e.dma_start(out=OGT[cb * 128:(cb + 1) * 128, t0:t0 + TT], in_=ob[:]), reads=[obk], writes=["OGT"])
                  elif g in (5, 6, 7):
                      of_, ofk = OFt.nxt()
                      r0 = (g - 5) * 1024 + cb * 128
                      em.op("dve" if cb % 2 else "act", lambda e: (e.tensor_copy(out=of_[:], in_=pm[:]) if cb % 2 else e.activation(out=of_[:], in_=pm[:], func=AF.Copy)), reads=[pmk], writes=[ofk])
                      em.store("sp", lambda e: e.dma_start(out=HYT[r0:r0 + 128, t0:t0 + TT], in_=of_[:]), reads=[ofk], writes=["HYT"])
                  else:
                      ob, obk = OB.nxt()
                      r0 = (g - 8) * 1024 + cb * 128
                      em.op("act", lambda e: e.activation(out=ob[:], in_=pm[:], func=AF.Sigmoid), reads=[pmk], writes=[obk])
                      em.store("sp", lambda e: e.dma_start(out=GT[r0:r0 + 128, t0:t0 + TT], in_=ob[:]), reads=[obk], writes=["GT"])

              def p1_items():
                  yield prologue_gen(0)
                  for tt in range(ntile):
                      groups = list(range(10)) if tt < HALF else [2, 3, 5, 6, 7]
                      yield wload_gen(tt, groups[0])
                      for gi, g in enumerate(groups):
                          if gi + 1 < len(groups):
                              yield wload_gen(tt, groups[gi + 1])
                          if gi == len(groups) // 2 and tt + 1 < ntile:
                              yield prologue_gen(tt + 1)
                          if g == 3:
                              for s in range(4):
                                  for hf in range(2):
                                      yield vblock_gen(tt, s, hf)
                          else:
                              for cb in range(8):
                                  yield block_gen(tt, g, cb)
              run_pipeline(p1_items())
              for d in range(2):
                  em.store("sp", lambda e, d=d: e.dma_start(out=DEC[d], in_=dec_t[d][:]), reads=[f"dec{d}"], writes=[f"DEC{d}"])
              em.flush()
        if stop_after <= 1:
            print("inst", em.n_inst, "waits", em.n_wait)
            return nc

        if 2 not in skip:
          with ExitStack() as es:
              sbf = lambda n, s, d: es.enter_context(nc.sbuf_tensor(uq(n), list(s), d))
              psf = lambda n, s, d: es.enter_context(nc.psum_tensor(uq(n), list(s), d))
              mk = [sbf("mkf", [64, 512], F32), sbf("mkb", [64, 512], F32)]
              dect = [sbf(f"dect{d}", [128, 8, 128], F32) for d in range(2)]
              S = sbf("S", [128, 8, 128], F32)
              Sb = sbf("Sb", [128, 8, 128], BF16)
              tmpS = sbf("tmpS", [128, 8, 128], F32)
              qdt = Rot(sbf, "qdt", 2, [128, 8, TT], BF16)
              kdt = Rot(sbf, "kdt", 2, [128, 8, TT], BF16)
              ktm = Rot(sbf, "ktm", 2, [64, 8, D], BF16)
              vtm = Rot(sbf, "vtm", 2, [64, 8, D], BF16)
              scb = Rot(sbf, "scb", 3, [64, 8, 64], BF16)
              Ot = Rot(sbf, "Ot", 2, [128, 8, TT], F32)
              Of = Rot(sbf, "Of", 3, [128, 8, TT], F32)
              pS = [psf(f"pS{i}", [64, 8, 64], F32) for i in range(2)]
              pO = [psf(f"pO{i}", [128, 8, 64], F32) for i in range(2)]
              pP = [psf(f"pP{i}", [128, 4, 128], F32) for i in range(4)]
              em.dma("sp", lambda e: e.dma_start(out=mk[0][:], in_=maskf), writes=["mk0"])
              em.dma("sp", lambda e: e.dma_start(out=mk[1][:], in_=maskb), writes=["mk1"])
              for d in range(2):
                  em.dma("sp", lambda e, d=d: e.dma_start(out=dect[d][:], in_=DEC[d]), writes=[f"dect{d}"])
              for d in range(2):
                  em.op("pool", lambda e: e.memset(S[:], 0.0), writes=["S"])
                  em.op("pool", lambda e: e.memset(Sb[:], 0.0), writes=["Sb"])
                  HALF = NTILE // 2
                  tiles = list(range(HALF)) if d == 0 else list(range(NTILE - 1, -1, -1))
                  tl = {}

                  def tload_gen(tt, d=d, tl=tl):
                      t0 = tt * TT
                      full = tt < HALF
                      kt_, ktk = ktm.nxt(); v_, vk = vtm.nxt()
                      ent = dict(kt=(kt_, ktk), v=(v_, vk))
                      em.dma("sp", lambda e: e.dma_start(out=kt_[:], in_=KDTM[d][t0:t0 + TT, :].rearrange("(c s) k -> s c k", s=64)), writes=[ktk])
                      em.dma("sp", lambda e: e.dma_start(out=v_[:], in_=VT[t0:t0 + TT, :].rearrange("(c s) k -> s c k", s=64)), writes=[vk])
                      if full:
                          q_, qk = qdt.nxt(); k_, kk_ = kdt.nxt(); o_, ok = Ot.nxt()
                          ent.update(q=(q_, qk), k=(k_, kk_), o=(o_, ok))
                          em.dma("sp", lambda e: e.dma_start(out=q_[:], in_=QD[d][:, :, t0:t0 + TT].rearrange("h k t -> k h t")), writes=[qk])
                          em.dma("sp", lambda e: e.dma_start(out=k_[:], in_=KD[d][:, :, t0:t0 + TT].rearrange("h k t -> k h t")), writes=[kk_])
                          if d == 1:
                              f_, fk = Of.nxt()
                              ent.update(f=(f_, fk))
                              em.dma("sp", lambda e: e.dma_start(out=f_[:], in_=OF[:, :, t0:t0 + TT].rearrange("h k t -> k h t")), reads=[f"OF{tt}"], writes=[fk])
                      tl[tt] = ent
                      yield

                  def chunk_gen(tt, c, last, d=d, tl=tl):
                      t0 = tt * TT
                      full = tt < HALF
                      ent = tl[tt]
                      kt_, ktk = ent["kt"]; v_, vk = ent["v"]
                      gc = tt * 8 + c
                      cs = slice(c * 64, (c + 1) * 64)
                      decb = dect[d][:, :, gc:gc + 1]
                      if full:
                          q_, qk = ent["q"]; k_, kk_ = ent["k"]; o_, ok = ent["o"]
                          ps_ = pS[gc % 2]; psk = f"pS{gc % 2}"
                          po_ = pO[gc % 2]; pok = f"pO{gc % 2}"
                          for h in range(8):
                              em.op("pe", lambda e, h=h: e.matmul(ps_[:, h, :], lhsT=k_[:, h, cs], rhs=q_[:, h, cs], start=True, stop=True),
                                    reads=[kk_, qk], writes=[psk])
                      pps = []
                      for hh in range(2):
                          pp = pP[(gc % 2) * 2 + hh]; ppk = f"pP{(gc % 2) * 2 + hh}"
                          pps.append((pp, ppk))
                          for h4 in range(4):
                              h = hh * 4 + h4
                              hs = slice(h * 128, (h + 1) * 128)
                              em.op("pe", lambda e, pp=pp, h4=h4, hs=hs: e.matmul(pp[:, h4, :], lhsT=kt_[:, c, hs], rhs=v_[:, c, hs], start=True, stop=True),
                                    reads=[ktk, vk], writes=[ppk])
                      yield
                      if full:
                          sb_, sbk = scb.nxt()
                          em.op("dve", lambda e: e.tensor_tensor(out=sb_[:], in0=ps_[:], in1=mk[d][:].rearrange("p (h t) -> p h t", h=8), op=ALU.mult),
                                reads=[psk, f"mk{d}"], writes=[sbk])
                          for h in range(8):
                              hs = slice(h * 128, (h + 1) * 128)
                              em.op("pe", lambda e, h=h, hs=hs: e.matmul(po_[:, h, :], lhsT=v_[:, c, hs], rhs=sb_[:, h, :], start=True, stop=False),
                                    reads=[vk, sbk], writes=[pok])
                              em.op("pe", lambda e, h=h: e.matmul(po_[:, h, :], lhsT=Sb[:, h, :], rhs=q_[:, h, cs], start=False, stop=True),
                                    reads=["Sb", qk], writes=[pok])
                      for hh in range(2):
                          pp, ppk = pps[hh]
                          h4s = slice(hh * 4, hh * 4 + 4)
                          em.op("dve", lambda e, pp=pp, h4s=h4s: e.tensor_tensor(out=S[:, h4s, :], in0=pp[:], in1=S[:, h4s, :], op=ALU.add),
                                reads=[ppk, "S"], writes=["S"])
                      yield
                      em.op("dve", lambda e: e.tensor_tensor(out=S[:], in0=S[:], in1=decb.to_broadcast([128, 8, 128]), op=ALU.mult),
                            reads=["S", f"dect{d}"], writes=["S"])
                      em.op("act", lambda e: e.activation(out=Sb[:], in_=S[:], func=AF.Copy), reads=["S"], writes=["Sb"])
                      if full:
                          if d == 0:
                              em.op("act", lambda e: e.activation(out=o_[:, :, cs], in_=po_[:], func=AF.Copy), reads=[pok], writes=[ok])
                          else:
                              f_, fk = ent["f"]
                              em.op("pool" if False else "dve", lambda e: e.tensor_tensor(out=o_[:, :, cs], in0=po_[:], in1=f_[:, :, cs], op=ALU.add), reads=[pok, fk], writes=[ok])
                          if last:
                              dst = OF if d == 0 else OT
                              em.store("sp", lambda e: e.dma_start(out=dst[:, :, t0:t0 + TT].rearrange("h k t -> k h t"), in_=o_[:]), reads=[ok], writes=[f"OF{tt}" if d == 0 else "OT"])

                  def p2_items(d=d, tiles=tiles):
                      yield tload_gen(tiles[0])
                      for ti, tt in enumerate(tiles):
                          if ti + 1 < len(tiles):
                              yield tload_gen(tiles[ti + 1])
                          chunks = list(range(8)) if d == 0 else list(range(7, -1, -1))
                          for ci, c in enumerate(chunks):
                              yield chunk_gen(tt, c, ci == 7)
                  run_pipeline(p2_items())
              em.flush()
        print("inst", em.n_inst, "waits", em.n_wait)
        if stop_after <= 2:
            return nc
        if 3 not in skip:
          with ExitStack() as es:
            sbf = lambda n, s, d: es.enter_context(nc.sbuf_tensor(uq(n), list(s), d))
            psf = lambda n, s, d: es.enter_context(nc.psum_tensor(uq(n), list(s), d))
            w1t = sbf("w1t", [33, 64], F32); w2t = sbf("w2t", [64, 64], F32); w3t = sbf("w3t", [64, 2048], F32)
            fb = sbf("fb", [64, 4], F32)
            fs = sbf("fs", [64, 4], F32)
            dcy = sbf("dcy", [128, 16], F32)
            hbias = sbf("hbias", [128, 8], F32)
            tpbs = Rot(sbf, "tpb", 2, [128, TT], F32)
            zt = Rot(sbf, "zt", 2, [33, TT], F32)
            ya = Rot(sbf, "ya", 3, [64, TT], F32)
            yb_ = Rot(sbf, "yb_", 3, [64, TT], F32)
            hd1 = Rot(sbf, "hd1", 2, [64, TT], F32)
            hd2 = Rot(sbf, "hd2", 2, [64, TT], F32)
            wn = Rot(sbf, "wn", 3, [128, TT], F32)
            fo = Rot(sbf, "fo", 3, [128, TT], F32)
            fob = Rot(sbf, "fob", 4, [128, TT], BF16)
            pF = [psf(f"pF{i}", [128, TT], F32) for i in range(3)]
            lag0 = sbf("lag0", [128, 16], F32); lagc = sbf("lagc", [128, 16], F32); lagb = sbf("lagb", [128, 16], BF16)
            selt = sbf("selt", [128, 2], F32)
            em.dma("sp", lambda e: e.dma_start(out=selt[:], in_=sel), writes=["selt"])
            em.dma("sp", lambda e: e.dma_start(out=w1t[:], in_=filt_w1), writes=["w1t"])
            em.dma("sp", lambda e: e.dma_start(out=w2t[:], in_=filt_w2), writes=["w2t"])
            em.dma("sp", lambda e: e.dma_start(out=w3t[:], in_=filt_w3), writes=["w3t"])
            em.dma("sp", lambda e: e.dma_start(out=fb[:], in_=filt_vec.rearrange("j k -> k j"), allow_slow_non_contiguous=True), writes=["fb"])
            em.dma("sp", lambda e: e.dma_start(out=dcy[:], in_=filt_decay.rearrange("o (b p) -> p (o b)", p=128), allow_slow_non_contiguous=True), writes=["dcy"])
            em.dma("sp", lambda e: e.dma_start(out=hbias[:], in_=hyena_bias.rearrange("o (b p) -> p (o b)", p=128), allow_slow_non_contiguous=True), writes=["hbias"])
            dcn = sbf("dcn", [128, 16], F32)
            em.op("dve", lambda e: e.tensor_scalar(out=dcn[:], in0=dcy[:], scalar1=-1.0, scalar2=None, op0=ALU.mult), reads=["dcy"], writes=["dcn"])
            em.op("dve", lambda e: e.tensor_tensor(out=dcy[:], in0=dcy[:], in1=dcn[:], op=ALU.min), reads=["dcy", "dcn"], writes=["dcy"])
            I2P = 1.0 / (2.0 * math.pi)
            for j in range(2):
                em.op("dve", lambda e, j=j: e.tensor_scalar(out=fs[:, 2 * j:2 * j + 1], in0=fb[:, 2 * j + 1:2 * j + 2], scalar1=I2P, scalar2=None, op0=ALU.mult), reads=["fb"], writes=["fs"])
                em.op("dve", lambda e, j=j: e.tensor_tensor(out=fs[:, 2 * j + 1:2 * j + 2], in0=fs[:, 2 * j:2 * j + 1], in1=fb[:, 2 * j:2 * j + 1], op=ALU.mult), reads=["fb", "fs"], writes=["fs"])
            MAGIC = 12582912.0

            def sin_layer(pm, pmk, j, hd, hdk):
                a, ak = ya.nxt(); b_, bk = yb_.nxt()
                em.op("dve", lambda e: e.tensor_scalar(out=a[:], in0=pm[0:64, :], scalar1=fs[:, 2 * j:2 * j + 1], scalar2=fs[:, 2 * j + 1:2 * j + 2], op0=ALU.mult, op1=ALU.add), reads=[pmk, "fs"], writes=[ak])
                em.op("dve", lambda e: e.tensor_scalar(out=b_[:], in0=a[:], scalar1=MAGIC, scalar2=None, op0=ALU.add), reads=[ak], writes=[bk])
                em.op("dve", lambda e: e.tensor_scalar(out=b_[:], in0=b_[:], scalar1=MAGIC, scalar2=None, op0=ALU.subtract), reads=[bk], writes=[bk])
                em.op("dve", lambda e: e.tensor_tensor(out=a[:], in0=a[:], in1=b_[:], op=ALU.subtract), reads=[ak, bk], writes=[ak])
                em.op("dve", lambda e: e.tensor_scalar(out=a[:], in0=a[:], scalar1=-0.499999, scalar2=0.499999, op0=ALU.max, op1=ALU.min), reads=[ak], writes=[ak])
                em.op("act", lambda e: e.activation(out=hd[:], in_=a[:], func=AF.Sin, scale=2.0 * math.pi), reads=[ak], writes=[hdk])

            pfc = [0]

            def npf():
                pfc[0] += 1
                return pF[pfc[0] % 3], f"pF{pfc[0] % 3}"
            tst = {}

            def sin_item(tt):
                t0 = tt * TT
                z_, zk = zt.nxt(); tp_, tpk = tpbs.nxt()
                em.dma("sp", lambda e: e.dma_start(out=z_[:], in_=zT[:, t0:t0 + TT]), writes=[zk])
                em.dma("sp", lambda e: e.dma_start(out=tp_[:], in_=tpos[:, t0:t0 + TT].partition_broadcast(128)), writes=[tpk])
                yield
                pm, pmk = npf()
                em.op("pe", lambda e: e.matmul(pm[0:64, :], lhsT=w1t[:], rhs=z_[:], start=True, stop=True), reads=["w1t", zk], writes=[pmk])
                h1_, h1k = hd1.nxt()
                sin_layer(pm, pmk, 0, h1_, h1k)
                yield
                pm2, pm2k = npf()
                em.op("pe", lambda e: e.matmul(pm2[0:64, :], lhsT=w2t[:], rhs=h1_[:], start=True, stop=True), reads=["w2t", h1k], writes=[pm2k])
                h2_, h2k = hd2.nxt()
                sin_layer(pm2, pm2k, 1, h2_, h2k)
                tst[tt] = (h2_, h2k, tp_, tpk)
                yield

            def cb_item(tt, cb):
                t0 = tt * TT
                h2_, h2k, tp_, tpk = tst[tt]
                pm, pmk = npf()
                em.op("pe", lambda e: e.matmul(pm[:], lhsT=w3t[:, cb * 128:(cb + 1) * 128], rhs=h2_[:], start=True, stop=True), reads=["w3t", h2k], writes=[pmk])
                w_, wk_ = wn.nxt(); f_, fk_ = fo.nxt(); fb_, fbk = fob.nxt()
                em.op("act", lambda e: e.activation(out=w_[:], in_=tp_[:], func=AF.Exp, scale=dcy[:, cb:cb + 1]), reads=[tpk, "dcy"], writes=[wk_])
                yield
                em.op("dve", lambda e: e.tensor_tensor(out=f_[:], in0=pm[:], in1=w_[:], op=ALU.mult), reads=[pmk, wk_], writes=[fk_])
                if tt == 0:
                    em.op("dve", lambda e: e.tensor_copy(out=lag0[:, cb:cb + 1], in_=f_[:, 0:1]), reads=[fk_], writes=["lag0"])
                yield
                em.op("pool" if cb % 2 else "act", lambda e: (e.tensor_copy(out=fb_[:], in_=f_[:]) if cb % 2 else e.activation(out=fb_[:], in_=f_[:], func=AF.Copy)), reads=[fk_], writes=[fbk])
                em.store("sp", lambda e: e.dma_start(out=HF[cb * 128:(cb + 1) * 128, t0:t0 + TT], in_=fb_[:]), reads=[fbk], writes=[f"HF0_{cb}" if tt == 0 else "HF"])

            def p3a_items():
                nt = NTILE if "3a" not in skip else 0
                if nt:
                    yield sin_item(0)
                for tt in range(nt):
                    if tt + 1 < nt:
                        yield sin_item(tt + 1)
                    for cb in range(16):
                        yield cb_item(tt, cb)
            run_pipeline(p3a_items())
            for tt in range(1 if "3a" not in skip else 0):
                if tt == 0:
                    em.op("dve", lambda e: e.memset(lagc[:], 0.0), writes=["lagc"])
                    em.op("dve", lambda e: e.tensor_scalar(out=lagc[:, 0:8], in0=lag0[:, 0:8], scalar1=selt[:, 0:1], scalar2=None, op0=ALU.mult), reads=["lag0", "selt"], writes=["lagc"])
                    em.op("dve", lambda e: e.scalar_tensor_tensor(out=lagc[:, 0:8], in0=lag0[:, 8:16], scalar=selt[:, 1:2], in1=lagc[:, 0:8], op0=ALU.mult, op1=ALU.add), reads=["lag0", "selt", "lagc"], writes=["lagc"])
                    em.op("dve", lambda e: e.tensor_tensor(out=lagc[:, 0:8], in0=lagc[:, 0:8], in1=hbias[:], op=ALU.add), reads=["lagc", "hbias"], writes=["lagc"])
                    em.op("dve", lambda e: e.tensor_copy(out=lagb[:], in_=lagc[:]), reads=["lagc"], writes=["lagb"])
                    em.dma("sp", lambda e: e.dma_start(out=HF.rearrange("(b p) t -> p b t", p=128)[:, :, 0:1], in_=lagb[:].unsqueeze(2), allow_slow_non_contiguous=True),
                           reads=["lagb"], writes=[f"HF0_{c_}" for c_ in range(16)])
            em.flush()

          with ExitStack() as es:
            sbf = lambda n, s, d: es.enter_context(nc.sbuf_tensor(uq(n), list(s), d))
            PW = 2048
            cw = sbf("cw", [128, 3, 24], F32)
            cbias = sbf("cbias", [128, 24], F32)
            hyin = [Rot(sbf, f"hyin{j}", 2, [128, PW + 2], F32) for j in range(3)]
            cv = [Rot(sbf, f"cv{j}", 2, [128, PW], F32) for j in range(3)]
            ub = Rot(sbf, "ub", 2, [128, PW], BF16)
            em.dma("sp", lambda e: e.dma_start(out=cw[:], in_=conv_w.rearrange("j (b p) -> p j b", p=128), allow_slow_non_contiguous=True), writes=["cw"])
            em.dma("sp", lambda e: e.dma_start(out=cbias[:], in_=conv_b.rearrange("o (b p) -> p (o b)", p=128), allow_slow_non_contiguous=True), writes=["cbias"])
            for cb in range(8 if "3b" not in skip else 0):
                for pc in range(L // PW):
                    t0 = pc * PW
                    outs = []
                    for j in range(3):
                        hy_, hyk = hyin[j].nxt(); c_, ck = cv[j].nxt()
                        blk = j * 8 + cb
                        r0 = blk * 128
                        lo = max(t0 - 1, 0); hi = min(t0 + PW + 1, L)
                        if t0 == 0:
                            em.op("pool", lambda e, hy_=hy_: e.memset(hy_[:, 0:1], 0.0), writes=[hyk])
                        if t0 + PW == L:
                            em.op("pool", lambda e, hy_=hy_: e.memset(hy_[:, PW + 1:PW + 2], 0.0), writes=[hyk])
                        o0 = lo - (t0 - 1)
                        em.dma("sp", lambda e, hy_=hy_, r0=r0, lo=lo, hi=hi, o0=o0: e.dma_start(out=hy_[:, o0:o0 + hi - lo], in_=HYT[r0:r0 + 128, lo:hi]), writes=[hyk])
                        eng = "dve"
                        em.op("act", lambda e, c_=c_, hy_=hy_, blk=blk: e.activation(out=c_[:], in_=hy_[:, 1:PW + 1], func=AF.Identity, scale=cw[:, 1, blk:blk + 1], bias=cbias[:, blk:blk + 1]), reads=[hyk, "cw", "cbias"], writes=[ck])
                        em.op(eng, lambda e, c_=c_, hy_=hy_, blk=blk: e.scalar_tensor_tensor(out=c_[:], in0=hy_[:, 0:PW], scalar=cw[:, 0, blk:blk + 1], in1=c_[:], op0=ALU.mult, op1=ALU.add), reads=[hyk, "cw", ck], writes=[ck])
                        em.op(eng, lambda e, c_=c_, hy_=hy_, blk=blk: e.scalar_tensor_tensor(out=c_[:], in0=hy_[:, 2:PW + 2], scalar=cw[:, 2, blk:blk + 1], in1=c_[:], op0=ALU.mult, op1=ALU.add), reads=[hyk, "cw", ck], writes=[ck])
                        outs.append((c_, ck))
                    u_, uk = ub.nxt()
                    em.op("pool", lambda e, u_=u_, a=outs[2][0], b=outs[1][0]: e.tensor_tensor(out=u_[:], in0=a[:], in1=b[:], op=ALU.mult), reads=[outs[2][1], outs[1][1]], writes=[uk])
                    em.store("sp", lambda e, u_=u_, cb=cb, t0=t0: e.dma_start(out=UT[cb * 128:(cb + 1) * 128, t0:t0 + PW], in_=u_[:]), reads=[uk], writes=["UT"])
                    em.store("sp", lambda e, a=outs[0][0], cb=cb, t0=t0: e.dma_start(out=X0T[cb * 128:(cb + 1) * 128, t0:t0 + PW], in_=a[:]), reads=[outs[0][1]], writes=["X0T"])
            em.flush()

          with ExitStack() as es:
            sbf = lambda n, s, d: es.enter_context(nc.sbuf_tensor(uq(n), list(s), d))
            psf = lambda n, s, d: es.enter_context(nc.psum_tensor(uq(n), list(s), d))
            cst = sbf("cst", [128, 256], F32)
            FRIb = sbf("FRIb", [128, 256], BF16); FRnIb = sbf("FRnIb", [128, 256], BF16)
            FIRb = sbf("FIRb", [128, 256], BF16); nFIb = sbf("nFIb", [128, 128], BF16)
            TWt = sbf("TWt", [128, 2, 128], F32); TWct = sbf("TWct", [128, 2, 128], F32)
            for nm, src, dst in (("FRI", FRI, FRIb), ("FRnI", FRnI, FRnIb), ("FIR", FIR, FIRb)):
                em.dma("sp", lambda e, src=src: e.dma_start(out=cst[:], in_=src), writes=["cst"])
                em.op("dve", lambda e, dst=dst: e.tensor_copy(out=dst[:], in_=cst[:]), reads=["cst"], writes=[nm])
            em.dma("sp", lambda e: e.dma_start(out=cst[:, 0:128], in_=nFI), writes=["cst"])
            em.op("dve", lambda e: e.tensor_copy(out=nFIb[:], in_=cst[:, 0:128]), reads=["cst"], writes=["nFI"])
            em.dma("sp", lambda e: e.dma_start(out=TWt[:].rearrange("p a b -> p (a b)"), in_=TW), writes=["TW"])
            em.dma("sp", lambda e: e.dma_start(out=TWct[:].rearrange("p a b -> p (a b)"), in_=TWc), writes=["TWc"])
            Min = Rot(sbf, "Min", 8, [64, 2, 128], BF16)
            P1 = Rot(sbf, "P1", 6, [128, 2, 2, 128], F32)
            P2 = Rot(sbf, "P2", 6, [128, 2, 2, 128], F32)
            B2 = Rot(sbf, "B2", 8, [128, 2, 2, 128], BF16)
            Y2 = Rot(sbf, "Y2", 4, [128, 2, 2, 128], BF16)
            D2 = Rot(sbf, "D2", 4, [128, 2, 2, 128], BF16)
            KFs = Rot(sbf, "KFs", 6, [128, 2, 2, 128], F32)
            YO = Rot(sbf, "YO", 3, [64, 2, 128], F32)
            pq = [psf(f"pq{i}", [128, 2, 2, 128], F32) for i in range(8)]
            pqk = [f"pq{i}" for i in range(8)]

            def bc4(t3, ri):
                return t3[:, ri:ri + 1, :].unsqueeze(1).to_broadcast([128, 2, 2, 128])

            def cmul(src, srck, tw, twk, p1, p1k, p2, p2k):
                em.op("dve", lambda e: e.tensor_tensor(out=p1[:], in0=src[:], in1=bc4(tw, 0), op=ALU.mult), reads=[srck, twk], writes=[p1k])
                em.op("dve", lambda e: e.tensor_tensor(out=p2[:], in0=src[:, :, ::-1, :], in1=bc4(tw, 1), op=ALU.mult), reads=[srck, twk], writes=[p2k])

            def flat(ap3):
                return ap3.rearrange("p a b -> p (a b)")

            def st2(o, ok_, b2, b2k, ch, conj, first, last):
                fi = nFIb if conj else FRIb[:, 128:256]
                nfi = FRIb[:, 128:256] if conj else nFIb
                em.op("pe", lambda e: e.matmul(flat(o), lhsT=FRIb[:, 0:128], rhs=flat(b2[:, ch, :, :]), start=first, stop=False), reads=["FRI", *b2k], writes=[ok_])
                em.op("pe", lambda e: e.matmul(o[:, 0, :], lhsT=nfi[:] if conj is False else nfi, rhs=b2[:, ch, 1, :], start=False, stop=False), reads=["FRI", "nFI", *b2k], writes=[ok_])
                em.op("pe", lambda e: e.matmul(o[:, 1, :], lhsT=fi[:] if conj else fi, rhs=b2[:, ch, 0, :], start=False, stop=last), reads=["FRI", "nFI", *b2k], writes=[ok_])

            npair = 512 if 31 not in skip else 4

            def stage1_gen(src_dram, c0, rhs1, rhs1k, tw, twk, pa, pak, res):
                m_, mk_ = Min.nxt()
                em.dma("sp", lambda e: e.dma_start(out=m_[:], in_=src_dram[c0:c0 + 2, :].rearrange("c (a b) -> a c b", b=128)), writes=[mk_])
                yield
                for ch in range(2):
                    em.op("pe", lambda e, ch=ch: e.matmul(flat(pa[:, ch, :, :]), lhsT=m_[:, ch, :], rhs=rhs1[0:64, :], start=True, stop=True), reads=[mk_, rhs1k], writes=[pak])
                yield
                p1, p1k = P1.nxt(); p2, p2k = P2.nxt(); b2, b2k = B2.nxt()
                cmul(pa, pak, tw, twk, p1, p1k, p2, p2k)
                yield
                em.op("pool", lambda e: e.tensor_tensor(out=b2[:, :, 0, :], in0=p1[:, :, 0, :], in1=p2[:, :, 0, :], op=ALU.subtract), reads=[p1k, p2k], writes=[b2k])
                em.op("pool", lambda e: e.tensor_tensor(out=b2[:, :, 1, :], in0=p1[:, :, 1, :], in1=p2[:, :, 1, :], op=ALU.add), reads=[p1k, p2k], writes=[b2k + "i"])
                res.append((b2, (b2k, b2k + "i")))
                yield

            def kf_gen(pr):
                c0 = pr * 2
                rf, rb = [], []
                pa0 = pq[(pr % 2) * 2]; pa0k = pqk[(pr % 2) * 2]
                pa1 = pq[(pr % 2) * 2 + 1]; pa1k = pqk[(pr % 2) * 2 + 1]
                g1 = stage1_gen(HF, c0, FRIb, "FRI", TWt, "TW", pa0, pa0k, rf)
                g2 = stage1_gen(HF, 1024 + c0, FRnIb, "FRnI", TWct, "TWc", pa1, pa1k, rb)
                for _ in range(4):
                    next(g1); next(g2)
                    yield
                bf_, bfk = rf[0]; bb_, bbk = rb[0]
                px = pq[4 + pr % 3]; pxk = pqk[4 + pr % 3]
                for ch in range(2):
                    st2(px[:, ch, :, :], pxk, bf_, bfk, ch, False, True, False)
                    st2(px[:, ch, :, :], pxk, bb_, bbk, ch, True, False, True)
                yield
                kf_, kfk = KFs.nxt()
                em.op("act", lambda e: e.activation(out=kf_[:], in_=px[:], func=AF.Copy), reads=[pxk], writes=[kfk])
                em.store("sp", lambda e: e.dma_start(out=KF[c0:c0 + 2].rearrange("c k a b -> k c a b"), in_=kf_[:]), reads=[kfk], writes=[f"KF{pr}"])

            def data_gen(pr):
                c0 = pr * 2
                kf_, kfk = KFs.nxt()
                em.dma("sp", lambda e: e.dma_start(out=kf_[:], in_=KF[c0:c0 + 2].rearrange("c k a b -> k c a b")), reads=[f"KF{pr}"], writes=[kfk])
                rf = []
                pa = pq[pr % 2]; pak = pqk[pr % 2]
                g1 = stage1_gen(UT, c0, FRIb, "FRI", TWt, "TW", pa, pak, rf)
                for _ in range(4):
                    next(g1)
                    yield
                b2, b2k = rf[0]
                px = pq[2 + pr % 2]; pxk = pqk[2 + pr % 2]
                for ch in range(2):
                    st2(px[:, ch, :, :], pxk, b2, b2k, ch, False, True, True)
                yield
                p1, p1k = P1.nxt(); p2, p2k = P2.nxt(); y2, y2k = Y2.nxt()
                em.op("dve", lambda e: e.tensor_tensor(out=p1[:], in0=px[:], in1=kf_[:, :, 0:1, :].to_broadcast([128, 2, 2, 128]), op=ALU.mult), reads=[pxk, kfk], writes=[p1k])
                em.op("dve", lambda e: e.tensor_tensor(out=p2[:], in0=px[:, :, ::-1, :], in1=kf_[:, :, 1:2, :].to_broadcast([128, 2, 2, 128]), op=ALU.mult), reads=[pxk, kfk], writes=[p2k])
                yield
                em.op("pool", lambda e: e.tensor_tensor(out=y2[:, :, 0, :], in0=p1[:, :, 0, :], in1=p2[:, :, 0, :], op=ALU.subtract), reads=[p1k, p2k], writes=[y2k])
                em.op("pool", lambda e: e.tensor_tensor(out=y2[:, :, 1, :], in0=p1[:, :, 1, :], in1=p2[:, :, 1, :], op=ALU.add), reads=[p1k, p2k], writes=[y2k + "i"])
                yield
                pc = pq[4 + pr % 2]; pck = pqk[4 + pr % 2]
                for ch in range(2):
                    o = flat(pc[:, ch, :, :])
                    em.op("pe", lambda e, o=o, ch=ch: e.matmul(o, lhsT=y2[:, ch, 0, :], rhs=FRnIb[:], start=True, stop=False), reads=[y2k, y2k + "i", "FRnI"], writes=[pck])
                    em.op("pe", lambda e, o=o, ch=ch: e.matmul(o, lhsT=y2[:, ch, 1, :], rhs=FIRb[:], start=False, stop=True), reads=[y2k, y2k + "i", "FIR"], writes=[pck])
                yield
                p1b, p1bk = P1.nxt(); p2b, p2bk = P2.nxt(); d2, d2k = D2.nxt()
                cmul(pc, pck, TWct, "TWc", p1b, p1bk, p2b, p2bk)
                yield
                em.op("pool", lambda e: e.tensor_tensor(out=d2[:, 0, :, :], in0=p1b[:, :, 0, :], in1=p2b[:, :, 0, :], op=ALU.subtract), reads=[p1bk, p2bk], writes=[d2k])
                em.op("pool", lambda e: e.tensor_tensor(out=d2[:, 1, :, :], in0=p1b[:, :, 1, :], in1=p2b[:, :, 1, :], op=ALU.add), reads=[p1bk, p2bk], writes=[d2k + "i"])
                yield
                py = pq[6 + pr % 2][0:64, 0, :, :]; pyk = pqk[6 + pr % 2]
                em.op("pe", lambda e: e.matmul(flat(py), lhsT=FRIb[:, 0:64], rhs=flat(d2[:, 0, :, :]), start=True, stop=False), reads=["FRI", d2k, d2k + "i"], writes=[pyk])
                em.op("pe", lambda e: e.matmul(flat(py), lhsT=FRIb[:, 128:192], rhs=flat(d2[:, 1, :, :]), start=False, stop=True), reads=["FRI", d2k, d2k + "i"], writes=[pyk])
                yield
                yo, yok = YO.nxt()
                em.op("act", lambda e: e.activation(out=yo[:], in_=py, func=AF.Copy, scale=1.0 / 16384.0), reads=[pyk], writes=[yok])
                em.store("sp", lambda e: e.dma_start(out=YCT[c0:c0 + 2, :].rearrange("c (a b) -> a c b", b=128), in_=yo[:]), reads=[yok], writes=["YCT"])

            run_pipeline(kf_gen(pr) for pr in range(npair))
            run_pipeline(data_gen(pr) for pr in range(npair))
            em.flush()
        print("inst", em.n_inst, "waits", em.n_wait)
        if stop_after <= 3:
            return nc
        TK = 256
        NTK = (L // 2) // TK
        if 4 not in skip:
          with ExitStack() as es:
            sbf = lambda n, s, d: es.enter_context(nc.sbuf_tensor(uq(n), list(s), d))
            psf = lambda n, s, d: es.enter_context(nc.psum_tensor(uq(n), list(s), d))
            wa = sbf("wa", [128, 8, D], BF16); wb = sbf("wb", [128, 8, D], BF16); wo = sbf("wo", [128, 8, D], BF16)
            for wt_, src, nm in ((wa, w_branch_a, "wa"), (wb, w_branch_b, "wb"), (wo, w_out, "wo")):
                for k in range(8):
                    em.dma("pool", lambda e, wt_=wt_, src=src, k=k: e.dma_start(out=wt_[:, k, :], in_=src[k * 128:(k + 1) * 128, :]), writes=[nm])
            ones = sbf("ones", [128, 128], F32)
            em.op("dve", lambda e: e.memset(ones[:], 1.0), writes=["ones"])
            gcol = sbf("gcol", [128, 1], F32)
            em.dma("sp", lambda e: e.dma_start(out=gcol[:], in_=hgrn_norm_g.rearrange("o v -> v o"), allow_slow_non_contiguous=True), writes=["gcol"])
            ot = Rot(sbf, "ot", 2, [128, 8, TK], F32)
            ogt = Rot(sbf, "ogt", 2, [128, 8, TK], BF16)
            sq = Rot(sbf, "sq", 2, [128, 8, TK], F32)
            rs = Rot(sbf, "rs", 2, [128, 2, TK], F32)
            tmpA = Rot(sbf, "tmpA", 2, [128, 2, TK], F32)
            At = Rot(sbf, "At", 2, [128, 8, TK], BF16)
            x0t = Rot(sbf, "x0t", 2, [128, 8, TK], F32)
            yct = Rot(sbf, "yct", 2, [128, 8, TK], F32)
            Bt = Rot(sbf, "Bt", 2, [128, 8, TK], BF16)
            gat = Rot(sbf, "gat", 2, [128, 8, 2, TK], BF16)
            tg = Rot(sbf, "tg", 2, [128, 2, TK], F32)
            mg = Rot(sbf, "mg", 2, [128, 8, TK], BF16)
            xt4 = Rot(sbf, "xt4", 2, [128, 2, D], F32)
            h1t = Rot(sbf, "h1t", 2, [128, 2, D], F32)
            pw = [psf(f"pw{i}", [128, 512], F32) for i in range(6)]
            pwi = [0]

            def npw():
                pwi[0] = (pwi[0] + 1) % 6
                return pw[pwi[0]], f"pw{pwi[0]}"

            for tk in range(NTK):
                tg0 = TOK0 + tk * TK
                o_, ok = ot.nxt(); og_, ogk = ogt.nxt(); s_, sk_ = sq.nxt(); a_, ak = At.nxt()
                em.dma("sp", lambda e, o_=o_, tk=tk: e.dma_start(out=o_[:], in_=OTh[:, :, tk * TK:(tk + 1) * TK].rearrange("h k t -> k h t")), writes=[ok])
                em.dma("sp", lambda e, og_=og_, tk=tk: e.dma_start(out=og_[:], in_=OGTh[:, tk * TK:(tk + 1) * TK].rearrange("(h k) t -> k h t", k=128)), writes=[ogk])
                em.op("act", lambda e, s_=s_, o_=o_: e.activation(out=s_[:], in_=o_[:], func=AF.Square), reads=[ok], writes=[sk_])
                for h2 in range(4):
                    p_, pk = npw()
                    for hh in range(2):
                        h = h2 * 2 + hh
                        em.op("pe", lambda e, p_=p_, s_=s_, h=h, hh=hh: e.matmul(p_[:, hh * TK:(hh + 1) * TK], lhsT=ones[:], rhs=s_[:, h, :], start=True, stop=True), reads=["ones", sk_], writes=[pk])
                    r_, rk = rs.nxt(); t_, tk_ = tmpA.nxt()
                    em.op("act", lambda e, r_=r_, p_=p_: e.activation(out=r_[:].rearrange("p a b -> p (a b)"), in_=p_[:], func=AF.Ln, scale=1.0 / 128, bias=EPS), reads=[pk], writes=[rk])
                    em.op("act", lambda e, r_=r_: e.activation(out=r_[:], in_=r_[:], func=AF.Exp, scale=-0.5), reads=[rk], writes=[rk])
                    em.op("dve", lambda e, t_=t_, o_=o_, r_=r_, h2=h2: e.scalar_tensor_tensor(out=t_[:], in0=o_[:, h2 * 2:h2 * 2 + 2, :], scalar=gcol[:, 0:1], in1=r_[:], op0=ALU.mult, op1=ALU.mult), reads=[ok, rk, "gcol"], writes=[tk_])
                    em.op("pool", lambda e, a_=a_, t_=t_, og_=og_, h2=h2: e.tensor_tensor(out=a_[:, h2 * 2:h2 * 2 + 2, :], in0=t_[:], in1=og_[:, h2 * 2:h2 * 2 + 2, :], op=ALU.mult), reads=[tk_, ogk], writes=[ak])
                if "AT" in dbg:
                    em.store("sp", lambda e, a_=a_, tk=tk: e.dma_start(out=AT[:, tk * TK:(tk + 1) * TK].rearrange("(h k) t -> k h t", k=128), in_=a_[:]), reads=[ak], writes=["AT"])
                x0_, x0k = x0t.nxt(); yc_, yck = yct.nxt(); b_, bk = Bt.nxt(); ga_, gak = gat.nxt()
                em.dma("sp", lambda e, x0_=x0_, tk=tk: e.dma_start(out=x0_[:], in_=X0Th[:, tk * TK:(tk + 1) * TK].rearrange("(h k) t -> k h t", k=128)), writes=[x0k])
                em.dma("sp", lambda e, yc_=yc_, tk=tk: e.dma_start(out=yc_[:], in_=YCTh[:, tk * TK:(tk + 1) * TK].rearrange("(h k) t -> k h t", k=128)), writes=[yck])
                for a2 in range(2):
                    em.dma("sp", lambda e, ga_=ga_, tk=tk, a2=a2: e.dma_start(out=ga_[:, :, a2, :], in_=GTh[a2 * 1024:(a2 + 1) * 1024, tk * TK:(tk + 1) * TK].rearrange("(h k) t -> k h t", k=128)), writes=[gak])
                em.op("pool", lambda e, b_=b_, x0_=x0_, yc_=yc_: e.tensor_tensor(out=b_[:], in0=x0_[:], in1=yc_[:], op=ALU.mult), reads=[x0k, yck], writes=[bk])
                m_, mk_ = mg.nxt()
                for db in range(8):
                    p_, pk = npw()
                    for k in range(8):
                        em.op("pe", lambda e, p_=p_, k=k, db=db, a_=a_: e.matmul(p_[:, 0:TK], lhsT=wa[:, k, db * 128:(db + 1) * 128], rhs=a_[:, k, :], start=(k == 0), stop=(k == 7)), reads=["wa", ak], writes=[pk])
                    for k in range(8):
                        em.op("pe", lambda e, p_=p_, k=k, db=db, b_=b_: e.matmul(p_[:, TK:2 * TK], lhsT=wb[:, k, db * 128:(db + 1) * 128], rhs=b_[:, k, :], start=(k == 0), stop=(k == 7)), reads=["wb", bk], writes=[pk])
                    t_, tk_ = tg.nxt()
                    em.op("dve", lambda e, t_=t_, p_=p_, ga_=ga_, db=db: e.tensor_tensor(out=t_[:].rearrange("p a b -> p (a b)"), in0=p_[:], in1=ga_[:, db, :, :].rearrange("p a b -> p (a b)"), op=ALU.mult), reads=[pk, gak], writes=[tk_])
                    em.op("pool", lambda e, m_=m_, t_=t_, db=db: e.tensor_tensor(out=m_[:, db, :], in0=t_[:, 0, :], in1=t_[:, 1, :], op=ALU.add), reads=[tk_], writes=[mk_])
                if "MG" in dbg:
                    em.store("sp", lambda e, m_=m_, tk=tk: e.dma_start(out=MG[:, tk * TK:(tk + 1) * TK].rearrange("(h k) t -> k h t", k=128), in_=m_[:]), reads=[mk_], writes=["MGd"])
                x_, xk = xt4.nxt(); h_, hk = h1t.nxt()
                em.dma("sp", lambda e, x_=x_, tk=tk: e.dma_start(out=x_[:], in_=xh[tk * TK:(tk + 1) * TK, :].rearrange("(s p) d -> p s d", p=128)), writes=[xk])
                for s in range(2):
                    for hf in range(2):
                        p_, pk = npw()
                        for k in range(8):
                            em.op("pe", lambda e, p_=p_, k=k, s=s, hf=hf, m_=m_: e.matmul(p_[:], lhsT=m_[:, k, s * 128:(s + 1) * 128], rhs=wo[:, k, hf * 512:(hf + 1) * 512], start=(k == 0), stop=(k == 7)), reads=["wo", mk_], writes=[pk])
                        em.op("dve", lambda e, h_=h_, p_=p_, x_=x_, s=s, hf=hf: e.tensor_tensor(out=h_[:, s, hf * 512:(hf + 1) * 512], in0=p_[:], in1=x_[:, s, hf * 512:(hf + 1) * 512], op=ALU.add), reads=[pk, xk], writes=[hk])
                em.store("sp", lambda e, h_=h_, tk=tk: e.dma_start(out=H1[tk * TK:(tk + 1) * TK, :].rearrange("(s p) d -> p s d", p=128), in_=h_[:]), reads=[hk], writes=["H1d"])
            em.flush()
        print("inst", em.n_inst, "waits", em.n_wait)
        if stop_after <= 4:
            return nc
        if 5 not in skip:
          with ExitStack() as es:
            sbf = lambda n, s, d: es.enter_context(nc.sbuf_tensor(uq(n), list(s), d))
            psf = lambda n, s, d: es.enter_context(nc.psum_tensor(uq(n), list(s), d))
            idf5 = sbf("idf5", [128, 128], F32); idb5 = sbf("idb5", [128, 128], BF16)
            em.dma("sp", lambda e: e.dma_start(out=idf5[:], in_=ident), writes=["idf5"])
            em.op("dve", lambda e: e.tensor_copy(out=idb5[:], in_=idf5[:]), reads=["idf5"], writes=["idb5"])
            for r in range(16):
                em.dma("pool", lambda e, r=r: e.dma_start(out=VBF[r * 1024:(r + 1) * 1024, :], in_=peer_v[r * 1024:(r + 1) * 1024, :]), writes=["VBF"])
            urow = Rot(sbf, "urow", 5, [128, D], BF16)
            uts = Rot(sbf, "uts", 4, [128, 8, 128], BF16)
            pU = [psf(f"pU{i}", [128, 8, 128], BF16) for i in range(2)]
            nj = 128 if 51 not in skip else 2
            def u_item(j):
                u_, uk = urow.nxt(); t_, tk_ = uts.nxt()
                em.dma("pool", lambda e: e.dma_start(out=u_[:], in_=peer_u.rearrange("(i j) d -> j i d", j=128)[j]), writes=[uk])
                yield
                yield
                p_ = pU[j % 2]; pk = f"pU{j % 2}"
                for k in range(8):
                    em.op("pe", lambda e, k=k: e.transpose(out=p_[:, k, :], in_=u_[:, k * 128:(k + 1) * 128], identity=idb5[:]), reads=[uk, "idb5"], writes=[pk])
                yield
                em.op("act" if j % 2 else "dve", lambda e: (e.activation(out=t_[:], in_=p_[:], func=AF.Copy) if j % 2 else e.tensor_copy(out=t_[:], in_=p_[:])), reads=[pk], writes=[tk_])
                em.store("sp", lambda e: e.dma_start(out=UTS[j], in_=t_[:]), reads=[tk_], writes=["UTS"])
            run_pipeline(u_item(j) for j in range(nj))
            em.flush()

          with ExitStack() as es:
            sbf = lambda n, s, d: es.enter_context(nc.sbuf_tensor(uq(n), list(s), d))
            psf = lambda n, s, d: es.enter_context(nc.psum_tensor(uq(n), list(s), d))
            wq = sbf("wq", [128, 8, 2048], BF16)
            for k in range(8):
                em.dma("pool", lambda e, k=k: e.dma_start(out=wq[:, k, :], in_=peer_w_q[k * 128:(k + 1) * 128, :]), writes=["wq"])
            idf = sbf("idf", [128, 128], F32)
            em.dma("sp", lambda e: e.dma_start(out=idf[:], in_=ident), writes=["idf"])
            iot = sbf("iot", [128, 128], F32)
            em.dma("sp", lambda e: e.dma_start(out=iot[:], in_=iota), writes=["iot"])
            gff = sbf("gff", [128, D], F32); gfin = sbf("gfin", [128, D], F32)
            em.dma("sp", lambda e: e.dma_start(out=gff[:], in_=norm_ffn_g.partition_broadcast(128)), writes=["gff"])
            em.dma("sp", lambda e: e.dma_start(out=gfin[:], in_=norm_final_g.partition_broadcast(128)), writes=["gfin"])
            skT = sbf("skT", [128, 16, 128], BF16)
            h1 = Rot(sbf, "h1", 2, [128, 2, D], F32)
            ss5 = sbf("ss5", [128, 2], F32); rstd5 = sbf("rstd5", [128, 2], F32)
            xn2 = sbf("xn2", [128, 2, D], F32)
            sqj = xn2[:, 1, :]
            xn2Ts = Rot(sbf, "xn2T", 2, [128, 8, TK], BF16)
            qT = sbf("qT", [128, 16, TK], BF16)
            scr = sbf("scr", [128, 16, 128], F32)
            skf = scr
            em.dma("sp", lambda e: e.dma_start(out=skf[:], in_=peer_sk.rearrange("j n c -> n j c")), writes=["scr"])
            scr2 = scr
            vals = sbf("vals", [128, 16, 16], F32)
            idxu = sbf("idxu", [128, 16, 16], U32)
            idxf = sbf("idxf", [128, 16, 16], F32)
            Cg = sbf("Cg", [128, 8, 256], F32); Cg2 = Cg
            cv = sbf("cv", [128, 8, 16], F32)
            posu = sbf("posu", [128, 8, 16], U32); pa_u = sbf("pa_u", [128, 8, 16], U32); pb_u = sbf("pb_u", [128, 8, 16], U32)
            paf = sbf("paf", [128, 8, 16], F32); pbf = sbf("pbf", [128, 8, 16], F32)
            eq = scr[:].rearrange("p a b -> p (a b)").rearrange("p (h k a) -> p h k a", h=8, k=16)
            ik = sbf("ik", [128, 8, 16], F32); jk = sbf("jk", [128, 8, 16], F32)
            ee = sbf("ee", [128, 8, 16], F32); zz = sbf("zz", [128, 8], F32); gg = sbf("gg", [128, 8, 16], F32)
            ikT = sbf("ikT", [128, TK], F32); jkT = sbf("jkT", [128, TK], F32); gT = sbf("gT", [128, TK], F32)
            njkT = sbf("njkT", [128, TK], F32)
            Ra = Rot(sbf, "Ra", 3, [128, 128], F32)
            Lt = Rot(sbf, "Lt", 8, [128, 128], BF16); Rt = Rot(sbf, "Rt", 8, [128, 128], BF16)
            Gs = sbf("Gs", [128, TK, 128], BF16)
            utj = Rot(sbf, "utj", 5, [128, 8, 128], BF16); vj = Rot(sbf, "vj", 5, [128, D], BF16)
            gact = Rot(sbf, "gact", 3, [128, TK], F32)
            ATj = Rot(sbf, "ATj", 3, [128, TK], BF16)

            acc = [psf(f"acc{i}", [128, 512], F32) for i in range(4)]
            pw = [psf(f"pw{i}", [128, 512], F32) for i in range(4)]
            pwi = [0]

            def npw():
                pwi[0] = (pwi[0] + 1) % 4
                return pw[pwi[0]], f"pw{pwi[0]}"
            pwd = [0]; pw2 = [0]

            def npw_d():
                pwd[0] = (pwd[0] + 1) % 2
                return pw[pwd[0]], f"pw{pwd[0]}"

            def npw2():
                pw2[0] = (pw2[0] + 1) % 2
                return pw[2 + pw2[0]], f"pw{2 + pw2[0]}"

            for j4 in range(4):
                p_, pk = npw()
                for jj in range(4):
                    j = j4 * 4 + jj
                    em.op("pe", lambda e, p_=p_, jj=jj, j=j: e.transpose(out=p_[:, jj * 128:(jj + 1) * 128], in_=skf[:, j, :], identity=idf[:]), reads=["scr", "idf"], writes=[pk])
                em.op("dve", lambda e, p_=p_, j4=j4: e.tensor_copy(out=skT[:, j4 * 4:(j4 + 1) * 4, :].rearrange("p a b -> p (a b)"), in_=p_[:]), reads=[pk], writes=["skT"])

            ntk = NTK if 52 not in skip else 1
            import os
            P5STOP = int(os.environ.get("P5STOP", "99"))
            def prep_gen(tk, st):
                thunks = []
                E_op = lambda *a_, **k_: thunks.append((em.op, a_, k_))
                E_dma = lambda *a_, **k_: thunks.append((em.dma, a_, k_))
                h_, hk = h1.nxt()
                xT2, xT2k = xn2Ts.nxt()
                st[tk] = (h_, hk, xT2, xT2k)
                E_dma("sp", lambda e, h_=h_, tk=tk: e.dma_start(out=h_[:], in_=H1[tk * TK:(tk + 1) * TK, :].rearrange("(s p) d -> p s d", p=128)), writes=[hk])
                for s in range(2):
                    E_op("act", lambda e, h_=h_, s=s: e.activation(out=sqj, in_=h_[:, s, :], func=AF.Square, accum_out=ss5[:, s:s + 1]), reads=[hk], writes=["xn21", "ss5"])
                E_op("act", lambda e: e.activation(out=rstd5[:], in_=ss5[:], func=AF.Ln, scale=1.0 / D, bias=EPS), reads=["ss5"], writes=["rstd5"])
                E_op("act", lambda e: e.activation(out=rstd5[:], in_=rstd5[:], func=AF.Exp, scale=-0.5), reads=["rstd5"], writes=["rstd5"])
                for s in range(2):
                    E_op("dve", lambda e, h_=h_, s=s: e.scalar_tensor_tensor(out=xn2[:, s, :], in0=h_[:, s, :], scalar=rstd5[:, s:s + 1], in1=gff[:], op0=ALU.mult, op1=ALU.mult), reads=[hk, "rstd5", "gff"], writes=[f"xn2{s}"])
                    for k4 in range(2):
                        p_, pk = npw2()
                        for kk in range(4):
                            k = k4 * 4 + kk
                            E_op("pe", lambda e, p_=p_, kk=kk, k=k, s=s: e.transpose(out=p_[:, kk * 128:(kk + 1) * 128], in_=xn2[:, s, k * 128:(k + 1) * 128], identity=idf[:]), reads=[f"xn2{s}", "idf"], writes=[pk])
                        E_op("act", lambda e, p_=p_, k4=k4, s=s: e.activation(out=xT2[:, k4 * 4:(k4 + 1) * 4, s * 128:(s + 1) * 128], in_=p_[:].rearrange("p (a b) -> p a b", a=4), func=AF.Copy), reads=[pk], writes=[xT2k])
                for j2 in range(8):
                    p_, pk = npw2()
                    for jj in range(2):
                        j = j2 * 2 + jj
                        for k in range(8):
                            E_op("pe", lambda e, p_=p_, jj=jj, j=j, k=k: e.matmul(p_[:, jj * TK:(jj + 1) * TK], lhsT=wq[:, k, j * 128:(j + 1) * 128], rhs=xT2[:, k, :], start=(k == 0), stop=(k == 7)), reads=["wq", xT2k], writes=[pk])
                    E_op("dve", lambda e, p_=p_, j2=j2: e.tensor_copy(out=qT[:, j2 * 2:j2 * 2 + 2, :].rearrange("p a b -> p (a b)"), in_=p_[:]), reads=[pk], writes=["qT"])
                for s in range(2):
                    for j4 in range(4):
                        p_, pk = npw2()
                        for jj in range(4):
                            j = j4 * 4 + jj
                            E_op("pe", lambda e, p_=p_, jj=jj, j=j, s=s: e.matmul(p_[:, jj * 128:(jj + 1) * 128], lhsT=qT[:, j, s * 128:(s + 1) * 128], rhs=skT[:, j, :], start=True, stop=True), reads=["qT", "skT"], writes=[pk])
                        E_op("act", lambda e, p_=p_, j4=j4: e.activation(out=scr[:, j4 * 4:(j4 + 1) * 4, :].rearrange("p a b -> p (a b)"), in_=p_[:], func=AF.Copy), reads=[pk], writes=["scr"])
                    for j in range(16):
                        E_op("dve", lambda e, j=j: e.max(out=vals[:, j, 0:8], in_=scr[:, j, :]), reads=["scr"], writes=["vals"])
                        E_op("dve", lambda e, j=j: e.max_index(out=idxu[:, j, 0:8], in_max=vals[:, j, 0:8], in_values=scr[:, j, :]), reads=["scr", "vals"], writes=["idxu"])
                        E_op("dve", lambda e, j=j: e.match_replace(out=scr2[:, j, :], in_to_replace=vals[:, j, 0:8], in_values=scr[:, j, :], imm_value=-1e30), reads=["scr", "vals"], writes=["scr"])
                        E_op("dve", lambda e, j=j: e.max(out=vals[:, j, 8:16], in_=scr2[:, j, :]), reads=["scr"], writes=["vals"])
                        E_op("dve", lambda e, j=j: e.max_index(out=idxu[:, j, 8:16], in_max=vals[:, j, 8:16], in_values=scr2[:, j, :]), reads=["scr", "vals"], writes=["idxu"])
                    E_op("dve", lambda e: e.tensor_copy(out=idxf[:], in_=idxu[:]), reads=["idxu"], writes=["idxf"])
                    v4 = vals[:].rearrange("p (h t) a -> p h t a", t=2)
                    i4 = idxf[:].rearrange("p (h t) a -> p h t a", t=2)
                    E_op("dve", lambda e, v4=v4: e.tensor_tensor(out=Cg[:].rearrange("p h (a b) -> p h a b", b=16), in0=v4[:, :, 0, :].unsqueeze(3).to_broadcast([128, 8, 16, 16]), in1=v4[:, :, 1, :].unsqueeze(2).to_broadcast([128, 8, 16, 16]), op=ALU.add), reads=["vals"], writes=["Cg"])
                    for h in range(8):
                        E_op("dve", lambda e, h=h: e.max(out=cv[:, h, 0:8], in_=Cg[:, h, :]), reads=["Cg"], writes=["cv"])
                        E_op("dve", lambda e, h=h: e.max_index(out=posu[:, h, 0:8], in_max=cv[:, h, 0:8], in_values=Cg[:, h, :]), reads=["Cg", "cv"], writes=["posu"])
                        E_op("dve", lambda e, h=h: e.match_replace(out=Cg2[:, h, :], in_to_replace=cv[:, h, 0:8], in_values=Cg[:, h, :], imm_value=-1e30), reads=["Cg", "cv"], writes=["Cg"])
                        E_op("dve", lambda e, h=h: e.max(out=cv[:, h, 8:16], in_=Cg2[:, h, :]), reads=["Cg"], writes=["cv"])
                        E_op("dve", lambda e, h=h: e.max_index(out=posu[:, h, 8:16], in_max=cv[:, h, 8:16], in_values=Cg2[:, h, :]), reads=["Cg", "cv"], writes=["posu"])
                    E_op("dve", lambda e: e.tensor_single_scalar(out=pa_u[:], in_=posu[:], scalar=4, op=ALU.logical_shift_right), reads=["posu"], writes=["pa_u"])
                    E_op("dve", lambda e: e.tensor_single_scalar(out=pb_u[:], in_=posu[:], scalar=15, op=ALU.bitwise_and), reads=["posu"], writes=["pb_u"])
                    E_op("dve", lambda e: e.tensor_copy(out=paf[:], in_=pa_u[:]), reads=["pa_u"], writes=["paf"])
                    E_op("dve", lambda e: e.tensor_copy(out=pbf[:], in_=pb_u[:]), reads=["pb_u"], writes=["pbf"])
                    io16 = iot[:, 0:16].unsqueeze(1).unsqueeze(1).to_broadcast([128, 8, 16, 16])
                    for (pf, pfk, plane, dst, dstk) in ((paf, "paf", 0, ik, "ik"), (pbf, "pbf", 1, jk, "jk")):
                        E_op("dve", lambda e, pf=pf: e.tensor_tensor(out=eq, in0=pf[:].unsqueeze(3).to_broadcast([128, 8, 16, 16]), in1=io16, op=ALU.is_equal), reads=[pfk, "iot"], writes=["scr"])
                        E_op("dve", lambda e, plane=plane, i4=i4: e.tensor_tensor(out=eq, in0=eq, in1=i4[:, :, plane, :].unsqueeze(2).to_broadcast([128, 8, 16, 16]), op=ALU.mult), reads=["scr", "idxf"], writes=["scr"])
                        E_op("dve", lambda e, dst=dst: e.tensor_reduce(out=dst[:], in_=eq, axis=AX.X, op=ALU.add), reads=["scr"], writes=[dstk])
                    E_op("dve", lambda e: e.tensor_tensor(out=ee[:], in0=cv[:], in1=cv[:, :, 0:1].to_broadcast([128, 8, 16]), op=ALU.subtract), reads=["cv"], writes=["ee"])
                    E_op("act", lambda e: e.activation(out=ee[:], in_=ee[:], func=AF.Exp), reads=["ee"], writes=["ee"])
                    E_op("dve", lambda e: e.tensor_reduce(out=zz[:], in_=ee[:], axis=AX.X, op=ALU.add), reads=["ee"], writes=["zz"])
                    E_op("dve", lambda e: e.reciprocal(out=zz[:], in_=zz[:]), reads=["zz"], writes=["zz"])
                    E_op("dve", lambda e: e.tensor_tensor(out=gg[:], in0=ee[:], in1=zz[:].unsqueeze(2).to_broadcast([128, 8, 16]), op=ALU.mult), reads=["ee", "zz"], writes=["gg"])
                    p_, pk = npw2()
                    for n_, (src, srck) in enumerate(((ik, "ik"), (jk, "jk"), (gg, "gg"))):
                        E_op("pe", lambda e, p_=p_, n_=n_, src=src: e.transpose(out=p_[:, n_ * 128:(n_ + 1) * 128], in_=src[:].rearrange("p h k -> p (h k)"), identity=idf[:]), reads=[srck, "idf"], writes=[pk])
                    E_op("dve", lambda e, p_=p_, s=s: e.tensor_copy(out=ikT[:, s * 128:(s + 1) * 128], in_=p_[:, 0:128]), reads=[pk], writes=["ikT"])
                    E_op("dve", lambda e, p_=p_, s=s: e.tensor_copy(out=jkT[:, s * 128:(s + 1) * 128], in_=p_[:, 128:256]), reads=[pk], writes=["jkT"])
                    E_op("dve", lambda e, p_=p_, s=s: e.tensor_copy(out=gT[:, s * 128:(s + 1) * 128], in_=p_[:, 256:384]), reads=[pk], writes=["gT"])
                for i_, (f_, a_, k_) in enumerate(thunks):
                    f_(*a_, **k_)
                    if i_ % 5 == 4:
                        yield

            sts = {}
            for _ in prep_gen(0, sts):
                pass
            for tk in range(ntk):
                h_, hk, cur_xT, cur_xTk = sts[tk]
                def g_gen(t4):
                    lr = []
                    for tq in range(4):
                        t = t4 * 4 + tq
                        l_, lk = Lt.nxt(); r_, rk = Rt.nxt()
                        em.op("dve", lambda e, l_=l_, t=t: e.tensor_scalar(out=l_[:], in0=iot[:], scalar1=ikT[:, t:t + 1], scalar2=gT[:, t:t + 1], op0=ALU.is_equal, op1=ALU.mult), reads=["iot", "ikT", "gT"], writes=[lk])
                        if t % 4 == 3:
                            em.op("dve", lambda e, r_=r_, t=t: e.tensor_scalar(out=r_[:], in0=iot[:], scalar1=jkT[:, t:t + 1], scalar2=None, op0=ALU.is_equal), reads=["iot", "jkT"], writes=[rk])
                        else:
                            ra_, rak = Ra.nxt()
                            em.op("act", lambda e, ra_=ra_, t=t: e.activation(out=ra_[:], in_=iot[:], func=AF.Abs, bias=njkT[:, t:t + 1]), reads=["iot", "njkT"], writes=[rak])
                            em.op("act", lambda e, ra_=ra_, r_=r_: e.activation(out=r_[:], in_=ra_[:], func=AF.Relu, scale=-1.0, bias=1.0), reads=[rak], writes=[rk])
                        lr.append((l_, lk, r_, rk))
                    yield
                    p_, pk = npw()
                    for tq in range(4):
                        l_, lk, r_, rk = lr[tq]
                        em.op("pe", lambda e, tq=tq, l_=l_, r_=r_: e.matmul(p_[:, tq * 128:(tq + 1) * 128], lhsT=l_[:], rhs=r_[:], start=True, stop=True), reads=[lk, rk], writes=[pk])
                    yield
                    em.op("dve", lambda e: e.tensor_copy(out=Gs[:, t4 * 4:(t4 + 1) * 4, :].rearrange("p a b -> p (a b)"), in_=p_[:]), reads=[pk], writes=["Gs"])
                em.op("dve", lambda e: e.tensor_scalar(out=njkT[:], in0=jkT[:], scalar1=-1.0, scalar2=None, op0=ALU.mult), reads=["jkT"], writes=["njkT"])
                run_pipeline(g_gen(t4) for t4 in range(TK // 4))
                def dense_gen(j, xT_=cur_xT, xTk_=cur_xTk):
                    u_, uk = utj.nxt(); v_, vk = vj.nxt()
                    em.dma("sp", lambda e: e.dma_start(out=u_[:], in_=UTS[j]), writes=[uk])
                    em.dma("sp", lambda e: e.dma_start(out=v_[:], in_=VBF.rearrange("(i j) d -> j i d", j=128)[j]), writes=[vk])
                    yield
                    yield
                    p_, pk = npw_d()
                    for k in range(8):
                        em.op("pe", lambda e, k=k: e.matmul(p_[:, 0:TK], lhsT=u_[:, k, :], rhs=xT_[:, k, :], start=(k == 0), stop=(k == 7)), reads=[uk, xTk_], writes=[pk])
                    yield
                    ga_, gak = gact.nxt(); a_, ak = ATj.nxt()
                    em.op("act", lambda e: e.activation(out=ga_[:], in_=p_[:, 0:TK], func=AF.Gelu_apprx_tanh), reads=[pk], writes=[gak])
                    yield
                    em.op("dve" if j % 2 else "pool", lambda e: e.tensor_tensor(out=a_[:], in0=ga_[:], in1=Gs[:, :, j], op=ALU.mult), reads=[gak, "Gs"], writes=[ak])
                    yield
                    for s in range(2):
                        for hf in range(2):
                            em.op("pe", lambda e, s=s, hf=hf: e.matmul(acc[s * 2 + hf][:], lhsT=a_[:, s * 128:(s + 1) * 128], rhs=v_[:, hf * 512:(hf + 1) * 512], start=(j == 0), stop=(j == 127)), reads=[ak, vk], writes=[f"acc{s * 2 + hf}"])
                import itertools
                nxt_prep = [prep_gen(tk + 1, sts)] if tk + 1 < ntk else []
                run_pipeline(itertools.chain((dense_gen(j) for j in range(128)), nxt_prep) if os.environ.get("NOOVL") else itertools.chain(nxt_prep, (dense_gen(j) for j in range(128))), max_active=24)
                for s in range(2):
                    for hf in range(2):
                        em.op("dve", lambda e, s=s, hf=hf, h_=h_: e.tensor_tensor(out=h_[:, s, hf * 512:(hf + 1) * 512], in0=acc[s * 2 + hf][:], in1=h_[:, s, hf * 512:(hf + 1) * 512], op=ALU.add), reads=[f"acc{s * 2 + hf}", hk], writes=[hk])
                for s in range(2):
                    em.op("act", lambda e, s=s, h_=h_: e.activation(out=sqj, in_=h_[:, s, :], func=AF.Square, accum_out=ss5[:, s:s + 1]), reads=[hk], writes=["xn21", "ss5"])
                em.op("act", lambda e: e.activation(out=rstd5[:], in_=ss5[:], func=AF.Ln, scale=1.0 / D, bias=EPS), reads=["ss5"], writes=["rstd5"])
                em.op("act", lambda e: e.activation(out=rstd5[:], in_=rstd5[:], func=AF.Exp, scale=-0.5), reads=["rstd5"], writes=["rstd5"])
                for s in range(2):
                    em.op("dve", lambda e, s=s, h_=h_: e.scalar_tensor_tensor(out=xn2[:, s, :], in0=h_[:, s, :], scalar=rstd5[:, s:s + 1], in1=gfin[:], op0=ALU.mult, op1=ALU.mult), reads=[hk, "rstd5", "gfin"], writes=[f"xn2{s}"])
                em.store("sp", lambda e, tk=tk: e.dma_start(out=out[tk * TK:(tk + 1) * TK, :].rearrange("(s p) d -> p s d", p=128), in_=xn2[:]), reads=["xn20", "xn21"], writes=[f"out{tk}"])
            em.flush()
        print("inst", em.n_inst, "waits", em.n_wait)
    return nc


_IN_NAMES = None


def core_inputs(inp, core):
    b, g = core // 2, core % 2
    m = dict(host_consts())
    xb = inp["x"][b]
    w_in = inp["w_in"][0]
    conv_w = inp["hyena_conv_w"][0]
    w3 = inp["filt_w3"][0]
    dec = inp["filt_decay"].reshape(2048)
    if g == 1:
        xb = xb[::-1]
        w_in = np.concatenate([w_in[:, 0:1024], w_in[:, 2048:3072], w_in[:, 1024:2048], w_in[:, 3072:]], axis=1)
        conv_w = conv_w[::-1]
        w3 = np.concatenate([w3[:, 1024:], w3[:, :1024]], axis=1)
        dec = np.concatenate([dec[1024:], dec[:1024]])
    m["x"] = np.ascontiguousarray(xb)
    m["xh"] = np.ascontiguousarray(xb[:L // 2])
    sel = np.zeros((128, 2), np.float32); sel[:, g] = 1.0
    m["sel"] = sel
    m["norm_mix_g"] = np.ascontiguousarray(inp["norm_mix_g"].reshape(1, D))
    m["w_in"] = np.ascontiguousarray(w_in)
    m["hgrn_lb_logits"] = np.ascontiguousarray(inp["hgrn_lb_logits"])
    m["filt_w1"] = np.ascontiguousarray(inp["filt_w1"][0]); m["filt_w2"] = np.ascontiguousarray(inp["filt_w2"][0])
    m["filt_w3"] = np.ascontiguousarray(w3)
    m["filt_vec"] = np.ascontiguousarray(np.stack([inp["filt_b1"][0], inp["filt_freq1"][0], inp["filt_b2"][0], inp["filt_freq2"][0]], 0))
    m["filt_decay"] = np.ascontiguousarray(dec.reshape(1, 2048))
    m["hyena_bias"] = np.ascontiguousarray(inp["hyena_bias"].reshape(1, 1024))
    m["conv_w"] = np.ascontiguousarray(conv_w); m["conv_b"] = np.ascontiguousarray(inp["hyena_conv_b"].reshape(1, 3072))
    m["hgrn_norm_g"] = np.ascontiguousarray(inp["hgrn_norm_g"].reshape(1, 128))
    for k in ("w_branch_a", "w_branch_b", "w_out", "peer_w_q", "peer_u", "peer_v"):
        m[k] = np.ascontiguousarray(inp[k][0])
    m["norm_ffn_g"] = np.ascontiguousarray(inp["norm_ffn_g"].reshape(1, D)); m["norm_final_g"] = np.ascontiguousarray(inp["norm_final_g"].reshape(1, D))
    m["peer_sk"] = np.ascontiguousarray(inp["peer_subkeys"][0].reshape(16, 128, 128))
    return m


def kernel(**inputs):
    inp = {k: np.asarray(v) for k, v in inputs.items()}
    nc = build()
    in_maps = [core_inputs(inp, c) for c in range(8)]
    res = run_bass_kernel_spmd(nc, in_maps, core_ids=list(range(8)))
    out = np.zeros((4, L, D), np.float32)
    for c in range(8):
        b, g = c // 2, c % 2
        r = np.asarray(res.results[c]["out"])
        if g == 0:
            out[b, :L // 2] = r
        else:
            out[b, L // 2:] = r[::-1]
    return out
```

```python
import math
import numpy as np
from contextlib import ExitStack
import concourse.bass as bass
import concourse.mybir as mybir
from concourse.bass_utils import run_bass_kernel_spmd

F32 = mybir.dt.float32
BF16 = mybir.dt.bfloat16
U32 = mybir.dt.uint32
ALU = mybir.AluOpType
AF = mybir.ActivationFunctionType
AX = mybir.AxisListType

L = 8192
D = 1024
NCOL = 10240
TT = 512
NTILE = L // TT
EPS = 1e-6

ENGS = ("pe", "act", "dve", "pool", "sp")
ENGMAP = {"pe": "tensor", "act": "scalar", "dve": "vector", "pool": "gpsimd", "sp": "sync"}
SEM_LIMIT = 30000
NDMA = 32


class Em:
    def __init__(self, nc, es):
        self.nc = nc
        self.es = es
        self.q = {e: [] for e in ENGS}
        self.sems = {e: [es.enter_context(nc.semaphore(f"s_{e}_0"))] for e in ENGS}
        self.cnt = {e: 0 for e in ENGS}
        self.dsem = [es.enter_context(nc.semaphore(f"d_{i}")) for i in range(NDMA)]
        self.dcnt = [0] * NDMA
        self.dnext = 0
        self.waited = {e: {} for e in ENGS}
        self.lastw = {}
        self.readers = {}
        self.n_inst = 0
        self.n_wait = 0
        self.pending = []

    def _tok_new(self, eng):
        if self.cnt[eng] >= SEM_LIMIT:
            self.sems[eng].append(
                self.es.enter_context(self.nc.semaphore(f"s_{eng}_{len(self.sems[eng])}")))
            self.cnt[eng] = 0
        self.cnt[eng] += 1
        return (self.sems[eng][-1], self.cnt[eng])

    NOKEYS = frozenset(["WIN", "QD0", "QD1", "KD0", "KD1", "KDTM0", "KDTM1", "VT", "OGT", "HYT", "GT", "HF", "UT",
                        "X0T", "YCT", "OT", "AT", "DEC0", "DEC1", "UTS", "VBF", "UBF"])

    def _deps(self, reads, writes):
        reads = [k for k in reads if k not in self.NOKEYS]
        writes = [k for k in writes if k not in self.NOKEYS]
        deps = []
        for k in reads:
            lw = self.lastw.get(k)
            if lw is not None:
                deps.append(lw)
        for k in writes:
            lw = self.lastw.get(k)
            if lw is not None:
                deps.append(lw)
            deps.extend(self.readers.get(k, ()))
        return deps

    def _emit_waits(self, eng, deps, skip_sems=()):
        w = self.waited[eng]
        need = {}
        for (sem, val) in deps:
            sid = id(sem)
            if sid in skip_sems:
                continue
            if w.get(sid, 0) >= val:
                continue
            if sid not in need or need[sid][1] < val:
                need[sid] = (sem, val)
        for sid, (sem, val) in need.items():
            w[sid] = val
            self.q[eng].append(("wait", sem, val))
            self.n_wait += 1

    def _record(self, tok, reads, writes):
        reads = [k for k in reads if k not in self.NOKEYS]
        writes = [k for k in writes if k not in self.NOKEYS]
        for k in reads:
            self.readers.setdefault(k, []).append(tok)
        for k in writes:
            self.lastw[k] = tok
            self.readers[k] = []

    NO_SELF_WAIT = ("pe",)
    STORE_DELAY = 48

    def _pending_tick(self, reads, writes, force=False):
        if not self.pending:
            return
        keep = []
        ws = set(writes)
        rs = set(reads)
        for p in self.pending:
            p[0] -= 1
            if force or p[0] <= 0 or (ws and (ws.intersection(p[3]) or ws.intersection(p[4]))) or (rs and rs.intersection(p[4])):
                self._dma_now(p[1], p[2], p[3], p[4])
            else:
                keep.append(p)
        self.pending = keep

    def store(self, eng, fn, reads=(), writes=()):
        self.pending.append([self.STORE_DELAY, eng, fn, list(reads), list(writes)])

    def op(self, eng, fn, reads=(), writes=()):
        self._pending_tick(reads, writes)
        deps = self._deps(reads, writes)
        skip = tuple(id(s) for s in self.sems[eng]) if eng in self.NO_SELF_WAIT else ()
        self._emit_waits(eng, deps, skip)
        tok = self._tok_new(eng)
        self.q[eng].append(("op", fn, tok, 1))
        self._record(tok, reads, writes)
        self.n_inst += 1
        return tok

    def dma(self, eng, fn, reads=(), writes=()):
        self._pending_tick(reads, writes)
        self._dma_now(eng, fn, reads, writes)

    def _dma_now(self, eng, fn, reads=(), writes=()):
        deps = self._deps(reads, writes)
        slot = self.dnext
        self.dnext = (self.dnext + 1) % NDMA
        if self.dcnt[slot] > 0:
            deps.append((self.dsem[slot], self.dcnt[slot]))
        self._emit_waits(eng, deps)
        self.dcnt[slot] += 16
        tok = (self.dsem[slot], self.dcnt[slot])
        self.q[eng].append(("op", fn, tok, 16))
        self._record(tok, reads, writes)
        self.n_inst += 1
        return tok

    def flush(self):
        nc = self.nc
        self._pending_tick((), (), force=True)
        final = []
        for i in range(NDMA):
            if self.dcnt[i]:
                final.append((self.dsem[i], self.dcnt[i]))
        for e in ENGS:
            if self.cnt[e]:
                final.append((self.sems[e][-1], self.cnt[e]))
        self._emit_waits("sp", final)
        with nc.Block() as block:
            for e in ENGS:
                items = self.q[e]

                def body(engine, items=items):
                    for it in items:
                        if it[0] == "wait":
                            engine.wait_ge(it[1], it[2])
                        else:
                            it[1](engine).then_inc(it[2][0], it[3])
                getattr(block, ENGMAP[e])(body)
        self.q = {e: [] for e in ENGS}
        self.lastw = {}
        self.readers = {}


def run_pipeline(gens, max_new_per_step=1, max_active=16):
    it = iter(gens)
    active = []
    exhausted = False
    while True:
        if not exhausted and len(active) < max_active:
            try:
                active.append(next(it))
            except StopIteration:
                exhausted = True
        if not active:
            if exhausted:
                break
            continue
        nxt = []
        for g in active:
            try:
                next(g)
                nxt.append(g)
            except StopIteration:
                pass
        active = nxt


class Rot:
    def __init__(self, sbf, name, n, shape, dt):
        self.t = [sbf(f"{name}{i}", shape, dt) for i in range(n)]
        self.k = [f"{name}{i}" for i in range(n)]
        self.i = -1

    def nxt(self):
        self.i = (self.i + 1) % len(self.t)
        return self.t[self.i], self.k[self.i]


def host_consts():
    c = {}
    c["ident"] = np.eye(128, dtype=np.float32)
    s = np.arange(64)
    mf = (s[:, None] <= s[None, :]).astype(np.float32)
    mb = (s[:, None] >= s[None, :]).astype(np.float32)
    c["maskf"] = np.ascontiguousarray(np.broadcast_to(mf[:, None, :], (64, 8, 64))).reshape(64, 512)
    c["maskb"] = np.ascontiguousarray(np.broadcast_to(mb[:, None, :], (64, 8, 64))).reshape(64, 512)
    t = np.arange(TT)
    rf = np.ones((128, TT), np.float32); rf[:, t % 64 == 0] = 0
    rb = np.ones((128, TT), np.float32); rb[:, t % 64 == 63] = 0
    c["rmf"] = rf
    c["rmb"] = rb
    n = np.arange(128, dtype=np.float64)
    ang = 2 * np.pi * np.outer(n, n) / 128.0
    Fr = np.cos(ang); Fi = -np.sin(ang)
    c["FRI"] = np.concatenate([Fr, Fi], 1).astype(np.float32)
    c["FRnI"] = np.concatenate([Fr, -Fi], 1).astype(np.float32)
    c["FIR"] = np.concatenate([Fi, Fr], 1).astype(np.float32)
    c["nFI"] = (-Fi).astype(np.float32)
    angt = 2 * np.pi * np.outer(n, n) / 16384.0
    Tr = np.cos(angt); Ti = -np.sin(angt)
    c["TW"] = np.concatenate([Tr, Ti], 1).astype(np.float32)
    c["TWc"] = np.concatenate([Tr, -Ti], 1).astype(np.float32)
    pos = np.arange(L, dtype=np.float32)
    tpos = pos / np.float32(L - 1)
    bands = np.linspace(1e-4, 15, 16, dtype=np.float32)
    angz = (np.float32(2.0 * math.pi / L) * pos[:, None]) * bands[None, :]
    z = np.concatenate([tpos[:, None], np.cos(angz), -np.sin(angz)], -1).astype(np.float32)
    c["zT"] = np.ascontiguousarray(z.T)
    c["tpos"] = np.ascontiguousarray(tpos.reshape(1, L))
    c["iota"] = np.ascontiguousarray(np.broadcast_to(np.arange(128, dtype=np.float32)[None, :], (128, 128)))
    return c


def build(stop_after=99, dbg=(), skip=(), ext_in=()):
    nc = bass.Bass("TRN2", target_bir_lowering=False)
    _uqc = [0]

    def uq(n):
        _uqc[0] += 1
        return f"{n}_u{_uqc[0]}"
    EI = dict(kind="ExternalInput")
    def din(name, shape, dt=F32):
        return nc.dram_tensor(name, list(shape), dt, **EI).ap()
    def dscr(name, shape, dt):
        kind = "ExternalOutput" if name in dbg else ("ExternalInput" if name in ext_in else "Internal")
        return nc.dram_tensor(name, list(shape), dt, kind=kind).ap()

    x = din("x", [L, D])
    xh = din("xh", [L // 2, D])
    norm_mix_g = din("norm_mix_g", [1, D])
    w_in = din("w_in", [D, NCOL])
    lbl = din("hgrn_lb_logits", [2, D])
    ident = din("ident", [128, 128])
    maskf = din("maskf", [64, 512]); maskb = din("maskb", [64, 512])
    rmf = din("rmf", [128, TT]); rmb = din("rmb", [128, TT])
    out = nc.dram_tensor("out", [L // 2, D], F32, kind="ExternalOutput").ap()

    WIN = dscr("WIN", [D, NCOL], BF16)
    QD = [dscr(f"QD{d}", [8, 128, L], BF16) for d in range(2)]
    KD = [dscr(f"KD{d}", [8, 128, L], BF16) for d in range(2)]
    KDTM = [dscr(f"KDTM{d}", [L, D], BF16) for d in range(2)]
    DEC = [dscr(f"DEC{d}", [128, 8, 128], F32) for d in range(2)]
    VT = dscr("VT", [L, D], BF16)
    OGT = dscr("OGT", [D, L], BF16)
    HYT = dscr("HYT", [3 * D, L], F32)
    GT = dscr("GT", [2 * D, L], BF16)
    OF = dscr("OF", [8, 128, L], F32)
    OT = dscr("OT", [8, 128, L], F32)
    filt_w1 = din("filt_w1", [33, 64]); filt_w2 = din("filt_w2", [64, 64]); filt_w3 = din("filt_w3", [64, 2048])
    filt_vec = din("filt_vec", [4, 64])
    filt_decay = din("filt_decay", [1, 2048]); hyena_bias = din("hyena_bias", [1, 1024])
    conv_w = din("conv_w", [3, 3072]); conv_b = din("conv_b", [1, 3072])
    zT = din("zT", [33, L]); tpos = din("tpos", [1, L]); sel = din("sel", [128, 2])
    FRI = din("FRI", [128, 256]); FRnI = din("FRnI", [128, 256]); FIR = din("FIR", [128, 256]); nFI = din("nFI", [128, 128])
    TW = din("TW", [128, 256]); TWc = din("TWc", [128, 256])
    HF = dscr("HF", [2048, L], BF16)
    UT = dscr("UT", [D, L], BF16)
    X0T = dscr("X0T", [D, L], F32)
    KF = dscr("KF", [D, 128, 2, 128], F32)
    YCT = dscr("YCT", [D, L], F32)
    hgrn_norm_g = din("hgrn_norm_g", [1, 128])
    w_branch_a = din("w_branch_a", [D, D]); w_branch_b = din("w_branch_b", [D, D]); w_out = din("w_out", [D, D])
    norm_ffn_g = din("norm_ffn_g", [1, D]); norm_final_g = din("norm_final_g", [1, D])
    peer_w_q = din("peer_w_q", [D, 2048]); peer_sk = din("peer_sk", [16, 128, 128])
    peer_u = din("peer_u", [16384, D]); peer_v = din("peer_v", [16384, D])
    iota = din("iota", [128, 128])
    AT = dscr("AT", [D, L // 2], BF16) if "AT" in dbg else None
    MG = dscr("MG", [D, L // 2], BF16) if "MG" in dbg else None
    PEERO = dscr("PEERO", [L // 2, D], F32) if "PEERO" in dbg else None
    H1 = dscr("H1", [L // 2, D], F32)
    VBF = dscr("VBF", [16384, D], BF16)
    UTS = dscr("UTS", [128, 128, 8, 128], BF16)
    TOK0 = 0
    OTh = OT[:, :, 0:L // 2]; OGTh = OGT[:, 0:L // 2]; X0Th = X0T[:, 0:L // 2]; YCTh = YCT[:, 0:L // 2]; GTh = GT[:, 0:L // 2]

    es0 = ExitStack()
    with es0:
        em = Em(nc, es0)

        for r in range(8):
            em.dma("pool", lambda e, r=r: e.dma_start(out=WIN[r * 128:(r + 1) * 128, :], in_=w_in[r * 128:(r + 1) * 128, :]),
                   writes=["WIN"])
        em.flush()
        if stop_after <= 0:
            return nc

        if 1 not in skip:
          with ExitStack() as es:
              sbf = lambda n, s, d: es.enter_context(nc.sbuf_tensor(uq(n), list(s), d))
              psf = lambda n, s, d: es.enter_context(nc.psum_tensor(uq(n), list(s), d))
              gt = sbf("gt", [128, D], F32)
              idf = sbf("idf", [128, 128], F32)
              idb = sbf("idb", [128, 128], BF16)
              lb2 = sbf("lb2", [128, 2, 8], F32)
              lbt = sbf("lbt", [128, 8], F32)
              olt = sbf("olt", [128, 8], F32)
              nolt = sbf("nolt", [128, 8], F32)
              rmf_t = sbf("rmf_t", [128, TT], F32)
              rmb_t = sbf("rmb_t", [128, TT], F32)
              dec_t = [sbf(f"dec_t{d}", [128, 8, 128], F32) for d in range(2)]
              xt = sbf("xt", [128, 4, D], F32)
              sqj = sbf("sqj", [128, D], F32)
              ss = sbf("ss", [128, 4], F32)
              rstd = sbf("rstd", [128, 4], F32)
              xn = sbf("xn", [128, 4, D], BF16)
              xnT = Rot(sbf, "xnT", 2, [128, 8, TT], BF16)
              wg = Rot(sbf, "wg", 3, [128, 8, 1024], BF16)
              QS = sbf("QS", [128, 8, TT], F32)
              SG = Rot(sbf, "SG", 3, [128, TT], F32)
              KK = Rot(sbf, "KK", 6, [128, TT], F32)
              LF = Rot(sbf, "LF", 3, [128, TT], F32)
              BB = Rot(sbf, "BB", 3, [128, TT], F32)
              E1 = Rot(sbf, "E1", 3, [128, TT], F32)
              E2 = Rot(sbf, "E2", 3, [128, TT], F32)
              QDs = Rot(sbf, "QDs", 4, [128, TT], BF16)
              KDs = Rot(sbf, "KDs", 5, [128, TT], BF16)
              KTs = Rot(sbf, "KTs", 3, [128, 4, 128], BF16)
              OB = Rot(sbf, "OB", 3, [128, TT], BF16)
              OFt = Rot(sbf, "OFt", 3, [128, TT], F32)
              pTs = [psf(f"pT{i}", [128, 8, 128], BF16) for i in range(2)]
              pM = [psf(f"pM{i}", [128, TT], F32) for i in range(4)]
              pK = [psf(f"pK{i}", [128, 4, 128], BF16) for i in range(2)]
              pmi = [0]

              em.dma("sp", lambda e: e.dma_start(out=gt[:], in_=norm_mix_g.partition_broadcast(128)), writes=["gt"])
              em.dma("sp", lambda e: e.dma_start(out=idf[:], in_=ident), writes=["idf"])
              em.dma("sp", lambda e: e.dma_start(out=rmf_t[:], in_=rmf), writes=["rmf"])
              em.dma("sp", lambda e: e.dma_start(out=rmb_t[:], in_=rmb), writes=["rmb"])
              em.dma("sp", lambda e: e.dma_start(out=lb2[:], in_=lbl.rearrange("t (h k) -> k t h", k=128), allow_slow_non_contiguous=True), writes=["lb2"])
              em.op("dve", lambda e: e.tensor_copy(out=idb[:], in_=idf[:]), reads=["idf"], writes=["idb"])
              em.op("dve", lambda e: e.tensor_tensor(out=lbt[:], in0=lb2[:, 1, :], in1=lb2[:, 0, :], op=ALU.subtract), reads=["lb2"], writes=["lbt"])
              em.op("act", lambda e: e.activation(out=lbt[:], in_=lbt[:], func=AF.Exp), reads=["lbt"], writes=["lbt"])
              em.op("dve", lambda e: e.tensor_scalar(out=lbt[:], in0=lbt[:], scalar1=1.0, scalar2=None, op0=ALU.add), reads=["lbt"], writes=["lbt"])
              em.op("dve", lambda e: e.reciprocal(out=lbt[:], in_=lbt[:]), reads=["lbt"], writes=["lbt"])
              em.op("dve", lambda e: e.tensor_scalar(out=olt[:], in0=lbt[:], scalar1=-1.0, scalar2=1.0, op0=ALU.mult, op1=ALU.add), reads=["lbt"], writes=["olt"])
              em.op("dve", lambda e: e.tensor_scalar(out=nolt[:], in0=olt[:], scalar1=-1.0, scalar2=None, op0=ALU.mult), reads=["olt"], writes=["nolt"])

              def next_pm():
                  pmi[0] = (pmi[0] + 1) % 4
                  return pM[pmi[0]], f"pM{pmi[0]}"

              ntile = NTILE if stop_after > 1 else 1
              HALF = NTILE // 2
              wcur = {}

              def prologue_gen(tt):
                  t0 = tt * TT
                  em.dma("sp", lambda e: e.dma_start(out=xt[:], in_=x[t0:t0 + TT, :].rearrange("(s p) d -> p s d", p=128)), writes=["xt"])
                  yield
                  for s in range(4):
                      em.op("act", lambda e, s=s: e.activation(out=sqj[:], in_=xt[:, s, :], func=AF.Square, accum_out=ss[:, s:s + 1]),
                            reads=["xt"], writes=["sqj", "ss"])
                  em.op("act", lambda e: e.activation(out=rstd[:], in_=ss[:], func=AF.Ln, scale=1.0 / D, bias=EPS), reads=["ss"], writes=["rstd"])
                  em.op("act", lambda e: e.activation(out=rstd[:], in_=rstd[:], func=AF.Exp, scale=-0.5), reads=["rstd"], writes=["rstd"])
                  yield
                  xT, xTk = xnT.nxt()
                  wcur[("xT", tt)] = (xT, xTk)
                  for s in range(4):
                      em.op("dve", lambda e, s=s: e.scalar_tensor_tensor(out=xn[:, s, :], in0=xt[:, s, :], scalar=rstd[:, s:s + 1], in1=gt[:], op0=ALU.mult, op1=ALU.mult),
                            reads=["xt", "rstd", "gt"], writes=[f"xn{s}"])
                  yield
                  for s in range(4):
                      pt_ = pTs[s % 2]; ptk = f"pT{s % 2}"
                      for k in range(8):
                          em.op("pe", lambda e, s=s, k=k, pt_=pt_: e.transpose(out=pt_[:, k, :], in_=xn[:, s, k * 128:(k + 1) * 128], identity=idb[:]),
                                reads=[f"xn{s}", "idb"], writes=[ptk])
                      em.op("act" if s % 2 else "dve", lambda e, s=s, pt_=pt_: (e.activation(out=xT[:, :, s * 128:(s + 1) * 128], in_=pt_[:], func=AF.Copy) if s % 2 else e.tensor_copy(out=xT[:, :, s * 128:(s + 1) * 128], in_=pt_[:])),
                            reads=[ptk], writes=[xTk])
                  yield

              def wload_gen(tt, g):
                  w, wk = wg.nxt()
                  wcur[(tt, g)] = (w, wk)
                  em.dma("sp", lambda e: e.dma_start(out=w[:], in_=WIN[:, g * 1024:(g + 1) * 1024].rearrange("(k p) c -> p k c", p=128)), writes=[wk])
                  yield

              def vblock_gen(tt, s, hf):
                  t0 = tt * TT
                  w, wk = wcur[(tt, 3)]; xT, xTk = wcur[("xT", tt)]
                  pm, pmk = next_pm()
                  for k in range(8):
                      em.op("pe", lambda e, k=k: e.matmul(pm[:], lhsT=xT[:, k, s * 128:(s + 1) * 128], rhs=w[:, k, hf * 512:(hf + 1) * 512], start=(k == 0), stop=(k == 7)),
                            reads=[xTk, wk], writes=[pmk])
                  yield
                  ob, obk = OB.nxt()
                  em.op("dve", lambda e: e.tensor_copy(out=ob[:], in_=pm[:]), reads=[pmk], writes=[obk])
                  em.store("sp", lambda e: e.dma_start(out=VT[t0 + s * 128:t0 + (s + 1) * 128, hf * 512:(hf + 1) * 512], in_=ob[:]), reads=[obk], writes=["VT"])

              def block_gen(tt, g, cb):
                  t0 = tt * TT
                  full = tt < HALF
                  w, wk = wcur[(tt, g)]; xT, xTk = wcur[("xT", tt)]
                  pm, pmk = next_pm()
                  for k in range(8):
                      em.op("pe", lambda e, k=k: e.matmul(pm[:], lhsT=w[:, k, cb * 128:(cb + 1) * 128], rhs=xT[:, k, :], start=(k == 0), stop=(k == 7)),
                            reads=[xTk, wk], writes=[pmk])
                  yield
                  if g == 0:
                      em.op("act", lambda e: e.activation(out=QS[:, cb, :], in_=pm[:], func=AF.Silu), reads=[pmk], writes=[f"QS{cb}"])
                  elif g in (1, 2):
                      d = g - 1
                      h = cb
                      sg, sgk = SG.nxt(); kk, kkk = KK.nxt(); lf, lfk = LF.nxt(); bb, bbk = BB.nxt()
                      e1, e1k = E1.nxt(); e2, e2k = E2.nxt(); qd, qdk = QDs.nxt(); kd, kdk = KDs.nxt()
                      kt, ktk = KTs.nxt()
                      em.op("act", lambda e: e.activation(out=sg[:], in_=pm[:], func=AF.Sigmoid), reads=[pmk], writes=[sgk])
                      yield
                      em.op("dve", lambda e: e.tensor_scalar(out=kk[:], in0=sg[:], scalar1=nolt[:, h:h + 1], scalar2=olt[:, h:h + 1], op0=ALU.mult, op1=ALU.add),
                            reads=[sgk, "nolt", "olt"], writes=[kkk])
                      em.op("act", lambda e: e.activation(out=lf[:], in_=sg[:], func=AF.Ln, scale=olt[:, h:h + 1], bias=lbt[:, h:h + 1]),
                            reads=[sgk, "olt", "lbt"], writes=[lfk])
                      yield
                      if d == 0:
                          em.op("dve", lambda e: e.tensor_tensor_scan(out=bb[:], data0=rmf_t[:], data1=lf[:], initial=0.0, op0=ALU.mult, op1=ALU.add),
                                reads=[lfk, "rmf"], writes=[bbk])
                      else:
                          em.op("dve", lambda e: e.tensor_tensor_scan(out=bb[:, ::-1], data0=rmb_t[:, ::-1], data1=lf[:, ::-1], initial=0.0, op0=ALU.mult, op1=ALU.add),
                                reads=[lfk, "rmb"], writes=[bbk])
                      yield
                      em.op("act", lambda e: e.activation(out=e1[:], in_=bb[:], func=AF.Exp), reads=[bbk], writes=[e1k])
                      em.op("act", lambda e: e.activation(out=e2[:], in_=bb[:], func=AF.Exp, scale=-1.0), reads=[bbk], writes=[e2k])
                      yield
                      if full:
                          em.op("pool", lambda e: e.tensor_tensor(out=qd[:], in0=QS[:, h, :], in1=e1[:], op=ALU.mult), reads=[f"QS{h}", e1k], writes=[qdk])
                      em.op("pool", lambda e: e.tensor_tensor(out=kd[:], in0=kk[:], in1=e2[:], op=ALU.mult), reads=[kkk, e2k], writes=[kdk])
                      off = 63 if d == 0 else 0
                      em.op("dve", lambda e: e.tensor_copy(out=dec_t[d][:, h, tt * 8:(tt + 1) * 8], in_=e1[:, off::64]),
                            reads=[e1k], writes=[f"dec{d}"])
                      if full:
                          em.store("sp", lambda e: e.dma_start(out=QD[d][h, :, t0:t0 + TT], in_=qd[:]), reads=[qdk], writes=[f"QD{d}"])
                          em.store("sp", lambda e: e.dma_start(out=KD[d][h, :, t0:t0 + TT], in_=kd[:]), reads=[kdk], writes=[f"KD{d}"])
                      yield
                      pk = pK[h % 2]; pkk = f"pK{h % 2}"
                      for s in range(4):
                          em.op("pe", lambda e, s=s: e.transpose(out=pk[:, s, :], in_=kd[:, s * 128:(s + 1) * 128], identity=idb[:]),
                                reads=[kdk, "idb"], writes=[pkk])
                      yield
                      em.op("dve", lambda e: e.tensor_copy(out=kt[:], in_=pk[:]), reads=[pkk], writes=[ktk])
                      em.store("sp", lambda e: e.dma_start(out=KDTM[d][t0:t0 + TT, h * 128:(h + 1) * 128].rearrange("(s p) k -> p s k", p=128), in_=kt[:]),
                               reads=[ktk], writes=[f"KDTM{d}"])
                  elif g == 4:
                      ob, obk = OB.nxt()
                      em.op("act", lambda e: e.activation(out=ob[:], in_=pm[:], func=AF.Silu), reads=[pmk], writes=[obk])
                      em.store("sp", lambda e: e.dma_start(out=OGT[cb * 128:(cb + 1) * 128, t0:t0 + TT], in_=ob[:]), reads=[obk], writes=["OGT"])
                  elif g in (5, 6, 7):
                      of_, ofk = OFt.nxt()
                      r0 = (g - 5) * 1024 + cb * 128
                      em.op("dve" if cb % 2 else "act", lambda e: (e.tensor_copy(out=of_[:], in_=pm[:]) if cb % 2 else e.activation(out=of_[:], in_=pm[:], func=AF.Copy)), reads=[pmk], writes=[ofk])
                      em.store("sp", lambda e: e.dma_start(out=HYT[r0:r0 + 128, t0:t0 + TT], in_=of_[:]), reads=[ofk], writes=["HYT"])
                  else:
                      ob, obk = OB.nxt()
                      r0 = (g - 8) * 1024 + cb * 128
                      em.op("act", lambda e: e.activation(out=ob[:], in_=pm[:], func=AF.Sigmoid), reads=[pmk], writes=[obk])
                      em.store("sp", lambda e: e.dma_start(out=GT[r0:r0 + 128, t0:t0 + TT], in_=ob[:]), reads=[obk], writes=["GT"])

              def p1_items():
                  yield prologue_gen(0)
                  for tt in range(ntile):
                      groups = list(range(10)) if tt < HALF else [2, 3, 5, 6, 7]
                      yield wload_gen(tt, groups[0])
                      for gi, g in enumerate(groups):
                          if gi + 1 < len(groups):
                              yield wload_gen(tt, groups[gi + 1])
                          if gi == len(groups) // 2 and tt + 1 < ntile:
                              yield prologue_gen(tt + 1)
                          if g == 3:
                              for s in range(4):
                                  for hf in range(2):
                                      yield vblock_gen(tt, s, hf)
                          else:
                              for cb in range(8):
                                  yield block_gen(tt, g, cb)
              run_pipeline(p1_items())
              for d in range(2):
                  em.store("sp", lambda e, d=d: e.dma_start(out=DEC[d], in_=dec_t[d][:]), reads=[f"dec{d}"], writes=[f"DEC{d}"])
              em.flush()
        if stop_after <= 1:
            print("inst", em.n_inst, "waits", em.n_wait)
            return nc

        if 2 not in skip:
          with ExitStack() as es:
              sbf = lambda n, s, d: es.enter_context(nc.sbuf_tensor(uq(n), list(s), d))
              psf = lambda n, s, d: es.enter_context(nc.psum_tensor(uq(n), list(s), d))
              mk = [sbf("mkf", [64, 512], F32), sbf("mkb", [64, 512], F32)]
              dect = [sbf(f"dect{d}", [128, 8, 128], F32) for d in range(2)]
              S = sbf("S", [128, 8, 128], F32)
              Sb = sbf("Sb", [128, 8, 128], BF16)
              tmpS = sbf("tmpS", [128, 8, 128], F32)
              qdt = Rot(sbf, "qdt", 2, [128, 8, TT], BF16)
              kdt = Rot(sbf, "kdt", 2, [128, 8, TT], BF16)
              ktm = Rot(sbf, "ktm", 2, [64, 8, D], BF16)
              vtm = Rot(sbf, "vtm", 2, [64, 8, D], BF16)
              scb = Rot(sbf, "scb", 3, [64, 8, 64], BF16)
              Ot = Rot(sbf, "Ot", 2, [128, 8, TT], F32)
              Of = Rot(sbf, "Of", 3, [128, 8, TT], F32)
              pS = [psf(f"pS{i}", [64, 8, 64], F32) for i in range(2)]
              pO = [psf(f"pO{i}", [128, 8, 64], F32) for i in range(2)]
              pP = [psf(f"pP{i}", [128, 4, 128], F32) for i in range(4)]
              em.dma("sp", lambda e: e.dma_start(out=mk[0][:], in_=maskf), writes=["mk0"])
              em.dma("sp", lambda e: e.dma_start(out=mk[1][:], in_=maskb), writes=["mk1"])
              for d in range(2):
                  em.dma("sp", lambda e, d=d: e.dma_start(out=dect[d][:], in_=DEC[d]), writes=[f"dect{d}"])
              for d in range(2):
                  em.op("pool", lambda e: e.memset(S[:], 0.0), writes=["S"])
                  em.op("pool", lambda e: e.memset(Sb[:], 0.0), writes=["Sb"])
                  HALF = NTILE // 2
                  tiles = list(range(HALF)) if d == 0 else list(range(NTILE - 1, -1, -1))
                  tl = {}

                  def tload_gen(tt, d=d, tl=tl):
                      t0 = tt * TT
                      full = tt < HALF
                      kt_, ktk = ktm.nxt(); v_, vk = vtm.nxt()
                      ent = dict(kt=(kt_, ktk), v=(v_, vk))
                      em.dma("sp", lambda e: e.dma_start(out=kt_[:], in_=KDTM[d][t0:t0 + TT, :].rearrange("(c s) k -> s c k", s=64)), writes=[ktk])
                      em.dma("sp", lambda e: e.dma_start(out=v_[:], in_=VT[t0:t0 + TT, :].rearrange("(c s) k -> s c k", s=64)), writes=[vk])
                      if full:
                          q_, qk = qdt.nxt(); k_, kk_ = kdt.nxt(); o_, ok = Ot.nxt()
                          ent.update(q=(q_, qk), k=(k_, kk_), o=(o_, ok))
                          em.dma("sp", lambda e: e.dma_start(out=q_[:], in_=QD[d][:, :, t0:t0 + TT].rearrange("h k t -> k h t")), writes=[qk])
                          em.dma("sp", lambda e: e.dma_start(out=k_[:], in_=KD[d][:, :, t0:t0 + TT].rearrange("h k t -> k h t")), writes=[kk_])
                          if d == 1:
                              f_, fk = Of.nxt()
                              ent.update(f=(f_, fk))
                              em.dma("sp", lambda e: e.dma_start(out=f_[:], in_=OF[:, :, t0:t0 + TT].rearrange("h k t -> k h t")), reads=[f"OF{tt}"], writes=[fk])
                      tl[tt] = ent
                      yield

                  def chunk_gen(tt, c, last, d=d, tl=tl):
                      t0 = tt * TT
                      full = tt < HALF
                      ent = tl[tt]
                      kt_, ktk = ent["kt"]; v_, vk = ent["v"]
                      gc = tt * 8 + c
                      cs = slice(c * 64, (c + 1) * 64)
                      decb = dect[d][:, :, gc:gc + 1]
                      if full:
                          q_, qk = ent["q"]; k_, kk_ = ent["k"]; o_, ok = ent["o"]
                          ps_ = pS[gc % 2]; psk = f"pS{gc % 2}"
                          po_ = pO[gc % 2]; pok = f"pO{gc % 2}"
                          for h in range(8):
                              em.op("pe", lambda e, h=h: e.matmul(ps_[:, h, :], lhsT=k_[:, h, cs], rhs=q_[:, h, cs], start=True, stop=True),
                                    reads=[kk_, qk], writes=[psk])
                      pps = []
                      for hh in range(2):
                          pp = pP[(gc % 2) * 2 + hh]; ppk = f"pP{(gc % 2) * 2 + hh}"
                          pps.append((pp, ppk))
                          for h4 in range(4):
                              h = hh * 4 + h4
                              hs = slice(h * 128, (h + 1) * 128)
                              em.op("pe", lambda e, pp=pp, h4=h4, hs=hs: e.matmul(pp[:, h4, :], lhsT=kt_[:, c, hs], rhs=v_[:, c, hs], start=True, stop=True),
                                    reads=[ktk, vk], writes=[ppk])
                      yield
                      if full:
                          sb_, sbk = scb.nxt()
                          em.op("dve", lambda e: e.tensor_tensor(out=sb_[:], in0=ps_[:], in1=mk[d][:].rearrange("p (h t) -> p h t", h=8), op=ALU.mult),
                                reads=[psk, f"mk{d}"], writes=[sbk])
                          for h in range(8):
                              hs = slice(h * 128, (h + 1) * 128)
                              em.op("pe", lambda e, h=h, hs=hs: e.matmul(po_[:, h, :], lhsT=v_[:, c, hs], rhs=sb_[:, h, :], start=True, stop=False),
                                    reads=[vk, sbk], writes=[pok])
                              em.op("pe", lambda e, h=h: e.matmul(po_[:, h, :], lhsT=Sb[:, h, :], rhs=q_[:, h, cs], start=False, stop=True),
                                    reads=["Sb", qk], writes=[pok])
                      for hh in range(2):
                          pp, ppk = pps[hh]
                          h4s = slice(hh * 4, hh * 4 + 4)
                          em.op("dve", lambda e, pp=pp, h4s=h4s: e.tensor_tensor(out=S[:, h4s, :], in0=pp[:], in1=S[:, h4s, :], op=ALU.add),
                                reads=[ppk, "S"], writes=["S"])
                      yield
                      em.op("dve", lambda e: e.tensor_tensor(out=S[:], in0=S[:], in1=decb.to_broadcast([128, 8, 128]), op=ALU.mult),
                            reads=["S", f"dect{d}"], writes=["S"])
                      em.op("act", lambda e: e.activation(out=Sb[:], in_=S[:], func=AF.Copy), reads=["S"], writes=["Sb"])
                      if full:
                          if d == 0:
                              em.op("act", lambda e: e.activation(out=o_[:, :, cs], in_=po_[:], func=AF.Copy), reads=[pok], writes=[ok])
                          else:
                              f_, fk = ent["f"]
                              em.op("pool" if False else "dve", lambda e: e.tensor_tensor(out=o_[:, :, cs], in0=po_[:], in1=f_[:, :, cs], op=ALU.add), reads=[pok, fk], writes=[ok])
                          if last:
                              dst = OF if d == 0 else OT
                              em.store("sp", lambda e: e.dma_start(out=dst[:, :, t0:t0 + TT].rearrange("h k t -> k h t"), in_=o_[:]), reads=[ok], writes=[f"OF{tt}" if d == 0 else "OT"])

                  def p2_items(d=d, tiles=tiles):
                      yield tload_gen(tiles[0])
                      for ti, tt in enumerate(tiles):
                          if ti + 1 < len(tiles):
                              yield tload_gen(tiles[ti + 1])
                          chunks = list(range(8)) if d == 0 else list(range(7, -1, -1))
                          for ci, c in enumerate(chunks):
                              yield chunk_gen(tt, c, ci == 7)
                  run_pipeline(p2_items())
              em.flush()
        print("inst", em.n_inst, "waits", em.n_wait)
        if stop_after <= 2:
            return nc
        if 3 not in skip:
          with ExitStack() as es:
            sbf = lambda n, s, d: es.enter_context(nc.sbuf_tensor(uq(n), list(s), d))
            psf = lambda n, s, d: es.enter_context(nc.psum_tensor(uq(n), list(s), d))
            w1t = sbf("w1t", [33, 64], F32); w2t = sbf("w2t", [64, 64], F32); w3t = sbf("w3t", [64, 2048], F32)
            fb = sbf("fb", [64, 4], F32)
            fs = sbf("fs", [64, 4], F32)
            dcy = sbf("dcy", [128, 16], F32)
            hbias = sbf("hbias", [128, 8], F32)
            tpbs = Rot(sbf, "tpb", 2, [128, TT], F32)
            zt = Rot(sbf, "zt", 2, [33, TT], F32)
            ya = Rot(sbf, "ya", 3, [64, TT], F32)
            yb_ = Rot(sbf, "yb_", 3, [64, TT], F32)
            hd1 = Rot(sbf, "hd1", 2, [64, TT], F32)
            hd2 = Rot(sbf, "hd2", 2, [64, TT], F32)
            wn = Rot(sbf, "wn", 3, [128, TT], F32)
            fo = Rot(sbf, "fo", 3, [128, TT], F32)
            fob = Rot(sbf, "fob", 4, [128, TT], BF16)
            pF = [psf(f"pF{i}", [128, TT], F32) for i in range(3)]
            lag0 = sbf("lag0", [128, 16], F32); lagc = sbf("lagc", [128, 16], F32); lagb = sbf("lagb", [128, 16], BF16)
            selt = sbf("selt", [128, 2], F32)
            em.dma("sp", lambda e: e.dma_start(out=selt[:], in_=sel), writes=["selt"])
            em.dma("sp", lambda e: e.dma_start(out=w1t[:], in_=filt_w1), writes=["w1t"])
            em.dma("sp", lambda e: e.dma_start(out=w2t[:], in_=filt_w2), writes=["w2t"])
            em.dma("sp", lambda e: e.dma_start(out=w3t[:], in_=filt_w3), writes=["w3t"])
            em.dma("sp", lambda e: e.dma_start(out=fb[:], in_=filt_vec.rearrange("j k -> k j"), allow_slow_non_contiguous=True), writes=["fb"])
            em.dma("sp", lambda e: e.dma_start(out=dcy[:], in_=filt_decay.rearrange("o (b p) -> p (o b)", p=128), allow_slow_non_contiguous=True), writes=["dcy"])
            em.dma("sp", lambda e: e.dma_start(out=hbias[:], in_=hyena_bias.rearrange("o (b p) -> p (o b)", p=128), allow_slow_non_contiguous=True), writes=["hbias"])
            dcn = sbf("dcn", [128, 16], F32)
            em.op("dve", lambda e: e.tensor_scalar(out=dcn[:], in0=dcy[:], scalar1=-1.0, scalar2=None, op0=ALU.mult), reads=["dcy"], writes=["dcn"])
            em.op("dve", lambda e: e.tensor_tensor(out=dcy[:], in0=dcy[:], in1=dcn[:], op=ALU.min), reads=["dcy", "dcn"], writes=["dcy"])
            I2P = 1.0 / (2.0 * math.pi)
            for j in range(2):
                em.op("dve", lambda e, j=j: e.tensor_scalar(out=fs[:, 2 * j:2 * j + 1], in0=fb[:, 2 * j + 1:2 * j + 2], scalar1=I2P, scalar2=None, op0=ALU.mult), reads=["fb"], writes=["fs"])
                em.op("dve", lambda e, j=j: e.tensor_tensor(out=fs[:, 2 * j + 1:2 * j + 2], in0=fs[:, 2 * j:2 * j + 1], in1=fb[:, 2 * j:2 * j + 1], op=ALU.mult), reads=["fb", "fs"], writes=["fs"])
            MAGIC = 12582912.0

            def sin_layer(pm, pmk, j, hd, hdk):
                a, ak = ya.nxt(); b_, bk = yb_.nxt()
                em.op("dve", lambda e: e.tensor_scalar(out=a[:], in0=pm[0:64, :], scalar1=fs[:, 2 * j:2 * j + 1], scalar2=fs[:, 2 * j + 1:2 * j + 2], op0=ALU.mult, op1=ALU.add), reads=[pmk, "fs"], writes=[ak])
                em.op("dve", lambda e: e.tensor_scalar(out=b_[:], in0=a[:], scalar1=MAGIC, scalar2=None, op0=ALU.add), reads=[ak], writes=[bk])
                em.op("dve", lambda e: e.tensor_scalar(out=b_[:], in0=b_[:], scalar1=MAGIC, scalar2=None, op0=ALU.subtract), reads=[bk], writes=[bk])
                em.op("dve", lambda e: e.tensor_tensor(out=a[:], in0=a[:], in1=b_[:], op=ALU.subtract), reads=[ak, bk], writes=[ak])
                em.op("dve", lambda e: e.tensor_scalar(out=a[:], in0=a[:], scalar1=-0.499999, scalar2=0.499999, op0=ALU.max, op1=ALU.min), reads=[ak], writes=[ak])
                em.op("act", lambda e: e.activation(out=hd[:], in_=a[:], func=AF.Sin, scale=2.0 * math.pi), reads=[ak], writes=[hdk])

            pfc = [0]

            def npf():
                pfc[0] += 1
                return pF[pfc[0] % 3], f"pF{pfc[0] % 3}"
            tst = {}

            def sin_item(tt):
                t0 = tt * TT
                z_, zk = zt.nxt(); tp_, tpk = tpbs.nxt()
                em.dma("sp", lambda e: e.dma_start(out=z_[:], in_=zT[:, t0:t0 + TT]), writes=[zk])
                em.dma("sp", lambda e: e.dma_start(out=tp_[:], in_=tpos[:, t0:t0 + TT].partition_broadcast(128)), writes=[tpk])
                yield
                pm, pmk = npf()
                em.op("pe", lambda e: e.matmul(pm[0:64, :], lhsT=w1t[:], rhs=z_[:], start=True, stop=True), reads=["w1t", zk], writes=[pmk])
                h1_, h1k = hd1.nxt()
                sin_layer(pm, pmk, 0, h1_, h1k)
                yield
                pm2, pm2k = npf()
                em.op("pe", lambda e: e.matmul(pm2[0:64, :], lhsT=w2t[:], rhs=h1_[:], start=True, stop=True), reads=["w2t", h1k], writes=[pm2k])
                h2_, h2k = hd2.nxt()
                sin_layer(pm2, pm2k, 1, h2_, h2k)
                tst[tt] = (h2_, h2k, tp_, tpk)
                yield

            def cb_item(tt, cb):
                t0 = tt * TT
                h2_, h2k, tp_, tpk = tst[tt]
                pm, pmk = npf()
                em.op("pe", lambda e: e.matmul(pm[:], lhsT=w3t[:, cb * 128:(cb + 1) * 128], rhs=h2_[:], start=True, stop=True), reads=["w3t", h2k], writes=[pmk])
                w_, wk_ = wn.nxt(); f_, fk_ = fo.nxt(); fb_, fbk = fob.nxt()
                em.op("act", lambda e: e.activation(out=w_[:], in_=tp_[:], func=AF.Exp, scale=dcy[:, cb:cb + 1]), reads=[tpk, "dcy"], writes=[wk_])
                yield
                em.op("dve", lambda e: e.tensor_tensor(out=f_[:], in0=pm[:], in1=w_[:], op=ALU.mult), reads=[pmk, wk_], writes=[fk_])
                if tt == 0:
                    em.op("dve", lambda e: e.tensor_copy(out=lag0[:, cb:cb + 1], in_=f_[:, 0:1]), reads=[fk_], writes=["lag0"])
                yield
                em.op("pool" if cb % 2 else "act", lambda e: (e.tensor_copy(out=fb_[:], in_=f_[:]) if cb % 2 else e.activation(out=fb_[:], in_=f_[:], func=AF.Copy)), reads=[fk_], writes=[fbk])
                em.store("sp", lambda e: e.dma_start(out=HF[cb * 128:(cb + 1) * 128, t0:t0 + TT], in_=fb_[:]), reads=[fbk], writes=[f"HF0_{cb}" if tt == 0 else "HF"])

            def p3a_items():
                nt = NTILE if "3a" not in skip else 0
                if nt:
                    yield sin_item(0)
                for tt in range(nt):
                    if tt + 1 < nt:
                        yield sin_item(tt + 1)
                    for cb in range(16):
                        yield cb_item(tt, cb)
            run_pipeline(p3a_items())
            for tt in range(1 if "3a" not in skip else 0):
                if tt == 0:
                    em.op("dve", lambda e: e.memset(lagc[:], 0.0), writes=["lagc"])
                    em.op("dve", lambda e: e.tensor_scalar(out=lagc[:, 0:8], in0=lag0[:, 0:8], scalar1=selt[:, 0:1], scalar2=None, op0=ALU.mult), reads=["lag0", "selt"], writes=["lagc"])
                    em.op("dve", lambda e: e.scalar_tensor_tensor(out=lagc[:, 0:8], in0=lag0[:, 8:16], scalar=selt[:, 1:2], in1=lagc[:, 0:8], op0=ALU.mult, op1=ALU.add), reads=["lag0", "selt", "lagc"], writes=["lagc"])
                    em.op("dve", lambda e: e.tensor_tensor(out=lagc[:, 0:8], in0=lagc[:, 0:8], in1=hbias[:], op=ALU.add), reads=["lagc", "hbias"], writes=["lagc"])
                    em.op("dve", lambda e: e.tensor_copy(out=lagb[:], in_=lagc[:]), reads=["lagc"], writes=["lagb"])
                    em.dma("sp", lambda e: e.dma_start(out=HF.rearrange("(b p) t -> p b t", p=128)[:, :, 0:1], in_=lagb[:].unsqueeze(2), allow_slow_non_contiguous=True),
                           reads=["lagb"], writes=[f"HF0_{c_}" for c_ in range(16)])
            em.flush()

          with ExitStack() as es:
            sbf = lambda n, s, d: es.enter_context(nc.sbuf_tensor(uq(n), list(s), d))
            PW = 2048
            cw = sbf("cw", [128, 3, 24], F32)
            cbias = sbf("cbias", [128, 24], F32)
            hyin = [Rot(sbf, f"hyin{j}", 2, [128, PW + 2], F32) for j in range(3)]
            cv = [Rot(sbf, f"cv{j}", 2, [128, PW], F32) for j in range(3)]
            ub = Rot(sbf, "ub", 2, [128, PW], BF16)
            em.dma("sp", lambda e: e.dma_start(out=cw[:], in_=conv_w.rearrange("j (b p) -> p j b", p=128), allow_slow_non_contiguous=True), writes=["cw"])
            em.dma("sp", lambda e: e.dma_start(out=cbias[:], in_=conv_b.rearrange("o (b p) -> p (o b)", p=128), allow_slow_non_contiguous=True), writes=["cbias"])
            for cb in range(8 if "3b" not in skip else 0):
                for pc in range(L // PW):
                    t0 = pc * PW
                    outs = []
                    for j in range(3):
                        hy_, hyk = hyin[j].nxt(); c_, ck = cv[j].nxt()
                        blk = j * 8 + cb
                        r0 = blk * 128
                        lo = max(t0 - 1, 0); hi = min(t0 + PW + 1, L)
                        if t0 == 0:
                            em.op("pool", lambda e, hy_=hy_: e.memset(hy_[:, 0:1], 0.0), writes=[hyk])
                        if t0 + PW == L:
                            em.op("pool", lambda e, hy_=hy_: e.memset(hy_[:, PW + 1:PW + 2], 0.0), writes=[hyk])
                        o0 = lo - (t0 - 1)
                        em.dma("sp", lambda e, hy_=hy_, r0=r0, lo=lo, hi=hi, o0=o0: e.dma_start(out=hy_[:, o0:o0 + hi - lo], in_=HYT[r0:r0 + 128, lo:hi]), writes=[hyk])
                        eng = "dve"
                        em.op("act", lambda e, c_=c_, hy_=hy_, blk=blk: e.activation(out=c_[:], in_=hy_[:, 1:PW + 1], func=AF.Identity, scale=cw[:, 1, blk:blk + 1], bias=cbias[:, blk:blk + 1]), reads=[hyk, "cw", "cbias"], writes=[ck])
                        em.op(eng, lambda e, c_=c_, hy_=hy_, blk=blk: e.scalar_tensor_tensor(out=c_[:], in0=hy_[:, 0:PW], scalar=cw[:, 0, blk:blk + 1], in1=c_[:], op0=ALU.mult, op1=ALU.add), reads=[hyk, "cw", ck], writes=[ck])
                        em.op(eng, lambda e, c_=c_, hy_=hy_, blk=blk: e.scalar_tensor_tensor(out=c_[:], in0=hy_[:, 2:PW + 2], scalar=cw[:, 2, blk:blk + 1], in1=c_[:], op0=ALU.mult, op1=ALU.add), reads=[hyk, "cw", ck], writes=[ck])
                        outs.append((c_, ck))
                    u_, uk = ub.nxt()
                    em.op("pool", lambda e, u_=u_, a=outs[2][0], b=outs[1][0]: e.tensor_tensor(out=u_[:], in0=a[:], in1=b[:], op=ALU.mult), reads=[outs[2][1], outs[1][1]], writes=[uk])
                    em.store("sp", lambda e, u_=u_, cb=cb, t0=t0: e.dma_start(out=UT[cb * 128:(cb + 1) * 128, t0:t0 + PW], in_=u_[:]), reads=[uk], writes=["UT"])
                    em.store("sp", lambda e, a=outs[0][0], cb=cb, t0=t0: e.dma_start(out=X0T[cb * 128:(cb + 1) * 128, t0:t0 + PW], in_=a[:]), reads=[outs[0][1]], writes=["X0T"])
            em.flush()

          with ExitStack() as es:
            sbf = lambda n, s, d: es.enter_context(nc.sbuf_tensor(uq(n), list(s), d))
            psf = lambda n, s, d: es.enter_context(nc.psum_tensor(uq(n), list(s), d))
            cst = sbf("cst", [128, 256], F32)
            FRIb = sbf("FRIb", [128, 256], BF16); FRnIb = sbf("FRnIb", [128, 256], BF16)
            FIRb = sbf("FIRb", [128, 256], BF16); nFIb = sbf("nFIb", [128, 128], BF16)
            TWt = sbf("TWt", [128, 2, 128], F32); TWct = sbf("TWct", [128, 2, 128], F32)
            for nm, src, dst in (("FRI", FRI, FRIb), ("FRnI", FRnI, FRnIb), ("FIR", FIR, FIRb)):
                em.dma("sp", lambda e, src=src: e.dma_start(out=cst[:], in_=src), writes=["cst"])
                em.op("dve", lambda e, dst=dst: e.tensor_copy(out=dst[:], in_=cst[:]), reads=["cst"], writes=[nm])
            em.dma("sp", lambda e: e.dma_start(out=cst[:, 0:128], in_=nFI), writes=["cst"])
            em.op("dve", lambda e: e.tensor_copy(out=nFIb[:], in_=cst[:, 0:128]), reads=["cst"], writes=["nFI"])
            em.dma("sp", lambda e: e.dma_start(out=TWt[:].rearrange("p a b -> p (a b)"), in_=TW), writes=["TW"])
            em.dma("sp", lambda e: e.dma_start(out=TWct[:].rearrange("p a b -> p (a b)"), in_=TWc), writes=["TWc"])
            Min = Rot(sbf, "Min", 8, [64, 2, 128], BF16)
            P1 = Rot(sbf, "P1", 6, [128, 2, 2, 128], F32)
            P2 = Rot(sbf, "P2", 6, [128, 2, 2, 128], F32)
            B2 = Rot(sbf, "B2", 8, [128, 2, 2, 128], BF16)
            Y2 = Rot(sbf, "Y2", 4, [128, 2, 2, 128], BF16)
            D2 = Rot(sbf, "D2", 4, [128, 2, 2, 128], BF16)
            KFs = Rot(sbf, "KFs", 6, [128, 2, 2, 128], F32)
            YO = Rot(sbf, "YO", 3, [64, 2, 128], F32)
            pq = [psf(f"pq{i}", [128, 2, 2, 128], F32) for i in range(8)]
            pqk = [f"pq{i}" for i in range(8)]

            def bc4(t3, ri):
                return t3[:, ri:ri + 1, :].unsqueeze(1).to_broadcast([128, 2, 2, 128])

            def cmul(src, srck, tw, twk, p1, p1k, p2, p2k):
                em.op("dve", lambda e: e.tensor_tensor(out=p1[:], in0=src[:], in1=bc4(tw, 0), op=ALU.mult), reads=[srck, twk], writes=[p1k])
                em.op("dve", lambda e: e.tensor_tensor(out=p2[:], in0=src[:, :, ::-1, :], in1=bc4(tw, 1), op=ALU.mult), reads=[srck, twk], writes=[p2k])

            def flat(ap3):
                return ap3.rearrange("p a b -> p (a b)")

            def st2(o, ok_, b2, b2k, ch, conj, first, last):
                fi = nFIb if conj else FRIb[:, 128:256]
                nfi = FRIb[:, 128:256] if conj else nFIb
                em.op("pe", lambda e: e.matmul(flat(o), lhsT=FRIb[:, 0:128], rhs=flat(b2[:, ch, :, :]), start=first, stop=False), reads=["FRI", *b2k], writes=[ok_])
                em.op("pe", lambda e: e.matmul(o[:, 0, :], lhsT=nfi[:] if conj is False else nfi, rhs=b2[:, ch, 1, :], start=False, stop=False), reads=["FRI", "nFI", *b2k], writes=[ok_])
                em.op("pe", lambda e: e.matmul(o[:, 1, :], lhsT=fi[:] if conj else fi, rhs=b2[:, ch, 0, :], start=False, stop=last), reads=["FRI", "nFI", *b2k], writes=[ok_])

            npair = 512 if 31 not in skip else 4

            def stage1_gen(src_dram, c0, rhs1, rhs1k, tw, twk, pa, pak, res):
                m_, mk_ = Min.nxt()
                em.dma("sp", lambda e: e.dma_start(out=m_[:], in_=src_dram[c0:c0 + 2, :].rearrange("c (a b) -> a c b", b=128)), writes=[mk_])
                yield
                for ch in range(2):
                    em.op("pe", lambda e, ch=ch: e.matmul(flat(pa[:, ch, :, :]), lhsT=m_[:, ch, :], rhs=rhs1[0:64, :], start=True, stop=True), reads=[mk_, rhs1k], writes=[pak])
                yield
                p1, p1k = P1.nxt(); p2, p2k = P2.nxt(); b2, b2k = B2.nxt()
                cmul(pa, pak, tw, twk, p1, p1k, p2, p2k)
                yield
                em.op("pool", lambda e: e.tensor_tensor(out=b2[:, :, 0, :], in0=p1[:, :, 0, :], in1=p2[:, :, 0, :], op=ALU.subtract), reads=[p1k, p2k], writes=[b2k])
                em.op("pool", lambda e: e.tensor_tensor(out=b2[:, :, 1, :], in0=p1[:, :, 1, :], in1=p2[:, :, 1, :], op=ALU.add), reads=[p1k, p2k], writes=[b2k + "i"])
                res.append((b2, (b2k, b2k + "i")))
                yield

            def kf_gen(pr):
                c0 = pr * 2
                rf, rb = [], []
                pa0 = pq[(pr % 2) * 2]; pa0k = pqk[(pr % 2) * 2]
                pa1 = pq[(pr % 2) * 2 + 1]; pa1k = pqk[(pr % 2) * 2 + 1]
                g1 = stage1_gen(HF, c0, FRIb, "FRI", TWt, "TW", pa0, pa0k, rf)
                g2 = stage1_gen(HF, 1024 + c0, FRnIb, "FRnI", TWct, "TWc", pa1, pa1k, rb)
                for _ in range(4):
                    next(g1); next(g2)
                    yield
                bf_, bfk = rf[0]; bb_, bbk = rb[0]
                px = pq[4 + pr % 3]; pxk = pqk[4 + pr % 3]
                for ch in range(2):
                    st2(px[:, ch, :, :], pxk, bf_, bfk, ch, False, True, False)
                    st2(px[:, ch, :, :], pxk, bb_, bbk, ch, True, False, True)
                yield
                kf_, kfk = KFs.nxt()
                em.op("act", lambda e: e.activation(out=kf_[:], in_=px[:], func=AF.Copy), reads=[pxk], writes=[kfk])
                em.store("sp", lambda e: e.dma_start(out=KF[c0:c0 + 2].rearrange("c k a b -> k c a b"), in_=kf_[:]), reads=[kfk], writes=[f"KF{pr}"])

            def data_gen(pr):
                c0 = pr * 2
                kf_, kfk = KFs.nxt()
                em.dma("sp", lambda e: e.dma_start(out=kf_[:], in_=KF[c0:c0 + 2].rearrange("c k a b -> k c a b")), reads=[f"KF{pr}"], writes=[kfk])
                rf = []
                pa = pq[pr % 2]; pak = pqk[pr % 2]
                g1 = stage1_gen(UT, c0, FRIb, "FRI", TWt, "TW", pa, pak, rf)
                for _ in range(4):
                    next(g1)
                    yield
                b2, b2k = rf[0]
                px = pq[2 + pr % 2]; pxk = pqk[2 + pr % 2]
                for ch in range(2):
                    st2(px[:, ch, :, :], pxk, b2, b2k, ch, False, True, True)
                yield
                p1, p1k = P1.nxt(); p2, p2k = P2.nxt(); y2, y2k = Y2.nxt()
                em.op("dve", lambda e: e.tensor_tensor(out=p1[:], in0=px[:], in1=kf_[:, :, 0:1, :].to_broadcast([128, 2, 2, 128]), op=ALU.mult), reads=[pxk, kfk], writes=[p1k])
                em.op("dve", lambda e: e.tensor_tensor(out=p2[:], in0=px[:, :, ::-1, :], in1=kf_[:, :, 1:2, :].to_broadcast([128, 2, 2, 128]), op=ALU.mult), reads=[pxk, kfk], writes=[p2k])
                yield
                em.op("pool", lambda e: e.tensor_tensor(out=y2[:, :, 0, :], in0=p1[:, :, 0, :], in1=p2[:, :, 0, :], op=ALU.subtract), reads=[p1k, p2k], writes=[y2k])
                em.op("pool", lambda e: e.tensor_tensor(out=y2[:, :, 1, :], in0=p1[:, :, 1, :], in1=p2[:, :, 1, :], op=ALU.add), reads=[p1k, p2k], writes=[y2k + "i"])
                yield
                pc = pq[4 + pr % 2]; pck = pqk[4 + pr % 2]
                for ch in range(2):
                    o = flat(pc[:, ch, :, :])
                    em.op("pe", lambda e, o=o, ch=ch: e.matmul(o, lhsT=y2[:, ch, 0, :], rhs=FRnIb[:], start=True, stop=False), reads=[y2k, y2k + "i", "FRnI"], writes=[pck])
                    em.op("pe", lambda e, o=o, ch=ch: e.matmul(o, lhsT=y2[:, ch, 1, :], rhs=FIRb[:], start=False, stop=True), reads=[y2k, y2k + "i", "FIR"], writes=[pck])
                yield
                p1b, p1bk = P1.nxt(); p2b, p2bk = P2.nxt(); d2, d2k = D2.nxt()
                cmul(pc, pck, TWct, "TWc", p1b, p1bk, p2b, p2bk)
                yield
                em.op("pool", lambda e: e.tensor_tensor(out=d2[:, 0, :, :], in0=p1b[:, :, 0, :], in1=p2b[:, :, 0, :], op=ALU.subtract), reads=[p1bk, p2bk], writes=[d2k])
                em.op("pool", lambda e: e.tensor_tensor(out=d2[:, 1, :, :], in0=p1b[:, :, 1, :], in1=p2b[:, :, 1, :], op=ALU.add), reads=[p1bk, p2bk], writes=[d2k + "i"])
                yield
                py = pq[6 + pr % 2][0:64, 0, :, :]; pyk = pqk[6 + pr % 2]
                em.op("pe", lambda e: e.matmul(flat(py), lhsT=FRIb[:, 0:64], rhs=flat(d2[:, 0, :, :]), start=True, stop=False), reads=["FRI", d2k, d2k + "i"], writes=[pyk])
                em.op("pe", lambda e: e.matmul(flat(py), lhsT=FRIb[:, 128:192], rhs=flat(d2[:, 1, :, :]), start=False, stop=True), reads=["FRI", d2k, d2k + "i"], writes=[pyk])
                yield
                yo, yok = YO.nxt()
                em.op("act", lambda e: e.activation(out=yo[:], in_=py, func=AF.Copy, scale=1.0 / 16384.0), reads=[pyk], writes=[yok])
                em.store("sp", lambda e: e.dma_start(out=YCT[c0:c0 + 2, :].rearrange("c (a b) -> a c b", b=128), in_=yo[:]), reads=[yok], writes=["YCT"])

            run_pipeline(kf_gen(pr) for pr in range(npair))
            run_pipeline(data_gen(pr) for pr in range(npair))
            em.flush()
        print("inst", em.n_inst, "waits", em.n_wait)
        if stop_after <= 3:
            return nc
        TK = 256
        NTK = (L // 2) // TK
        if 4 not in skip:
          with ExitStack() as es:
            sbf = lambda n, s, d: es.enter_context(nc.sbuf_tensor(uq(n), list(s), d))
            psf = lambda n, s, d: es.enter_context(nc.psum_tensor(uq(n), list(s), d))
            wa = sbf("wa", [128, 8, D], BF16); wb = sbf("wb", [128, 8, D], BF16); wo = sbf("wo", [128, 8, D], BF16)
            for wt_, src, nm in ((wa, w_branch_a, "wa"), (wb, w_branch_b, "wb"), (wo, w_out, "wo")):
                for k in range(8):
                    em.dma("pool", lambda e, wt_=wt_, src=src, k=k: e.dma_start(out=wt_[:, k, :], in_=src[k * 128:(k + 1) * 128, :]), writes=[nm])
            ones = sbf("ones", [128, 128], F32)
            em.op("dve", lambda e: e.memset(ones[:], 1.0), writes=["ones"])
            gcol = sbf("gcol", [128, 1], F32)
            em.dma("sp", lambda e: e.dma_start(out=gcol[:], in_=hgrn_norm_g.rearrange("o v -> v o"), allow_slow_non_contiguous=True), writes=["gcol"])
            ot = Rot(sbf, "ot", 2, [128, 8, TK], F32)
            ogt = Rot(sbf, "ogt", 2, [128, 8, TK], BF16)
            sq = Rot(sbf, "sq", 2, [128, 8, TK], F32)
            rs = Rot(sbf, "rs", 2, [128, 2, TK], F32)
            tmpA = Rot(sbf, "tmpA", 2, [128, 2, TK], F32)
            At = Rot(sbf, "At", 2, [128, 8, TK], BF16)
            x0t = Rot(sbf, "x0t", 2, [128, 8, TK], F32)
            yct = Rot(sbf, "yct", 2, [128, 8, TK], F32)
            Bt = Rot(sbf, "Bt", 2, [128, 8, TK], BF16)
            gat = Rot(sbf, "gat", 2, [128, 8, 2, TK], BF16)
            tg = Rot(sbf, "tg", 2, [128, 2, TK], F32)
            mg = Rot(sbf, "mg", 2, [128, 8, TK], BF16)
            xt4 = Rot(sbf, "xt4", 2, [128, 2, D], F32)
            h1t = Rot(sbf, "h1t", 2, [128, 2, D], F32)
            pw = [psf(f"pw{i}", [128, 512], F32) for i in range(6)]
            pwi = [0]

            def npw():
                pwi[0] = (pwi[0] + 1) % 6
                return pw[pwi[0]], f"pw{pwi[0]}"

            for tk in range(NTK):
                tg0 = TOK0 + tk * TK
                o_, ok = ot.nxt(); og_, ogk = ogt.nxt(); s_, sk_ = sq.nxt(); a_, ak = At.nxt()
                em.dma("sp", lambda e, o_=o_, tk=tk: e.dma_start(out=o_[:], in_=OTh[:, :, tk * TK:(tk + 1) * TK].rearrange("h k t -> k h t")), writes=[ok])
                em.dma("sp", lambda e, og_=og_, tk=tk: e.dma_start(out=og_[:], in_=OGTh[:, tk * TK:(tk + 1) * TK].rearrange("(h k) t -> k h t", k=128)), writes=[ogk])
                em.op("act", lambda e, s_=s_, o_=o_: e.activation(out=s_[:], in_=o_[:], func=AF.Square), reads=[ok], writes=[sk_])
                for h2 in range(4):
                    p_, pk = npw()
                    for hh in range(2):
                        h = h2 * 2 + hh
                        em.op("pe", lambda e, p_=p_, s_=s_, h=h, hh=hh: e.matmul(p_[:, hh * TK:(hh + 1) * TK], lhsT=ones[:], rhs=s_[:, h, :], start=True, stop=True), reads=["ones", sk_], writes=[pk])
                    r_, rk = rs.nxt(); t_, tk_ = tmpA.nxt()
                    em.op("act", lambda e, r_=r_, p_=p_: e.activation(out=r_[:].rearrange("p a b -> p (a b)"), in_=p_[:], func=AF.Ln, scale=1.0 / 128, bias=EPS), reads=[pk], writes=[rk])
                    em.op("act", lambda e, r_=r_: e.activation(out=r_[:], in_=r_[:], func=AF.Exp, scale=-0.5), reads=[rk], writes=[rk])
                    em.op("dve", lambda e, t_=t_, o_=o_, r_=r_, h2=h2: e.scalar_tensor_tensor(out=t_[:], in0=o_[:, h2 * 2:h2 * 2 + 2, :], scalar=gcol[:, 0:1], in1=r_[:], op0=ALU.mult, op1=ALU.mult), reads=[ok, rk, "gcol"], writes=[tk_])
                    em.op("pool", lambda e, a_=a_, t_=t_, og_=og_, h2=h2: e.tensor_tensor(out=a_[:, h2 * 2:h2 * 2 + 2, :], in0=t_[:], in1=og_[:, h2 * 2:h2 * 2 + 2, :], op=ALU.mult), reads=[tk_, ogk], writes=[ak])
                if "AT" in dbg:
                    em.store("sp", lambda e, a_=a_, tk=tk: e.dma_start(out=AT[:, tk * TK:(tk + 1) * TK].rearrange("(h k) t -> k h t", k=128), in_=a_[:]), reads=[ak], writes=["AT"])
                x0_, x0k = x0t.nxt(); yc_, yck = yct.nxt(); b_, bk = Bt.nxt(); ga_, gak = gat.nxt()
                em.dma("sp", lambda e, x0_=x0_, tk=tk: e.dma_start(out=x0_[:], in_=X0Th[:, tk * TK:(tk + 1) * TK].rearrange("(h k) t -> k h t", k=128)), writes=[x0k])
                em.dma("sp", lambda e, yc_=yc_, tk=tk: e.dma_start(out=yc_[:], in_=YCTh[:, tk * TK:(tk + 1) * TK].rearrange("(h k) t -> k h t", k=128)), writes=[yck])
                for a2 in range(2):
                    em.dma("sp", lambda e, ga_=ga_, tk=tk, a2=a2: e.dma_start(out=ga_[:, :, a2, :], in_=GTh[a2 * 1024:(a2 + 1) * 1024, tk * TK:(tk + 1) * TK].rearrange("(h k) t -> k h t", k=128)), writes=[gak])
                em.op("pool", lambda e, b_=b_, x0_=x0_, yc_=yc_: e.tensor_tensor(out=b_[:], in0=x0_[:], in1=yc_[:], op=ALU.mult), reads=[x0k, yck], writes=[bk])
                m_, mk_ = mg.nxt()
                for db in range(8):
                    p_, pk = npw()
                    for k in range(8):
                        em.op("pe", lambda e, p_=p_, k=k, db=db, a_=a_: e.matmul(p_[:, 0:TK], lhsT=wa[:, k, db * 128:(db + 1) * 128], rhs=a_[:, k, :], start=(k == 0), stop=(k == 7)), reads=["wa", ak], writes=[pk])
                    for k in range(8):
                        em.op("pe", lambda e, p_=p_, k=k, db=db, b_=b_: e.matmul(p_[:, TK:2 * TK], lhsT=wb[:, k, db * 128:(db + 1) * 128], rhs=b_[:, k, :], start=(k == 0), stop=(k == 7)), reads=["wb", bk], writes=[pk])
                    t_, tk_ = tg.nxt()
                    em.op("dve", lambda e, t_=t_, p_=p_, ga_=ga_, db=db: e.tensor_tensor(out=t_[:].rearrange("p a b -> p (a b)"), in0=p_[:], in1=ga_[:, db, :, :].rearrange("p a b -> p (a b)"), op=ALU.mult), reads=[pk, gak], writes=[tk_])
                    em.op("pool", lambda e, m_=m_, t_=t_, db=db: e.tensor_tensor(out=m_[:, db, :], in0=t_[:, 0, :], in1=t_[:, 1, :], op=ALU.add), reads=[tk_], writes=[mk_])
                if "MG" in dbg:
                    em.store("sp", lambda e, m_=m_, tk=tk: e.dma_start(out=MG[:, tk * TK:(tk + 1) * TK].rearrange("(h k) t -> k h t", k=128), in_=m_[:]), reads=[mk_], writes=["MGd"])
                x_, xk = xt4.nxt(); h_, hk = h1t.nxt()
                em.dma("sp", lambda e, x_=x_, tk=tk: e.dma_start(out=x_[:], in_=xh[tk * TK:(tk + 1) * TK, :].rearrange("(s p) d -> p s d", p=128)), writes=[xk])
                for s in range(2):
                    for hf in range(2):
                        p_, pk = npw()
                        for k in range(8):
                            em.op("pe", lambda e, p_=p_, k=k, s=s, hf=hf, m_=m_: e.matmul(p_[:], lhsT=m_[:, k, s * 128:(s + 1) * 128], rhs=wo[:, k, hf * 512:(hf + 1) * 512], start=(k == 0), stop=(k == 7)), reads=["wo", mk_], writes=[pk])
                        em.op("dve", lambda e, h_=h_, p_=p_, x_=x_, s=s, hf=hf: e.tensor_tensor(out=h_[:, s, hf * 512:(hf + 1) * 512], in0=p_[:], in1=x_[:, s, hf * 512:(hf + 1) * 512], op=ALU.add), reads=[pk, xk], writes=[hk])
                em.store("sp", lambda e, h_=h_, tk=tk: e.dma_start(out=H1[tk * TK:(tk + 1) * TK, :].rearrange("(s p) d -> p s d", p=128), in_=h_[:]), reads=[hk], writes=["H1d"])
            em.flush()
        print("inst", em.n_inst, "waits", em.n_wait)
        if stop_after <= 4:
            return nc
        if 5 not in skip:
          with ExitStack() as es:
            sbf = lambda n, s, d: es.enter_context(nc.sbuf_tensor(uq(n), list(s), d))
            psf = lambda n, s, d: es.enter_context(nc.psum_tensor(uq(n), list(s), d))
            idf5 = sbf("idf5", [128, 128], F32); idb5 = sbf("idb5", [128, 128], BF16)
            em.dma("sp", lambda e: e.dma_start(out=idf5[:], in_=ident), writes=["idf5"])
            em.op("dve", lambda e: e.tensor_copy(out=idb5[:], in_=idf5[:]), reads=["idf5"], writes=["idb5"])
            for r in range(16):
                em.dma("pool", lambda e, r=r: e.dma_start(out=VBF[r * 1024:(r + 1) * 1024, :], in_=peer_v[r * 1024:(r + 1) * 1024, :]), writes=["VBF"])
            urow = Rot(sbf, "urow", 5, [128, D], BF16)
            uts = Rot(sbf, "uts", 4, [128, 8, 128], BF16)
            pU = [psf(f"pU{i}", [128, 8, 128], BF16) for i in range(2)]
            nj = 128 if 51 not in skip else 2
            def u_item(j):
                u_, uk = urow.nxt(); t_, tk_ = uts.nxt()
                em.dma("pool", lambda e: e.dma_start(out=u_[:], in_=peer_u.rearrange("(i j) d -> j i d", j=128)[j]), writes=[uk])
                yield
                yield
                p_ = pU[j % 2]; pk = f"pU{j % 2}"
                for k in range(8):
                    em.op("pe", lambda e, k=k: e.transpose(out=p_[:, k, :], in_=u_[:, k * 128:(k + 1) * 128], identity=idb5[:]), reads=[uk, "idb5"], writes=[pk])
                yield
                em.op("act" if j % 2 else "dve", lambda e: (e.activation(out=t_[:], in_=p_[:], func=AF.Copy) if j % 2 else e.tensor_copy(out=t_[:], in_=p_[:])), reads=[pk], writes=[tk_])
                em.store("sp", lambda e: e.dma_start(out=UTS[j], in_=t_[:]), reads=[tk_], writes=["UTS"])
            run_pipeline(u_item(j) for j in range(nj))
            em.flush()

          with ExitStack() as es:
            sbf = lambda n, s, d: es.enter_context(nc.sbuf_tensor(uq(n), list(s), d))
            psf = lambda n, s, d: es.enter_context(nc.psum_tensor(uq(n), list(s), d))
            wq = sbf("wq", [128, 8, 2048], BF16)
            for k in range(8):
                em.dma("pool", lambda e, k=k: e.dma_start(out=wq[:, k, :], in_=peer_w_q[k * 128:(k + 1) * 128, :]), writes=["wq"])
            idf = sbf("idf", [128, 128], F32)
            em.dma("sp", lambda e: e.dma_start(out=idf[:], in_=ident), writes=["idf"])
            iot = sbf("iot", [128, 128], F32)
            em.dma("sp", lambda e: e.dma_start(out=iot[:], in_=iota), writes=["iot"])
            gff = sbf("gff", [128, D], F32); gfin = sbf("gfin", [128, D], F32)
            em.dma("sp", lambda e: e.dma_start(out=gff[:], in_=norm_ffn_g.partition_broadcast(128)), writes=["gff"])
            em.dma("sp", lambda e: e.dma_start(out=gfin[:], in_=norm_final_g.partition_broadcast(128)), writes=["gfin"])
            skT = sbf("skT", [128, 16, 128], BF16)
            h1 = Rot(sbf, "h1", 2, [128, 2, D], F32)
            ss5 = sbf("ss5", [128, 2], F32); rstd5 = sbf("rstd5", [128, 2], F32)
            xn2 = sbf("xn2", [128, 2, D], F32)
            sqj = xn2[:, 1, :]
            xn2Ts = Rot(sbf, "xn2T", 2, [128, 8, TK], BF16)
            qT = sbf("qT", [128, 16, TK], BF16)
            scr = sbf("scr", [128, 16, 128], F32)
            skf = scr
            em.dma("sp", lambda e: e.dma_start(out=skf[:], in_=peer_sk.rearrange("j n c -> n j c")), writes=["scr"])
            scr2 = scr
            vals = sbf("vals", [128, 16, 16], F32)
            idxu = sbf("idxu", [128, 16, 16], U32)
            idxf = sbf("idxf", [128, 16, 16], F32)
            Cg = sbf("Cg", [128, 8, 256], F32); Cg2 = Cg
            cv = sbf("cv", [128, 8, 16], F32)
            posu = sbf("posu", [128, 8, 16], U32); pa_u = sbf("pa_u", [128, 8, 16], U32); pb_u = sbf("pb_u", [128, 8, 16], U32)
            paf = sbf("paf", [128, 8, 16], F32); pbf = sbf("pbf", [128, 8, 16], F32)
            eq = scr[:].rearrange("p a b -> p (a b)").rearrange("p (h k a) -> p h k a", h=8, k=16)
            ik = sbf("ik", [128, 8, 16], F32); jk = sbf("jk", [128, 8, 16], F32)
            ee = sbf("ee", [128, 8, 16], F32); zz = sbf("zz", [128, 8], F32); gg = sbf("gg", [128, 8, 16], F32)
            ikT = sbf("ikT", [128, TK], F32); jkT = sbf("jkT", [128, TK], F32); gT = sbf("gT", [128, TK], F32)
            njkT = sbf("njkT", [128, TK], F32)
            Ra = Rot(sbf, "Ra", 3, [128, 128], F32)
            Lt = Rot(sbf, "Lt", 8, [128, 128], BF16); Rt = Rot(sbf, "Rt", 8, [128, 128], BF16)
            Gs = sbf("Gs", [128, TK, 128], BF16)
            utj = Rot(sbf, "utj", 5, [128, 8, 128], BF16); vj = Rot(sbf, "vj", 5, [128, D], BF16)
            gact = Rot(sbf, "gact", 3, [128, TK], F32)
            ATj = Rot(sbf, "ATj", 3, [128, TK], BF16)

            acc = [psf(f"acc{i}", [128, 512], F32) for i in range(4)]
            pw = [psf(f"pw{i}", [128, 512], F32) for i in range(4)]
            pwi = [0]

            def npw():
                pwi[0] = (pwi[0] + 1) % 4
                return pw[pwi[0]], f"pw{pwi[0]}"
            pwd = [0]; pw2 = [0]

            def npw_d():
                pwd[0] = (pwd[0] + 1) % 2
                return pw[pwd[0]], f"pw{pwd[0]}"

            def npw2():
                pw2[0] = (pw2[0] + 1) % 2
                return pw[2 + pw2[0]], f"pw{2 + pw2[0]}"

            for j4 in range(4):
                p_, pk = npw()
                for jj in range(4):
                    j = j4 * 4 + jj
                    em.op("pe", lambda e, p_=p_, jj=jj, j=j: e.transpose(out=p_[:, jj * 128:(jj + 1) * 128], in_=skf[:, j, :], identity=idf[:]), reads=["scr", "idf"], writes=[pk])
                em.op("dve", lambda e, p_=p_, j4=j4: e.tensor_copy(out=skT[:, j4 * 4:(j4 + 1) * 4, :].rearrange("p a b -> p (a b)"), in_=p_[:]), reads=[pk], writes=["skT"])

            ntk = NTK if 52 not in skip else 1
            import os
            P5STOP = int(os.environ.get("P5STOP", "99"))
            def prep_gen(tk, st):
                thunks = []
                E_op = lambda *a_, **k_: thunks.append((em.op, a_, k_))
                E_dma = lambda *a_, **k_: thunks.append((em.dma, a_, k_))
                h_, hk = h1.nxt()
                xT2, xT2k = xn2Ts.nxt()
                st[tk] = (h_, hk, xT2, xT2k)
                E_dma("sp", lambda e, h_=h_, tk=tk: e.dma_start(out=h_[:], in_=H1[tk * TK:(tk + 1) * TK, :].rearrange("(s p) d -> p s d", p=128)), writes=[hk])
                for s in range(2):
                    E_op("act", lambda e, h_=h_, s=s: e.activation(out=sqj, in_=h_[:, s, :], func=AF.Square, accum_out=ss5[:, s:s + 1]), reads=[hk], writes=["xn21", "ss5"])
                E_op("act", lambda e: e.activation(out=rstd5[:], in_=ss5[:], func=AF.Ln, scale=1.0 / D, bias=EPS), reads=["ss5"], writes=["rstd5"])
                E_op("act", lambda e: e.activation(out=rstd5[:], in_=rstd5[:], func=AF.Exp, scale=-0.5), reads=["rstd5"], writes=["rstd5"])
                for s in range(2):
                    E_op("dve", lambda e, h_=h_, s=s: e.scalar_tensor_tensor(out=xn2[:, s, :], in0=h_[:, s, :], scalar=rstd5[:, s:s + 1], in1=gff[:], op0=ALU.mult, op1=ALU.mult), reads=[hk, "rstd5", "gff"], writes=[f"xn2{s}"])
                    for k4 in range(2):
                        p_, pk = npw2()
                        for kk in range(4):
                            k = k4 * 4 + kk
                            E_op("pe", lambda e, p_=p_, kk=kk, k=k, s=s: e.transpose(out=p_[:, kk * 128:(kk + 1) * 128], in_=xn2[:, s, k * 128:(k + 1) * 128], identity=idf[:]), reads=[f"xn2{s}", "idf"], writes=[pk])
                        E_op("act", lambda e, p_=p_, k4=k4, s=s: e.activation(out=xT2[:, k4 * 4:(k4 + 1) * 4, s * 128:(s + 1) * 128], in_=p_[:].rearrange("p (a b) -> p a b", a=4), func=AF.Copy), reads=[pk], writes=[xT2k])
                for j2 in range(8):
                    p_, pk = npw2()
                    for jj in range(2):
                        j = j2 * 2 + jj
                        for k in range(8):
                            E_op("pe", lambda e, p_=p_, jj=jj, j=j, k=k: e.matmul(p_[:, jj * TK:(jj + 1) * TK], lhsT=wq[:, k, j * 128:(j + 1) * 128], rhs=xT2[:, k, :], start=(k == 0), stop=(k == 7)), reads=["wq", xT2k], writes=[pk])
                    E_op("dve", lambda e, p_=p_, j2=j2: e.tensor_copy(out=qT[:, j2 * 2:j2 * 2 + 2, :].rearrange("p a b -> p (a b)"), in_=p_[:]), reads=[pk], writes=["qT"])
                for s in range(2):
                    for j4 in range(4):
                        p_, pk = npw2()
                        for jj in range(4):
                            j = j4 * 4 + jj
                            E_op("pe", lambda e, p_=p_, jj=jj, j=j, s=s: e.matmul(p_[:, jj * 128:(jj + 1) * 128], lhsT=qT[:, j, s * 128:(s + 1) * 128], rhs=skT[:, j, :], start=True, stop=True), reads=["qT", "skT"], writes=[pk])
                        E_op("act", lambda e, p_=p_, j4=j4: e.activation(out=scr[:, j4 * 4:(j4 + 1) * 4, :].rearrange("p a b -> p (a b)"), in_=p_[:], func=AF.Copy), reads=[pk], writes=["scr"])
                    for j in range(16):
                        E_op("dve", lambda e, j=j: e.max(out=vals[:, j, 0:8], in_=scr[:, j, :]), reads=["scr"], writes=["vals"])
                        E_op("dve", lambda e, j=j: e.max_index(out=idxu[:, j, 0:8], in_max=vals[:, j, 0:8], in_values=scr[:, j, :]), reads=["scr", "vals"], writes=["idxu"])
                        E_op("dve", lambda e, j=j: e.match_replace(out=scr2[:, j, :], in_to_replace=vals[:, j, 0:8], in_values=scr[:, j, :], imm_value=-1e30), reads=["scr", "vals"], writes=["scr"])
                        E_op("dve", lambda e, j=j: e.max(out=vals[:, j, 8:16], in_=scr2[:, j, :]), reads=["scr"], writes=["vals"])
                        E_op("dve", lambda e, j=j: e.max_index(out=idxu[:, j, 8:16], in_max=vals[:, j, 8:16], in_values=scr2[:, j, :]), reads=["scr", "vals"], writes=["idxu"])
                    E_op("dve", lambda e: e.tensor_copy(out=idxf[:], in_=idxu[:]), reads=["idxu"], writes=["idxf"])
                    v4 = vals[:].rearrange("p (h t) a -> p h t a", t=2)
                    i4 = idxf[:].rearrange("p (h t) a -> p h t a", t=2)
                    E_op("dve", lambda e, v4=v4: e.tensor_tensor(out=Cg[:].rearrange("p h (a b) -> p h a b", b=16), in0=v4[:, :, 0, :].unsqueeze(3).to_broadcast([128, 8, 16, 16]), in1=v4[:, :, 1, :].unsqueeze(2).to_broadcast([128, 8, 16, 16]), op=ALU.add), reads=["vals"], writes=["Cg"])
                    for h in range(8):
                        E_op("dve", lambda e, h=h: e.max(out=cv[:, h, 0:8], in_=Cg[:, h, :]), reads=["Cg"], writes=["cv"])
                        E_op("dve", lambda e, h=h: e.max_index(out=posu[:, h, 0:8], in_max=cv[:, h, 0:8], in_values=Cg[:, h, :]), reads=["Cg", "cv"], writes=["posu"])
                        E_op("dve", lambda e, h=h: e.match_replace(out=Cg2[:, h, :], in_to_replace=cv[:, h, 0:8], in_values=Cg[:, h, :], imm_value=-1e30), reads=["Cg", "cv"], writes=["Cg"])
                        E_op("dve", lambda e, h=h: e.max(out=cv[:, h, 8:16], in_=Cg2[:, h, :]), reads=["Cg"], writes=["cv"])
                        E_op("dve", lambda e, h=h: e.max_index(out=posu[:, h, 8:16], in_max=cv[:, h, 8:16], in_values=Cg2[:, h, :]), reads=["Cg", "cv"], writes=["posu"])
                    E_op("dve", lambda e: e.tensor_single_scalar(out=pa_u[:], in_=posu[:], scalar=4, op=ALU.logical_shift_right), reads=["posu"], writes=["pa_u"])
                    E_op("dve", lambda e: e.tensor_single_scalar(out=pb_u[:], in_=posu[:], scalar=15, op=ALU.bitwise_and), reads=["posu"], writes=["pb_u"])
                    E_op("dve", lambda e: e.tensor_copy(out=paf[:], in_=pa_u[:]), reads=["pa_u"], writes=["paf"])
                    E_op("dve", lambda e: e.tensor_copy(out=pbf[:], in_=pb_u[:]), reads=["pb_u"], writes=["pbf"])
                    io16 = iot[:, 0:16].unsqueeze(1).unsqueeze(1).to_broadcast([128, 8, 16, 16])
                    for (pf, pfk, plane, dst, dstk) in ((paf, "paf", 0, ik, "ik"), (pbf, "pbf", 1, jk, "jk")):
                        E_op("dve", lambda e, pf=pf: e.tensor_tensor(out=eq, in0=pf[:].unsqueeze(3).to_broadcast([128, 8, 16, 16]), in1=io16, op=ALU.is_equal), reads=[pfk, "iot"], writes=["scr"])
                        E_op("dve", lambda e, plane=plane, i4=i4: e.tensor_tensor(out=eq, in0=eq, in1=i4[:, :, plane, :].unsqueeze(2).to_broadcast([128, 8, 16, 16]), op=ALU.mult), reads=["scr", "idxf"], writes=["scr"])
                        E_op("dve", lambda e, dst=dst: e.tensor_reduce(out=dst[:], in_=eq, axis=AX.X, op=ALU.add), reads=["scr"], writes=[dstk])
                    E_op("dve", lambda e: e.tensor_tensor(out=ee[:], in0=cv[:], in1=cv[:, :, 0:1].to_broadcast([128, 8, 16]), op=ALU.subtract), reads=["cv"], writes=["ee"])
                    E_op("act", lambda e: e.activation(out=ee[:], in_=ee[:], func=AF.Exp), reads=["ee"], writes=["ee"])
                    E_op("dve", lambda e: e.tensor_reduce(out=zz[:], in_=ee[:], axis=AX.X, op=ALU.add), reads=["ee"], writes=["zz"])
                    E_op("dve", lambda e: e.reciprocal(out=zz[:], in_=zz[:]), reads=["zz"], writes=["zz"])
                    E_op("dve", lambda e: e.tensor_tensor(out=gg[:], in0=ee[:], in1=zz[:].unsqueeze(2).to_broadcast([128, 8, 16]), op=ALU.mult), reads=["ee", "zz"], writes=["gg"])
                    p_, pk = npw2()
                    for n_, (src, srck) in enumerate(((ik, "ik"), (jk, "jk"), (gg, "gg"))):
                        E_op("pe", lambda e, p_=p_, n_=n_, src=src: e.transpose(out=p_[:, n_ * 128:(n_ + 1) * 128], in_=src[:].rearrange("p h k -> p (h k)"), identity=idf[:]), reads=[srck, "idf"], writes=[pk])
                    E_op("dve", lambda e, p_=p_, s=s: e.tensor_copy(out=ikT[:, s * 128:(s + 1) * 128], in_=p_[:, 0:128]), reads=[pk], writes=["ikT"])
                    E_op("dve", lambda e, p_=p_, s=s: e.tensor_copy(out=jkT[:, s * 128:(s + 1) * 128], in_=p_[:, 128:256]), reads=[pk], writes=["jkT"])
                    E_op("dve", lambda e, p_=p_, s=s: e.tensor_copy(out=gT[:, s * 128:(s + 1) * 128], in_=p_[:, 256:384]), reads=[pk], writes=["gT"])
                for i_, (f_, a_, k_) in enumerate(thunks):
                    f_(*a_, **k_)
                    if i_ % 5 == 4:
                        yield

            sts = {}
            for _ in prep_gen(0, sts):
                pass
            for tk in range(ntk):
                h_, hk, cur_xT, cur_xTk = sts[tk]
                def g_gen(t4):
                    lr = []
                    for tq in range(4):
                        t = t4 * 4 + tq
                        l_, lk = Lt.nxt(); r_, rk = Rt.nxt()
                        em.op("dve", lambda e, l_=l_, t=t: e.tensor_scalar(out=l_[:], in0=iot[:], scalar1=ikT[:, t:t + 1], scalar2=gT[:, t:t + 1], op0=ALU.is_equal, op1=ALU.mult), reads=["iot", "ikT", "gT"], writes=[lk])
                        em.op("act", lambda e, r_=r_, t=t: e.activation(out=r_[:], in_=iot[:], func=AF.Derivative_Erf, scale=7.0, bias=njkT[:, t:t + 1]), reads=["iot", "njkT"], writes=[rk])
                        lr.append((l_, lk, r_, rk))
                    yield
                    p_, pk = npw()
                    for tq in range(4):
                        l_, lk, r_, rk = lr[tq]
                        em.op("pe", lambda e, tq=tq, l_=l_, r_=r_: e.matmul(p_[:, tq * 128:(tq + 1) * 128], lhsT=l_[:], rhs=r_[:], start=True, stop=True), reads=[lk, rk], writes=[pk])
                    yield
                    em.op("dve", lambda e: e.tensor_copy(out=Gs[:, t4 * 4:(t4 + 1) * 4, :].rearrange("p a b -> p (a b)"), in_=p_[:]), reads=[pk], writes=["Gs"])
                em.op("dve", lambda e: e.tensor_scalar(out=njkT[:], in0=jkT[:], scalar1=-7.0, scalar2=None, op0=ALU.mult), reads=["jkT"], writes=["njkT"])
                em.op("dve", lambda e: e.tensor_scalar(out=gT[:], in0=gT[:], scalar1=1.0 / 1.125, scalar2=None, op0=ALU.mult), reads=["gT"], writes=["gT"])
                run_pipeline(g_gen(t4) for t4 in range(TK // 4))
                def dense_gen(j, xT_=cur_xT, xTk_=cur_xTk):
                    u_, uk = utj.nxt(); v_, vk = vj.nxt()
                    em.dma("sp", lambda e: e.dma_start(out=u_[:], in_=UTS[j]), writes=[uk])
                    em.dma("sp", lambda e: e.dma_start(out=v_[:], in_=VBF.rearrange("(i j) d -> j i d", j=128)[j]), writes=[vk])
                    yield
                    yield
                    p_, pk = npw_d()
                    for k in range(8):
                        em.op("pe", lambda e, k=k: e.matmul(p_[:, 0:TK], lhsT=u_[:, k, :], rhs=xT_[:, k, :], start=(k == 0), stop=(k == 7)), reads=[uk, xTk_], writes=[pk])
                    yield
                    ga_, gak = gact.nxt(); a_, ak = ATj.nxt()
                    em.op("act", lambda e: e.activation(out=ga_[:], in_=p_[:, 0:TK], func=AF.Gelu_apprx_tanh), reads=[pk], writes=[gak])
                    yield
                    em.op("dve" if j % 2 else "pool", lambda e: e.tensor_tensor(out=a_[:], in0=ga_[:], in1=Gs[:, :, j], op=ALU.mult), reads=[gak, "Gs"], writes=[ak])
                    yield
                    for s in range(2):
                        for hf in range(2):
                            em.op("pe", lambda e, s=s, hf=hf: e.matmul(acc[s * 2 + hf][:], lhsT=a_[:, s * 128:(s + 1) * 128], rhs=v_[:, hf * 512:(hf + 1) * 512], start=(j == 0), stop=(j == 127)), reads=[ak, vk], writes=[f"acc{s * 2 + hf}"])
                import itertools
                nxt_prep = [prep_gen(tk + 1, sts)] if tk + 1 < ntk else []
                run_pipeline(itertools.chain((dense_gen(j) for j in range(128)), nxt_prep) if os.environ.get("NOOVL") else itertools.chain(nxt_prep, (dense_gen(j) for j in range(128))), max_active=24)
                for s in range(2):
                    for hf in range(2):
                        em.op("dve", lambda e, s=s, hf=hf, h_=h_: e.tensor_tensor(out=h_[:, s, hf * 512:(hf + 1) * 512], in0=acc[s * 2 + hf][:], in1=h_[:, s, hf * 512:(hf + 1) * 512], op=ALU.add), reads=[f"acc{s * 2 + hf}", hk], writes=[hk])
                for s in range(2):
                    em.op("act", lambda e, s=s, h_=h_: e.activation(out=sqj, in_=h_[:, s, :], func=AF.Square, accum_out=ss5[:, s:s + 1]), reads=[hk], writes=["xn21", "ss5"])
                em.op("act", lambda e: e.activation(out=rstd5[:], in_=ss5[:], func=AF.Ln, scale=1.0 / D, bias=EPS), reads=["ss5"], writes=["rstd5"])
                em.op("act", lambda e: e.activation(out=rstd5[:], in_=rstd5[:], func=AF.Exp, scale=-0.5), reads=["rstd5"], writes=["rstd5"])
                for s in range(2):
                    em.op("dve", lambda e, s=s, h_=h_: e.scalar_tensor_tensor(out=xn2[:, s, :], in0=h_[:, s, :], scalar=rstd5[:, s:s + 1], in1=gfin[:], op0=ALU.mult, op1=ALU.mult), reads=[hk, "rstd5", "gfin"], writes=[f"xn2{s}"])
                em.store("sp", lambda e, tk=tk: e.dma_start(out=out[tk * TK:(tk + 1) * TK, :].rearrange("(s p) d -> p s d", p=128), in_=xn2[:]), reads=["xn20", "xn21"], writes=[f"out{tk}"])
            em.flush()
        print("inst", em.n_inst, "waits", em.n_wait)
    return nc


_IN_NAMES = None


def core_inputs(inp, core):
    b, g = core // 2, core % 2
    m = dict(host_consts())
    xb = inp["x"][b]
    w_in = inp["w_in"][0]
    conv_w = inp["hyena_conv_w"][0]
    w3 = inp["filt_w3"][0]
    dec = inp["filt_decay"].reshape(2048)
    if g == 1:
        xb = xb[::-1]
        w_in = np.concatenate([w_in[:, 0:1024], w_in[:, 2048:3072], w_in[:, 1024:2048], w_in[:, 3072:]], axis=1)
        conv_w = conv_w[::-1]
        w3 = np.concatenate([w3[:, 1024:], w3[:, :1024]], axis=1)
        dec = np.concatenate([dec[1024:], dec[:1024]])
    m["x"] = np.ascontiguousarray(xb)
    m["xh"] = np.ascontiguousarray(xb[:L // 2])
    sel = np.zeros((128, 2), np.float32); sel[:, g] = 1.0
    m["sel"] = sel
    m["norm_mix_g"] = np.ascontiguousarray(inp["norm_mix_g"].reshape(1, D))
    m["w_in"] = np.ascontiguousarray(w_in)
    m["hgrn_lb_logits"] = np.ascontiguousarray(inp["hgrn_lb_logits"])
    m["filt_w1"] = np.ascontiguousarray(inp["filt_w1"][0]); m["filt_w2"] = np.ascontiguousarray(inp["filt_w2"][0])
    m["filt_w3"] = np.ascontiguousarray(w3)
    m["filt_vec"] = np.ascontiguousarray(np.stack([inp["filt_b1"][0], inp["filt_freq1"][0], inp["filt_b2"][0], inp["filt_freq2"][0]], 0))
    m["filt_decay"] = np.ascontiguousarray(dec.reshape(1, 2048))
    m["hyena_bias"] = np.ascontiguousarray(inp["hyena_bias"].reshape(1, 1024))
    m["conv_w"] = np.ascontiguousarray(conv_w); m["conv_b"] = np.ascontiguousarray(inp["hyena_conv_b"].reshape(1, 3072))
    m["hgrn_norm_g"] = np.ascontiguousarray(inp["hgrn_norm_g"].reshape(1, 128))
    for k in ("w_branch_a", "w_branch_b", "w_out", "peer_w_q", "peer_u", "peer_v"):
        m[k] = np.ascontiguousarray(inp[k][0])
    m["norm_ffn_g"] = np.ascontiguousarray(inp["norm_ffn_g"].reshape(1, D)); m["norm_final_g"] = np.ascontiguousarray(inp["norm_final_g"].reshape(1, D))
    m["peer_sk"] = np.ascontiguousarray(inp["peer_subkeys"][0].reshape(16, 128, 128))
    return m


def kernel(**inputs):
    inp = {k: np.asarray(v) for k, v in inputs.items()}
    nc = build()
    in_maps = [core_inputs(inp, c) for c in range(8)]
    res = run_bass_kernel_spmd(nc, in_maps, core_ids=list(range(8)))
    out = np.zeros((4, L, D), np.float32)
    for c in range(8):
        b, g = c // 2, c % 2
        r = np.asarray(res.results[c]["out"])
        if g == 0:
            out[b, :L // 2] = r
        else:
            out[b, L // 2:] = r[::-1]
    return out
```
